# Optimizing a Trainium2 kernel written in Bass

```python
import math
import jax, jax.numpy as jnp
from jax import lax
import numpy as np

D_MODEL = 1024
BATCH = 16
SEQ = 256
DEPTH = 2
DEC_BATCH = 4
DEC_SEQ = 1024
PAST_LEN = 512

GRID_W = 64
Q_BLOCK = 128
ROPE_THETA = 10000.0
EPS = 1e-6
RWKV_GN_EPS = 64e-5
HA = 4
DH_A = 64
DV_A = 2 * DH_A
HB = 8
HKV_B = 2
DH_B = 64
HC = 4
DK_C = 128
DV_C = 128
HGRN_CHUNK = 32
HD = 8
DH_D = 64
W_LORA = 64
A_LORA = 64
G_LORA = 128
D_FF = 2816

N_EVEN = (DEPTH + 1) // 2
N_ODD = DEPTH // 2
D_A = HA * DV_A
D_B = HB * DH_B
D_C = HC * DV_C
D_D = HD * DH_D
EVEN_SIZES = (HA * 2 * DH_A, HA * 2 * DH_A, HA * DV_A, HB * DH_B, HKV_B * DH_B, HKV_B * DH_B)
EVEN_PROJ = sum(EVEN_SIZES)
HGRN_SIZES = (HC * DK_C, HC * DK_C, HC * DK_C, HC * DV_C, D_C)
HGRN_PROJ = sum(HGRN_SIZES)
RWKV_SIZES = (D_D, D_D, D_D, 2 * W_LORA, A_LORA, G_LORA)
RWKV_PROJ = sum(RWKV_SIZES)
ODD_PROJ = HGRN_PROJ + RWKV_PROJ
F32 = jnp.float32

kernel_name = 'hybrid_diffusion_prefix_trunk_step'


def split_last(x, sizes):
    outs, start = [], 0
    for n in sizes:
        outs.append(x[..., start:start + n])
        start += n
    return outs


def rms_norm(x, g):
    xf = x.astype(F32)
    y = xf * lax.rsqrt(jnp.mean(xf * xf, axis=-1, keepdims=True) + EPS)
    return (y * g.astype(F32)).astype(x.dtype)


def centred_conv3(x, w, b):
    xp = jnp.pad(x, ((0, 0), (1, 1), (0, 0)))
    return w[0] * xp[:, :-2] + w[1] * xp[:, 1:-1] + w[2] * xp[:, 2:] + b


def token_shift_mix(x, mu):
    xp = jnp.pad(x, ((0, 0), (1, 1), (0, 0)))
    return x + mu * (0.5 * (xp[:, :-2] + xp[:, 2:]) - x)


def grid_positions(rows):
    r = jnp.broadcast_to(jnp.arange(rows, dtype=jnp.int32)[:, None], (rows, GRID_W)).reshape(-1)
    c = jnp.broadcast_to(jnp.arange(GRID_W, dtype=jnp.int32)[None, :], (rows, GRID_W)).reshape(-1)
    return r, c


def rope_axis(x, pos):
    half = x.shape[-1] // 2
    freq = jnp.power(ROPE_THETA, -jnp.arange(half, dtype=F32) / half)
    ang = pos.astype(F32)[:, None] * freq[None, :]
    cos, sin = jnp.cos(ang)[:, None, :], jnp.sin(ang)[:, None, :]
    xf = x.astype(F32)
    x1, x2 = xf[..., :half], xf[..., half:]
    return jnp.concatenate([x1 * cos - x2 * sin, x1 * sin + x2 * cos], axis=-1).astype(x.dtype)


def rope_2d(x, rows, cols):
    d = x.shape[-1]
    return jnp.concatenate([rope_axis(x[..., :d // 2], rows), rope_axis(x[..., d // 2:], cols)], axis=-1)


def block_attend(q, k, v):
    b, g, r, s, d = q.shape
    nb = s // Q_BLOCK
    scale = d ** -0.5
    qb = jnp.moveaxis(q.reshape(b, g, r, nb, Q_BLOCK, d), 3, 0)

    def one_block(qi):
        sc = jnp.einsum('bgrqd,bgtd->bgrqt', qi, k).astype(F32) * scale
        p = jax.nn.softmax(sc, axis=-1).astype(v.dtype)
        return jnp.einsum('bgrqt,bgte->bgrqe', p, v)

    out = lax.map(one_block, qb)
    return jnp.moveaxis(out, 0, 3).reshape(b, g, r, s, v.shape[-1])


def even_mixer(h, P, i, lidx, pos, ctx):
    bsz, s, _ = h.shape
    qa, ka, va, qb, kb, vb = split_last(h @ P['ev_w_in'][i], EVEN_SIZES)
    qa = qa.reshape(bsz, s, 2 * HA, DH_A)
    ka = ka.reshape(bsz, s, 2 * HA, DH_A)
    qb = rms_norm(qb.reshape(bsz, s, HB, DH_B), P['b_q_norm_g'][i])
    kb = rms_norm(kb.reshape(bsz, s, HKV_B, DH_B), P['b_k_norm_g'][i])
    if pos is not None:
        qa, ka, qb, kb = [rope_2d(t, pos[0], pos[1]) for t in (qa, ka, qb, kb)]

    def heads_first(t):
        return jnp.transpose(t, (0, 2, 1, 3))

    own = (heads_first(ka.reshape(bsz, s, HA, 2 * DH_A)),
           heads_first(va.reshape(bsz, s, HA, DV_A)),
           heads_first(kb),
           heads_first(vb.reshape(bsz, s, HKV_B, DH_B)))
    if ctx is None:
        k_a, v_a, k_b, v_b = own
    else:
        k_a, v_a, k_b, v_b = [jnp.concatenate([cx, ow], axis=2) for cx, ow in zip(ctx, own)]

    qa = heads_first(qa.reshape(bsz, s, HA, 2 * DH_A))
    a1 = block_attend(qa[:, :, None, :, :DH_A], k_a[..., :DH_A], v_a)[:, :, 0]
    a2 = block_attend(qa[:, :, None, :, DH_A:], k_a[..., DH_A:], v_a)[:, :, 0]
    lam_init = 0.8 - 0.6 * math.exp(-0.3 * lidx)
    lq1, lk1, lq2, lk2 = P['a_lambda'][i].astype(F32)
    lam = jnp.exp(jnp.sum(lq1 * lk1)) - jnp.exp(jnp.sum(lq2 * lk2)) + lam_init
    oa = rms_norm(a1.astype(F32) - lam * a2.astype(F32), P['a_subln_g'][i]) * (1.0 - lam_init)
    oa = jnp.transpose(oa, (0, 2, 1, 3)).reshape(bsz, s, D_A).astype(h.dtype)

    qg = heads_first(qb).reshape(bsz, HKV_B, HB // HKV_B, s, DH_B)
    ob = block_attend(qg, k_b, v_b).reshape(bsz, HB, s, DH_B)
    ob = jnp.transpose(ob, (0, 2, 1, 3)).reshape(bsz, s, D_B)
    return jnp.concatenate([oa, ob], axis=-1) @ P['ev_w_out'][i], own


def hgrn_chunk_scan(q, k, logf, v, s0):
    b, s, h, dk = q.shape
    dv = v.shape[-1]
    nc, L = s // HGRN_CHUNK, HGRN_CHUNK

    def chunks(t):
        return jnp.moveaxis(t.reshape(b, nc, L, h, t.shape[-1]), 1, 0)

    causal = jnp.tril(jnp.ones((L, L), dtype=bool))[None, :, :, None, None]

    def step(state, inp):
        qc, kc, lfc, vc = inp
        cum = jnp.cumsum(lfc, axis=1)
        o_inter = jnp.einsum('bthk,bhkv->bthv', qc * jnp.exp(cum), state)
        diff = jnp.where(causal, cum[:, :, None] - cum[:, None, :], -jnp.inf)
        scores = jnp.einsum('bthk,bshk,btshk->bths', qc, kc, jnp.exp(diff))
        o_intra = jnp.einsum('bths,bshv->bthv', scores, vc)
        dec_end = jnp.exp(cum[:, -1:] - cum)
        state = state * jnp.exp(cum[:, -1])[..., None] + jnp.einsum('bshk,bshv->bhkv', kc * dec_end, vc)
        return state, o_inter + o_intra

    s_fin, o = lax.scan(step, s0, (chunks(q), chunks(k), chunks(logf), chunks(v)))
    return jnp.moveaxis(o, 0, 1).reshape(b, s, h, dv), s_fin


def hgrn_direction(q, f_logit, lb, v, s0, reverse):
    if reverse:
        q, f_logit, v = [jnp.flip(t, axis=1) for t in (q, f_logit, v)]
    f = lb + (1.0 - lb) * jax.nn.sigmoid(f_logit.astype(F32))
    o, s_fin = hgrn_chunk_scan(q.astype(F32), 1.0 - f, jnp.log(f), v.astype(F32), s0)
    if reverse:
        o = jnp.flip(o, axis=1)
    return o, s_fin


def rwkv_direction(r, w, kk, a, v, kt, s0, reverse):
    def step(state, inp):
        r_t, w_t, kk_t, a_t, v_t, k_t = inp
        sa = jnp.einsum('bhvk,bhk->bhv', state, -kk_t)
        state = (state * w_t[:, :, None, :] + sa[..., None] * (kk_t * a_t)[:, :, None, :]
                 + v_t[..., None] * k_t[:, :, None, :])
        return state, jnp.einsum('bhvk,bhk->bhv', state, r_t)

    xs = tuple(jnp.moveaxis(t.astype(F32), 1, 0) for t in (r, w, kk, a, v, kt))
    s_fin, o = lax.scan(step, s0, xs, reverse=reverse)
    return jnp.moveaxis(o, 0, 1), s_fin


def odd_mixer(h, P, i, lidx, ctx):
    bsz, s, _ = h.shape
    proj = h @ P['od_w_in'][i]
    if ctx is None:
        s_hgrn0 = jnp.zeros((bsz, 2, HC, DK_C, DV_C), F32)
        s_rwkv0 = jnp.zeros((bsz, 2, HD, DH_D, DH_D), F32)
    else:
        s_hgrn0, s_rwkv0 = ctx[0].astype(F32), ctx[1].astype(F32)

    def heads(t, nh):
        return t.reshape(bsz, s, nh, -1)

    q, f_fw, f_bw, v_c, g_c = split_last(proj[..., :HGRN_PROJ], HGRN_SIZES)
    lb_sm = jax.nn.softmax(P['hgrn_lb_logits'].astype(F32), axis=1)
    lb = (jnp.cumsum(lb_sm, axis=1)[:, lidx] - lb_sm[:, 0]).reshape(2, HC, DK_C)
    qh, vh = heads(jax.nn.silu(q), HC), heads(v_c, HC)
    oc_f, sc_f = hgrn_direction(qh, heads(f_fw, HC), lb[0], vh, s_hgrn0[:, 0], False)
    oc_b, sc_b = hgrn_direction(qh, heads(f_bw, HC), lb[1], vh, s_hgrn0[:, 1], True)
    o_c = rms_norm(oc_f + oc_b, P['hgrn_norm_g'][i]).reshape(bsz, s, D_C) * jax.nn.silu(g_c.astype(F32))

    pd = token_shift_mix(proj[..., HGRN_PROJ:], P['rwkv_mu'][i])
    r, k, v, wd, ad, gd = [t.astype(F32) for t in split_last(pd, RWKV_SIZES)]
    a = jax.nn.sigmoid(P['rwkv_a0'][i] + ad @ P['rwkv_a_up'][i])
    g = jax.nn.sigmoid(gd) @ P['rwkv_g_up'][i]
    kk = heads(k * P['rwkv_k_k'][i], HD)
    kk = kk / jnp.maximum(jnp.sqrt(jnp.sum(kk * kk, axis=-1, keepdims=True)), 1e-12)
    kt = heads(k * (1.0 + (a - 1.0) * P['rwkv_k_a'][i]), HD)
    rh, vh_d, ah = heads(r, HD), heads(v, HD), heads(a, HD)
    o_dirs, s_dirs = [], []
    for dr in range(2):
        z = P['rwkv_w0'][i, dr] + jnp.tanh(wd[..., dr * W_LORA:(dr + 1) * W_LORA]) @ P['rwkv_w_up'][i, dr]
        w = jnp.exp(-jnp.exp(-jax.nn.softplus(-z) - 0.5))
        o_dr, s_dr = rwkv_direction(rh, heads(w, HD), kk, ah, vh_d, kt, s_rwkv0[:, dr], dr == 1)
        o_dirs.append(o_dr)
        s_dirs.append(s_dr)
    o = o_dirs[0] + o_dirs[1]
    mu = jnp.mean(o, axis=-1, keepdims=True)
    var = jnp.mean(jnp.square(o - mu), axis=-1, keepdims=True)
    o = ((o - mu) * lax.rsqrt(var + RWKV_GN_EPS) * P['rwkv_ln_g'][i].reshape(HD, DH_D)
         + P['rwkv_ln_b'][i].reshape(HD, DH_D))
    o = o + jnp.sum(rh * kt * P['rwkv_r_k'][i].reshape(HD, DH_D), axis=-1, keepdims=True) * vh_d
    o_d = o.reshape(bsz, s, D_D) * g
    out = jnp.concatenate([o_c, o_d], axis=-1).astype(h.dtype) @ P['od_w_out'][i]
    return out, (jnp.stack([sc_f, sc_b], axis=1), jnp.stack(s_dirs, axis=1))


def conv_ffn(h, P, l):
    u = centred_conv3(h @ P['ffn_w_up'][l], P['ffn_conv_w'][l], P['ffn_conv_b'][l])
    val, gate = jnp.split(u, 2, axis=-1)
    return (jax.nn.silu(gate) * val) @ P['ffn_w_down'][l]


def run_trunk(x, cond, pos, ctx_in, P):
    collect = ([], [], [], [], [], [])
    for l in range(DEPTH):
        i = l // 2
        mod = jax.nn.silu(cond) @ P['ada_w'][l] + P['ada_b'][l]
        sh1, sc1, g1, sh2, sc2, g2 = jnp.split(mod[:, None, :], 6, axis=-1)
        h = rms_norm(x, P['norm_mix_g'][l]) * (1.0 + sc1) + sh1
        if l % 2 == 0:
            ctx = None if ctx_in is None else tuple(t[:, i] for t in ctx_in[:4])
            mix, own = even_mixer(h, P, i, l, pos, ctx)
            if ctx_in is None:
                for j in range(4):
                    collect[j].append(own[j])
        else:
            ctx = None if ctx_in is None else tuple(t[:, i] for t in ctx_in[4:])
            mix, fin = odd_mixer(h, P, i, l, ctx)
            if ctx_in is None:
                collect[4].append(fin[0].astype(x.dtype))
                collect[5].append(fin[1].astype(x.dtype))
        x = x + g1 * mix
        h = rms_norm(x, P['norm_ffn_g'][l]) * (1.0 + sc2) + sh2
        x = x + g2 * conv_ffn(h, P, l)
    y = rms_norm(x, P['final_norm_g'])
    new_ctx = None if ctx_in is not None else tuple(jnp.stack(cl, axis=1) for cl in collect)
    return y, new_ctx


def setup_inputs(seed: int = 0) -> dict:
    key = jax.random.key(seed)
    ks = iter(jax.random.split(key, 64))

    def nrm(shape, scale):
        return scale * jax.random.normal(next(ks), shape, F32)

    def gain(shape):
        return 1.0 + 0.05 * jax.random.normal(next(ks), shape, F32)

    d = D_MODEL
    return {
        'x_prompt': nrm((BATCH, SEQ, d), 1.0),
        'x_sample': nrm((DEC_BATCH, DEC_SEQ, d), 1.0),
        'cache_a_k': nrm((DEC_BATCH, N_EVEN, HA, PAST_LEN, 2 * DH_A), 1.0),
        'cache_a_v': nrm((DEC_BATCH, N_EVEN, HA, PAST_LEN, DV_A), 1.0),
        'cache_b_k': nrm((DEC_BATCH, N_EVEN, HKV_B, PAST_LEN, DH_B), 1.0),
        'cache_b_v': nrm((DEC_BATCH, N_EVEN, HKV_B, PAST_LEN, DH_B), 1.0),
        'state_hgrn': nrm((DEC_BATCH, N_ODD, 2, HC, DK_C, DV_C), 0.5),
        'state_rwkv': nrm((DEC_BATCH, N_ODD, 2, HD, DH_D, DH_D), 0.3),
        'c': nrm((DEC_BATCH, d), 1.0),
        'c_ctx': nrm((d,), 1.0),
        'ada_w': nrm((DEPTH, d, 6 * d), 0.3 * d ** -0.5),
        'ada_b': nrm((DEPTH, 6 * d), 0.02),
        'norm_mix_g': gain((DEPTH, d)),
        'norm_ffn_g': gain((DEPTH, d)),
        'final_norm_g': gain((d,)),
        'ev_w_in': nrm((N_EVEN, d, EVEN_PROJ), d ** -0.5),
        'ev_w_out': nrm((N_EVEN, D_A + D_B, d), (D_A + D_B) ** -0.5),
        'a_lambda': nrm((N_EVEN, 4, DH_A), 0.1),
        'a_subln_g': gain((N_EVEN, DV_A)),
        'b_q_norm_g': gain((N_EVEN, DH_B)),
        'b_k_norm_g': gain((N_EVEN, DH_B)),
        'od_w_in': nrm((N_ODD, d, ODD_PROJ), d ** -0.5),
        'od_w_out': nrm((N_ODD, D_C + D_D, d), (D_C + D_D) ** -0.5),
        'hgrn_lb_logits': nrm((2, DEPTH, HC * DK_C), 0.5),
        'hgrn_norm_g': gain((N_ODD, DV_C)),
        'rwkv_mu': jax.random.uniform(next(ks), (N_ODD, RWKV_PROJ), F32),
        'rwkv_w0': nrm((N_ODD, 2, D_D), 0.5),
        'rwkv_w_up': nrm((N_ODD, 2, W_LORA, D_D), 0.1),
        'rwkv_a0': nrm((N_ODD, D_D), 0.3),
        'rwkv_a_up': nrm((N_ODD, A_LORA, D_D), 0.1),
        'rwkv_g_up': nrm((N_ODD, G_LORA, D_D), G_LORA ** -0.5),
        'rwkv_k_k': 0.85 + nrm((N_ODD, D_D), 0.1),
        'rwkv_k_a': gain((N_ODD, D_D)),
        'rwkv_r_k': nrm((N_ODD, D_D), 0.1),
        'rwkv_ln_g': gain((N_ODD, D_D)),
        'rwkv_ln_b': nrm((N_ODD, D_D), 0.02),
        'ffn_w_up': nrm((DEPTH, d, 2 * D_FF), d ** -0.5),
        'ffn_conv_w': nrm((DEPTH, 3, 2 * D_FF), 0.2) + jnp.array([0.0, 1.0, 0.0], F32)[None, :, None],
        'ffn_conv_b': nrm((DEPTH, 2 * D_FF), 0.02),
        'ffn_w_down': nrm((DEPTH, D_FF, d), D_FF ** -0.5),
    }


def reference(x_prompt, x_sample, cache_a_k, cache_a_v, cache_b_k, cache_b_v, state_hgrn, state_rwkv,
              c, c_ctx, ada_w, ada_b, norm_mix_g, norm_ffn_g, final_norm_g,
              ev_w_in, ev_w_out, a_lambda, a_subln_g, b_q_norm_g, b_k_norm_g,
              od_w_in, od_w_out, hgrn_lb_logits, hgrn_norm_g,
              rwkv_mu, rwkv_w0, rwkv_w_up, rwkv_a0, rwkv_a_up, rwkv_g_up,
              rwkv_k_k, rwkv_k_a, rwkv_r_k, rwkv_ln_g, rwkv_ln_b,
              ffn_w_up, ffn_conv_w, ffn_conv_b, ffn_w_down):
    P = {
        'ada_w': ada_w, 'ada_b': ada_b, 'norm_mix_g': norm_mix_g, 'norm_ffn_g': norm_ffn_g,
        'final_norm_g': final_norm_g, 'ev_w_in': ev_w_in, 'ev_w_out': ev_w_out,
        'a_lambda': a_lambda, 'a_subln_g': a_subln_g, 'b_q_norm_g': b_q_norm_g, 'b_k_norm_g': b_k_norm_g,
        'od_w_in': od_w_in, 'od_w_out': od_w_out, 'hgrn_lb_logits': hgrn_lb_logits,
        'hgrn_norm_g': hgrn_norm_g, 'rwkv_mu': rwkv_mu, 'rwkv_w0': rwkv_w0, 'rwkv_w_up': rwkv_w_up,
        'rwkv_a0': rwkv_a0, 'rwkv_a_up': rwkv_a_up, 'rwkv_g_up': rwkv_g_up, 'rwkv_k_k': rwkv_k_k,
        'rwkv_k_a': rwkv_k_a, 'rwkv_r_k': rwkv_r_k, 'rwkv_ln_g': rwkv_ln_g, 'rwkv_ln_b': rwkv_ln_b,
        'ffn_w_up': ffn_w_up, 'ffn_conv_w': ffn_conv_w, 'ffn_conv_b': ffn_conv_b, 'ffn_w_down': ffn_w_down,
    }
    y_prompt, new_ctx = run_trunk(x_prompt, c_ctx[None, :], None, None, P)
    new_cache_a_k, new_cache_a_v, new_cache_b_k, new_cache_b_v, new_state_hgrn, new_state_rwkv = new_ctx
    ROWS = x_sample.shape[1] // GRID_W
    pos = grid_positions(ROWS)
    ctx_cached = (cache_a_k, cache_a_v, cache_b_k, cache_b_v, state_hgrn, state_rwkv)
    y_sample, _ = run_trunk(x_sample, c, pos, ctx_cached, P)
    return (y_prompt, y_sample, new_cache_a_k, new_cache_a_v, new_cache_b_k, new_cache_b_v, new_state_hgrn, new_state_rwkv)
```

```python
import numpy as np
from contextlib import ExitStack
import concourse.bass as bass
import concourse.mybir as mybir
from concourse.bass_utils import run_bass_kernel_spmd

F32 = mybir.dt.float32
BF16 = mybir.dt.bfloat16
AF = mybir.ActivationFunctionType
ALU = mybir.AluOpType
AX = mybir.AxisListType

ENGS = ('pe', 'dve', 'act', 'pool', 'sp')
NDS = 16


class Op:
    __slots__ = ('eng', 'fn', 'reads', 'writes', 'deps', 'need_inc', 'semkey', 'count', 'dma', 'prev_dma')

    def __init__(self, eng, fn, reads, writes, dma):
        self.eng = eng
        self.fn = fn
        self.reads = reads
        self.writes = writes
        self.deps = []
        self.need_inc = False
        self.semkey = None
        self.count = 0
        self.dma = dma
        self.prev_dma = None


class Prog:
    def __init__(self):
        self.nc = bass.Bass("TRN2", target_bir_lowering=False)
        nc = self.nc
        self.E = {'pe': nc.tensor, 'dve': nc.vector, 'act': nc.scalar, 'pool': nc.gpsimd, 'sp': nc.sync}
        self.es = ExitStack()
        self.sems = {}
        for e in ENGS:
            self.sems[e] = self.es.enter_context(nc.semaphore("sem_" + e))
        for q in ('sp', 'pool', 'act'):
            for i in range(NDS):
                self.sems[('dma', q, i)] = self.es.enter_context(nc.semaphore("dsem_%s_%d" % (q, i)))
        self.cnt = {k: 0 for k in self.sems}
        self.dma_rr = {'sp': 0, 'pool': 0, 'act': 0}
        self.last_dma_on_sem = {}
        self.seen = {e: {} for e in ENGS}
        self.pending = []
        self.last_writer = {}
        self.readers = {}
        self.n_ins = 0
        self.excl = set()
        self.alias = {}

    def op(self, eng, fn, r=(), w=()):
        self.pending.append(Op(eng, fn, tuple(r), tuple(w), False))

    def dma(self, q, fn, r=(), w=()):
        self.pending.append(Op(q, fn, tuple(r), tuple(w), True))

    def _wait(self, eng, key, val):
        if val <= 0:
            return
        if self.seen[eng].get(key, 0) >= val:
            return
        self.E[eng].wait_ge(self.sems[key], val)
        self.n_ins += 1
        self.seen[eng][key] = val

    def flush(self, barrier=True):
        ops = self.pending
        self.pending = []
        lw, rd = self.last_writer, self.readers
        for op in ops:
            deps = []
            if self.alias:
                op.reads = tuple(self.alias.get(b, b) for b in op.reads)
                op.writes = tuple(self.alias.get(b, b) for b in op.writes)
            ex = [b for b in op.reads if (b[0] if isinstance(b, tuple) else b) in self.excl]
            if ex:
                op.writes = tuple(op.writes) + tuple(b for b in ex if b not in op.writes)
                op.reads = tuple(b for b in op.reads if b not in ex)
            for b in op.reads:
                d = lw.get(b)
                if d is not None:
                    deps.append(d)
            for b in op.writes:
                d = lw.get(b)
                if d is not None:
                    deps.append(d)
                deps.extend(rd.get(b, ()))
            op.deps = [d for d in deps if d is not op]
            for d in op.deps:
                d.need_inc = True
            for b in op.reads:
                rd.setdefault(b, []).append(op)
            for b in op.writes:
                lw[b] = op
                rd[b] = []
        if barrier:
            last = {}
            for op in ops:
                last[op.eng] = op
            for op in last.values():
                op.need_inc = True
        for op in ops:
            if op.dma:
                i = self.dma_rr[op.eng] % NDS
                self.dma_rr[op.eng] += 1
                key = ('dma', op.eng, i)
                op.semkey = key
                self.cnt[key] += 16
                op.count = self.cnt[key]
                op.prev_dma = self.cnt[key] - 16
            elif op.need_inc:
                op.semkey = op.eng
                self.cnt[op.eng] += 1
                op.count = self.cnt[op.eng]
        for op in ops:
            need = {}
            for d in op.deps:
                if d.eng == 'pe' and op.eng == 'pe' and not d.dma and not op.dma:
                    continue
                k = d.semkey
                if need.get(k, 0) < d.count:
                    need[k] = d.count
            if op.dma and op.prev_dma:
                k = op.semkey
                if need.get(k, 0) < op.prev_dma:
                    need[k] = op.prev_dma
            for k, v in need.items():
                self._wait(op.eng, k, v)
            ins = op.fn()
            self.n_ins += 1
            if op.dma:
                ins.then_inc(self.sems[op.semkey], 16)
            elif op.need_inc:
                ins.then_inc(self.sems[op.eng], 1)
            op.fn = None
        if barrier:
            self.barrier()

    def barrier(self):
        for k, v in self.cnt.items():
            if k == 'sp':
                continue
            self._wait('sp', k, v)
        ins = self.E['sp'].nop()
        self.cnt['sp'] += 1
        ins.then_inc(self.sems['sp'], 1)
        for e in ENGS:
            if e != 'sp':
                self._wait(e, 'sp', self.cnt['sp'])
            for k, v in self.cnt.items():
                self.seen[e][k] = v
        self.last_writer = {}
        self.readers = {}


T = 1024
D = 1024
NCH = 8
DFF = 2816
NFF = 22
EPS = 1e-6


class VecPack:
    def __init__(self):
        self.cols = {}
        self.n = 0

    def add(self, name, ncols):
        self.cols[name] = (self.n, ncols)
        self.n += ncols
        return self.cols[name]


def build_vec_layout():
    vp = VecPack()
    vp.add('cond', 8)
    vp.add('km1', 1)
    vp.add('keep', 1)
    for l in range(2):
        vp.add('ada_b%d' % l, 48)
        vp.add('nmg%d' % l, 8)
        vp.add('nfg%d' % l, 8)
        vp.add('cw0_%d' % l, 44)
        vp.add('cw1_%d' % l, 44)
        vp.add('cw2_%d' % l, 44)
        vp.add('cb_%d' % l, 44)
    vp.add('fng', 8)
    vp.add('gq', 1); vp.add('gq_sw', 1); vp.add('gk', 1); vp.add('gk_sw', 1)
    vp.add('amask', 48)
    vp.add('lb0', 8); vp.add('lb1', 8); vp.add('mu', 15); vp.add('w0', 8)
    vp.add('a0', 4); vp.add('k_k', 4); vp.add('k_a', 4); vp.add('r_k', 4)
    return vp


def fm(v):
    v = np.asarray(v, dtype=np.float32)
    return np.ascontiguousarray(v.reshape(-1, 128).T)


class Builder:
    def __init__(self, debug=(), layers=(0, 1), do_mix=True):
        self.layers = layers
        import os
        self.stop = os.environ.get('KSTOP', '')
        self.skip = set(os.environ.get('KSKIP', '').split(','))
        self.do_mix = do_mix
        self.P = Prog()
        self.nc = self.P.nc
        self.debug = debug
        self.vp = build_vec_layout()
        self.dbg_outs = {}
        self.P.excl.update(['tp', 'mps', 'ssp', 'ups', 'ftp', 'pq', 'pqs', 'pss', 'ptm', 'ptp', 'sT', 'acc', 'otp', 'pmx'])

    def dram_in(self, name, shape, dt=F32):
        return self.nc.dram_tensor(name, list(shape), dt, kind="ExternalInput").ap()

    def dram_out(self, name, shape, dt=F32):
        return self.nc.dram_tensor(name, list(shape), dt, kind="ExternalOutput").ap()

    def sb(self, es, name, shape, dt):
        self.uid = getattr(self, 'uid', 0) + 1
        return es.enter_context(self.nc.sbuf_tensor("sb%d_%s" % (self.uid, name), list(shape), dt))

    def ps(self, es, name, shape, dt=F32):
        self.uid = getattr(self, 'uid', 0) + 1
        return es.enter_context(self.nc.psum_tensor("ps%d_%s" % (self.uid, name), list(shape), dt))

    def vcol(self, name, j=0, n=1):
        c0, nc_ = self.vp.cols[name]
        return self.vecs[:, c0 + j:c0 + j + n]

    def dump(self, name, sbt, shape, reads):
        if name not in self.debug:
            return
        P, nc = self.P, self.nc
        P.flush()
        dt = sbt.dtype if hasattr(sbt, 'dtype') else F32
        o = self.dram_out("dbg_" + name, shape, dt)
        self.dbg_outs[name] = (shape, dt)
        P.dma('sp', lambda: nc.sync.dma_start(out=o, in_=sbt[:]), r=reads, w=[('dbg', name)])

    def build(self):
        P, nc = self.P, self.nc
        top = ExitStack()
        self.top = top
        self.x_d = self.dram_in("x", [T, D])
        self.vecs_d = self.dram_in("vecs", [128, self.vp.n])
        self.ident_d = self.dram_in("ident", [128, 128])
        self.ada_w_d = self.dram_in("ada_w", [2, D, 6 * D])
        self.ffn_up_d = self.dram_in("ffn_w_up", [2, D, 2 * DFF])
        self.ffn_dn_d = self.dram_in("ffn_w_down", [2, DFF, D])
        self.y_d = self.dram_out("y", [T, D])
        self.ev_w_in_d = self.dram_in("ev_w_in", [D, 2304])
        self.ev_wx_d = self.dram_in("ev_wx", [D, 2176])
        self.ev_w_out_d = self.dram_in("ev_w_out", [D, D])
        self.rope_d = self.dram_in("rope", [2, 128, T])
        self.bvec_d = self.dram_in("bvec", [128, 448])
        self.ctx_ak_d = self.dram_in("ctx_ak", [4, 512, 128])
        self.ctx_av_d = self.dram_in("ctx_av", [4, 512, 128])
        self.ctx_bk_d = self.dram_in("ctx_bk", [2, 512, 64])
        self.ctx_bv_d = self.dram_in("ctx_bv", [2, 512, 64])
        self.o_ak_d = self.dram_out("o_ak", [4, 4, 256, 128])
        self.o_av_d = self.dram_out("o_av", [4, 4, 256, 128])
        self.o_bk_d = self.dram_out("o_bk", [4, 2, 256, 64])
        self.o_bv_d = self.dram_out("o_bv", [4, 2, 256, 64])
        self.od_w_in_d = self.dram_in("od_w_in", [D, 4416])
        self.od_w_out_d = self.dram_in("od_w_out", [D, D])
        self.masks_d = self.dram_in("masks", [128, 4, 128])
        self.bv1_d = self.dram_in("bv1", [128, 1280])
        self.st_h_d = self.dram_in("st_h", [2, 4, 128, 128])
        self.st_r_d = self.dram_in("st_r", [2, 8, 64, 64])
        self.w_up_d = self.dram_in("w_up", [128, 512])
        self.a_up_d = self.dram_in("a_up", [64, 512])
        self.g_up_d = self.dram_in("g_up", [128, 512])
        self.o_sh_d = self.dram_out("o_sh", [4, 2, 4, 128, 128])
        self.o_sr_d = self.dram_out("o_sr", [4, 2, 8, 64, 64])
        self.xT = self.sb(top, "xT", [128, NCH, T], F32)
        self.hT = self.sb(top, "hT", [128, NCH, T], BF16)
        self.vecs = self.sb(top, "vecs", [128, self.vp.n], F32)
        self.ident_f = self.sb(top, "ident_f", [128, 128], F32)
        self.ident_b = self.sb(top, "ident_b", [128, 128], BF16)
        self.ones_b = self.sb(top, "ones_b", [128, 128], BF16)
        self.mod = [self.sb(top, "mod%d" % l, [128, 48], F32) for l in range(2)]
        self.gm1 = [self.sb(top, "gm1_%d" % l, [128, 8], F32) for l in range(2)]
        self.gm2 = [self.sb(top, "gm2_%d" % l, [128, 8], F32) for l in range(2)]
        self.wk0 = [self.sb(top, "wk0_%d" % l, [128, 44], F32) for l in range(2)]
        self.wk2 = [self.sb(top, "wk2_%d" % l, [128, 44], F32) for l in range(2)]
        self.rstd = self.sb(top, "rstd", [128, T], F32)
        self.epsc = self.sb(top, "epsc", [128, 4], F32)
        self.bd_ones = self.sb(top, "bd_ones", [128, 128], BF16)
        self.NWB = 0
        self.wbuf = []
        self.wrr = 0

        self.phase_init()
        for l in range(2):
            if l in self.layers:
                if l == 0 and self.do_mix:
                    self.even_mixer()
                elif l == 1 and self.do_mix:
                    self.odd_mixer()
                else:
                    self.phase_mix(l)
                self.phase_ffn(l)
        self.phase_final()
        top.close()
        P.es.close()

    def phase_init(self):
        P, nc = self.P, self.nc
        with ExitStack() as es:
            xin = [self.sb(es, "xin%d" % i, [128, D], F32) for i in range(2)]
            scond = self.sb(es, "scond", [128, 8], BF16)
            abuf = [self.sb(es, "abuf%d" % i, [128, 6 * D], BF16) for i in range(2)]
            tp = [self.ps(es, "tp%d" % i, [128, 512]) for i in range(2)]
            mps = self.ps(es, "mps", [128, 48])

            P.dma('sp', lambda: nc.sync.dma_start(out=self.vecs[:], in_=self.vecs_d), w=['vecs'])
            P.dma('sp', lambda: nc.sync.dma_start(out=self.ident_f[:], in_=self.ident_d), w=['ident_f'])
            P.op('dve', lambda: nc.vector.tensor_copy(out=self.ident_b[:], in_=self.ident_f[:]), r=['ident_f'], w=['ident_b'])
            P.op('pool', lambda: nc.gpsimd.memset(self.ones_b[:], 1.0), w=['ones_b'])
            P.op('pool', lambda: nc.gpsimd.memset(self.epsc[:, 0:1], EPS), w=['epsc'])
            P.op('pool', lambda: nc.gpsimd.memset(self.epsc[:, 1:2], 1e-24), w=['epsc'])
            P.op('pool', lambda: nc.gpsimd.memset(self.epsc[:, 2:3], GN_EPS), w=['epsc'])
            P.op('pool', lambda: nc.gpsimd.memset(self.bd_ones[:], 0.0), w=['bd_ones'])
            P.op('pool', lambda: nc.gpsimd.memset(self.bd_ones[0:64, 0:64], 1.0), w=['bd_ones'])
            P.op('pool', lambda: nc.gpsimd.memset(self.bd_ones[64:128, 64:128], 1.0), w=['bd_ones'])
            for tt in range(8):
                xi = xin[tt % 2]
                P.dma('sp', lambda xi=xi, tt=tt: nc.sync.dma_start(out=xi[:], in_=self.x_d[tt * 128:(tt + 1) * 128, :]),
                      w=[('xin', tt % 2)])
                for g in range(2):
                    tpp = tp[g]
                    for cc in range(4):
                        c = g * 4 + cc
                        P.op('pe', lambda xi=xi, tpp=tpp, cc=cc, c=c: nc.tensor.transpose(
                            tpp[:, cc * 128:(cc + 1) * 128], xi[:, c * 128:(c + 1) * 128], self.ident_f[:]),
                            r=[('xin', tt % 2), 'ident_f'], w=[('tp', g)])
                    eng = 'act' if g == 0 else 'dve'
                    if g == 0:
                        P.op('act', lambda tpp=tpp, g=g, tt=tt: nc.scalar.copy(
                            out=self.xT[:, g * 4:(g + 1) * 4, tt * 128:(tt + 1) * 128],
                            in_=tpp[:].rearrange("p (c t) -> p c t", c=4)),
                            r=[('tp', g)], w=[('xT', c_) for c_ in range(g * 4, g * 4 + 4)])
                    else:
                        P.op('dve', lambda tpp=tpp, g=g, tt=tt: nc.vector.tensor_copy(
                            out=self.xT[:, g * 4:(g + 1) * 4, tt * 128:(tt + 1) * 128],
                            in_=tpp[:].rearrange("p (c t) -> p c t", c=4)),
                            r=[('tp', g)], w=[('xT', c_) for c_ in range(g * 4, g * 4 + 4)])
            P.op('act', lambda: nc.scalar.activation(out=scond[:], in_=self.vcol('cond', 0, 8), func=AF.Silu),
                 r=['vecs'], w=['scond'])
            for l in range(2):
                for kc in range(8):
                    ab = abuf[kc % 2]
                    P.dma('pool', lambda ab=ab, l=l, kc=kc: nc.gpsimd.dma_start(
                        out=ab[:], in_=self.ada_w_d[l, kc * 128:(kc + 1) * 128, :]), w=[('abuf', kc % 2)])

                    def mm(ab=ab, kc=kc):
                        ins = None
                        for col in range(48):
                            ins = nc.tensor.matmul(mps[:, col:col + 1], ab[:, col * 128:(col + 1) * 128], scond[:, kc:kc + 1],
                                                   start=(kc == 0 and col == 0), stop=(kc == 7), skip_group_check=True)
                        return ins
                    P.op('pe', mm, r=[('abuf', kc % 2), 'scond'], w=['mps'])
                md = self.mod[l]
                P.op('dve', lambda md=md, l=l: nc.vector.tensor_tensor(out=md[:], in0=mps[:], in1=self.vcol('ada_b%d' % l, 0, 48),
                                                                       op=ALU.add), r=['mps', 'vecs'], w=[('mod', l)])
                P.op('dve', lambda md=md, l=l: nc.vector.scalar_tensor_tensor(
                    out=self.gm1[l][:], in0=md[:, 8:16], scalar=1.0, in1=self.vcol('nmg%d' % l, 0, 8),
                    op0=ALU.add, op1=ALU.mult), r=[('mod', l), 'vecs'], w=[('gm1', l)])
                P.op('dve', lambda md=md, l=l: nc.vector.scalar_tensor_tensor(
                    out=self.gm2[l][:], in0=md[:, 32:40], scalar=1.0, in1=self.vcol('nfg%d' % l, 0, 8),
                    op0=ALU.add, op1=ALU.mult), r=[('mod', l), 'vecs'], w=[('gm2', l)])
                P.op('dve', lambda l=l: nc.vector.tensor_scalar(
                    out=self.wk0[l][:], in0=self.vcol('cw0_%d' % l, 0, 44), scalar1=self.vcol('km1'), scalar2=None,
                    op0=ALU.mult), r=['vecs'], w=[('wk0', l)])
                P.op('dve', lambda l=l: nc.vector.tensor_scalar(
                    out=self.wk2[l][:], in0=self.vcol('cw2_%d' % l, 0, 44), scalar1=self.vcol('km1'), scalar2=None,
                    op0=ALU.mult), r=['vecs'], w=[('wk2', l)])
            self.dump('mod0', self.mod[0], [128, 48], [('mod', 0)])
            self.dump('xT', self.xT, [128, NCH, T], [('xT', c) for c in range(8)])
            P.flush()

    def rmsnorm(self, es, gm, sh, out_fn):
        P, nc = self.P, self.nc
        sq = self.sb(es, "sq", [128, NCH, T], BF16)
        tmp = [self.sb(es, "ntmp%d" % i, [128, T], F32) for i in range(2)]
        ssp = [self.ps(es, "ssp%d" % i, [128, 512]) for i in range(2)]
        for c in range(8):
            P.op('act', lambda c=c: nc.scalar.activation(out=sq[:, c, :], in_=self.xT[:, c, :], func=AF.Square),
                 r=[('xT', c)], w=[('sq', c)])
        for th in range(2):
            def mm(th=th):
                ins = None
                for c in range(8):
                    ins = nc.tensor.matmul(ssp[th][:], self.ones_b[:], sq[:, c, th * 512:(th + 1) * 512],
                                           start=(c == 0), stop=(c == 7))
                return ins
            P.op('pe', mm, r=[('sq', c) for c in range(8)] + ['ones_b'], w=[('ssp', th)])
            P.op('act', lambda th=th: nc.scalar.activation(
                out=self.rstd[:, th * 512:(th + 1) * 512], in_=ssp[th][:], func=AF.Sqrt, scale=1.0 / D, bias=self.epsc[:, 0:1]),
                r=[('ssp', th), 'epsc'], w=[('rstd', th)])
            P.op('dve', lambda th=th: nc.vector.reciprocal(
                out=self.rstd[:, th * 512:(th + 1) * 512], in_=self.rstd[:, th * 512:(th + 1) * 512]),
                r=[('rstd', th)], w=[('rstd', th)])
        for c in range(8):
            tm = tmp[c % 2]
            P.op('dve', lambda c=c, tm=tm: nc.vector.tensor_tensor(out=tm[:], in0=self.xT[:, c, :], in1=self.rstd[:],
                                                                   op=ALU.mult),
                 r=[('xT', c), ('rstd', 0), ('rstd', 1)], w=[('ntmp', c % 2)])
            out_ap, wkeys = out_fn(c)
            bias = sh(c) if sh is not None else 0.0
            P.op('act', lambda c=c, tm=tm, out_ap=out_ap, bias=bias: nc.scalar.activation(
                out=out_ap, in_=tm[:], func=AF.Identity, scale=gm(c), bias=bias),
                r=[('ntmp', c % 2), 'gmsh'], w=wkeys)

    def alloc_w(self, es, n, elems):
        self.NWB = n
        self.wbuf = [self.sb(es, "wbuf%d" % i, [128, elems], BF16) for i in range(n)]
        self.wrr = 0

    def load_w(self, src_ap, view, key_extra=None):
        P, nc = self.P, self.nc
        i = self.wrr % self.NWB
        self.wrr += 1
        a, b = view
        dst = self.wbuf[i][:, 0:a * b].rearrange("p (a b) -> p a b", a=a)
        P.dma('pool', lambda: nc.gpsimd.dma_start(out=dst, in_=src_ap), w=[('wbuf', i)])
        return dst, ('wbuf', i)

    def phase_mix(self, l):
        P, nc = self.P, self.nc
        with ExitStack() as es:
            self.rmsnorm(es, lambda c: self.gm1[l][:, c:c + 1], lambda c: self.mod[l][:, c:c + 1],
                         lambda c: (self.hT[:, c, :], [('hT', c)]))
            if l == 0:
                self.dump('h0T', self.hT, [128, NCH, T], [('hT', c) for c in range(8)])
            P.flush()

    def phase_ffn(self, l):
        P, nc = self.P, self.nc
        with ExitStack() as es:
            with ExitStack() as es2:
                self.rmsnorm(es2, lambda c: self.gm2[l][:, c:c + 1], lambda c: self.mod[l][:, 24 + c:25 + c],
                             lambda c: (self.hT[:, c, :], [('hT', c)]))
                P.flush()
            self.alloc_w(es, 4, 4096)
            gT = self.sb(es, "gT", [128, NFF, T], BF16)
            cv = [self.sb(es, "cv%d" % i, [128, T], F32) for i in range(4)]
            sg = [self.sb(es, "sg%d" % i, [128, T], F32) for i in range(2)]
            ups = [self.ps(es, "ups%d" % i, [128, T]) for i in range(4)]
            cw = lambda nm, j: self.vcol('%s_%d' % (nm, l), j)
            for j in range(NFF):
                g_, jj_ = j // 4, j % 4
                if jj_ == 0:
                    ncol_ = 512 if g_ < 5 else 256
                    wts = []
                    for half in range(2):
                        c0 = half * DFF + g_ * 512
                        src = self.ffn_up_d[l, :, c0:c0 + ncol_].rearrange("(k p) n -> p k n", p=128)
                        wts.append(self.load_w(src, (8, ncol_)))
                for half in range(2):
                    wt, wkey = wts[half]
                    pi = (j % 2) * 2 + half
                    up = ups[pi]
                    cvb = cv[pi]
                    jj = half * NFF + j
                    for th in range(2):
                        def mm(wt=wt, up=up, th=th, jj_=jj_):
                            ins = None
                            for k in range(8):
                                ins = nc.tensor.matmul(up[:, th * 512:(th + 1) * 512], wt[:, k, jj_ * 128:(jj_ + 1) * 128],
                                                       self.hT[:, k, th * 512:(th + 1) * 512],
                                                       start=(k == 0), stop=(k == 7))
                            return ins
                        P.op('pe', mm, r=[wkey] + [('hT', k) for k in range(8)], w=[('ups', pi, th)])
                    ur = [('ups', pi, 0), ('ups', pi, 1)]
                    ck = ('cv', pi)
                    P.op('act', lambda up=up, cvb=cvb, jj=jj: nc.scalar.activation(
                        out=cvb[:], in_=up[:], func=AF.Identity, scale=cw('cw1', jj), bias=cw('cb', jj)),
                        r=ur + ['vecs'], w=[ck])
                    P.op('dve', lambda up=up, cvb=cvb, jj=jj: nc.vector.scalar_tensor_tensor(
                        out=cvb[:, 1:T], in0=up[:, 0:T - 1], scalar=cw('cw0', jj), in1=cvb[:, 1:T],
                        op0=ALU.mult, op1=ALU.add), r=ur + ['vecs', ck], w=[ck])
                    P.op('dve', lambda up=up, cvb=cvb, jj=jj: nc.vector.scalar_tensor_tensor(
                        out=cvb[:, 0:T - 1], in0=up[:, 1:T], scalar=cw('cw2', jj), in1=cvb[:, 0:T - 1],
                        op0=ALU.mult, op1=ALU.add), r=ur + ['vecs', ck], w=[ck])
                    P.op('dve', lambda up=up, cvb=cvb, jj=jj: nc.vector.scalar_tensor_tensor(
                        out=cvb[:, 256:T:256], in0=up[:, 255:T - 1:256], scalar=self.wk0[l][:, jj:jj + 1],
                        in1=cvb[:, 256:T:256], op0=ALU.mult, op1=ALU.add), r=ur + [('wk0', l), ck], w=[ck])
                    P.op('dve', lambda up=up, cvb=cvb, jj=jj: nc.vector.scalar_tensor_tensor(
                        out=cvb[:, 255:T - 1:256], in0=up[:, 256:T:256], scalar=self.wk2[l][:, jj:jj + 1],
                        in1=cvb[:, 255:T - 1:256], op0=ALU.mult, op1=ALU.add), r=ur + [('wk2', l), ck], w=[ck])
                pv = (j % 2) * 2
                sgb = sg[j % 2]
                P.op('act', lambda sgb=sgb, pv=pv: nc.scalar.activation(out=sgb[:], in_=cv[pv + 1][:], func=AF.Silu),
                     r=[('cv', pv + 1)], w=[('sg', j % 2)])
                P.op('dve', lambda sgb=sgb, pv=pv, j=j: nc.vector.tensor_tensor(
                    out=gT[:, j, :], in0=sgb[:], in1=cv[pv][:], op=ALU.mult),
                    r=[('sg', j % 2), ('cv', pv)], w=[('gT', j)])
            if l == 0:
                self.dump('gT0', gT, [128, NFF, T], [('gT', j) for j in range(NFF)])
            dps = [ups[0], ups[1]]
            for c in range(8):
                src = self.ffn_dn_d[l, :, c * 128:(c + 1) * 128].rearrange("(k p) n -> p k n", p=128)
                wt, wkey = self.load_w(src, (NFF, 128))
                for th in range(2):
                    pi = th
                    dp = ups[c % 2][:, th * 512:(th + 1) * 512]

                    def mm(wt=wt, dp=dp, th=th):
                        ins = None
                        for k in range(NFF):
                            ins = nc.tensor.matmul(dp, wt[:, k, :], gT[:, k, th * 512:(th + 1) * 512],
                                                   start=(k == 0), stop=(k == NFF - 1))
                        return ins
                    P.op('pe', mm, r=[wkey] + [('gT', k) for k in range(NFF)], w=[('ups', c % 2, th)])
                    P.op('dve', lambda dp=dp, c=c, th=th: nc.vector.scalar_tensor_tensor(
                        out=self.xT[:, c, th * 512:(th + 1) * 512], in0=dp, scalar=self.mod[l][:, 40 + c:41 + c],
                        in1=self.xT[:, c, th * 512:(th + 1) * 512], op0=ALU.mult, op1=ALU.add),
                        r=[('ups', c % 2, th), ('xT', c), ('mod', l)], w=[('xT', c)])
            P.flush()

    def phase_final(self):
        P, nc = self.P, self.nc
        with ExitStack() as es:
            yT = self.sb(es, "yT", [128, NCH, T], F32)
            self.rmsnorm(es, lambda c: self.vcol('fng', c), None, lambda c: (yT[:, c, :], [('yT', c)]))
            yo = [self.sb(es, "yo%d" % i, [128, D], F32) for i in range(2)]
            tp = [self.ps(es, "ftp%d" % i, [128, 512]) for i in range(2)]
            for tt in range(8):
                yb = yo[tt % 2]
                for g in range(2):
                    for cc in range(4):
                        c = g * 4 + cc
                        P.op('pe', lambda tt=tt, g=g, cc=cc, c=c: nc.tensor.transpose(
                            tp[g][:, cc * 128:(cc + 1) * 128], yT[:, c, tt * 128:(tt + 1) * 128], self.ident_f[:]),
                            r=[('yT', c), 'ident_f'], w=[('ftp', g)])
                    if g == 0:
                        P.op('act', lambda yb=yb, g=g: nc.scalar.copy(out=yb[:, g * 512:(g + 1) * 512], in_=tp[g][:]),
                             r=[('ftp', g)], w=[('yo', tt % 2, g)])
                    else:
                        P.op('dve', lambda yb=yb, g=g: nc.vector.tensor_copy(out=yb[:, g * 512:(g + 1) * 512], in_=tp[g][:]),
                             r=[('ftp', g)], w=[('yo', tt % 2, g)])
                P.dma('sp', lambda yb=yb, tt=tt: nc.sync.dma_start(out=self.y_d[tt * 128:(tt + 1) * 128, :], in_=yb[:]),
                      r=[('yo', tt % 2, 0), ('yo', tt % 2, 1)], w=[('y', tt)])
            P.flush()


def _even_mixer(self):
    P, nc = self.P, self.nc
    l = 0
    SC = 0.125
    with ExitStack() as es:
        qaT = self.sb(es, "qaT", [128, 4, T], BF16)
        kaT = self.sb(es, "kaT", [128, 4, 512 + T], BF16)
        qbT = self.sb(es, "qbT", [128, 4, T], BF16)
        kbT = self.sb(es, "kbT", [128, 512 + T], BF16)
        vA = self.sb(es, "vA", [128, 12, 4, 130], BF16)
        vB = self.sb(es, "vB", [128, 12, 2, 66], BF16)
        ocat = self.sb(es, "ocat", [128, 8, D], BF16)
        bvec = self.sb(es, "bvec", [128, 448], F32)
        nlam = self.sb(es, "nlam", [128, 1], F32)
        gsub8 = self.sb(es, "gsub8", [128, 128], F32)
        with ExitStack() as e0:
            self.rmsnorm(e0, lambda c: self.gm1[l][:, c:c + 1], lambda c: self.mod[l][:, c:c + 1],
                         lambda c: (self.hT[:, c, :], [('hT', c)]))
            P.flush()
        with ExitStack() as e1:
            self.alloc_w(e1, 4, 4096)
            cosT = self.sb(e1, "cosT", [128, T], F32)
            sinT = self.sb(e1, "sinT", [128, T], F32)
            cak = self.sb(e1, "cak", [128, 4, 4, 128], BF16)
            cbk = self.sb(e1, "cbk", [128, 4, 128], BF16)
            t1 = [self.sb(e1, "rt1_%d" % i, [128, 512], F32) for i in range(2)]
            t2 = [self.sb(e1, "rt2_%d" % i, [128, 512], F32) for i in range(2)]
            sqb = self.sb(e1, "sqb", [128, 512], BF16)
            rsb = self.sb(e1, "rsb", [128, 512], F32)
            stg = [self.sb(e1, "stg%d" % i, [128, 512], F32) for i in range(2)]
            kbs = self.sb(e1, "kbs", [128, 128], F32)
            ssb = self.sb(e1, "ssb", [128, 2], F32)
            junk = self.sb(e1, "junk", [128, 128], F32)
            lpr = self.sb(e1, "lpr", [128, 2, 64], F32)
            lsum = self.sb(e1, "lsum", [128, 2], F32)
            pq = [self.ps(e1, "pq%d" % i, [128, 512]) for i in range(2)]
            pqs = [self.ps(e1, "pqs%d" % i, [128, 512]) for i in range(2)]
            pss = self.ps(e1, "pss", [128, 512])
            ptm = [self.ps(e1, "ptm%d" % i, [128, 512]) for i in range(2)]
            ptp = self.ps(e1, "ptp", [128, 1024], BF16)

            P.dma('sp', lambda: nc.sync.dma_start(out=cosT[:], in_=self.rope_d[0]), w=['cosT'])
            P.dma('sp', lambda: nc.sync.dma_start(out=sinT[:], in_=self.rope_d[1]), w=['sinT'])
            P.dma('sp', lambda: nc.sync.dma_start(out=bvec[:], in_=self.bvec_d), w=['bvec'])
            P.op('dve', lambda: nc.vector.tensor_tensor(
                out=lpr[:], in0=bvec[:, 192:448].rearrange("p (a b e) -> p a b e", a=2, b=2)[:, :, 0, :],
                in1=bvec[:, 192:448].rearrange("p (a b e) -> p a b e", a=2, b=2)[:, :, 1, :], op=ALU.mult),
                r=['bvec'], w=['lpr'])
            P.op('dve', lambda: nc.vector.reduce_sum(out=lsum[:], in_=lpr[:], axis=AX.X), r=['lpr'], w=['lsum'])
            P.op('act', lambda: nc.scalar.activation(out=lsum[:], in_=lsum[:], func=AF.Exp), r=['lsum'], w=['lsum'])
            P.op('dve', lambda: nc.vector.tensor_tensor(out=nlam[:], in0=lsum[:, 1:2], in1=lsum[:, 0:1], op=ALU.subtract),
                 r=['lsum'], w=['nlam'])
            P.op('dve', lambda: nc.vector.tensor_scalar_add(out=nlam[:], in0=nlam[:], scalar1=-0.2), r=['nlam'], w=['nlam'])
            P.op('dve', lambda: nc.vector.tensor_scalar_mul(out=gsub8[:], in0=bvec[:, 0:128], scalar1=0.8),
                 r=['bvec'], w=['gsub8'])
            P.op('pool', lambda: nc.gpsimd.memset(vA[:, :, :, 128:129], 1.0), w=['vA_ones'])
            P.op('pool', lambda: nc.gpsimd.memset(vB[:, :, :, 64:65], 1.0), w=['vB_ones'])
            if self.stop == 'B1l':
                P.flush(); return
            for kt in range(4):
                P.dma('pool', lambda kt=kt: nc.gpsimd.dma_start(
                    out=cak[:, kt], in_=self.ctx_ak_d[:, kt * 128:(kt + 1) * 128, :].rearrange("h p e -> p h e")),
                    w=[('cak', kt)])
                P.dma('pool', lambda kt=kt: nc.gpsimd.dma_start(
                    out=vA[:, kt, :, 0:128], in_=self.ctx_av_d[:, kt * 128:(kt + 1) * 128, :].rearrange("h p e -> p h e")),
                    w=[('vA', kt)])
                P.dma('pool', lambda kt=kt: nc.gpsimd.dma_start(
                    out=cbk[:, kt, :].rearrange("p (h e) -> p h e", h=2),
                    in_=self.ctx_bk_d[:, kt * 128:(kt + 1) * 128, :].rearrange("h p e -> p h e")), w=[('cbk', kt)])
                P.dma('pool', lambda kt=kt: nc.gpsimd.dma_start(
                    out=vB[:, kt, :, 0:64], in_=self.ctx_bv_d[:, kt * 128:(kt + 1) * 128, :].rearrange("h p e -> p h e")),
                    w=[('vB', kt)])
            for h in range(5):
                for kt in range(4):
                    src = cak[:, kt, h, :] if h < 4 else cbk[:, kt, :]
                    P.op('pe', lambda src=src, kt=kt: nc.tensor.transpose(ptp[:, kt * 128:(kt + 1) * 128], src, self.ident_b[:]),
                         r=[('cak', kt), ('cbk', kt), 'ident_b'], w=['ptp'])
                dst = kaT[:, h, 0:512] if h < 4 else kbT[:, 0:512]
                P.op('dve', lambda dst=dst: nc.vector.tensor_copy(out=dst, in_=ptp[:, 0:512]), r=['ptp'],
                     w=[('kaTc', h)])
            if self.stop == 'B1c':
                P.flush(); return
            chunks = []
            for a in range(4):
                chunks.append((lambda th, a=a: qaT[:, a, th * 512:(th + 1) * 512], (self.ev_w_in_d, a * 128),
                               (self.ev_wx_d, 512 + a * 128), None, ('qaT', a)))
            for a in range(4):
                chunks.append((lambda th, a=a: kaT[:, a, 512 + th * 512:512 + (th + 1) * 512], (self.ev_w_in_d, 512 + a * 128),
                               (self.ev_wx_d, 1024 + a * 128), None, ('kaT', a)))
            for c in range(4):
                chunks.append((lambda th, c=c: qbT[:, c, th * 512:(th + 1) * 512], (self.ev_wx_d, c * 128),
                               (self.ev_wx_d, 1536 + c * 128), ('gq', 'gq_sw'), ('qbT', c)))
            chunks.append((lambda th: kbT[:, 512 + th * 512:512 + (th + 1) * 512], (self.ev_w_in_d, 2048),
                           (self.ev_wx_d, 2048), ('gk', 'gk_sw'), ('kbT',)))
            it = 0
            for ci_, (dst_fn, (wd, c0), (wsd, cs0), gn, dkey) in enumerate(chunks):
                if ci_ % 4 == 0:
                    nb_ = 512 if ci_ < 12 else 128
                    wn_, wnk = self.load_w(wd[:, c0:c0 + nb_].rearrange("(k p) n -> p k n", p=128), (8, nb_))
                    ws_, wsk = self.load_w(wsd[:, cs0:cs0 + nb_].rearrange("(k p) n -> p k n", p=128), (8, nb_))
                wo_ = (ci_ % 4) * 128
                wn = wn_[:, :, wo_:wo_ + 128]
                ws = ws_[:, :, wo_:wo_ + 128]
                for th in range(2):
                    b = it % 2
                    it += 1
                    for (pp, ww, wk_, nm) in ((pq[b], wn, wnk, 'pq'), (pqs[b], ws, wsk, 'pqs')):
                        def mm(pp=pp, ww=ww, th=th):
                            ins = None
                            for k in range(8):
                                ins = nc.tensor.matmul(pp[:], ww[:, k, :], self.hT[:, k, th * 512:(th + 1) * 512],
                                                       start=(k == 0), stop=(k == 7))
                            return ins
                        P.op('pe', mm, r=[wk_] + [('hT', k) for k in range(8)], w=[(nm, b)])
                    dst = dst_fn(th)
                    if gn is None:
                        P.op('dve', lambda b=b, th=th: nc.vector.tensor_tensor(
                            out=t1[b][:], in0=pq[b][:], in1=cosT[:, th * 512:(th + 1) * 512], op=ALU.mult),
                            r=[('pq', b), 'cosT'], w=[('t1', b)])
                        P.op('dve', lambda b=b, th=th: nc.vector.tensor_tensor(
                            out=t2[b][:], in0=pqs[b][:], in1=sinT[:, th * 512:(th + 1) * 512], op=ALU.mult),
                            r=[('pqs', b), 'sinT'], w=[('t2', b)])
                        P.op('dve', lambda b=b, dst=dst: nc.vector.tensor_tensor(out=dst, in0=t1[b][:], in1=t2[b][:], op=ALU.add),
                             r=[('t1', b), ('t2', b)], w=[dkey + (th,)])
                    else:
                        P.op('act', lambda b=b: nc.scalar.activation(out=sqb[:], in_=pq[b][:], func=AF.Square),
                             r=[('pq', b)], w=['sqb'])
                        P.op('pe', lambda: nc.tensor.matmul(pss[:], self.bd_ones[:], sqb[:], start=True, stop=True),
                             r=['sqb', 'bd_ones'], w=['pss'])
                        P.op('act', lambda: nc.scalar.activation(out=rsb[:], in_=pss[:], func=AF.Ln, scale=1.0 / 64,
                                                                 bias=self.epsc[:, 0:1]), r=['pss', 'epsc'], w=['rsb'])
                        P.op('act', lambda: nc.scalar.activation(out=rsb[:], in_=rsb[:], func=AF.Exp, scale=-0.5),
                             r=['rsb'], w=['rsb'])
                        P.op('dve', lambda b=b, gn=gn: nc.vector.scalar_tensor_tensor(
                            out=t1[b][:], in0=pq[b][:], scalar=self.vcol(gn[0]), in1=rsb[:], op0=ALU.mult, op1=ALU.mult),
                            r=[('pq', b), 'rsb', 'vecs'], w=[('t1', b)])
                        P.op('dve', lambda b=b, gn=gn: nc.vector.scalar_tensor_tensor(
                            out=t2[b][:], in0=pqs[b][:], scalar=self.vcol(gn[1]), in1=rsb[:], op0=ALU.mult, op1=ALU.mult),
                            r=[('pqs', b), 'rsb', 'vecs'], w=[('t2', b)])
                        P.op('dve', lambda b=b, th=th: nc.vector.tensor_tensor(
                            out=t1[b][:], in0=t1[b][:], in1=cosT[:, th * 512:(th + 1) * 512], op=ALU.mult),
                            r=[('t1', b), 'cosT'], w=[('t1', b)])
                        P.op('dve', lambda b=b, th=th: nc.vector.tensor_tensor(
                            out=t2[b][:], in0=t2[b][:], in1=sinT[:, th * 512:(th + 1) * 512], op=ALU.mult),
                            r=[('t2', b), 'sinT'], w=[('t2', b)])
                        P.op('dve', lambda b=b, dst=dst: nc.vector.tensor_tensor(out=dst, in0=t1[b][:], in1=t2[b][:], op=ALU.add),
                             r=[('t1', b), ('t2', b)], w=[dkey + (th,)])
            if self.stop == 'B1r':
                P.flush(); return
            wkv = []
            for (c0, n) in ((512, 512), (1024, 512), (2048, 256)):
                wkv.append(self.load_w(self.ev_w_in_d[:, c0:c0 + n].rearrange("(k p) n -> p k n", p=128), (8, n)) + (n,))
            si = 0
            for tt in range(8):
                seg, r0 = tt // 2, (tt % 2) * 128
                for bi, (wt, wkey, n) in enumerate(wkv):
                    pm = ptm[(tt * 3 + bi) % 2]
                    pk = ('ptm', (tt * 3 + bi) % 2)

                    def mm(wt=wt, pm=pm, n=n, tt=tt):
                        ins = None
                        for k in range(8):
                            ins = nc.tensor.matmul(pm[:, 0:n], self.hT[:, k, tt * 128:(tt + 1) * 128], wt[:, k, :],
                                                   start=(k == 0), stop=(k == 7))
                        return ins
                    P.op('pe', mm, r=[wkey] + [('hT', k) for k in range(8)], w=[pk])
                    if 'tm_evac' in self.skip: continue
                    if 'tm_evac2' in self.skip and bi == 2: continue
                    if bi < 2:
                        sg_ = stg[si % 2]
                        sk = ('stg', si % 2)
                        si += 1
                        P.op('act', lambda sg_=sg_, pm=pm: nc.scalar.copy(out=sg_[:], in_=pm[:]), r=[pk], w=[sk])
                        od = self.o_ak_d if bi == 0 else self.o_av_d
                        if 'outdma' not in self.skip: P.dma('sp', lambda sg_=sg_, od=od, seg=seg, r0=r0: nc.sync.dma_start(
                            out=od[seg, :, r0:r0 + 128, :].rearrange("h p e -> p h e"),
                            in_=sg_[:].rearrange("p (h e) -> p h e", h=4)), r=[sk], w=[('ocache', bi, tt)])
                        if bi == 1 and 'vcopy' not in self.skip:
                            P.op('dve', lambda pm=pm, tt=tt: nc.vector.tensor_copy(
                                out=vA[:, 4 + tt, :, 0:128], in_=pm[:].rearrange("p (h e) -> p h e", h=4)),
                                r=[pk], w=[('vA', 4 + tt)])
                    else:
                        sg_ = stg[si % 2]
                        sk = ('stg', si % 2)
                        si += 1
                        P.op('dve', lambda pm=pm, tt=tt: nc.vector.tensor_copy(
                            out=vB[:, 4 + tt, :, 0:64], in_=pm[:, 128:256].rearrange("p (h e) -> p h e", h=2)),
                            r=[pk], w=[('vB', 4 + tt)])
                        P.op('act', lambda sg_=sg_, pm=pm: nc.scalar.copy(out=sg_[:, 128:256], in_=pm[:, 128:256]), r=[pk], w=[sk])
                        P.op('act', lambda pm=pm: nc.scalar.copy(out=kbs[:], in_=pm[:, 0:128]), r=[pk], w=['kbs'])
                        for h in range(2):
                            if 'accum' in self.skip: continue
                            P.op('dve', lambda h=h: nc.vector.scalar_tensor_tensor(
                                out=junk[:, 0:64], in0=kbs[:, h * 64:(h + 1) * 64], scalar=1.0, in1=kbs[:, h * 64:(h + 1) * 64],
                                op0=ALU.mult, op1=ALU.mult, accum_out=ssb[:, h:h + 1]), r=['kbs'], w=['junk', ('ssb', h)])
                        P.op('act', lambda: nc.scalar.activation(out=ssb[:], in_=ssb[:], func=AF.Ln, scale=1.0 / 64,
                                                                 bias=self.epsc[:, 0:1]),
                             r=[('ssb', 0), ('ssb', 1), 'epsc'], w=[('ssb', 0), ('ssb', 1)])
                        P.op('act', lambda: nc.scalar.activation(out=ssb[:], in_=ssb[:], func=AF.Exp, scale=-0.5),
                             r=[('ssb', 0), ('ssb', 1)], w=[('ssb', 0), ('ssb', 1)])
                        for h in range(2):
                            P.op('dve', lambda h=h, sg_=sg_: nc.vector.scalar_tensor_tensor(
                                out=sg_[:, h * 64:(h + 1) * 64], in0=kbs[:, h * 64:(h + 1) * 64], scalar=ssb[:, h:h + 1],
                                in1=bvec[:, 128:192], op0=ALU.mult, op1=ALU.mult),
                                r=['kbs', ('ssb', h), 'bvec'], w=[sk])
                        if 'outdma2' not in self.skip: P.dma('sp', lambda sg_=sg_, seg=seg, r0=r0: nc.sync.dma_start(
                            out=self.o_bk_d[seg, :, r0:r0 + 128, :].rearrange("h p e -> p h e"),
                            in_=sg_[:, 0:128].rearrange("p (h e) -> p h e", h=2)), r=[sk], w=[('ocache', 2, tt)])
                        if 'outdma2' not in self.skip: P.dma('sp', lambda sg_=sg_, seg=seg, r0=r0: nc.sync.dma_start(
                            out=self.o_bv_d[seg, :, r0:r0 + 128, :].rearrange("h p e -> p h e"),
                            in_=sg_[:, 128:256].rearrange("p (h e) -> p h e", h=2)), r=[sk], w=[('ocache', 3, tt)])
            if self.stop == 'B1a':
                P.flush(); return
            self.dump('qaT', qaT, [128, 4, T], [])
            self.dump('kaT', kaT, [128, 4, 512 + T], [])
            self.dump('qbT', qbT, [128, 4, T], [])
            self.dump('kbT', kbT, [128, 512 + T], [])
            P.flush()
        if self.stop == 'B1':
            return
        with ExitStack() as e2:
            NPT = 6
            pt = [self.sb(e2, "pt%d" % i, [128, 256], BF16) for i in range(NPT)]
            dbuf = self.sb(e2, "dbuf", [128, 32, 128], F32)
            ssq = self.sb(e2, "ssq", [128, 32], F32)
            a1b = [self.sb(e2, "a1b%d" % i, [128, 128], F32) for i in range(2)]
            rr = self.sb(e2, "rr", [128, 8], F32)
            junk2 = self.sb(e2, "junk2", [128, 128], F32)
            sT = [self.ps(e2, "sT%d" % i, [128, 512]) for i in range(4)]
            acc = [self.ps(e2, "acc%d" % i, [128, 512]) for i in range(4)]
            steps = []
            g = 0
            for a in range(4):
                for s_ in range(4):
                    for kb in range(12):
                        for comp in range(2):
                            steps.append(dict(kind='A', a=a, s=s_, comp=comp, kb=kb, g=g + comp, last=(kb == 11)))
                    g += 2
            for c_ in range(4):
                for s_ in range(4):
                    for kb in range(12):
                        for hi in range(2):
                            steps.append(dict(kind='B', h=c_ + 4 * hi, s=s_, kb=kb, g=g + hi, last=(kb == 11)))
                    g += 2
            LA = 2
            nst = len(steps)

            def emit_S(i, st):
                slot = i % 4
                dstp = sT[slot][:, 0:256]
                s_, kb = st['s'], st['kb']
                if st['kind'] == 'A':
                    a, comp = st['a'], st['comp']
                    lhsT = kaT[comp * 64:(comp + 1) * 64, a, kb * 128:(kb + 1) * 128]
                    rhs = qaT[comp * 64:(comp + 1) * 64, a, s_ * 256:(s_ + 1) * 256]
                else:
                    h = st['h']
                    gk = h // 4
                    lhsT = kbT[gk * 64:(gk + 1) * 64, kb * 128:(kb + 1) * 128]
                    rhs = qbT[gk * 64:(gk + 1) * 64, h % 4, s_ * 256:(s_ + 1) * 256]
                P.op('pe', lambda: nc.tensor.matmul(dstp, lhsT, rhs, start=True, stop=True), r=[], w=[('sT', slot)])
                ptb = pt[i % NPT]
                col = s_ * 12 + kb
                P.op('act', lambda: nc.scalar.activation(out=ptb[:], in_=dstp, func=AF.Exp, scale=SC,
                                                         bias=self.vcol('amask', col)),
                     r=[('sT', slot)], w=[('pt', i % NPT)])

            def emit_PV(i, st):
                ptb = pt[i % NPT]
                kb = st['kb']
                ab = acc[st['g'] % 4]
                if st['kind'] == 'A':
                    rhs = vA[:, kb, st["a"], 0:129]
                    n = 129
                else:
                    rhs = vB[:, kb, st["h"] // 4, 0:65]
                    n = 65
                for qh in range(2):
                    P.op('pe', lambda qh=qh: nc.tensor.matmul(ab[:, qh * 256:qh * 256 + n], ptb[:, qh * 128:(qh + 1) * 128], rhs,
                                                              start=(kb == 0 and qh == 0), stop=(kb == 11),
                                                              skip_group_check=True),
                         r=[('pt', i % NPT)], w=[('acc', st['g'] % 4)])
                if not st['last']:
                    return
                s_ = st['s']
                if st['kind'] == 'A':
                    if st['comp'] == 0:
                        return
                    a = st['a']
                    ab0, ab1 = acc[(st['g'] - 1) % 4], acc[st['g'] % 4]
                    k0, k1 = ('acc', (st['g'] - 1) % 4), ('acc', st['g'] % 4)
                    for qh in range(2):
                        u = (a * 4 + s_) * 2 + qh
                        o0 = qh * 256
                        P.op('dve', lambda o0=o0: nc.vector.reciprocal(out=rr[:, 0:1], in_=ab0[:, o0 + 128:o0 + 129]), r=[k0], w=['rr0'])
                        P.op('dve', lambda o0=o0: nc.vector.reciprocal(out=rr[:, 1:2], in_=ab1[:, o0 + 128:o0 + 129]), r=[k1], w=['rr1'])
                        P.op('dve', lambda: nc.vector.tensor_tensor(out=rr[:, 2:3], in0=rr[:, 1:2], in1=nlam[:], op=ALU.mult),
                             r=['rr1'], w=['rr2'])
                        a1 = a1b[u % 2]
                        P.op('dve', lambda o0=o0, a1=a1: nc.vector.tensor_scalar_mul(out=a1[:], in0=ab0[:, o0:o0 + 128], scalar1=rr[:, 0:1]),
                             r=[k0, 'rr0'], w=[('a1b', u % 2)])
                        P.op('dve', lambda o0=o0, a1=a1, u=u: nc.vector.scalar_tensor_tensor(
                            out=dbuf[:, u, :], in0=ab1[:, o0:o0 + 128], scalar=rr[:, 2:3], in1=a1[:], op0=ALU.mult, op1=ALU.add),
                            r=[k1, 'rr2', ('a1b', u % 2)], w=[('dbuf', u)])
                        P.op('dve', lambda u=u: nc.vector.scalar_tensor_tensor(
                            out=junk2[:], in0=dbuf[:, u, :], scalar=1.0, in1=dbuf[:, u, :], op0=ALU.mult, op1=ALU.mult,
                            accum_out=ssq[:, u:u + 1]), r=[('dbuf', u)], w=['junk2', ('ssq', u)])
                else:
                    h = st['h']
                    ab0 = acc[st['g'] % 4]
                    k0 = ('acc', st['g'] % 4)
                    for qh in range(2):
                        o0 = qh * 256
                        qt = s_ * 2 + qh
                        P.op('dve', lambda o0=o0: nc.vector.reciprocal(out=rr[:, 4:5], in_=ab0[:, o0 + 64:o0 + 65]), r=[k0], w=['rr4'])
                        P.op('dve', lambda o0=o0, qt=qt, h=h: nc.vector.tensor_scalar_mul(
                            out=ocat[:, qt, 512 + h * 64:512 + (h + 1) * 64], in0=ab0[:, o0:o0 + 64], scalar1=rr[:, 4:5]),
                            r=[k0, 'rr4'], w=[('ocat', qt, 4 + h // 2)])

            for j in range(nst // 2 + 1):
                if 2 * j < nst:
                    emit_S(2 * j, steps[2 * j])
                    emit_S(2 * j + 1, steps[2 * j + 1])
                if j >= 1:
                    emit_PV(2 * j - 2, steps[2 * j - 2])
                    emit_PV(2 * j - 1, steps[2 * j - 1])
            P.op('act', lambda: nc.scalar.activation(out=ssq[:], in_=ssq[:], func=AF.Ln, scale=1.0 / 128, bias=self.epsc[:, 0:1]),
                 r=[('ssq', u) for u in range(32)], w=['rstdA'])
            P.op('act', lambda: nc.scalar.activation(out=ssq[:], in_=ssq[:], func=AF.Exp, scale=-0.5), r=['rstdA'], w=['rstdA'])
            for a in range(4):
                for s_ in range(4):
                    for qh in range(2):
                        u = (a * 4 + s_) * 2 + qh
                        qt = s_ * 2 + qh
                        eng = 'dve'
                        E = nc.vector
                        P.op(eng, lambda E=E, u=u, qt=qt, a=a: E.scalar_tensor_tensor(
                            out=ocat[:, qt, a * 128:(a + 1) * 128], in0=dbuf[:, u, :], scalar=ssq[:, u:u + 1], in1=gsub8[:],
                            op0=ALU.mult, op1=ALU.mult), r=[('dbuf', u), 'rstdA', 'gsub8'], w=[('ocat', qt, a)])
            self.dump('ocat', ocat, [128, 8, D], [])
            P.flush()
        if self.stop == 'B2':
            return
        with ExitStack() as e3:
            self.alloc_w(e3, 2, 4096)
            ptp = [self.ps(e3, "otp%d" % i, [128, 1024], BF16) for i in range(2)]
            pmx = [self.ps(e3, "pmx%d" % i, [128, 512]) for i in range(2)]
            n = 0
            for c in range(8):
                for gq in range(2):
                    pp = ptp[n % 2]
                    for j in range(4):
                        qt = gq * 4 + j
                        P.op('pe', lambda pp=pp, j=j, qt=qt, c=c: nc.tensor.transpose(
                            pp[:, j * 128:(j + 1) * 128], ocat[:, qt, c * 128:(c + 1) * 128], self.ident_b[:]),
                            r=[], w=[('otp', n % 2)])
                    if n % 2 == 0:
                        P.op('dve', lambda pp=pp, c=c, gq=gq: nc.vector.tensor_copy(out=self.hT[:, c, gq * 512:(gq + 1) * 512], in_=pp[:, 0:512]),
                             r=[('otp', n % 2)], w=[('hT', c)])
                    else:
                        P.op('act', lambda pp=pp, c=c, gq=gq: nc.scalar.copy(out=self.hT[:, c, gq * 512:(gq + 1) * 512], in_=pp[:, 0:512]),
                             r=[('otp', n % 2)], w=[('hT', c)])
                    n += 1
            self.out_proj(self.ev_w_out_d, pmx, l)
            self.dump('xm0', self.xT, [128, NCH, T], [('xT', c) for c in range(8)])
            P.flush()


def _out_proj(self, w_d, pmx, l):
    P, nc = self.P, self.nc
    n = 0
    for c in range(8):
        if c % 4 == 0:
            wt, wkey = self.load_w(w_d[:, c * 128:c * 128 + 512].rearrange("(k p) n -> p k n", p=128), (8, 512))
        co = (c % 4) * 128
        for th in range(2):
            pm = pmx[n % 2]
            pk = ('pmx', n % 2)
            n += 1

            def mm(wt=wt, pm=pm, th=th, co=co):
                ins = None
                for k in range(8):
                    ins = nc.tensor.matmul(pm[:], wt[:, k, co:co + 128], self.hT[:, k, th * 512:(th + 1) * 512],
                                           start=(k == 0), stop=(k == 7))
                return ins
            P.op('pe', mm, r=[wkey] + [('hT', k) for k in range(8)], w=[pk])
            P.op('dve', lambda pm=pm, c=c, th=th: nc.vector.scalar_tensor_tensor(
                out=self.xT[:, c, th * 512:(th + 1) * 512], in0=pm[:], scalar=self.mod[l][:, 16 + c:17 + c],
                in1=self.xT[:, c, th * 512:(th + 1) * 512], op0=ALU.mult, op1=ALU.add),
                r=[pk, ('xT', c)], w=[('xT', c)])


Builder.even_mixer = _even_mixer
Builder.out_proj = _out_proj


class Em:
    def __init__(self, B):
        self.B, self.P, self.nc = B, B.P, B.nc

    def V(self, eng):
        return self.nc.vector if eng == 'dve' else self.nc.gpsimd

    def tt(self, eng, out, in0, in1, op, r, w):
        self.P.op(eng, lambda: self.V(eng).tensor_tensor(out=out, in0=in0, in1=in1, op=op), r=r, w=w)

    def ts(self, eng, out, in0, s1, s2, op0, op1, r, w):
        if s2 is None:
            self.P.op(eng, lambda: self.V(eng).tensor_scalar(out=out, in0=in0, scalar1=s1, scalar2=None, op0=op0), r=r, w=w)
        else:
            self.P.op(eng, lambda: self.V(eng).tensor_scalar(out=out, in0=in0, scalar1=s1, scalar2=s2, op0=op0, op1=op1), r=r, w=w)

    def stt(self, out, in0, scalar, in1, op0, op1, r, w, accum=None):
        if accum is None:
            self.P.op('dve', lambda: self.nc.vector.scalar_tensor_tensor(out=out, in0=in0, scalar=scalar, in1=in1, op0=op0, op1=op1), r=r, w=w)
        else:
            self.P.op('dve', lambda: self.nc.vector.scalar_tensor_tensor(out=out, in0=in0, scalar=scalar, in1=in1, op0=op0, op1=op1,
                                                                         accum_out=accum), r=r, w=w)

    def act(self, out, in_, func, r, w, scale=1.0, bias=0.0):
        self.P.op('act', lambda: self.nc.scalar.activation(out=out, in_=in_, func=func, scale=scale, bias=bias), r=r, w=w)

    def cp(self, eng, out, in_, r, w):
        if eng == 'act':
            self.P.op('act', lambda: self.nc.scalar.copy(out=out, in_=in_), r=r, w=w)
        else:
            self.P.op(eng, lambda: self.V(eng).tensor_copy(out=out, in_=in_), r=r, w=w)

    def mm(self, out, lhsT, rhs, r, w, start=True, stop=True, sgc=False):
        if sgc:
            self.P.op('pe', lambda: self.nc.tensor.matmul(out, lhsT, rhs, start=start, stop=stop, skip_group_check=True), r=r, w=w)
        else:
            self.P.op('pe', lambda: self.nc.tensor.matmul(out, lhsT, rhs, start=start, stop=stop), r=r, w=w)

    def tr(self, out, in_, ident, r, w):
        self.P.op('pe', lambda: self.nc.tensor.transpose(out, in_, ident), r=r, w=w)

    def dma(self, q, out, in_, r, w):
        E = self.nc.sync if q == 'sp' else self.nc.gpsimd
        self.P.dma(q, lambda: E.dma_start(out=out, in_=in_), r=r, w=w)

    def memset(self, eng, ap, val, w):
        self.P.op(eng, lambda: self.V(eng).memset(ap, val), w=w)


HG0 = 0
RW0 = 2560
LWS = -0.6065306597126334
GN_EPS = 64e-5


def _proj_fm(self, em, w_d, c0, ncols, pp, pkey, pkeys=None):
    nc = self.nc
    wt, wkey = self.load_w(w_d[:, c0:c0 + ncols].rearrange("(k p) n -> p k n", p=128), (8, ncols))
    for th in range(2):
        def mm(wt=wt, th=th):
            ins = None
            for k in range(8):
                ins = nc.tensor.matmul(pp[0:ncols, th * 512:(th + 1) * 512], wt[:, k, :], self.hT[:, k, th * 512:(th + 1) * 512],
                                       start=(k == 0), stop=(k == 7))
            return ins
        self.P.op('pe', mm, r=[wkey] + [('hT', k) for k in range(8)], w=[pkeys[th] if pkeys else (pkey, th)])


def _decay(self, em, lw, Gp, D, rev, key):
    nc = self.nc
    self.P.op('dve', lambda: nc.vector.tensor_tensor_scan(out=Gp[:, 1:T + 1], data0=self.onesT[:], data1=lw, initial=0.0,
                                                          op0=ALU.mult, op1=ALU.add), r=[key + '_lw', 'onesT'], w=[key + '_Gp'])
    v3 = lambda ap: ap.rearrange("p (c l) -> p c l", l=64)
    if not rev:
        em.tt('dve', v3(D), v3(Gp[:, 1:T + 1]), Gp[:, 0:T:64].unsqueeze(2).broadcast_to([128, 16, 64]), ALU.subtract,
              r=[key + '_Gp'], w=[key + '_D'])
    else:
        em.tt('dve', v3(D), Gp[:, 64:T + 1:64].unsqueeze(2).broadcast_to([128, 16, 64]), v3(Gp[:, 0:T]), ALU.subtract,
              r=[key + '_Gp'], w=[key + '_D'])


def _odd_mixer(self):
    P, nc = self.P, self.nc
    em = Em(self)
    l = 1
    wd = self.od_w_in_d
    with ExitStack() as es:
        ocat = self.sb(es, "ocat1", [128, 8, D], BF16)
        self.onesT = self.sb(es, "onesT", [128, T], F32)
        masks = self.sb(es, "masks", [128, 4, 128], F32)
        bv1 = self.sb(es, "bv1", [128, 1280], F32)
        with ExitStack() as e0:
            self.rmsnorm(e0, lambda c: self.gm1[l][:, c:c + 1], lambda c: self.mod[l][:, c:c + 1],
                         lambda c: (self.hT[:, c, :], [('hT', c)]))
            em.memset('pool', self.onesT[:], 1.0, ['onesT'])
            em.dma('sp', masks[:], self.masks_d, [], ['masks'])
            em.dma('sp', bv1[:], self.bv1_d, [], ['bv1'])
            P.flush()
        self.dump('h1T', self.hT, [128, NCH, T], [])
        with ExitStack() as e1:
            self.alloc_w(e1, 3, 4096)
            osum = self.sb(e1, "osum_h", [128, 8, 512], F32)
            vtok = self.sb(e1, "vtok_h", [128, 8, 512], BF16)
            gsil = self.sb(e1, "gsil", [128, 8, 512], BF16)
            lbv = self.sb(e1, "lbv", [128, 8], F32)
            omlb = self.sb(e1, "omlb", [128, 8], F32)
            ssh = self.sb(e1, "ssh", [128, 32], F32)
            junk = self.sb(e1, "junkh", [128, 128], F32)
            HB = []
            for s_ in range(2):
                hb = {}
                for nm in ('fl', 'kf', 'lg', 'qs'):
                    hb[nm] = self.sb(e1, "h%s%d" % (nm, s_), [128, T], F32)
                hb['Gp'] = self.sb(e1, "hGp%d" % s_, [128, T + 1], F32)
                for nm in ('qt', 'qA', 'qB', 'ktl'):
                    hb[nm] = self.sb(e1, "h%s%d" % (nm, s_), [128, T], BF16)
                hb['Am'] = [self.sb(e1, "hAm%d_%d" % (s_, i), [128, 128], BF16) for i in range(2)]
                hb['ktok'] = [self.sb(e1, "hktok%d_%d" % (s_, i), [128, 128], BF16) for i in range(2)]
                hb['S'] = self.sb(e1, "hS%d" % s_, [128, 128], F32)
                hb['Stmp'] = self.sb(e1, "hStmp%d" % s_, [128, 128], F32)
                hb['Sb'] = [self.sb(e1, "hSb%d_%d" % (s_, i), [128, 128], BF16) for i in range(2)]
                hb['hbA'] = self.ps(e1, "hbA%d" % s_, [128, T])
                hb['hbB'] = self.ps(e1, "hbB%d" % s_, [128, T])
                HB.append(hb)
            ptk = [HB[0]['hbB'][:, 0:512], HB[0]['hbB'][:, 512:1024]]
            P.excl.update(['hbA', 'hbB'])
            P.alias.update({('ptk', 0): ('hbB', 0, 0), ('ptk', 1): ('hbB', 0, 1)})
            for s_ in range(2):
                em.memset('pool', HB[s_]['qA'][:], 0.0, [('h', s_, 'qA')])
                em.memset('pool', HB[s_]['qB'][:], 0.0, [('h', s_, 'qB')])
            em.tt('dve', lbv[:], self.vcol('lb1', 0, 8), self.vcol('lb0', 0, 8), ALU.subtract, r=[], w=['lbv'])
            em.act(lbv[:], lbv[:], AF.Sigmoid, r=['lbv'], w=['lbv'])
            em.ts('dve', omlb[:], lbv[:], -1.0, 1.0, ALU.mult, ALU.add, r=['lbv'], w=['omlb'])
            wv = self.load_w(wd[:, 1536:2048].rearrange("(k p) n -> p k n", p=128), (8, 512))
            wg = self.load_w(wd[:, 2048:2560].rearrange("(k p) n -> p k n", p=128), (8, 512))
            for tt in range(8):
                for bi, (wt, wkey) in enumerate((wv, wg)):
                    pm = ptk[bi]

                    def mm(wt=wt, pm=pm, tt=tt):
                        ins = None
                        for k in range(8):
                            ins = nc.tensor.matmul(pm[:], self.hT[:, k, tt * 128:(tt + 1) * 128], wt[:, k, :], start=(k == 0), stop=(k == 7))
                        return ins
                    P.op('pe', mm, r=[wkey], w=[('ptk', bi)])
                    if bi == 0:
                        em.cp('dve', vtok[:, tt, :], pm[:], r=[('ptk', bi)], w=[('vtok', tt)])
                    else:
                        em.act(gsil[:, tt, :], pm[:], AF.Silu, r=[('ptk', bi)], w=[('gsil', tt)])
            def hchain(slot, dr, hc):
                rev = dr == 1
                mask = masks[:, 1 if rev else 0, :]
                ci = dr * 4 + hc
                Kk = lambda n: ('h', slot, n)
                hb = HB[slot]
                fl, kf, lg, Gp, qsc = hb['fl'], hb['kf'], hb['lg'], hb['Gp'], hb['qs']
                qt, qA, qB, ktl = hb['qt'], hb['qA'], hb['qB'], hb['ktl']
                Am, ktok, S, Stmp, Sb = hb['Am'], hb['ktok'], hb['S'], hb['Stmp'], hb['Sb']
                hbA, hbB = hb['hbA'], hb['hbB']
                kA0, kA1, kB0, kB1 = ('hbA', slot, 0), ('hbA', slot, 1), ('hbB', slot, 0), ('hbB', slot, 1)
                Dd = lg
                enD = fl
                eD = Gp
                psc = hbA[:, 0:128]
                pktr = hbA[:, 512:1024].bitcast(BF16)[:, 0:128]
                po = hbB[:, 0:128]
                pds = hbB[:, 512:640]
                _proj_fm(self, em, wd, hc * 128, 128, hbA, None, pkeys=[kA0, kA1])
                em.act(qsc[:], hbA[:], AF.Silu, r=[kA0, kA1], w=[Kk('qs')])
                yield
                _proj_fm(self, em, wd, 512 + dr * 512 + hc * 128, 128, hbA, None, pkeys=[kA0, kA1])
                em.act(fl[:], hbA[:], AF.Sigmoid, r=[kA0, kA1], w=[Kk('fl')])
                em.ts('dve', fl[:], fl[:], omlb[:, ci:ci + 1], lbv[:, ci:ci + 1], ALU.mult, ALU.add, r=[Kk('fl'), 'omlb', 'lbv'], w=[Kk('fl')])
                yield
                em.ts('pool', kf[:], fl[:], -1.0, 1.0, ALU.mult, ALU.add, r=[Kk('fl')], w=[Kk('kf')])
                em.act(lg[:], fl[:], AF.Ln, r=[Kk('fl')], w=[Kk('lg')])
                em.memset('pool', Gp[:, 0:1], 0.0, [Kk('Gp')])
                P.op('dve', lambda: nc.vector.tensor_tensor_scan(out=Gp[:, 1:T + 1], data0=self.onesT[:], data1=lg[:], initial=0.0,
                                                                 op0=ALU.mult, op1=ALU.add), r=[Kk('lg'), 'onesT'], w=[Kk('Gp')])
                yield
                v3 = lambda ap: ap.rearrange("p (c l) -> p c l", l=64)
                if not rev:
                    em.tt('dve', v3(Dd[:]), v3(Gp[:, 1:T + 1]), Gp[:, 0:T:64].unsqueeze(2).broadcast_to([128, 16, 64]), ALU.subtract,
                          r=[Kk('Gp'), Kk('lg')], w=[Kk('lg')])
                else:
                    em.tt('dve', v3(Dd[:]), Gp[:, 64:T + 1:64].unsqueeze(2).broadcast_to([128, 16, 64]), v3(Gp[:, 0:T]), ALU.subtract,
                          r=[Kk('Gp'), Kk('lg')], w=[Kk('lg')])
                em.act(eD[:, 0:T], Dd[:], AF.Exp, r=[Kk('lg'), Kk('Gp')], w=[Kk('Gp')])
                em.act(enD[:], Dd[:], AF.Exp, r=[Kk('lg'), Kk('fl')], w=[Kk('fl')], scale=-1.0)
                yield
                em.tt('dve', qt[:], qsc[:], eD[:, 0:T], ALU.mult, r=[Kk('qs'), Kk('Gp')], w=[Kk('qt')])
                h3 = lambda ap: ap.rearrange("p (t l) -> p t l", l=128)
                em.cp('pool', h3(qA[:])[:, :, 0:64], h3(qt[:])[:, :, 0:64], r=[Kk('qt')], w=[Kk('qA')])
                em.cp('pool', h3(qB[:])[:, :, 64:128], h3(qt[:])[:, :, 64:128], r=[Kk('qt')], w=[Kk('qB')])
                em.tt('dve', ktl[:], kf[:], enD[:], ALU.mult, r=[Kk('kf'), Kk('fl')], w=[Kk('ktl')])
                em.dma('sp', S[:], self.st_h_d[dr, hc], [], [Kk('S')])
                em.cp('pool', Sb[0][:], S[:], r=[Kk('S')], w=[Kk('Sb0')])
                yield
                sbi = 0
                for ti in range(8):
                    tt = 7 - ti if rev else ti
                    tl = slice(tt * 128, (tt + 1) * 128)
                    b_ = ti % 2
                    em.mm(psc, ktl[:, tl], qt[:, tl], r=[Kk('ktl'), Kk('qt')], w=[kA0])
                    em.tt('dve', Am[b_][:], psc, mask, ALU.mult, r=[kA0, 'masks'], w=[Kk('Am%d' % b_)])
                    em.tr(pktr, ktl[:, tl], self.ident_b[:], r=[Kk('ktl')], w=[kA1])
                    em.cp('act', ktok[b_][:], pktr, r=[kA1], w=[Kk('ktok%d' % b_)])
                    vt = vtok[:, tt, hc * 128:(hc + 1) * 128]
                    order = (1, 0) if rev else (0, 1)
                    em.mm(po, Am[b_][:], vt, r=[Kk('Am%d' % b_), ('vtok', tt)], w=[kB0], start=True, stop=False)
                    yield
                    for oi, c in enumerate(order):
                        qh = qA if c == 0 else qB
                        cr = slice(c * 64, (c + 1) * 64)
                        em.mm(pds, ktok[b_][cr, :], vtok[cr, tt, hc * 128:(hc + 1) * 128], r=[Kk('ktok%d' % b_), ('vtok', tt)], w=[kB1])
                        em.mm(po, qh[:, tl], Sb[sbi][:], r=[Kk('qA'), Kk('qB'), Kk('Sb%d' % sbi)], w=[kB0], start=False, stop=(oi == 1))
                        em.tt('dve', Stmp[:], pds, S[:], ALU.add, r=[kB1, Kk('S')], w=[Kk('Stmp')])
                        cg = tt * 2 + c
                        ecol = cg * 64 + (0 if rev else 63)
                        em.ts('dve', S[:], Stmp[:], eD[:, ecol:ecol + 1], None, ALU.mult, None, r=[Kk('Stmp'), Kk('Gp')], w=[Kk('S')])
                        seg_end = (cg % 4 == 0) if rev else (cg % 4 == 3)
                        if seg_end:
                            seg = cg // 4
                            em.dma('sp', self.o_sh_d[seg, dr, hc], S[:], r=[Kk('S')], w=[('o_sh', seg, dr, hc)])
                            last = (cg == 0) if rev else (cg == 15)
                            if not last:
                                em.ts('dve', S[:], S[:], self.vcol('keep'), None, ALU.mult, None, r=[Kk('S')], w=[Kk('S')])
                        sbi = 1 - sbi
                        em.cp('pool', Sb[sbi][:], S[:], r=[Kk('S')], w=[Kk('Sb%d' % sbi)])
                        yield
                    if dr == 0:
                        em.cp('act', osum[:, tt, hc * 128:(hc + 1) * 128], po, r=[kB0], w=[('osum', tt, hc)])
                    else:
                        em.tt('dve', osum[:, tt, hc * 128:(hc + 1) * 128], po, osum[:, tt, hc * 128:(hc + 1) * 128], ALU.add,
                              r=[kB0, ('osum', tt, hc)], w=[('osum', tt, hc)])

            for dr in range(2):
                for hp in range(2):
                    gens = [hchain(0, dr, 2 * hp), hchain(1, dr, 2 * hp + 1)]
                    while gens:
                        for g_ in list(gens):
                            try:
                                next(g_)
                            except StopIteration:
                                gens.remove(g_)
            for tt in range(8):
                for hc in range(4):
                    u = tt * 4 + hc
                    em.stt(junk[:], osum[:, tt, hc * 128:(hc + 1) * 128], 1.0, osum[:, tt, hc * 128:(hc + 1) * 128], ALU.mult, ALU.mult,
                           r=[('osum', tt, hc)], w=['junkh', ('ssh', u)], accum=ssh[:, u:u + 1])
            em.act(ssh[:], ssh[:], AF.Ln, r=[('ssh', u) for u in range(32)], w=['rsh'], scale=1.0 / 128, bias=self.epsc[:, 0:1])
            em.act(ssh[:], ssh[:], AF.Exp, r=['rsh'], w=['rsh'], scale=-1.0 * 0.5)
            for tt in range(8):
                for hc in range(4):
                    u = tt * 4 + hc
                    em.stt(osum[:, tt, hc * 128:(hc + 1) * 128], osum[:, tt, hc * 128:(hc + 1) * 128], ssh[:, u:u + 1], bv1[:, 0:128],
                           ALU.mult, ALU.mult, r=[('osum', tt, hc), 'rsh', 'bv1'], w=[('osum', tt, hc)])
                em.tt('pool', ocat[:, tt, 0:512], osum[:, tt, :], gsil[:, tt, :], ALU.mult,
                      r=[('osum', tt, hc) for hc in range(4)] + [('gsil', tt)], w=[('ocat', tt, 0)])
            self.dump('ocat_h', ocat, [128, 8, D], [])
            P.flush()
        if self.stop == 'C1':
            return
        self.odd_rwkv(em, ocat, masks, bv1)
        with ExitStack() as e3:
            self.alloc_w(e3, 2, 4096)
            ptp = [self.ps(e3, "otp%d" % i, [128, 1024], BF16) for i in range(2)]
            pmx = [self.ps(e3, "pmx%d" % i, [128, 512]) for i in range(2)]
            n = 0
            for c in range(8):
                for gq in range(2):
                    pp = ptp[n % 2]
                    for j in range(4):
                        qt_ = gq * 4 + j
                        em.tr(pp[:, j * 128:(j + 1) * 128], ocat[:, qt_, c * 128:(c + 1) * 128], self.ident_b[:], r=[], w=[('otp', n % 2)])
                    em.cp('dve' if n % 2 == 0 else 'act', self.hT[:, c, gq * 512:(gq + 1) * 512], pp[:, 0:512], r=[('otp', n % 2)], w=[('hT', c)])
                    n += 1
            self.out_proj(self.od_w_out_d, pmx, l)
            self.dump('xm1', self.xT, [128, NCH, T], [])
            P.flush()


Builder.odd_mixer = _odd_mixer


def _odd_rwkv(self, em, ocat, masks, bv1):
    P, nc = self.P, self.nc
    wd = self.od_w_in_d
    idf = self.ident_f
    with ExitStack() as e2:
        sbf = lambda n, s, d=F32: self.sb(e2, n, s, d)
        e2b = ExitStack()
        sbb = lambda n, s, d=F32: self.sb(e2b, n, s, d)
        self.alloc_w(e2, 3, 1024)
        osum = sbf("osum_r", [128, 8, 512])
        bonus = sbf("bonus", [128, 8, 8])
        twd = sbf("twd", [128, T], BF16)
        adT = sbf("adT", [64, T], BF16)
        sgd = sbf("sgd", [128, T], BF16)
        wup = sbf("wup", [128, 512], BF16)
        aup = sbf("aup", [64, 512], BF16)
        gup = sbf("gup", [128, 512], BF16)
        hsel = sbf("hsel", [128, 2])
        omm = sbf("omm", [128, 15]); hmu = sbf("hmu", [128, 15]); hk = sbf("hk", [128, 15])
        zst = sbf("zst", [64, 16, 64]); sstg = [sbf("sstg%d" % i, [64, 64]) for i in range(2)]
        zst_b = sbf("zst_b", [64, 16, 64], BF16)

        m2 = sbf("m2", [128, 2, 256])
        pA2 = [self.ps(e2, "pA2_%d" % i, [128, T]) for i in range(2)]
        pBC = [self.ps(e2, "pBC_%d" % i, [128, T]) for i in range(2)]
        pj = pA2[0]
        pT = pj[:, 0:512]
        pS = pBC[0][:, 512:1024]
        P.excl.update(['pA', 'pB'])
        P.alias.update({('pj', 0): ('pA', 0, 0), ('pj', 1): ('pA', 0, 1), 'pT': ('pA', 0, 0), 'pS': ('pB', 0, 1)})
        vtok = sbb("vtok_p", [128, 8, 128], BF16)
        r_p = sbb("r_p", [128, T]); k_p = sbb("k_p", [128, T]); v_p = sbb("v_p", [128, T])
        a_p = sbb("a_p", [128, T]); kk_p = sbb("kk_p", [128, T]); kt_p = sbb("kt_p", [128, T]); b_p = sbb("b_p", [128, T])
        tmpf = sbb("tmpf", [128, T]); sqb = sbb("sqr", [128, T], BF16)
        Gp = sbb("Gpr", [128, T + 1]); Dd = self.rstd
        KR = [sbb("kr%d" % i, [128, 8, 256], BF16) for i in range(2)]; BE = [sbb("be%d" % i, [128, T], BF16) for i in range(2)]; TA = [sbb("ta%d" % i, [128, T], BF16) for i in range(2)]
        ED1 = sbb("eD1", [128, T])
        pd = tmpf; lw = v_p; Dp = a_p; enD = Gp; ED = [tmpf, ED1]
        P.alias.update({'pd': 'tmpf', 'lwraw': 'v_p', 'r_lw': 'v_p', 'Dp': 'a_p', 'enDr': 'r_Gp', ('eDr', 0): 'tmpf'})
        btk = [sbb("btk%d" % i, [128, 128], BF16) for i in range(2)]; ttk = [sbb("ttk%d" % i, [128, 128], BF16) for i in range(2)]
        nktk = [sbb("nktk%d" % i, [128, 128], BF16) for i in range(2)]
        M1b = [sbb("M1b%d" % i, [128, 2, 256], BF16) for i in range(2)]; M2b = [sbb("M2b%d" % i, [128, 2, 256], BF16) for i in range(2)]
        YA = [[sbb("YA%d_%d" % (s_, i), [128, 2, 128], BF16) for i in range(2)] for s_ in range(2)]
        YT_ = [[sbb("YT%d_%d" % (s_, i), [128, 2, 128], BF16) for i in range(2)] for s_ in range(2)]
        QQ = [[sbb("QQ%d_%d" % (s_, i), [128, 2, 128], BF16) for i in range(2)] for s_ in range(2)]
        rxb = [sbb("rxb%d" % i, [128, 2, 128], BF16) for i in range(2)]; xsb = [sbb("xsb%d" % i, [128, 2, 128], BF16) for i in range(2)]
        RAb = [sbb("RAb%d" % i, [64, 2, 128], BF16) for i in range(2)]; RBb = [sbb("RBb%d" % i, [64, 2, 128], BF16) for i in range(2)]
        GTb = [sbb("GTb%d" % i, [64, 2, 128]) for i in range(2)]; Z1b = [sbb("Z1b%d" % i, [64, 2, 128]) for i in range(2)]
        ztb = [sbb("ztb%d" % i, [64, 2, 64]) for i in range(2)]

        em.memset('pool', hsel[:], 0.0, ['hsel'])
        em.memset('pool', hsel[0:64, 0:1], 1.0, ['hsel'])
        em.memset('pool', hsel[64:128, 1:2], 1.0, ['hsel'])
        for s_ in range(2):
            em.memset('pool', RAb[s_][:], 0.0, [('t', s_, 'ra')])
            em.memset('pool', RBb[s_][:], 0.0, [('t', s_, 'rb')])
        em.memset('pool', Gp[:, 0:1], 0.0, ['r_Gp'])
        for dr in range(2):
            em.cp('pool', m2[:, dr, 0:128], masks[:, 2 + dr, :], r=['masks'], w=['m2'])
            em.cp('pool', m2[:, dr, 128:256], masks[:, dr, :], r=['masks'], w=['m2'])
        em.ts('dve', omm[:], self.vcol('mu', 0, 15), -1.0, 1.0, ALU.mult, ALU.add, r=[], w=['omm'])
        em.ts('dve', hmu[:], self.vcol('mu', 0, 15), 0.5, None, ALU.mult, None, r=[], w=['hmu'])
        em.ts('dve', hk[:], hmu[:], self.vcol('km1'), None, ALU.mult, None, r=['hmu'], w=['hk'])
        em.dma('pool', wup[:], self.w_up_d, [], ['wup'])
        em.dma('pool', aup[:], self.a_up_d, [], ['aup'])
        em.dma('pool', gup[:], self.g_up_d, [], ['gup'])
        sld = osum[0:64, 0:2, :].rearrange("p a (b k) -> p (a b) k", k=64)
        em.dma('sp', sld, self.st_r_d.rearrange("d h v k -> v (d h) k"), [], ['sld'])
        for i in range(16):
            em.tr(pS[0:64, 0:64], sld[:, i, :], idf[0:64, 0:64], r=['sld'], w=['pS'])
            em.cp('dve', zst[:, i, :], pS[0:64, 0:64], r=['pS'], w=[('zst', i)])
            em.cp('pool', zst_b[:, i, :], zst[:, i, :], r=[('zst', i)], w=[('zstb', i)])
        P.flush()

        def tshift(dst, j, nrows, rkeys, wkeys):
            pr = pj[0:nrows, :]
            em.act(dst, pr, AF.Identity, r=rkeys + ['omm'], w=wkeys, scale=omm[0:nrows, j:j + 1])
            em.stt(dst[:, 1:T], pr[:, 0:T - 1], hmu[0:nrows, j:j + 1], dst[:, 1:T], ALU.mult, ALU.add, r=rkeys + wkeys + ['hmu'], w=wkeys)
            em.stt(dst[:, 0:T - 1], pr[:, 1:T], hmu[0:nrows, j:j + 1], dst[:, 0:T - 1], ALU.mult, ALU.add, r=rkeys + wkeys + ['hmu'], w=wkeys)
            em.stt(dst[:, 256:T:256], pr[:, 255:T - 1:256], hk[0:nrows, j:j + 1], dst[:, 256:T:256], ALU.mult, ALU.add,
                   r=rkeys + wkeys + ['hk'], w=wkeys)
            em.stt(dst[:, 255:T - 1:256], pr[:, 256:T:256], hk[0:nrows, j:j + 1], dst[:, 255:T - 1:256], ALU.mult, ALU.add,
                   r=rkeys + wkeys + ['hk'], w=wkeys)

        PJ = [('pj', 0), ('pj', 1)]
        _proj_fm(self, em, wd, RW0 + 1536, 128, pj, 'pj')
        tshift(pd[:], 12, 128, PJ, ['pd'])
        em.act(twd[:], pd[:], AF.Tanh, r=['pd'], w=['twd'])
        _proj_fm(self, em, wd, RW0 + 1664, 64, pj, 'pj')
        tshift(pd[0:64, :], 13, 64, PJ, ['pd'])
        em.cp('pool', adT[:], pd[0:64, :], r=['pd'], w=['adT'])
        _proj_fm(self, em, wd, RW0 + 1728, 128, pj, 'pj')
        tshift(pd[:], 14, 128, PJ, ['pd'])
        em.act(sgd[:], pd[:], AF.Sigmoid, r=['pd'], w=['sgd'])
        if self.stop == 'C2a':
            P.flush(); e2b.close(); return
        for p in range(4):
            for nm, dst, j in (('r', r_p, p), ('k', k_p, 4 + p), ('v', v_p, 8 + p)):
                _proj_fm(self, em, wd, RW0 + j * 128, 128, pj, 'pj')
                tshift(dst[:], j, 128, PJ, [nm + '_p'])
            for tt in range(8):
                em.tr(pT[:, 0:128], v_p[:, tt * 128:(tt + 1) * 128], idf[:], r=['v_p'], w=['pT'])
                em.cp('act', vtok[:, tt, :], pT[:, 0:128], r=['pT'], w=[('vtok', tt)])
            for th in range(2):
                em.mm(pj[:, th * 512:(th + 1) * 512], aup[:, p * 128:(p + 1) * 128], adT[:, th * 512:(th + 1) * 512], r=['aup', 'adT'], w=[('pj', th)])
            em.act(a_p[:], pj[:], AF.Sigmoid, r=PJ, w=['a_p'], bias=self.vcol('a0', p))
            em.ts('dve', kk_p[:], k_p[:], self.vcol('k_k', p), None, ALU.mult, None, r=['k_p'], w=['kk_p'])
            em.act(sqb[:], kk_p[:], AF.Square, r=['kk_p'], w=['sqr'])
            for th in range(2):
                em.mm(pj[:, th * 512:(th + 1) * 512], self.bd_ones[:], sqb[:, th * 512:(th + 1) * 512], r=['sqr'], w=[('pj', th)])
            em.act(tmpf[:], pj[:], AF.Ln, r=PJ, w=['tmpf'], bias=self.epsc[:, 1:2])
            em.act(tmpf[:], tmpf[:], AF.Exp, r=['tmpf'], w=['tmpf'], scale=-0.5)
            em.tt('dve', kk_p[:], kk_p[:], tmpf[:], ALU.mult, r=['kk_p', 'tmpf'], w=['kk_p'])
            em.ts('dve', kt_p[:], a_p[:], -1.0, self.vcol('k_a', p), ALU.add, ALU.mult, r=['a_p'], w=['kt_p'])
            em.stt(kt_p[:], kt_p[:], 1.0, k_p[:], ALU.add, ALU.mult, r=['kt_p', 'k_p'], w=['kt_p'])
            em.tt('pool', b_p[:], a_p[:], kk_p[:], ALU.mult, r=['a_p', 'kk_p'], w=['b_p'])
            em.stt(tmpf[:], r_p[:], self.vcol('r_k', p), kt_p[:], ALU.mult, ALU.mult, r=['r_p', 'kt_p', 'tmpf'], w=['tmpf'])
            for tt in range(8):
                em.mm(pT[:, 0:2], tmpf[:, tt * 128:(tt + 1) * 128], hsel[:], r=['tmpf', 'hsel'], w=['pT'])
                em.cp('act', bonus[:, tt, 2 * p:2 * p + 2], pT[:, 0:2], r=['pT'], w=[('bonus', tt, p)])
                em.tt('pool', ocat[:, tt, 512 + p * 128:512 + (p + 1) * 128].rearrange('p (h e) -> p h e', e=64),
                      vtok[:, tt, :].rearrange('p (h e) -> p h e', e=64), bonus[:, tt, 2 * p:2 * p + 2].unsqueeze(2).broadcast_to([128, 2, 64]),
                      ALU.mult, r=[('vtok', tt), ('bonus', tt, p)], w=[('ocat', tt, 1)])
            def dir_prep(dr):
                rev = dr == 1
                kr, be, ta, eD = KR[dr], BE[dr], TA[dr], ED[dr]
                dslc = slice(dr * 64, (dr + 1) * 64)
                for th in range(2):
                    em.mm(pj[:, th * 512:(th + 1) * 512], wup[dslc, p * 128:(p + 1) * 128], twd[dslc, th * 512:(th + 1) * 512],
                          r=['wup', 'twd'], w=[('pj', th)])
                em.act(lw[:], pj[:], AF.Sigmoid, r=PJ, w=['lwraw'], bias=self.vcol('w0', dr * 4 + p))
                yield
                em.ts('dve', lw[:], lw[:], LWS, None, ALU.mult, None, r=['lwraw'], w=['r_lw'])
                yield
                em.memset('pool', Gp[:, 0:1], 0.0, ['r_Gp'])
                yield
                _decay(self, em, lw[:], Gp, Dd[:], rev, 'r')
                yield
                em.tt('pool', Dp[:], Dd[:], lw[:], ALU.subtract, r=['r_D', 'r_lw'], w=['Dp'])
                yield
                em.act(eD[:], Dd[:], AF.Exp, r=['r_D'], w=[('eDr', dr)])
                yield
                em.act(enD[:, 0:T], Dd[:], AF.Exp, r=['r_D'], w=['enDr'], scale=-1.0)
                yield
                em.act(Dp[:], Dp[:], AF.Exp, r=['Dp'], w=['Dp'])
                yield
                t3 = lambda ap: ap.rearrange("p (t l) -> p t l", l=128)
                em.tt('dve', kr[:, :, 0:128], t3(kk_p[:]), t3(Dp[:]), ALU.mult, r=['kk_p', 'Dp'], w=[('kr', dr)])
                yield
                em.tt('dve', kr[:, :, 128:256], t3(r_p[:]), t3(eD[:]), ALU.mult, r=['r_p', ('eDr', dr)], w=[('kr', dr)])
                yield
                em.tt('pool', be[:], b_p[:], enD[:, 0:T], ALU.mult, r=['b_p', 'enDr'], w=[('be', dr)])
                yield
                em.tt('pool', ta[:], kt_p[:], enD[:, 0:T], ALU.mult, r=['kt_p', 'enDr'], w=[('ta', dr)])
                yield

            def dir_tiles(dr, extra):
                rev = dr == 1
                kr, be, ta, eD = KR[dr], BE[dr], TA[dr], ED[dr]
                mk = m2[:, dr, :]
                mk3 = masks[:, 3 - dr, :]
                def tchain(slot, ti):
                    tt = 7 - ti if rev else ti
                    tl = slice(tt * 128, (tt + 1) * 128)
                    bb = slot
                    A2, BC = pA2[slot], pBC[slot]
                    kA = [('pA', slot, 0), ('pA', slot, 1)]
                    kB = [('pB', slot, 0), ('pB', slot, 1)]
                    m1, m2_, ya, yt, qq = M1b[slot], M2b[slot], YA[slot], YT_[slot], QQ[slot]
                    rx, xs, ra, rb, gt, z1, zt = rxb[slot], xsb[slot], RAb[slot], RBb[slot], GTb[slot], Z1b[slot], ztb[slot]
                    K_ = lambda n: ('t', slot, n)
                    zi0 = dr * 8 + 2 * p
                    HR = [slice(0, 64), slice(64, 128)]
                    v2 = lambda ap, n: ap.rearrange("p (h n) -> p h n", n=n)
                    vh = lambda ap: ap.rearrange("p (h n) -> p h n", n=512)
                    pTs = BC[:, 0:512].bitcast(BF16)
                    em.tr(pTs[:, 0:128], kr[:, tt, 0:128], self.ident_b[:], r=[('kr', dr)], w=[kB[0]])
                    em.tr(pTs[:, 128:256], be[:, tl], self.ident_b[:], r=[('be', dr)], w=[kB[0]])
                    em.tr(pTs[:, 256:384], ta[:, tl], self.ident_b[:], r=[('ta', dr)], w=[kB[0]])
                    em.act(nktk[bb][:], pTs[:, 0:128], AF.Identity, r=[kB[0]], w=[('nktk', bb)], scale=-1.0)
                    em.cp('dve', btk[bb][:], pTs[:, 128:256], r=[kB[0]], w=[('btk', bb)])
                    em.cp('act', ttk[bb][:], pTs[:, 256:384], r=[kB[0]], w=[('ttk', bb)])
                    for hh in range(2):
                        em.mm(A2[:, hh * 512:hh * 512 + 256], be[HR[hh], tl], kr[HR[hh], tt, :], r=[('be', dr), ('kr', dr)], w=[kA[hh]])
                    em.tt('dve', m1[:], vh(A2[:])[:, :, 0:256], mk.unsqueeze(1).broadcast_to([128, 2, 256]), ALU.mult, r=kA + ['m2'], w=[K_('m1')])
                    for hh in range(2):
                        em.mm(BC[:, hh * 512:hh * 512 + 128], kr[HR[hh], tt, 0:128], be[HR[hh], tl], r=[('kr', dr), ('be', dr)], w=[kB[hh]])
                    em.tt('dve', ya[0][:], vh(BC[:])[:, :, 0:128], mk3.unsqueeze(1).broadcast_to([128, 2, 128]), ALU.mult,
                          r=kB + ['masks'], w=[K_('yy0')])
                    yield
                    for hh in range(2):
                        em.mm(A2[:, hh * 512:hh * 512 + 256], ta[HR[hh], tl], kr[HR[hh], tt, :], r=[('ta', dr), ('kr', dr)], w=[kA[hh]])
                    em.tt('dve', m2_[:], vh(A2[:])[:, :, 0:256], mk.unsqueeze(1).broadcast_to([128, 2, 256]), ALU.mult, r=kA + ['m2'], w=[K_('m2')])
                    em.tt('pool', qq[0][:], idf[:].unsqueeze(1).broadcast_to([128, 2, 128]), m1[:, :, 0:128], ALU.subtract, r=[K_('m1')], w=[K_('qq0')])
                    yield
                    for hh in range(2):
                        em.mm(BC[:, 768 + hh * 64:832 + hh * 64], m2_[:, hh, 0:128], vtok[:, tt, hh * 64:(hh + 1) * 64], r=[K_('m2'), ('vtok', tt)], w=[kB[1]])
                    em.act(rx[:, :, 64:128], v2(BC[:, 768:896], 64), AF.Identity, r=[kB[1]], w=[K_('rx')], scale=-1.0)
                    em.cp('act', rx[:, :, 0:64], v2(nktk[bb][:], 64), r=[('nktk', bb)], w=[K_('rx')])
                    yi, qi = 0, 0
                    for j in range(6):
                        yn = 1 - yi
                        for hh in range(2):
                            Yc = ya[yi][:, hh, :]
                            YTc = m1[:, hh, 0:128] if j == 0 else yt[yi][:, hh, :]
                            rk = [K_('yy%d' % yi)] + ([K_('m1')] if j == 0 else [])
                            if j < 5:
                                em.mm(BC[:, hh * 256:hh * 256 + 128], YTc, Yc, r=rk, w=[kB[0]])
                                if j < 4:
                                    em.mm(BC[:, hh * 256 + 128:hh * 256 + 256], Yc, YTc, r=rk, w=[kB[0]])
                            if j >= 1:
                                em.mm(BC[:, 512 + hh * 128:640 + hh * 128], Yc, qq[qi][:, hh, :], r=[K_('yy%d' % yi), K_('qq%d' % qi)], w=[kB[1]])
                        if j < 5:
                            em.cp('act', ya[yn][:], v2(BC[:, 0:512], 256)[:, :, 0:128], r=[kB[0]], w=[K_('yy%d' % yn)])
                            if j < 4:
                                em.cp('dve', yt[yn][:], v2(BC[:, 0:512], 256)[:, :, 128:256], r=[kB[0]], w=[K_('yy%d' % yn)])
                        if j >= 1:
                            em.tt('dve', qq[1 - qi][:], v2(BC[:, 512:768], 128), qq[qi][:], ALU.add, r=[kB[1], K_('qq%d' % qi)], w=[K_('qq%d' % (1 - qi))])
                            qi = 1 - qi
                        yi = yn
                        yield
                    for hh in range(2):
                        em.mm(BC[:, 512 + hh * 128:640 + hh * 128], qq[qi][:, hh, :], rx[:, hh, :], r=[K_('qq%d' % qi), K_('rx')], w=[kB[1]])
                    em.cp('act', xs[:], v2(BC[:, 512:768], 128), r=[kB[1]], w=[K_('xs')])
                    yield
                    for hh in range(2):
                        em.mm(BC[0:64, 768 + hh * 128:896 + hh * 128], xs[:, hh, 0:64], m1[:, hh, 128:256], r=[K_('xs'), K_('m1')], w=[kB[1]])
                    for hh in range(2):
                        em.tt('dve', ra[:, hh, 0:64], BC[0:64, 768 + hh * 128:832 + hh * 128], kr[HR[hh], tt, 128:192], ALU.add, r=[kB[1], ('kr', dr)], w=[K_('ra')])
                        em.tt('dve', rb[:, hh, 64:128], BC[0:64, 832 + hh * 128:896 + hh * 128], kr[HR[hh], tt, 192:256], ALU.add, r=[kB[1], ('kr', dr)], w=[K_('rb')])
                    for hh in range(2):
                        em.mm(A2[:, 256 + hh * 64:320 + hh * 64], m1[:, hh, 128:256], xs[:, hh, 64:128], r=[K_('m1'), K_('xs')], w=[kA[0]],
                              start=(hh == 0), stop=False, sgc=True)
                        em.mm(A2[:, 256 + hh * 64:320 + hh * 64], m2_[:, hh, 128:256], vtok[:, tt, hh * 64:(hh + 1) * 64], r=[K_('m2'), ('vtok', tt)], w=[kA[0]],
                              start=False, stop=False, sgc=True)
                    for hh in range(2):
                        for c in range(2):
                            cr = slice(c * 64, (c + 1) * 64)
                            o_ = c * 512 + hh * 64
                            em.mm(BC[0:64, o_:o_ + 64], xs[cr, hh, 0:64], btk[bb][cr, hh * 64:(hh + 1) * 64], r=[K_('xs'), ('btk', bb)], w=[kB[c]])
                    g4 = lambda ap: ap.rearrange("p c (h e) -> p c h e", e=64)
                    em.tt('dve', g4(gt[:]), g4(vh(BC[0:64, :])[:, :, 0:128]),
                          idf[0:64, 0:64].unsqueeze(1).unsqueeze(1).broadcast_to([64, 2, 2, 64]), ALU.add, r=kB, w=[K_('gt')])
                    yield
                    for hh in range(2):
                        for c in range(2):
                            cr = slice(c * 64, (c + 1) * 64)
                            o_ = c * 512 + 128 + hh * 64
                            em.mm(BC[0:64, o_:o_ + 64], btk[bb][cr, hh * 64:(hh + 1) * 64], xs[cr, hh, 64:128], r=[K_('xs'), ('btk', bb)], w=[kB[c]],
                                  start=True, stop=False, sgc=True)
                            em.mm(BC[0:64, o_:o_ + 64], ttk[bb][cr, hh * 64:(hh + 1) * 64], vtok[cr, tt, hh * 64:(hh + 1) * 64],
                                  r=[('ttk', bb), ('vtok', tt)], w=[kB[c]], start=False, stop=True, sgc=True)
                    em.cp('act', z1[:], vh(BC[0:64, :])[:, :, 128:256], r=kB, w=[K_('z1')])
                    yield
                    order = (1, 0) if rev else (0, 1)
                    for oi, c in enumerate(order):
                        for hh in range(2):
                            zi = zi0 + hh
                            em.mm(BC[0:64, 256 + hh * 64:320 + hh * 64], gt[:, c, hh * 64:(hh + 1) * 64], zst[:, zi, :], r=[K_('gt'), ('zst', zi)], w=[kB[0]])
                        for hh in range(2):
                            zi = zi0 + hh
                            Rh = ra if c == 0 else rb
                            em.mm(A2[:, 256 + hh * 64:320 + hh * 64], Rh[:, hh, :], zst_b[:, zi, :], r=[K_('ra'), K_('rb'), ('zstb', zi)], w=[kA[0]],
                                  start=False, stop=(oi == 1), sgc=True)
                        em.tt('dve', zt[:], v2(BC[0:64, 256:384], 64), v2(z1[:, c, :], 64), ALU.add, r=[kB[0], K_('z1')], w=[K_('zt')])
                        cg = tt * 2 + c
                        ecol = cg * 64 + (0 if rev else 63)
                        seg_end = (cg % 4 == 0) if rev else (cg % 4 == 3)
                        for hh in range(2):
                            zi = zi0 + hh
                            head = 2 * p + hh
                            em.ts('dve', zst[:, zi, :], zt[:, hh, :], eD[HR[hh], ecol:ecol + 1], None, ALU.mult, None, r=[K_('zt'), ('eDr', dr)], w=[('zst', zi)])
                            if seg_end:
                                seg = cg // 4
                                sg = sstg[hh]
                                sk = ('sstg', hh)
                                em.tr(BC[0:64, 640 + hh * 64:704 + hh * 64], zst[:, zi, :], idf[0:64, 0:64], r=[('zst', zi)], w=[kB[1]])
                                em.cp('act', sg[:], BC[0:64, 640 + hh * 64:704 + hh * 64], r=[kB[1]], w=[sk])
                                em.dma('sp', self.o_sr_d[seg, dr, head], sg[:], r=[sk], w=[('o_sr', seg, dr, head)])
                                last = (cg == 0) if rev else (cg == 15)
                                if not last:
                                    em.ts('dve', zst[:, zi, :], zst[:, zi, :], self.vcol('keep')[0:64, :], None, ALU.mult, None,
                                          r=[('zst', zi)], w=[('zst', zi)])
                        em.cp('act', zst_b[:, zi0:zi0 + 2, :], zst[:, zi0:zi0 + 2, :], r=[('zst', zi0), ('zst', zi0 + 1)],
                              w=[('zstb', zi0), ('zstb', zi0 + 1)])
                        yield
                    ocols = slice(p * 128, (p + 1) * 128)
                    if dr == 0:
                        em.cp('act', osum[:, tt, ocols], A2[:, 256:384], r=[kA[0]], w=[('osum', tt, 2 * p), ('osum', tt, 2 * p + 1)])
                    else:
                        em.tt('dve', osum[:, tt, ocols], A2[:, 256:384], osum[:, tt, ocols], ALU.add,
                              r=[kA[0], ('osum', tt, 2 * p), ('osum', tt, 2 * p + 1)], w=[('osum', tt, 2 * p), ('osum', tt, 2 * p + 1)])

                NSTART = 2
                active = []
                nxt = 0
                while active or nxt < 8:
                    if nxt < 8 and len(active) < 2 and (not active or active[0][1] >= NSTART):
                        active.append([tchain(nxt % 2, nxt), 0])
                        nxt += 1
                    for ent in list(active):
                        try:
                            next(ent[0])
                            ent[1] += 1
                        except StopIteration:
                            active.remove(ent)
                    if extra is not None:
                        try:
                            next(extra)
                        except StopIteration:
                            extra = None
                if extra is not None:
                    for _ in extra:
                        pass
            for _ in dir_prep(0):
                pass
            dir_tiles(0, dir_prep(1))
            dir_tiles(1, None)
        P.flush()
        e2b.close()
        gtok = sbf("gtok", [128, 8, 512], BF16)
        for tt in range(8):
            em.mm(pT[:], sgd[:, tt * 128:(tt + 1) * 128], gup[:], r=['sgd', 'gup'], w=['pT'])
            em.cp('act', gtok[:, tt, :], pT[:], r=['pT'], w=[('gtok', tt)])
        mean = sbf("mean", [128, 8, 8]); var = sbf("var", [128, 8, 8]); sq2 = sbf("sq2", [128, 512]); cen = sbf("cen", [128, 8, 512])
        h4 = lambda ap: ap.rearrange("p (h e) -> p h e", e=64)
        for tt in range(8):
            OK = [('osum', tt, h) for h in range(8)]
            P.op('dve', lambda tt=tt: nc.vector.reduce_sum(out=mean[:, tt, :], in_=h4(osum[:, tt, :]), axis=AX.X), r=OK, w=[('mean', tt)])
            em.ts('dve', mean[:, tt, :], mean[:, tt, :], 1.0 / 64, None, ALU.mult, None, r=[('mean', tt)], w=[('mean', tt)])
            em.tt('dve', h4(cen[:, tt, :]), h4(osum[:, tt, :]), mean[:, tt, :].unsqueeze(2).broadcast_to([128, 8, 64]), ALU.subtract,
                  r=OK + [('mean', tt)], w=[('cen', tt)])
            em.tt('pool', sq2[:], cen[:, tt, :], cen[:, tt, :], ALU.mult, r=[('cen', tt)], w=['sq2'])
            P.op('dve', lambda tt=tt: nc.vector.reduce_sum(out=var[:, tt, :], in_=h4(sq2[:]), axis=AX.X), r=['sq2'], w=[('var', tt)])
        VK = [('var', tt) for tt in range(8)]
        em.act(var[:], var[:], AF.Ln, r=VK, w=['rstdr'], scale=1.0 / 64, bias=self.epsc[:, 2:3])
        em.act(var[:], var[:], AF.Exp, r=['rstdr'], w=['rstdr'], scale=-0.5)
        for tt in range(8):
            c3 = h4(cen[:, tt, :])
            em.tt('dve', c3, c3, var[:, tt, :].unsqueeze(2).broadcast_to([128, 8, 64]), ALU.mult, r=[('cen', tt), 'rstdr'], w=[('cen', tt)])
            em.tt('pool', cen[:, tt, :], cen[:, tt, :], bv1[:, 128:640], ALU.mult, r=[('cen', tt)], w=[('cen', tt)])
            em.tt('pool', cen[:, tt, :], cen[:, tt, :], bv1[:, 640:1152], ALU.add, r=[('cen', tt)], w=[('cen', tt)])
            em.tt('dve', cen[:, tt, :], cen[:, tt, :], ocat[:, tt, 512:1024], ALU.add, r=[('cen', tt), ('ocat', tt, 1)], w=[('cen', tt)])
            em.tt('pool', ocat[:, tt, 512:1024], cen[:, tt, :], gtok[:, tt, :], ALU.mult, r=[('cen', tt), ('gtok', tt)], w=[('ocat', tt, 1)])
        self.dump('ocat_r', ocat, [128, 8, D], [])
        P.flush()


Builder.odd_rwkv = _odd_rwkv


GRID_W = 64
def rope_tables(prompt):
    T = 1024
    if prompt:
        return np.stack([np.ones((128, T), np.float32), np.zeros((128, T), np.float32)])
    rows = (np.arange(T) // GRID_W).astype(np.float32)
    cols = (np.arange(T) % GRID_W).astype(np.float32)
    half = 16
    freq = np.power(np.float32(10000.0), -np.arange(half, dtype=np.float32) / half).astype(np.float32)
    cos = np.zeros((64, T), np.float32); sin = np.zeros((64, T), np.float32)
    for d in range(64):
        pos = rows if d < 32 else cols
        w = d % 32
        i = w % 16
        ang = (pos * freq[i]).astype(np.float32)
        cos[d] = np.cos(ang)
        sin[d] = -np.sin(ang) if w < 16 else np.sin(ang)
    return np.stack([np.concatenate([cos, cos]), np.concatenate([sin, sin])]).astype(np.float32)

def partner64():
    p = np.arange(64)
    w = p % 32
    return np.where(w < 16, p + 16, p - 16)

def ev_wx(ev_w_in):
    W = ev_w_in
    pr = partner64()
    qa = W[:, 0:512]; ka = W[:, 512:1024]; qb = W[:, 1536:2048]; kb = W[:, 2048:2176]
    hb_order = [0, 4, 1, 5, 2, 6, 3, 7]
    qbp = np.concatenate([qb[:, h * 64:(h + 1) * 64] for h in hb_order], axis=1)
    def sw(M):
        n = M.shape[1] // 64
        return np.concatenate([M[:, h * 64:(h + 1) * 64][:, pr] for h in range(n)], axis=1)
    return np.ascontiguousarray(np.concatenate([qbp, sw(qa), sw(ka), sw(qbp), sw(kb)], axis=1))

def amask(prompt):
    M = np.zeros((128, 48), np.float32)
    if prompt:
        for s in range(4):
            for kb in range(12):
                ok = kb >= 4 and (kb - 4) // 2 == s
                if not ok:
                    M[:, s * 12 + kb] = -30000.0
    return M

def host_vecs(vp, inp, cond, prompt):
    keep = 0.0 if prompt else 1.0
    V = np.zeros((128, vp.n), np.float32)
    def put(name, arr):
        c0, n = vp.cols[name]
        assert arr.shape == (128, n), (name, arr.shape, n)
        V[:, c0:c0+n] = arr
    put('cond', fm(cond))
    put('km1', np.full((128,1), keep - 1.0, np.float32))
    put('keep', np.full((128,1), keep, np.float32))
    for l in range(2):
        put('ada_b%d'%l, fm(inp['ada_b'][l]))
        put('nmg%d'%l, fm(inp['norm_mix_g'][l]))
        put('nfg%d'%l, fm(inp['norm_ffn_g'][l]))
        for i in range(3):
            put('cw%d_%d'%(i,l), fm(inp['ffn_conv_w'][l, i]))
        put('cb_%d'%l, fm(inp['ffn_conv_b'][l]))
    put('fng', fm(inp['final_norm_g']))
    pr = partner64()
    gq = inp['b_q_norm_g'][0]; gk = inp['b_k_norm_g'][0]
    put('gq', np.tile(gq, 2)[:, None]); put('gq_sw', np.tile(gq[pr], 2)[:, None])
    put('gk', np.tile(gk, 2)[:, None]); put('gk_sw', np.tile(gk[pr], 2)[:, None])
    put('amask', amask(prompt))
    return V

def bvec(inp):
    b = np.concatenate([inp['a_subln_g'][0], inp['b_k_norm_g'][0], inp['a_lambda'][0].reshape(-1)]).astype(np.float32)
    return np.ascontiguousarray(np.broadcast_to(b[None, :], (128, 448)))

def core_inputs(vp, inp, core):
    prompt = core < 4
    m = {}
    if prompt:
        m['x'] = np.ascontiguousarray(inp['x_prompt'][4 * core:4 * core + 4].reshape(1024, 1024))
        cond = inp['c_ctx']
        m['ctx_ak'] = np.zeros((4, 512, 128), np.float32); m['ctx_av'] = np.zeros((4, 512, 128), np.float32)
        m['ctx_bk'] = np.zeros((2, 512, 64), np.float32); m['ctx_bv'] = np.zeros((2, 512, 64), np.float32)
    else:
        b = core - 4
        m['x'] = np.ascontiguousarray(inp['x_sample'][b])
        cond = inp['c'][b]
        m['ctx_ak'] = np.ascontiguousarray(inp['cache_a_k'][b, 0]); m['ctx_av'] = np.ascontiguousarray(inp['cache_a_v'][b, 0])
        m['ctx_bk'] = np.ascontiguousarray(inp['cache_b_k'][b, 0]); m['ctx_bv'] = np.ascontiguousarray(inp['cache_b_v'][b, 0])
    m['vecs'] = host_vecs(vp, inp, cond, prompt)
    m['ident'] = np.eye(128, dtype=np.float32)
    m['rope'] = rope_tables(prompt)
    m['bvec'] = bvec(inp)
    m['ada_w'] = inp['ada_w']; m['ffn_w_up'] = inp['ffn_w_up']; m['ffn_w_down'] = inp['ffn_w_down']
    m['ev_w_in'] = np.ascontiguousarray(inp['ev_w_in'][0]); m['ev_wx'] = ev_wx(inp['ev_w_in'][0])
    m['ev_w_out'] = np.ascontiguousarray(inp['ev_w_out'][0])
    return m


def masks_const():
    idx = np.arange(128)
    blk = (idx[:, None] // 64) == (idx[None, :] // 64)
    s, t = idx[:, None], idx[None, :]
    M = np.stack([blk & (s <= t), blk & (s >= t), blk & (s < t), blk & (s > t)], axis=1)
    return np.ascontiguousarray(M.astype(np.float32))


def odd_inputs(vp, inp, core, m):
    prompt = core < 4
    V = m['vecs']
    def put(name, arr):
        c0, n = vp.cols[name]
        assert arr.shape == (128, n), (name, arr.shape, n)
        V[:, c0:c0+n] = arr
    lb = inp['hgrn_lb_logits']
    put('lb0', fm(lb[:, 0, :].reshape(-1)))
    put('lb1', fm(lb[:, 1, :].reshape(-1)))
    mu = inp['rwkv_mu'][0]
    MU = np.zeros((128, 15), np.float32)
    MU[:, 0:13] = fm(mu[0:1664])
    MU[0:64, 13] = mu[1664:1728]
    MU[:, 14] = mu[1728:1856]
    put('mu', MU)
    put('w0', fm(inp['rwkv_w0'][0].reshape(-1)))
    put('a0', fm(inp['rwkv_a0'][0]))
    put('k_k', fm(inp['rwkv_k_k'][0]))
    put('k_a', fm(inp['rwkv_k_a'][0]))
    put('r_k', fm(inp['rwkv_r_k'][0]))
    m['od_w_in'] = np.ascontiguousarray(inp['od_w_in'][0])
    m['od_w_out'] = np.ascontiguousarray(inp['od_w_out'][0])
    m['masks'] = masks_const()
    b = np.concatenate([inp['hgrn_norm_g'][0], inp['rwkv_ln_g'][0], inp['rwkv_ln_b'][0], np.zeros(128, np.float32)]).astype(np.float32)
    m['bv1'] = np.ascontiguousarray(np.broadcast_to(b[None, :], (128, 1280)))
    if prompt:
        m['st_h'] = np.zeros((2, 4, 128, 128), np.float32)
        m['st_r'] = np.zeros((2, 8, 64, 64), np.float32)
    else:
        bb = core - 4
        m['st_h'] = np.ascontiguousarray(inp['state_hgrn'][bb, 0])
        m['st_r'] = np.ascontiguousarray(inp['state_rwkv'][bb, 0])
    m['w_up'] = np.ascontiguousarray(inp['rwkv_w_up'][0].reshape(128, 512))
    m['a_up'] = np.ascontiguousarray(inp['rwkv_a_up'][0])
    m['g_up'] = np.ascontiguousarray(inp['rwkv_g_up'][0])
    return m


_BUILT = {}


def kernel(**inputs):
    inp = {k: np.asarray(v) for k, v in inputs.items()}
    B = Builder(debug=(), layers=(0, 1))
    B.build()
    maps = []
    for core in range(8):
        m = core_inputs(B.vp, inp, core)
        m = odd_inputs(B.vp, inp, core, m)
        maps.append(m)
    res = run_bass_kernel_spmd(B.nc, maps, core_ids=list(range(8)))
    R = res.results
    y_prompt = np.zeros((16, 256, 1024), np.float32)
    y_sample = np.zeros((4, 1024, 1024), np.float32)
    ak = np.zeros((16, 1, 4, 256, 128), np.float32)
    av = np.zeros((16, 1, 4, 256, 128), np.float32)
    bk = np.zeros((16, 1, 2, 256, 64), np.float32)
    bv = np.zeros((16, 1, 2, 256, 64), np.float32)
    sh = np.zeros((16, 1, 2, 4, 128, 128), np.float32)
    sr = np.zeros((16, 1, 2, 8, 64, 64), np.float32)
    for c in range(4):
        r = R[c]
        y_prompt[4 * c:4 * c + 4] = np.asarray(r['y']).reshape(4, 256, 1024)
        ak[4 * c:4 * c + 4, 0] = np.asarray(r['o_ak'])
        av[4 * c:4 * c + 4, 0] = np.asarray(r['o_av'])
        bk[4 * c:4 * c + 4, 0] = np.asarray(r['o_bk'])
        bv[4 * c:4 * c + 4, 0] = np.asarray(r['o_bv'])
        sh[4 * c:4 * c + 4, 0] = np.asarray(r['o_sh'])
        sr[4 * c:4 * c + 4, 0] = np.asarray(r['o_sr'])
    for b in range(4):
        y_sample[b] = np.asarray(R[4 + b]['y'])
    return (y_prompt, y_sample, ak, av, bk, bv, sh, sr)
```

```python
import numpy as np
from contextlib import ExitStack
import concourse.bass as bass
import concourse.mybir as mybir
from concourse.bass_utils import run_bass_kernel_spmd

F32 = mybir.dt.float32
BF16 = mybir.dt.bfloat16
AF = mybir.ActivationFunctionType
ALU = mybir.AluOpType
AX = mybir.AxisListType

ENGS = ('pe', 'dve', 'act', 'pool', 'sp')
NDS = 16


class Op:
    __slots__ = ('eng', 'fn', 'reads', 'writes', 'deps', 'need_inc', 'semkey', 'count', 'dma', 'prev_dma')

    def __init__(self, eng, fn, reads, writes, dma):
        self.eng = eng
        self.fn = fn
        self.reads = reads
        self.writes = writes
        self.deps = []
        self.need_inc = False
        self.semkey = None
        self.count = 0
        self.dma = dma
        self.prev_dma = None


class Prog:
    def __init__(self):
        self.nc = bass.Bass("TRN2", target_bir_lowering=False)
        nc = self.nc
        self.E = {'pe': nc.tensor, 'dve': nc.vector, 'act': nc.scalar, 'pool': nc.gpsimd, 'sp': nc.sync}
        self.es = ExitStack()
        self.sems = {}
        for e in ENGS:
            self.sems[e] = self.es.enter_context(nc.semaphore("sem_" + e))
        for q in ('sp', 'pool', 'act'):
            for i in range(NDS):
                self.sems[('dma', q, i)] = self.es.enter_context(nc.semaphore("dsem_%s_%d" % (q, i)))
        self.cnt = {k: 0 for k in self.sems}
        self.dma_rr = {'sp': 0, 'pool': 0, 'act': 0}
        self.last_dma_on_sem = {}
        self.seen = {e: {} for e in ENGS}
        self.pending = []
        self.last_writer = {}
        self.readers = {}
        self.n_ins = 0
        self.excl = set()
        self.alias = {}

    def op(self, eng, fn, r=(), w=()):
        self.pending.append(Op(eng, fn, tuple(r), tuple(w), False))

    def dma(self, q, fn, r=(), w=()):
        self.pending.append(Op(q, fn, tuple(r), tuple(w), True))

    def _wait(self, eng, key, val):
        if val <= 0:
            return
        if self.seen[eng].get(key, 0) >= val:
            return
        self.E[eng].wait_ge(self.sems[key], val)
        self.n_ins += 1
        self.seen[eng][key] = val

    def flush(self, barrier=True):
        ops = self.pending
        self.pending = []
        lw, rd = self.last_writer, self.readers
        for op in ops:
            deps = []
            if self.alias:
                op.reads = tuple(self.alias.get(b, b) for b in op.reads)
                op.writes = tuple(self.alias.get(b, b) for b in op.writes)
            ex = [b for b in op.reads if (b[0] if isinstance(b, tuple) else b) in self.excl]
            if ex:
                op.writes = tuple(op.writes) + tuple(b for b in ex if b not in op.writes)
                op.reads = tuple(b for b in op.reads if b not in ex)
            for b in op.reads:
                d = lw.get(b)
                if d is not None:
                    deps.append(d)
            for b in op.writes:
                d = lw.get(b)
                if d is not None:
                    deps.append(d)
                deps.extend(rd.get(b, ()))
            op.deps = [d for d in deps if d is not op]
            for d in op.deps:
                d.need_inc = True
            for b in op.reads:
                rd.setdefault(b, []).append(op)
            for b in op.writes:
                lw[b] = op
                rd[b] = []
        if barrier:
            last = {}
            for op in ops:
                last[op.eng] = op
            for op in last.values():
                op.need_inc = True
        for op in ops:
            if op.dma:
                i = self.dma_rr[op.eng] % NDS
                self.dma_rr[op.eng] += 1
                key = ('dma', op.eng, i)
                op.semkey = key
                self.cnt[key] += 16
                op.count = self.cnt[key]
                op.prev_dma = self.cnt[key] - 16
            elif op.need_inc:
                op.semkey = op.eng
                self.cnt[op.eng] += 1
                op.count = self.cnt[op.eng]
        for op in ops:
            need = {}
            for d in op.deps:
                if d.eng == 'pe' and op.eng == 'pe' and not d.dma and not op.dma:
                    continue
                k = d.semkey
                if need.get(k, 0) < d.count:
                    need[k] = d.count
            if op.dma and op.prev_dma:
                k = op.semkey
                if need.get(k, 0) < op.prev_dma:
                    need[k] = op.prev_dma
            for k, v in need.items():
                self._wait(op.eng, k, v)
            ins = op.fn()
            self.n_ins += 1
            if op.dma:
                ins.then_inc(self.sems[op.semkey], 16)
            elif op.need_inc:
                ins.then_inc(self.sems[op.eng], 1)
            op.fn = None
        if barrier:
            self.barrier()

    def barrier(self):
        for k, v in self.cnt.items():
            if k == 'sp':
                continue
            self._wait('sp', k, v)
        ins = self.E['sp'].nop()
        self.cnt['sp'] += 1
        ins.then_inc(self.sems['sp'], 1)
        for e in ENGS:
            if e != 'sp':
                self._wait(e, 'sp', self.cnt['sp'])
            for k, v in self.cnt.items():
                self.seen[e][k] = v
        self.last_writer = {}
        self.readers = {}


T = 1024
D = 1024
NCH = 8
DFF = 2816
NFF = 22
EPS = 1e-6


class VecPack:
    def __init__(self):
        self.cols = {}
        self.n = 0

    def add(self, name, ncols):
        self.cols[name] = (self.n, ncols)
        self.n += ncols
        return self.cols[name]


def build_vec_layout():
    vp = VecPack()
    vp.add('cond', 8)
    vp.add('km1', 1)
    vp.add('keep', 1)
    for l in range(2):
        vp.add('ada_b%d' % l, 48)
        vp.add('nmg%d' % l, 8)
        vp.add('nfg%d' % l, 8)
        vp.add('cw0_%d' % l, 44)
        vp.add('cw1_%d' % l, 44)
        vp.add('cw2_%d' % l, 44)
        vp.add('cb_%d' % l, 44)
    vp.add('fng', 8)
    vp.add('gq', 1); vp.add('gq_sw', 1); vp.add('gk', 1); vp.add('gk_sw', 1)
    vp.add('amask', 48)
    vp.add('lb0', 8); vp.add('lb1', 8); vp.add('mu', 15); vp.add('w0', 8)
    vp.add('a0', 4); vp.add('k_k', 4); vp.add('k_a', 4); vp.add('r_k', 4)
    return vp


def fm(v):
    v = np.asarray(v, dtype=np.float32)
    return np.ascontiguousarray(v.reshape(-1, 128).T)


class Builder:
    def __init__(self, debug=(), layers=(0, 1), do_mix=True):
        self.layers = layers
        import os
        self.stop = os.environ.get('KSTOP', '')
        self.skip = set(os.environ.get('KSKIP', '').split(','))
        self.do_mix = do_mix
        self.P = Prog()
        self.nc = self.P.nc
        self.debug = debug
        self.vp = build_vec_layout()
        self.dbg_outs = {}
        self.P.excl.update(['tp', 'mps', 'ssp', 'ups', 'ftp', 'pq', 'pqs', 'pss', 'ptm', 'ptp', 'sT', 'acc', 'otp', 'pmx'])

    def dram_in(self, name, shape, dt=F32):
        return self.nc.dram_tensor(name, list(shape), dt, kind="ExternalInput").ap()

    def dram_out(self, name, shape, dt=F32):
        return self.nc.dram_tensor(name, list(shape), dt, kind="ExternalOutput").ap()

    def sb(self, es, name, shape, dt):
        self.uid = getattr(self, 'uid', 0) + 1
        return es.enter_context(self.nc.sbuf_tensor("sb%d_%s" % (self.uid, name), list(shape), dt))

    def ps(self, es, name, shape, dt=F32):
        self.uid = getattr(self, 'uid', 0) + 1
        return es.enter_context(self.nc.psum_tensor("ps%d_%s" % (self.uid, name), list(shape), dt))

    def vcol(self, name, j=0, n=1):
        c0, nc_ = self.vp.cols[name]
        return self.vecs[:, c0 + j:c0 + j + n]

    def dump(self, name, sbt, shape, reads):
        if name not in self.debug:
            return
        P, nc = self.P, self.nc
        P.flush()
        dt = sbt.dtype if hasattr(sbt, 'dtype') else F32
        o = self.dram_out("dbg_" + name, shape, dt)
        self.dbg_outs[name] = (shape, dt)
        P.dma('sp', lambda: nc.sync.dma_start(out=o, in_=sbt[:]), r=reads, w=[('dbg', name)])

    def build(self):
        P, nc = self.P, self.nc
        top = ExitStack()
        self.top = top
        self.x_d = self.dram_in("x", [T, D])
        self.vecs_d = self.dram_in("vecs", [128, self.vp.n])
        self.ident_d = self.dram_in("ident", [128, 128])
        self.ada_w_d = self.dram_in("ada_w", [2, D, 6 * D])
        self.ffn_up_d = self.dram_in("ffn_w_up", [2, D, 2 * DFF])
        self.ffn_dn_d = self.dram_in("ffn_w_down", [2, DFF, D])
        self.y_d = self.dram_out("y", [T, D])
        self.ev_w_in_d = self.dram_in("ev_w_in", [D, 2304])
        self.ev_wx_d = self.dram_in("ev_wx", [D, 2176])
        self.ev_w_out_d = self.dram_in("ev_w_out", [D, D])
        self.rope_d = self.dram_in("rope", [2, 128, T])
        self.bvec_d = self.dram_in("bvec", [128, 448])
        self.ctx_ak_d = self.dram_in("ctx_ak", [4, 512, 128])
        self.ctx_av_d = self.dram_in("ctx_av", [4, 512, 128])
        self.ctx_bk_d = self.dram_in("ctx_bk", [2, 512, 64])
        self.ctx_bv_d = self.dram_in("ctx_bv", [2, 512, 64])
        self.o_ak_d = self.dram_out("o_ak", [4, 4, 256, 128])
        self.o_av_d = self.dram_out("o_av", [4, 4, 256, 128])
        self.o_bk_d = self.dram_out("o_bk", [4, 2, 256, 64])
        self.o_bv_d = self.dram_out("o_bv", [4, 2, 256, 64])
        self.od_w_in_d = self.dram_in("od_w_in", [D, 4416])
        self.od_w_out_d = self.dram_in("od_w_out", [D, D])
        self.masks_d = self.dram_in("masks", [128, 4, 128])
        self.bv1_d = self.dram_in("bv1", [128, 1280])
        self.st_h_d = self.dram_in("st_h", [2, 4, 128, 128])
        self.st_r_d = self.dram_in("st_r", [2, 8, 64, 64])
        self.w_up_d = self.dram_in("w_up", [128, 512])
        self.a_up_d = self.dram_in("a_up", [64, 512])
        self.g_up_d = self.dram_in("g_up", [128, 512])
        self.o_sh_d = self.dram_out("o_sh", [4, 2, 4, 128, 128])
        self.o_sr_d = self.dram_out("o_sr", [4, 2, 8, 64, 64])
        self.xT = self.sb(top, "xT", [128, NCH, T], F32)
        self.hT = self.sb(top, "hT", [128, NCH, T], BF16)
        self.vecs = self.sb(top, "vecs", [128, self.vp.n], F32)
        self.ident_f = self.sb(top, "ident_f", [128, 128], F32)
        self.ident_b = self.sb(top, "ident_b", [128, 128], BF16)
        self.ones_b = self.sb(top, "ones_b", [128, 128], BF16)
        self.mod = [self.sb(top, "mod%d" % l, [128, 48], F32) for l in range(2)]
        self.gm1 = [self.sb(top, "gm1_%d" % l, [128, 8], F32) for l in range(2)]
        self.gm2 = [self.sb(top, "gm2_%d" % l, [128, 8], F32) for l in range(2)]
        self.wk0 = [self.sb(top, "wk0_%d" % l, [128, 44], F32) for l in range(2)]
        self.wk2 = [self.sb(top, "wk2_%d" % l, [128, 44], F32) for l in range(2)]
        self.rstd = self.sb(top, "rstd", [128, T], F32)
        self.epsc = self.sb(top, "epsc", [128, 4], F32)
        self.bd_ones = self.sb(top, "bd_ones", [128, 128], BF16)
        self.NWB = 0
        self.wbuf = []
        self.wrr = 0

        self.phase_init()
        for l in range(2):
            if l in self.layers:
                if l == 0 and self.do_mix:
                    self.even_mixer()
                elif l == 1 and self.do_mix:
                    self.odd_mixer()
                else:
                    self.phase_mix(l)
                self.phase_ffn(l)
        self.phase_final()
        top.close()
        P.es.close()

    def phase_init(self):
        P, nc = self.P, self.nc
        with ExitStack() as es:
            xin = [self.sb(es, "xin%d" % i, [128, D], F32) for i in range(2)]
            scond = self.sb(es, "scond", [128, 8], BF16)
            abuf = [self.sb(es, "abuf%d" % i, [128, 6 * D], BF16) for i in range(2)]
            tp = [self.ps(es, "tp%d" % i, [128, 512]) for i in range(2)]
            mps = self.ps(es, "mps", [128, 48])

            P.dma('sp', lambda: nc.sync.dma_start(out=self.vecs[:], in_=self.vecs_d), w=['vecs'])
            P.dma('sp', lambda: nc.sync.dma_start(out=self.ident_f[:], in_=self.ident_d), w=['ident_f'])
            P.op('dve', lambda: nc.vector.tensor_copy(out=self.ident_b[:], in_=self.ident_f[:]), r=['ident_f'], w=['ident_b'])
            P.op('pool', lambda: nc.gpsimd.memset(self.ones_b[:], 1.0), w=['ones_b'])
            P.op('pool', lambda: nc.gpsimd.memset(self.epsc[:, 0:1], EPS), w=['epsc'])
            P.op('pool', lambda: nc.gpsimd.memset(self.epsc[:, 1:2], 1e-24), w=['epsc'])
            P.op('pool', lambda: nc.gpsimd.memset(self.epsc[:, 2:3], GN_EPS), w=['epsc'])
            P.op('pool', lambda: nc.gpsimd.memset(self.bd_ones[:], 0.0), w=['bd_ones'])
            P.op('pool', lambda: nc.gpsimd.memset(self.bd_ones[0:64, 0:64], 1.0), w=['bd_ones'])
            P.op('pool', lambda: nc.gpsimd.memset(self.bd_ones[64:128, 64:128], 1.0), w=['bd_ones'])
            for tt in range(8):
                xi = xin[tt % 2]
                P.dma('sp', lambda xi=xi, tt=tt: nc.sync.dma_start(out=xi[:], in_=self.x_d[tt * 128:(tt + 1) * 128, :]),
                      w=[('xin', tt % 2)])
                for g in range(2):
                    tpp = tp[g]
                    for cc in range(4):
                        c = g * 4 + cc
                        P.op('pe', lambda xi=xi, tpp=tpp, cc=cc, c=c: nc.tensor.transpose(
                            tpp[:, cc * 128:(cc + 1) * 128], xi[:, c * 128:(c + 1) * 128], self.ident_f[:]),
                            r=[('xin', tt % 2), 'ident_f'], w=[('tp', g)])
                    eng = 'act' if g == 0 else 'dve'
                    if g == 0:
                        P.op('act', lambda tpp=tpp, g=g, tt=tt: nc.scalar.copy(
                            out=self.xT[:, g * 4:(g + 1) * 4, tt * 128:(tt + 1) * 128],
                            in_=tpp[:].rearrange("p (c t) -> p c t", c=4)),
                            r=[('tp', g)], w=[('xT', c_) for c_ in range(g * 4, g * 4 + 4)])
                    else:
                        P.op('dve', lambda tpp=tpp, g=g, tt=tt: nc.vector.tensor_copy(
                            out=self.xT[:, g * 4:(g + 1) * 4, tt * 128:(tt + 1) * 128],
                            in_=tpp[:].rearrange("p (c t) -> p c t", c=4)),
                            r=[('tp', g)], w=[('xT', c_) for c_ in range(g * 4, g * 4 + 4)])
            P.op('act', lambda: nc.scalar.activation(out=scond[:], in_=self.vcol('cond', 0, 8), func=AF.Silu),
                 r=['vecs'], w=['scond'])
            for l in range(2):
                for kc in range(8):
                    ab = abuf[kc % 2]
                    P.dma('pool', lambda ab=ab, l=l, kc=kc: nc.gpsimd.dma_start(
                        out=ab[:], in_=self.ada_w_d[l, kc * 128:(kc + 1) * 128, :]), w=[('abuf', kc % 2)])

                    def mm(ab=ab, kc=kc):
                        ins = None
                        for col in range(48):
                            ins = nc.tensor.matmul(mps[:, col:col + 1], ab[:, col * 128:(col + 1) * 128], scond[:, kc:kc + 1],
                                                   start=(kc == 0 and col == 0), stop=(kc == 7), skip_group_check=True)
                        return ins
                    P.op('pe', mm, r=[('abuf', kc % 2), 'scond'], w=['mps'])
                md = self.mod[l]
                P.op('dve', lambda md=md, l=l: nc.vector.tensor_tensor(out=md[:], in0=mps[:], in1=self.vcol('ada_b%d' % l, 0, 48),
                                                                       op=ALU.add), r=['mps', 'vecs'], w=[('mod', l)])
                P.op('dve', lambda md=md, l=l: nc.vector.scalar_tensor_tensor(
                    out=self.gm1[l][:], in0=md[:, 8:16], scalar=1.0, in1=self.vcol('nmg%d' % l, 0, 8),
                    op0=ALU.add, op1=ALU.mult), r=[('mod', l), 'vecs'], w=[('gm1', l)])
                P.op('dve', lambda md=md, l=l: nc.vector.scalar_tensor_tensor(
                    out=self.gm2[l][:], in0=md[:, 32:40], scalar=1.0, in1=self.vcol('nfg%d' % l, 0, 8),
                    op0=ALU.add, op1=ALU.mult), r=[('mod', l), 'vecs'], w=[('gm2', l)])
                P.op('dve', lambda l=l: nc.vector.tensor_scalar(
                    out=self.wk0[l][:], in0=self.vcol('cw0_%d' % l, 0, 44), scalar1=self.vcol('km1'), scalar2=None,
                    op0=ALU.mult), r=['vecs'], w=[('wk0', l)])
                P.op('dve', lambda l=l: nc.vector.tensor_scalar(
                    out=self.wk2[l][:], in0=self.vcol('cw2_%d' % l, 0, 44), scalar1=self.vcol('km1'), scalar2=None,
                    op0=ALU.mult), r=['vecs'], w=[('wk2', l)])
            self.dump('mod0', self.mod[0], [128, 48], [('mod', 0)])
            self.dump('xT', self.xT, [128, NCH, T], [('xT', c) for c in range(8)])
            P.flush()

    def rmsnorm(self, es, gm, sh, out_fn):
        P, nc = self.P, self.nc
        sq = self.sb(es, "sq", [128, NCH, T], BF16)
        tmp = [self.sb(es, "ntmp%d" % i, [128, T], F32) for i in range(2)]
        ssp = [self.ps(es, "ssp%d" % i, [128, 512]) for i in range(2)]
        for c in range(8):
            P.op('act', lambda c=c: nc.scalar.activation(out=sq[:, c, :], in_=self.xT[:, c, :], func=AF.Square),
                 r=[('xT', c)], w=[('sq', c)])
        for th in range(2):
            def mm(th=th):
                ins = None
                for c in range(8):
                    ins = nc.tensor.matmul(ssp[th][:], self.ones_b[:], sq[:, c, th * 512:(th + 1) * 512],
                                           start=(c == 0), stop=(c == 7))
                return ins
            P.op('pe', mm, r=[('sq', c) for c in range(8)] + ['ones_b'], w=[('ssp', th)])
            P.op('act', lambda th=th: nc.scalar.activation(
                out=self.rstd[:, th * 512:(th + 1) * 512], in_=ssp[th][:], func=AF.Sqrt, scale=1.0 / D, bias=self.epsc[:, 0:1]),
                r=[('ssp', th), 'epsc'], w=[('rstd', th)])
            P.op('dve', lambda th=th: nc.vector.reciprocal(
                out=self.rstd[:, th * 512:(th + 1) * 512], in_=self.rstd[:, th * 512:(th + 1) * 512]),
                r=[('rstd', th)], w=[('rstd', th)])
        for c in range(8):
            tm = tmp[c % 2]
            P.op('dve', lambda c=c, tm=tm: nc.vector.tensor_tensor(out=tm[:], in0=self.xT[:, c, :], in1=self.rstd[:],
                                                                   op=ALU.mult),
                 r=[('xT', c), ('rstd', 0), ('rstd', 1)], w=[('ntmp', c % 2)])
            out_ap, wkeys = out_fn(c)
            bias = sh(c) if sh is not None else 0.0
            P.op('act', lambda c=c, tm=tm, out_ap=out_ap, bias=bias: nc.scalar.activation(
                out=out_ap, in_=tm[:], func=AF.Identity, scale=gm(c), bias=bias),
                r=[('ntmp', c % 2), 'gmsh'], w=wkeys)

    def alloc_w(self, es, n, elems):
        self.NWB = n
        self.wbuf = [self.sb(es, "wbuf%d" % i, [128, elems], BF16) for i in range(n)]
        self.wrr = 0

    def load_w(self, src_ap, view, key_extra=None):
        P, nc = self.P, self.nc
        i = self.wrr % self.NWB
        self.wrr += 1
        a, b = view
        dst = self.wbuf[i][:, 0:a * b].rearrange("p (a b) -> p a b", a=a)
        P.dma('pool', lambda: nc.gpsimd.dma_start(out=dst, in_=src_ap), w=[('wbuf', i)])
        return dst, ('wbuf', i)

    def phase_mix(self, l):
        P, nc = self.P, self.nc
        with ExitStack() as es:
            self.rmsnorm(es, lambda c: self.gm1[l][:, c:c + 1], lambda c: self.mod[l][:, c:c + 1],
                         lambda c: (self.hT[:, c, :], [('hT', c)]))
            if l == 0:
                self.dump('h0T', self.hT, [128, NCH, T], [('hT', c) for c in range(8)])
            P.flush()

    def phase_ffn(self, l):
        P, nc = self.P, self.nc
        with ExitStack() as es:
            self.alloc_w(es, 4, 4096)
            pre_w = []
            for half in range(2):
                src = self.ffn_up_d[l, :, half * DFF:half * DFF + 512].rearrange("(k p) n -> p k n", p=128)
                pre_w.append(self.load_w(src, (8, 512)))
            with ExitStack() as es2:
                self.rmsnorm(es2, lambda c: self.gm2[l][:, c:c + 1], lambda c: self.mod[l][:, 24 + c:25 + c],
                             lambda c: (self.hT[:, c, :], [('hT', c)]))
                P.flush()
            gT = self.sb(es, "gT", [128, NFF, T], BF16)
            cv = [self.sb(es, "cv%d" % i, [128, T], F32) for i in range(4)]
            sg = [self.sb(es, "sg%d" % i, [128, T], F32) for i in range(2)]
            ups = [self.ps(es, "ups%d" % i, [128, T]) for i in range(4)]
            cw = lambda nm, j: self.vcol('%s_%d' % (nm, l), j)
            for j in range(NFF):
                g_, jj_ = j // 4, j % 4
                if jj_ == 0 and g_ == 0:
                    wts = pre_w
                elif jj_ == 0:
                    ncol_ = 512 if g_ < 5 else 256
                    wts = []
                    for half in range(2):
                        c0 = half * DFF + g_ * 512
                        src = self.ffn_up_d[l, :, c0:c0 + ncol_].rearrange("(k p) n -> p k n", p=128)
                        wts.append(self.load_w(src, (8, ncol_)))
                for half in range(2):
                    wt, wkey = wts[half]
                    pi = (j % 2) * 2 + half
                    up = ups[pi]
                    cvb = cv[pi]
                    jj = half * NFF + j
                    for th in range(2):
                        def mm(wt=wt, up=up, th=th, jj_=jj_):
                            ins = None
                            for k in range(8):
                                ins = nc.tensor.matmul(up[:, th * 512:(th + 1) * 512], wt[:, k, jj_ * 128:(jj_ + 1) * 128],
                                                       self.hT[:, k, th * 512:(th + 1) * 512],
                                                       start=(k == 0), stop=(k == 7))
                            return ins
                        P.op('pe', mm, r=[wkey] + [('hT', k) for k in range(8)], w=[('ups', pi, th)])
                    ur = [('ups', pi, 0), ('ups', pi, 1)]
                    ck = ('cv', pi)
                    P.op('act', lambda up=up, cvb=cvb, jj=jj: nc.scalar.activation(
                        out=cvb[:], in_=up[:], func=AF.Identity, scale=cw('cw1', jj), bias=cw('cb', jj)),
                        r=ur + ['vecs'], w=[ck])
                    P.op('dve', lambda up=up, cvb=cvb, jj=jj: nc.vector.scalar_tensor_tensor(
                        out=cvb[:, 1:T], in0=up[:, 0:T - 1], scalar=cw('cw0', jj), in1=cvb[:, 1:T],
                        op0=ALU.mult, op1=ALU.add), r=ur + ['vecs', ck], w=[ck])
                    P.op('dve', lambda up=up, cvb=cvb, jj=jj: nc.vector.scalar_tensor_tensor(
                        out=cvb[:, 0:T - 1], in0=up[:, 1:T], scalar=cw('cw2', jj), in1=cvb[:, 0:T - 1],
                        op0=ALU.mult, op1=ALU.add), r=ur + ['vecs', ck], w=[ck])
                    P.op('dve', lambda up=up, cvb=cvb, jj=jj: nc.vector.scalar_tensor_tensor(
                        out=cvb[:, 256:T:256], in0=up[:, 255:T - 1:256], scalar=self.wk0[l][:, jj:jj + 1],
                        in1=cvb[:, 256:T:256], op0=ALU.mult, op1=ALU.add), r=ur + [('wk0', l), ck], w=[ck])
                    P.op('dve', lambda up=up, cvb=cvb, jj=jj: nc.vector.scalar_tensor_tensor(
                        out=cvb[:, 255:T - 1:256], in0=up[:, 256:T:256], scalar=self.wk2[l][:, jj:jj + 1],
                        in1=cvb[:, 255:T - 1:256], op0=ALU.mult, op1=ALU.add), r=ur + [('wk2', l), ck], w=[ck])
                pv = (j % 2) * 2
                sgb = sg[j % 2]
                P.op('act', lambda sgb=sgb, pv=pv: nc.scalar.activation(out=sgb[:], in_=cv[pv + 1][:], func=AF.Silu),
                     r=[('cv', pv + 1)], w=[('sg', j % 2)])
                P.op('dve', lambda sgb=sgb, pv=pv, j=j: nc.vector.tensor_tensor(
                    out=gT[:, j, :], in0=sgb[:], in1=cv[pv][:], op=ALU.mult),
                    r=[('sg', j % 2), ('cv', pv)], w=[('gT', j)])
            if l == 0:
                self.dump('gT0', gT, [128, NFF, T], [('gT', j) for j in range(NFF)])
            dps = [ups[0], ups[1]]
            for c in range(8):
                src = self.ffn_dn_d[l, :, c * 128:(c + 1) * 128].rearrange("(k p) n -> p k n", p=128)
                wt, wkey = self.load_w(src, (NFF, 128))
                for th in range(2):
                    pi = th
                    dp = ups[c % 2][:, th * 512:(th + 1) * 512]

                    def mm(wt=wt, dp=dp, th=th):
                        ins = None
                        for k in range(NFF):
                            ins = nc.tensor.matmul(dp, wt[:, k, :], gT[:, k, th * 512:(th + 1) * 512],
                                                   start=(k == 0), stop=(k == NFF - 1))
                        return ins
                    P.op('pe', mm, r=[wkey] + [('gT', k) for k in range(NFF)], w=[('ups', c % 2, th)])
                    P.op('dve', lambda dp=dp, c=c, th=th: nc.vector.scalar_tensor_tensor(
                        out=self.xT[:, c, th * 512:(th + 1) * 512], in0=dp, scalar=self.mod[l][:, 40 + c:41 + c],
                        in1=self.xT[:, c, th * 512:(th + 1) * 512], op0=ALU.mult, op1=ALU.add),
                        r=[('ups', c % 2, th), ('xT', c), ('mod', l)], w=[('xT', c)])
            P.flush()

    def phase_final(self):
        P, nc = self.P, self.nc
        with ExitStack() as es:
            yT = self.sb(es, "yT", [128, NCH, T], F32)
            self.rmsnorm(es, lambda c: self.vcol('fng', c), None, lambda c: (yT[:, c, :], [('yT', c)]))
            yo = [self.sb(es, "yo%d" % i, [128, D], F32) for i in range(2)]
            tp = [self.ps(es, "ftp%d" % i, [128, 512]) for i in range(2)]
            for tt in range(8):
                yb = yo[tt % 2]
                for g in range(2):
                    for cc in range(4):
                        c = g * 4 + cc
                        P.op('pe', lambda tt=tt, g=g, cc=cc, c=c: nc.tensor.transpose(
                            tp[g][:, cc * 128:(cc + 1) * 128], yT[:, c, tt * 128:(tt + 1) * 128], self.ident_f[:]),
                            r=[('yT', c), 'ident_f'], w=[('ftp', g)])
                    if g == 0:
                        P.op('act', lambda yb=yb, g=g: nc.scalar.copy(out=yb[:, g * 512:(g + 1) * 512], in_=tp[g][:]),
                             r=[('ftp', g)], w=[('yo', tt % 2, g)])
                    else:
                        P.op('dve', lambda yb=yb, g=g: nc.vector.tensor_copy(out=yb[:, g * 512:(g + 1) * 512], in_=tp[g][:]),
                             r=[('ftp', g)], w=[('yo', tt % 2, g)])
                P.dma('sp', lambda yb=yb, tt=tt: nc.sync.dma_start(out=self.y_d[tt * 128:(tt + 1) * 128, :], in_=yb[:]),
                      r=[('yo', tt % 2, 0), ('yo', tt % 2, 1)], w=[('y', tt)])
            P.flush()


def _even_mixer(self):
    P, nc = self.P, self.nc
    l = 0
    SC = 0.125
    with ExitStack() as es:
        qaT = self.sb(es, "qaT", [128, 4, T], BF16)
        kaT = self.sb(es, "kaT", [128, 4, 512 + T], BF16)
        qbT = self.sb(es, "qbT", [128, 4, T], BF16)
        kbT = self.sb(es, "kbT", [128, 512 + T], BF16)
        vA = self.sb(es, "vA", [128, 12, 4, 130], BF16)
        vB = self.sb(es, "vB", [128, 12, 2, 66], BF16)
        ocat = self.sb(es, "ocat", [128, 8, D], BF16)
        bvec = self.sb(es, "bvec", [128, 448], F32)
        nlam = self.sb(es, "nlam", [128, 1], F32)
        gsub8 = self.sb(es, "gsub8", [128, 128], F32)
        with ExitStack() as e0:
            self.rmsnorm(e0, lambda c: self.gm1[l][:, c:c + 1], lambda c: self.mod[l][:, c:c + 1],
                         lambda c: (self.hT[:, c, :], [('hT', c)]))
            P.flush()
        with ExitStack() as e1:
            self.alloc_w(e1, 4, 4096)
            cosT = self.sb(e1, "cosT", [128, T], F32)
            sinT = self.sb(e1, "sinT", [128, T], F32)
            cak = self.sb(e1, "cak", [128, 4, 4, 128], BF16)
            cbk = self.sb(e1, "cbk", [128, 4, 128], BF16)
            t1 = [self.sb(e1, "rt1_%d" % i, [128, 512], F32) for i in range(2)]
            t2 = [self.sb(e1, "rt2_%d" % i, [128, 512], F32) for i in range(2)]
            sqb = self.sb(e1, "sqb", [128, 512], BF16)
            rsb = self.sb(e1, "rsb", [128, 512], F32)
            stg = [self.sb(e1, "stg%d" % i, [128, 512], F32) for i in range(2)]
            kbs = self.sb(e1, "kbs", [128, 128], F32)
            ssb = self.sb(e1, "ssb", [128, 2], F32)
            junk = self.sb(e1, "junk", [128, 128], F32)
            lpr = self.sb(e1, "lpr", [128, 2, 64], F32)
            lsum = self.sb(e1, "lsum", [128, 2], F32)
            pq = [self.ps(e1, "pq%d" % i, [128, 512]) for i in range(2)]
            pqs = [self.ps(e1, "pqs%d" % i, [128, 512]) for i in range(2)]
            pss = self.ps(e1, "pss", [128, 512])
            ptm = [self.ps(e1, "ptm%d" % i, [128, 512]) for i in range(2)]
            ptp = self.ps(e1, "ptp", [128, 1024], BF16)

            P.dma('sp', lambda: nc.sync.dma_start(out=cosT[:], in_=self.rope_d[0]), w=['cosT'])
            P.dma('sp', lambda: nc.sync.dma_start(out=sinT[:], in_=self.rope_d[1]), w=['sinT'])
            P.dma('sp', lambda: nc.sync.dma_start(out=bvec[:], in_=self.bvec_d), w=['bvec'])
            P.op('dve', lambda: nc.vector.tensor_tensor(
                out=lpr[:], in0=bvec[:, 192:448].rearrange("p (a b e) -> p a b e", a=2, b=2)[:, :, 0, :],
                in1=bvec[:, 192:448].rearrange("p (a b e) -> p a b e", a=2, b=2)[:, :, 1, :], op=ALU.mult),
                r=['bvec'], w=['lpr'])
            P.op('dve', lambda: nc.vector.reduce_sum(out=lsum[:], in_=lpr[:], axis=AX.X), r=['lpr'], w=['lsum'])
            P.op('act', lambda: nc.scalar.activation(out=lsum[:], in_=lsum[:], func=AF.Exp), r=['lsum'], w=['lsum'])
            P.op('dve', lambda: nc.vector.tensor_tensor(out=nlam[:], in0=lsum[:, 1:2], in1=lsum[:, 0:1], op=ALU.subtract),
                 r=['lsum'], w=['nlam'])
            P.op('dve', lambda: nc.vector.tensor_scalar_add(out=nlam[:], in0=nlam[:], scalar1=-0.2), r=['nlam'], w=['nlam'])
            P.op('dve', lambda: nc.vector.tensor_scalar_mul(out=gsub8[:], in0=bvec[:, 0:128], scalar1=0.8),
                 r=['bvec'], w=['gsub8'])
            P.op('pool', lambda: nc.gpsimd.memset(vA[:, :, :, 128:129], 1.0), w=['vA_ones'])
            P.op('pool', lambda: nc.gpsimd.memset(vB[:, :, :, 64:65], 1.0), w=['vB_ones'])
            if self.stop == 'B1l':
                P.flush(); return
            for kt in range(4):
                P.dma('pool', lambda kt=kt: nc.gpsimd.dma_start(
                    out=cak[:, kt], in_=self.ctx_ak_d[:, kt * 128:(kt + 1) * 128, :].rearrange("h p e -> p h e")),
                    w=[('cak', kt)])
                P.dma('pool', lambda kt=kt: nc.gpsimd.dma_start(
                    out=vA[:, kt, :, 0:128], in_=self.ctx_av_d[:, kt * 128:(kt + 1) * 128, :].rearrange("h p e -> p h e")),
                    w=[('vA', kt)])
                P.dma('pool', lambda kt=kt: nc.gpsimd.dma_start(
                    out=cbk[:, kt, :].rearrange("p (h e) -> p h e", h=2),
                    in_=self.ctx_bk_d[:, kt * 128:(kt + 1) * 128, :].rearrange("h p e -> p h e")), w=[('cbk', kt)])
                P.dma('pool', lambda kt=kt: nc.gpsimd.dma_start(
                    out=vB[:, kt, :, 0:64], in_=self.ctx_bv_d[:, kt * 128:(kt + 1) * 128, :].rearrange("h p e -> p h e")),
                    w=[('vB', kt)])
            for h in range(5):
                for kt in range(4):
                    src = cak[:, kt, h, :] if h < 4 else cbk[:, kt, :]
                    P.op('pe', lambda src=src, kt=kt: nc.tensor.transpose(ptp[:, kt * 128:(kt + 1) * 128], src, self.ident_b[:]),
                         r=[('cak', kt), ('cbk', kt), 'ident_b'], w=['ptp'])
                dst = kaT[:, h, 0:512] if h < 4 else kbT[:, 0:512]
                P.op('dve', lambda dst=dst: nc.vector.tensor_copy(out=dst, in_=ptp[:, 0:512]), r=['ptp'],
                     w=[('kaTc', h)])
            if self.stop == 'B1c':
                P.flush(); return
            chunks = []
            for a in range(4):
                chunks.append((lambda th, a=a: qaT[:, a, th * 512:(th + 1) * 512], (self.ev_w_in_d, a * 128),
                               (self.ev_wx_d, 512 + a * 128), None, ('qaT', a)))
            for a in range(4):
                chunks.append((lambda th, a=a: kaT[:, a, 512 + th * 512:512 + (th + 1) * 512], (self.ev_w_in_d, 512 + a * 128),
                               (self.ev_wx_d, 1024 + a * 128), None, ('kaT', a)))
            for c in range(4):
                chunks.append((lambda th, c=c: qbT[:, c, th * 512:(th + 1) * 512], (self.ev_wx_d, c * 128),
                               (self.ev_wx_d, 1536 + c * 128), ('gq', 'gq_sw'), ('qbT', c)))
            chunks.append((lambda th: kbT[:, 512 + th * 512:512 + (th + 1) * 512], (self.ev_w_in_d, 2048),
                           (self.ev_wx_d, 2048), ('gk', 'gk_sw'), ('kbT',)))
            it = 0
            for ci_, (dst_fn, (wd, c0), (wsd, cs0), gn, dkey) in enumerate(chunks):
                if ci_ % 4 == 0:
                    nb_ = 512 if ci_ < 12 else 128
                    wn_, wnk = self.load_w(wd[:, c0:c0 + nb_].rearrange("(k p) n -> p k n", p=128), (8, nb_))
                    ws_, wsk = self.load_w(wsd[:, cs0:cs0 + nb_].rearrange("(k p) n -> p k n", p=128), (8, nb_))
                wo_ = (ci_ % 4) * 128
                wn = wn_[:, :, wo_:wo_ + 128]
                ws = ws_[:, :, wo_:wo_ + 128]
                for th in range(2):
                    b = it % 2
                    it += 1
                    for (pp, ww, wk_, nm) in ((pq[b], wn, wnk, 'pq'), (pqs[b], ws, wsk, 'pqs')):
                        def mm(pp=pp, ww=ww, th=th):
                            ins = None
                            for k in range(8):
                                ins = nc.tensor.matmul(pp[:], ww[:, k, :], self.hT[:, k, th * 512:(th + 1) * 512],
                                                       start=(k == 0), stop=(k == 7))
                            return ins
                        P.op('pe', mm, r=[wk_] + [('hT', k) for k in range(8)], w=[(nm, b)])
                    dst = dst_fn(th)
                    if gn is None:
                        P.op('dve', lambda b=b, th=th: nc.vector.tensor_tensor(
                            out=t1[b][:], in0=pq[b][:], in1=cosT[:, th * 512:(th + 1) * 512], op=ALU.mult),
                            r=[('pq', b), 'cosT'], w=[('t1', b)])
                        P.op('dve', lambda b=b, th=th: nc.vector.tensor_tensor(
                            out=t2[b][:], in0=pqs[b][:], in1=sinT[:, th * 512:(th + 1) * 512], op=ALU.mult),
                            r=[('pqs', b), 'sinT'], w=[('t2', b)])
                        P.op('dve', lambda b=b, dst=dst: nc.vector.tensor_tensor(out=dst, in0=t1[b][:], in1=t2[b][:], op=ALU.add),
                             r=[('t1', b), ('t2', b)], w=[dkey + (th,)])
                    else:
                        P.op('act', lambda b=b: nc.scalar.activation(out=sqb[:], in_=pq[b][:], func=AF.Square),
                             r=[('pq', b)], w=['sqb'])
                        P.op('pe', lambda: nc.tensor.matmul(pss[:], self.bd_ones[:], sqb[:], start=True, stop=True),
                             r=['sqb', 'bd_ones'], w=['pss'])
                        P.op('act', lambda: nc.scalar.activation(out=rsb[:], in_=pss[:], func=AF.Ln, scale=1.0 / 64,
                                                                 bias=self.epsc[:, 0:1]), r=['pss', 'epsc'], w=['rsb'])
                        P.op('act', lambda: nc.scalar.activation(out=rsb[:], in_=rsb[:], func=AF.Exp, scale=-0.5),
                             r=['rsb'], w=['rsb'])
                        P.op('dve', lambda b=b, gn=gn: nc.vector.scalar_tensor_tensor(
                            out=t1[b][:], in0=pq[b][:], scalar=self.vcol(gn[0]), in1=rsb[:], op0=ALU.mult, op1=ALU.mult),
                            r=[('pq', b), 'rsb', 'vecs'], w=[('t1', b)])
                        P.op('dve', lambda b=b, gn=gn: nc.vector.scalar_tensor_tensor(
                            out=t2[b][:], in0=pqs[b][:], scalar=self.vcol(gn[1]), in1=rsb[:], op0=ALU.mult, op1=ALU.mult),
                            r=[('pqs', b), 'rsb', 'vecs'], w=[('t2', b)])
                        P.op('dve', lambda b=b, th=th: nc.vector.tensor_tensor(
                            out=t1[b][:], in0=t1[b][:], in1=cosT[:, th * 512:(th + 1) * 512], op=ALU.mult),
                            r=[('t1', b), 'cosT'], w=[('t1', b)])
                        P.op('dve', lambda b=b, th=th: nc.vector.tensor_tensor(
                            out=t2[b][:], in0=t2[b][:], in1=sinT[:, th * 512:(th + 1) * 512], op=ALU.mult),
                            r=[('t2', b), 'sinT'], w=[('t2', b)])
                        P.op('dve', lambda b=b, dst=dst: nc.vector.tensor_tensor(out=dst, in0=t1[b][:], in1=t2[b][:], op=ALU.add),
                             r=[('t1', b), ('t2', b)], w=[dkey + (th,)])
            if self.stop == 'B1r':
                P.flush(); return
            wkv = []
            for (c0, n) in ((512, 512), (1024, 512), (2048, 256)):
                wkv.append(self.load_w(self.ev_w_in_d[:, c0:c0 + n].rearrange("(k p) n -> p k n", p=128), (8, n)) + (n,))
            si = 0
            for tt in range(8):
                seg, r0 = tt // 2, (tt % 2) * 128
                for bi, (wt, wkey, n) in enumerate(wkv):
                    pm = ptm[(tt * 3 + bi) % 2]
                    pk = ('ptm', (tt * 3 + bi) % 2)

                    def mm(wt=wt, pm=pm, n=n, tt=tt):
                        ins = None
                        for k in range(8):
                            ins = nc.tensor.matmul(pm[:, 0:n], self.hT[:, k, tt * 128:(tt + 1) * 128], wt[:, k, :],
                                                   start=(k == 0), stop=(k == 7))
                        return ins
                    P.op('pe', mm, r=[wkey] + [('hT', k) for k in range(8)], w=[pk])
                    if 'tm_evac' in self.skip: continue
                    if 'tm_evac2' in self.skip and bi == 2: continue
                    if bi < 2:
                        sg_ = stg[si % 2]
                        sk = ('stg', si % 2)
                        si += 1
                        P.op('act', lambda sg_=sg_, pm=pm: nc.scalar.copy(out=sg_[:], in_=pm[:]), r=[pk], w=[sk])
                        od = self.o_ak_d if bi == 0 else self.o_av_d
                        if 'outdma' not in self.skip: P.dma('sp', lambda sg_=sg_, od=od, seg=seg, r0=r0: nc.sync.dma_start(
                            out=od[seg, :, r0:r0 + 128, :].rearrange("h p e -> p h e"),
                            in_=sg_[:].rearrange("p (h e) -> p h e", h=4)), r=[sk], w=[('ocache', bi, tt)])
                        if bi == 1 and 'vcopy' not in self.skip:
                            P.op('dve', lambda pm=pm, tt=tt: nc.vector.tensor_copy(
                                out=vA[:, 4 + tt, :, 0:128], in_=pm[:].rearrange("p (h e) -> p h e", h=4)),
                                r=[pk], w=[('vA', 4 + tt)])
                    else:
                        sg_ = stg[si % 2]
                        sk = ('stg', si % 2)
                        si += 1
                        P.op('dve', lambda pm=pm, tt=tt: nc.vector.tensor_copy(
                            out=vB[:, 4 + tt, :, 0:64], in_=pm[:, 128:256].rearrange("p (h e) -> p h e", h=2)),
                            r=[pk], w=[('vB', 4 + tt)])
                        P.op('act', lambda sg_=sg_, pm=pm: nc.scalar.copy(out=sg_[:, 128:256], in_=pm[:, 128:256]), r=[pk], w=[sk])
                        P.op('act', lambda pm=pm: nc.scalar.copy(out=kbs[:], in_=pm[:, 0:128]), r=[pk], w=['kbs'])
                        for h in range(2):
                            if 'accum' in self.skip: continue
                            P.op('dve', lambda h=h: nc.vector.scalar_tensor_tensor(
                                out=junk[:, 0:64], in0=kbs[:, h * 64:(h + 1) * 64], scalar=1.0, in1=kbs[:, h * 64:(h + 1) * 64],
                                op0=ALU.mult, op1=ALU.mult, accum_out=ssb[:, h:h + 1]), r=['kbs'], w=['junk', ('ssb', h)])
                        P.op('act', lambda: nc.scalar.activation(out=ssb[:], in_=ssb[:], func=AF.Ln, scale=1.0 / 64,
                                                                 bias=self.epsc[:, 0:1]),
                             r=[('ssb', 0), ('ssb', 1), 'epsc'], w=[('ssb', 0), ('ssb', 1)])
                        P.op('act', lambda: nc.scalar.activation(out=ssb[:], in_=ssb[:], func=AF.Exp, scale=-0.5),
                             r=[('ssb', 0), ('ssb', 1)], w=[('ssb', 0), ('ssb', 1)])
                        for h in range(2):
                            P.op('dve', lambda h=h, sg_=sg_: nc.vector.scalar_tensor_tensor(
                                out=sg_[:, h * 64:(h + 1) * 64], in0=kbs[:, h * 64:(h + 1) * 64], scalar=ssb[:, h:h + 1],
                                in1=bvec[:, 128:192], op0=ALU.mult, op1=ALU.mult),
                                r=['kbs', ('ssb', h), 'bvec'], w=[sk])
                        if 'outdma2' not in self.skip: P.dma('sp', lambda sg_=sg_, seg=seg, r0=r0: nc.sync.dma_start(
                            out=self.o_bk_d[seg, :, r0:r0 + 128, :].rearrange("h p e -> p h e"),
                            in_=sg_[:, 0:128].rearrange("p (h e) -> p h e", h=2)), r=[sk], w=[('ocache', 2, tt)])
                        if 'outdma2' not in self.skip: P.dma('sp', lambda sg_=sg_, seg=seg, r0=r0: nc.sync.dma_start(
                            out=self.o_bv_d[seg, :, r0:r0 + 128, :].rearrange("h p e -> p h e"),
                            in_=sg_[:, 128:256].rearrange("p (h e) -> p h e", h=2)), r=[sk], w=[('ocache', 3, tt)])
            if self.stop == 'B1a':
                P.flush(); return
            self.dump('qaT', qaT, [128, 4, T], [])
            self.dump('kaT', kaT, [128, 4, 512 + T], [])
            self.dump('qbT', qbT, [128, 4, T], [])
            self.dump('kbT', kbT, [128, 512 + T], [])
            P.flush()
        if self.stop == 'B1':
            return
        with ExitStack() as e2:
            NPT = 6
            pt = [self.sb(e2, "pt%d" % i, [128, 256], BF16) for i in range(NPT)]
            dbuf = self.sb(e2, "dbuf", [128, 32, 128], F32)
            ssq = self.sb(e2, "ssq", [128, 32], F32)
            a1b = [self.sb(e2, "a1b%d" % i, [128, 128], F32) for i in range(2)]
            rr = self.sb(e2, "rr", [128, 8], F32)
            junk2 = self.sb(e2, "junk2", [128, 128], F32)
            sT = [self.ps(e2, "sT%d" % i, [128, 512]) for i in range(4)]
            acc = [self.ps(e2, "acc%d" % i, [128, 512]) for i in range(4)]
            steps = []
            g = 0
            for a in range(4):
                for s_ in range(4):
                    for kb in range(12):
                        for comp in range(2):
                            steps.append(dict(kind='A', a=a, s=s_, comp=comp, kb=kb, g=g + comp, last=(kb == 11)))
                    g += 2
            for c_ in range(4):
                for s_ in range(4):
                    for kb in range(12):
                        for hi in range(2):
                            steps.append(dict(kind='B', h=c_ + 4 * hi, s=s_, kb=kb, g=g + hi, last=(kb == 11)))
                    g += 2
            LA = 2
            nst = len(steps)

            def emit_S(i, st):
                slot = i % 4
                dstp = sT[slot][:, 0:256]
                s_, kb = st['s'], st['kb']
                if st['kind'] == 'A':
                    a, comp = st['a'], st['comp']
                    lhsT = kaT[comp * 64:(comp + 1) * 64, a, kb * 128:(kb + 1) * 128]
                    rhs = qaT[comp * 64:(comp + 1) * 64, a, s_ * 256:(s_ + 1) * 256]
                else:
                    h = st['h']
                    gk = h // 4
                    lhsT = kbT[gk * 64:(gk + 1) * 64, kb * 128:(kb + 1) * 128]
                    rhs = qbT[gk * 64:(gk + 1) * 64, h % 4, s_ * 256:(s_ + 1) * 256]
                P.op('pe', lambda: nc.tensor.matmul(dstp, lhsT, rhs, start=True, stop=True), r=[], w=[('sT', slot)])
                ptb = pt[i % NPT]
                col = s_ * 12 + kb
                P.op('act', lambda: nc.scalar.activation(out=ptb[:], in_=dstp, func=AF.Exp, scale=SC,
                                                         bias=self.vcol('amask', col)),
                     r=[('sT', slot)], w=[('pt', i % NPT)])

            def emit_PV(i, st):
                ptb = pt[i % NPT]
                kb = st['kb']
                ab = acc[st['g'] % 4]
                if st['kind'] == 'A':
                    rhs = vA[:, kb, st["a"], 0:129]
                    n = 129
                else:
                    rhs = vB[:, kb, st["h"] // 4, 0:65]
                    n = 65
                for qh in range(2):
                    P.op('pe', lambda qh=qh: nc.tensor.matmul(ab[:, qh * 256:qh * 256 + n], ptb[:, qh * 128:(qh + 1) * 128], rhs,
                                                              start=(kb == 0 and qh == 0), stop=(kb == 11),
                                                              skip_group_check=True),
                         r=[('pt', i % NPT)], w=[('acc', st['g'] % 4)])
                if not st['last']:
                    return
                s_ = st['s']
                if st['kind'] == 'A':
                    if st['comp'] == 0:
                        return
                    a = st['a']
                    ab0, ab1 = acc[(st['g'] - 1) % 4], acc[st['g'] % 4]
                    k0, k1 = ('acc', (st['g'] - 1) % 4), ('acc', st['g'] % 4)
                    for qh in range(2):
                        u = (a * 4 + s_) * 2 + qh
                        o0 = qh * 256
                        P.op('dve', lambda o0=o0: nc.vector.reciprocal(out=rr[:, 0:1], in_=ab0[:, o0 + 128:o0 + 129]), r=[k0], w=['rr0'])
                        P.op('dve', lambda o0=o0: nc.vector.reciprocal(out=rr[:, 1:2], in_=ab1[:, o0 + 128:o0 + 129]), r=[k1], w=['rr1'])
                        P.op('dve', lambda: nc.vector.tensor_tensor(out=rr[:, 2:3], in0=rr[:, 1:2], in1=nlam[:], op=ALU.mult),
                             r=['rr1'], w=['rr2'])
                        a1 = a1b[u % 2]
                        P.op('dve', lambda o0=o0, a1=a1: nc.vector.tensor_scalar_mul(out=a1[:], in0=ab0[:, o0:o0 + 128], scalar1=rr[:, 0:1]),
                             r=[k0, 'rr0'], w=[('a1b', u % 2)])
                        P.op('dve', lambda o0=o0, a1=a1, u=u: nc.vector.scalar_tensor_tensor(
                            out=dbuf[:, u, :], in0=ab1[:, o0:o0 + 128], scalar=rr[:, 2:3], in1=a1[:], op0=ALU.mult, op1=ALU.add),
                            r=[k1, 'rr2', ('a1b', u % 2)], w=[('dbuf', u)])
                        P.op('dve', lambda u=u: nc.vector.scalar_tensor_tensor(
                            out=junk2[:], in0=dbuf[:, u, :], scalar=1.0, in1=dbuf[:, u, :], op0=ALU.mult, op1=ALU.mult,
                            accum_out=ssq[:, u:u + 1]), r=[('dbuf', u)], w=['junk2', ('ssq', u)])
                else:
                    h = st['h']
                    ab0 = acc[st['g'] % 4]
                    k0 = ('acc', st['g'] % 4)
                    for qh in range(2):
                        o0 = qh * 256
                        qt = s_ * 2 + qh
                        P.op('dve', lambda o0=o0: nc.vector.reciprocal(out=rr[:, 4:5], in_=ab0[:, o0 + 64:o0 + 65]), r=[k0], w=['rr4'])
                        P.op('dve', lambda o0=o0, qt=qt, h=h: nc.vector.tensor_scalar_mul(
                            out=ocat[:, qt, 512 + h * 64:512 + (h + 1) * 64], in0=ab0[:, o0:o0 + 64], scalar1=rr[:, 4:5]),
                            r=[k0, 'rr4'], w=[('ocat', qt, 4 + h // 2)])

            for j in range(nst // 2 + 1):
                if 2 * j < nst:
                    emit_S(2 * j, steps[2 * j])
                    emit_S(2 * j + 1, steps[2 * j + 1])
                if j >= 1:
                    emit_PV(2 * j - 2, steps[2 * j - 2])
                    emit_PV(2 * j - 1, steps[2 * j - 1])
            P.op('act', lambda: nc.scalar.activation(out=ssq[:], in_=ssq[:], func=AF.Ln, scale=1.0 / 128, bias=self.epsc[:, 0:1]),
                 r=[('ssq', u) for u in range(32)], w=['rstdA'])
            P.op('act', lambda: nc.scalar.activation(out=ssq[:], in_=ssq[:], func=AF.Exp, scale=-0.5), r=['rstdA'], w=['rstdA'])
            for a in range(4):
                for s_ in range(4):
                    for qh in range(2):
                        u = (a * 4 + s_) * 2 + qh
                        qt = s_ * 2 + qh
                        eng = 'dve'
                        E = nc.vector
                        P.op(eng, lambda E=E, u=u, qt=qt, a=a: E.scalar_tensor_tensor(
                            out=ocat[:, qt, a * 128:(a + 1) * 128], in0=dbuf[:, u, :], scalar=ssq[:, u:u + 1], in1=gsub8[:],
                            op0=ALU.mult, op1=ALU.mult), r=[('dbuf', u), 'rstdA', 'gsub8'], w=[('ocat', qt, a)])
            self.dump('ocat', ocat, [128, 8, D], [])
            P.flush()
        if self.stop == 'B2':
            return
        with ExitStack() as e3:
            self.alloc_w(e3, 2, 4096)
            ptp = [self.ps(e3, "otp%d" % i, [128, 1024], BF16) for i in range(2)]
            pmx = [self.ps(e3, "pmx%d" % i, [128, 512]) for i in range(2)]
            n = 0
            for c in range(8):
                for gq in range(2):
                    pp = ptp[n % 2]
                    for j in range(4):
                        qt = gq * 4 + j
                        P.op('pe', lambda pp=pp, j=j, qt=qt, c=c: nc.tensor.transpose(
                            pp[:, j * 128:(j + 1) * 128], ocat[:, qt, c * 128:(c + 1) * 128], self.ident_b[:]),
                            r=[], w=[('otp', n % 2)])
                    if n % 2 == 0:
                        P.op('dve', lambda pp=pp, c=c, gq=gq: nc.vector.tensor_copy(out=self.hT[:, c, gq * 512:(gq + 1) * 512], in_=pp[:, 0:512]),
                             r=[('otp', n % 2)], w=[('hT', c)])
                    else:
                        P.op('act', lambda pp=pp, c=c, gq=gq: nc.scalar.copy(out=self.hT[:, c, gq * 512:(gq + 1) * 512], in_=pp[:, 0:512]),
                             r=[('otp', n % 2)], w=[('hT', c)])
                    n += 1
            self.out_proj(self.ev_w_out_d, pmx, l)
            self.dump('xm0', self.xT, [128, NCH, T], [('xT', c) for c in range(8)])
            P.flush()


def _out_proj(self, w_d, pmx, l):
    P, nc = self.P, self.nc
    n = 0
    for c in range(8):
        if c % 4 == 0:
            wt, wkey = self.load_w(w_d[:, c * 128:c * 128 + 512].rearrange("(k p) n -> p k n", p=128), (8, 512))
        co = (c % 4) * 128
        for th in range(2):
            pm = pmx[n % 2]
            pk = ('pmx', n % 2)
            n += 1

            def mm(wt=wt, pm=pm, th=th, co=co):
                ins = None
                for k in range(8):
                    ins = nc.tensor.matmul(pm[:], wt[:, k, co:co + 128], self.hT[:, k, th * 512:(th + 1) * 512],
                                           start=(k == 0), stop=(k == 7))
                return ins
            P.op('pe', mm, r=[wkey] + [('hT', k) for k in range(8)], w=[pk])
            P.op('dve', lambda pm=pm, c=c, th=th: nc.vector.scalar_tensor_tensor(
                out=self.xT[:, c, th * 512:(th + 1) * 512], in0=pm[:], scalar=self.mod[l][:, 16 + c:17 + c],
                in1=self.xT[:, c, th * 512:(th + 1) * 512], op0=ALU.mult, op1=ALU.add),
                r=[pk, ('xT', c)], w=[('xT', c)])


Builder.even_mixer = _even_mixer
Builder.out_proj = _out_proj


class Em:
    def __init__(self, B):
        self.B, self.P, self.nc = B, B.P, B.nc

    def V(self, eng):
        return self.nc.vector if eng == 'dve' else self.nc.gpsimd

    def tt(self, eng, out, in0, in1, op, r, w):
        self.P.op(eng, lambda: self.V(eng).tensor_tensor(out=out, in0=in0, in1=in1, op=op), r=r, w=w)

    def ts(self, eng, out, in0, s1, s2, op0, op1, r, w):
        if s2 is None:
            self.P.op(eng, lambda: self.V(eng).tensor_scalar(out=out, in0=in0, scalar1=s1, scalar2=None, op0=op0), r=r, w=w)
        else:
            self.P.op(eng, lambda: self.V(eng).tensor_scalar(out=out, in0=in0, scalar1=s1, scalar2=s2, op0=op0, op1=op1), r=r, w=w)

    def stt(self, out, in0, scalar, in1, op0, op1, r, w, accum=None):
        if accum is None:
            self.P.op('dve', lambda: self.nc.vector.scalar_tensor_tensor(out=out, in0=in0, scalar=scalar, in1=in1, op0=op0, op1=op1), r=r, w=w)
        else:
            self.P.op('dve', lambda: self.nc.vector.scalar_tensor_tensor(out=out, in0=in0, scalar=scalar, in1=in1, op0=op0, op1=op1,
                                                                         accum_out=accum), r=r, w=w)

    def act(self, out, in_, func, r, w, scale=1.0, bias=0.0):
        self.P.op('act', lambda: self.nc.scalar.activation(out=out, in_=in_, func=func, scale=scale, bias=bias), r=r, w=w)

    def cp(self, eng, out, in_, r, w):
        if eng == 'act':
            self.P.op('act', lambda: self.nc.scalar.copy(out=out, in_=in_), r=r, w=w)
        else:
            self.P.op(eng, lambda: self.V(eng).tensor_copy(out=out, in_=in_), r=r, w=w)

    def mm(self, out, lhsT, rhs, r, w, start=True, stop=True, sgc=False):
        if sgc:
            self.P.op('pe', lambda: self.nc.tensor.matmul(out, lhsT, rhs, start=start, stop=stop, skip_group_check=True), r=r, w=w)
        else:
            self.P.op('pe', lambda: self.nc.tensor.matmul(out, lhsT, rhs, start=start, stop=stop), r=r, w=w)

    def tr(self, out, in_, ident, r, w):
        self.P.op('pe', lambda: self.nc.tensor.transpose(out, in_, ident), r=r, w=w)

    def dma(self, q, out, in_, r, w):
        E = self.nc.sync if q == 'sp' else self.nc.gpsimd
        self.P.dma(q, lambda: E.dma_start(out=out, in_=in_), r=r, w=w)

    def memset(self, eng, ap, val, w):
        self.P.op(eng, lambda: self.V(eng).memset(ap, val), w=w)


HG0 = 0
RW0 = 2560
LWS = -0.6065306597126334
GN_EPS = 64e-5


def _proj_fm(self, em, w_d, c0, ncols, pp, pkey, pkeys=None):
    nc = self.nc
    wt, wkey = self.load_w(w_d[:, c0:c0 + ncols].rearrange("(k p) n -> p k n", p=128), (8, ncols))
    for th in range(2):
        def mm(wt=wt, th=th):
            ins = None
            for k in range(8):
                ins = nc.tensor.matmul(pp[0:ncols, th * 512:(th + 1) * 512], wt[:, k, :], self.hT[:, k, th * 512:(th + 1) * 512],
                                       start=(k == 0), stop=(k == 7))
            return ins
        self.P.op('pe', mm, r=[wkey] + [('hT', k) for k in range(8)], w=[pkeys[th] if pkeys else (pkey, th)])


def _decay(self, em, lw, Gp, D, rev, key):
    nc = self.nc
    self.P.op('dve', lambda: nc.vector.tensor_tensor_scan(out=Gp[:, 1:T + 1], data0=self.onesT[:], data1=lw, initial=0.0,
                                                          op0=ALU.mult, op1=ALU.add), r=[key + '_lw', 'onesT'], w=[key + '_Gp'])
    v3 = lambda ap: ap.rearrange("p (c l) -> p c l", l=64)
    if not rev:
        em.tt('dve', v3(D), v3(Gp[:, 1:T + 1]), Gp[:, 0:T:64].unsqueeze(2).broadcast_to([128, 16, 64]), ALU.subtract,
              r=[key + '_Gp'], w=[key + '_D'])
    else:
        em.tt('dve', v3(D), Gp[:, 64:T + 1:64].unsqueeze(2).broadcast_to([128, 16, 64]), v3(Gp[:, 0:T]), ALU.subtract,
              r=[key + '_Gp'], w=[key + '_D'])


def _odd_mixer(self):
    P, nc = self.P, self.nc
    em = Em(self)
    l = 1
    wd = self.od_w_in_d
    with ExitStack() as es:
        ocat = self.sb(es, "ocat1", [128, 8, D], BF16)
        self.onesT = self.sb(es, "onesT", [128, T], F32)
        masks = self.sb(es, "masks", [128, 4, 128], F32)
        bv1 = self.sb(es, "bv1", [128, 1280], F32)
        with ExitStack() as e0:
            self.rmsnorm(e0, lambda c: self.gm1[l][:, c:c + 1], lambda c: self.mod[l][:, c:c + 1],
                         lambda c: (self.hT[:, c, :], [('hT', c)]))
            em.memset('pool', self.onesT[:], 1.0, ['onesT'])
            em.dma('sp', masks[:], self.masks_d, [], ['masks'])
            em.dma('sp', bv1[:], self.bv1_d, [], ['bv1'])
            P.flush()
        self.dump('h1T', self.hT, [128, NCH, T], [])
        with ExitStack() as e1:
            self.alloc_w(e1, 3, 4096)
            osum = self.sb(e1, "osum_h", [128, 8, 512], F32)
            vtok = self.sb(e1, "vtok_h", [128, 8, 512], BF16)
            gsil = self.sb(e1, "gsil", [128, 8, 512], BF16)
            lbv = self.sb(e1, "lbv", [128, 8], F32)
            omlb = self.sb(e1, "omlb", [128, 8], F32)
            ssh = self.sb(e1, "ssh", [128, 32], F32)
            junk = self.sb(e1, "junkh", [128, 128], F32)
            HB = []
            for s_ in range(2):
                hb = {}
                for nm in ('fl', 'kf', 'lg', 'qs'):
                    hb[nm] = self.sb(e1, "h%s%d" % (nm, s_), [128, T], F32)
                hb['Gp'] = self.sb(e1, "hGp%d" % s_, [128, T + 1], F32)
                for nm in ('qt', 'qA', 'qB', 'ktl'):
                    hb[nm] = self.sb(e1, "h%s%d" % (nm, s_), [128, T], BF16)
                hb['Am'] = [self.sb(e1, "hAm%d_%d" % (s_, i), [128, 128], BF16) for i in range(2)]
                hb['ktok'] = [self.sb(e1, "hktok%d_%d" % (s_, i), [128, 128], BF16) for i in range(2)]
                hb['S'] = self.sb(e1, "hS%d" % s_, [128, 128], F32)
                hb['Stmp'] = self.sb(e1, "hStmp%d" % s_, [128, 128], F32)
                hb['Sb'] = [self.sb(e1, "hSb%d_%d" % (s_, i), [128, 128], BF16) for i in range(2)]
                hb['hbA'] = self.ps(e1, "hbA%d" % s_, [128, T])
                hb['hbB'] = self.ps(e1, "hbB%d" % s_, [128, T])
                HB.append(hb)
            ptk = [HB[0]['hbB'][:, 0:512], HB[0]['hbB'][:, 512:1024]]
            P.excl.update(['hbA', 'hbB'])
            P.alias.update({('ptk', 0): ('hbB', 0, 0), ('ptk', 1): ('hbB', 0, 1)})
            for s_ in range(2):
                em.memset('pool', HB[s_]['qA'][:], 0.0, [('h', s_, 'qA')])
                em.memset('pool', HB[s_]['qB'][:], 0.0, [('h', s_, 'qB')])
            em.tt('dve', lbv[:], self.vcol('lb1', 0, 8), self.vcol('lb0', 0, 8), ALU.subtract, r=[], w=['lbv'])
            em.act(lbv[:], lbv[:], AF.Sigmoid, r=['lbv'], w=['lbv'])
            em.ts('dve', omlb[:], lbv[:], -1.0, 1.0, ALU.mult, ALU.add, r=['lbv'], w=['omlb'])
            wv = self.load_w(wd[:, 1536:2048].rearrange("(k p) n -> p k n", p=128), (8, 512))
            wg = self.load_w(wd[:, 2048:2560].rearrange("(k p) n -> p k n", p=128), (8, 512))
            for tt in range(8):
                for bi, (wt, wkey) in enumerate((wv, wg)):
                    pm = ptk[bi]

                    def mm(wt=wt, pm=pm, tt=tt):
                        ins = None
                        for k in range(8):
                            ins = nc.tensor.matmul(pm[:], self.hT[:, k, tt * 128:(tt + 1) * 128], wt[:, k, :], start=(k == 0), stop=(k == 7))
                        return ins
                    P.op('pe', mm, r=[wkey], w=[('ptk', bi)])
                    if bi == 0:
                        em.cp('dve', vtok[:, tt, :], pm[:], r=[('ptk', bi)], w=[('vtok', tt)])
                    else:
                        em.act(gsil[:, tt, :], pm[:], AF.Silu, r=[('ptk', bi)], w=[('gsil', tt)])
            def hchain(slot, dr, hc):
                rev = dr == 1
                mask = masks[:, 1 if rev else 0, :]
                ci = dr * 4 + hc
                Kk = lambda n: ('h', slot, n)
                hb = HB[slot]
                fl, kf, lg, Gp, qsc = hb['fl'], hb['kf'], hb['lg'], hb['Gp'], hb['qs']
                qt, qA, qB, ktl = hb['qt'], hb['qA'], hb['qB'], hb['ktl']
                Am, ktok, S, Stmp, Sb = hb['Am'], hb['ktok'], hb['S'], hb['Stmp'], hb['Sb']
                hbA, hbB = hb['hbA'], hb['hbB']
                kA0, kA1, kB0, kB1 = ('hbA', slot, 0), ('hbA', slot, 1), ('hbB', slot, 0), ('hbB', slot, 1)
                Dd = lg
                enD = fl
                eD = Gp
                psc = hbA[:, 0:128]
                pktr = hbA[:, 512:1024].bitcast(BF16)[:, 0:128]
                po = hbB[:, 0:128]
                pds = hbB[:, 512:640]
                _proj_fm(self, em, wd, hc * 128, 128, hbA, None, pkeys=[kA0, kA1])
                em.act(qsc[:], hbA[:], AF.Silu, r=[kA0, kA1], w=[Kk('qs')])
                yield
                _proj_fm(self, em, wd, 512 + dr * 512 + hc * 128, 128, hbA, None, pkeys=[kA0, kA1])
                em.act(fl[:], hbA[:], AF.Sigmoid, r=[kA0, kA1], w=[Kk('fl')])
                em.ts('dve', fl[:], fl[:], omlb[:, ci:ci + 1], lbv[:, ci:ci + 1], ALU.mult, ALU.add, r=[Kk('fl'), 'omlb', 'lbv'], w=[Kk('fl')])
                yield
                em.ts('pool', kf[:], fl[:], -1.0, 1.0, ALU.mult, ALU.add, r=[Kk('fl')], w=[Kk('kf')])
                em.act(lg[:], fl[:], AF.Ln, r=[Kk('fl')], w=[Kk('lg')])
                em.memset('pool', Gp[:, 0:1], 0.0, [Kk('Gp')])
                P.op('dve', lambda: nc.vector.tensor_tensor_scan(out=Gp[:, 1:T + 1], data0=self.onesT[:], data1=lg[:], initial=0.0,
                                                                 op0=ALU.mult, op1=ALU.add), r=[Kk('lg'), 'onesT'], w=[Kk('Gp')])
                yield
                v3 = lambda ap: ap.rearrange("p (c l) -> p c l", l=64)
                if not rev:
                    em.tt('dve', v3(Dd[:]), v3(Gp[:, 1:T + 1]), Gp[:, 0:T:64].unsqueeze(2).broadcast_to([128, 16, 64]), ALU.subtract,
                          r=[Kk('Gp'), Kk('lg')], w=[Kk('lg')])
                else:
                    em.tt('dve', v3(Dd[:]), Gp[:, 64:T + 1:64].unsqueeze(2).broadcast_to([128, 16, 64]), v3(Gp[:, 0:T]), ALU.subtract,
                          r=[Kk('Gp'), Kk('lg')], w=[Kk('lg')])
                em.act(eD[:, 0:T], Dd[:], AF.Exp, r=[Kk('lg'), Kk('Gp')], w=[Kk('Gp')])
                em.act(enD[:], Dd[:], AF.Exp, r=[Kk('lg'), Kk('fl')], w=[Kk('fl')], scale=-1.0)
                yield
                em.tt('dve', qt[:], qsc[:], eD[:, 0:T], ALU.mult, r=[Kk('qs'), Kk('Gp')], w=[Kk('qt')])
                h3 = lambda ap: ap.rearrange("p (t l) -> p t l", l=128)
                em.cp('pool', h3(qA[:])[:, :, 0:64], h3(qt[:])[:, :, 0:64], r=[Kk('qt')], w=[Kk('qA')])
                em.cp('pool', h3(qB[:])[:, :, 64:128], h3(qt[:])[:, :, 64:128], r=[Kk('qt')], w=[Kk('qB')])
                em.tt('dve', ktl[:], kf[:], enD[:], ALU.mult, r=[Kk('kf'), Kk('fl')], w=[Kk('ktl')])
                em.dma('sp', S[:], self.st_h_d[dr, hc], [], [Kk('S')])
                em.cp('pool', Sb[0][:], S[:], r=[Kk('S')], w=[Kk('Sb0')])
                yield
                sbi = 0
                for ti in range(8):
                    tt = 7 - ti if rev else ti
                    tl = slice(tt * 128, (tt + 1) * 128)
                    b_ = ti % 2
                    em.mm(psc, ktl[:, tl], qt[:, tl], r=[Kk('ktl'), Kk('qt')], w=[kA0])
                    em.tt('dve', Am[b_][:], psc, mask, ALU.mult, r=[kA0, 'masks'], w=[Kk('Am%d' % b_)])
                    em.tr(pktr, ktl[:, tl], self.ident_b[:], r=[Kk('ktl')], w=[kA1])
                    em.cp('act', ktok[b_][:], pktr, r=[kA1], w=[Kk('ktok%d' % b_)])
                    vt = vtok[:, tt, hc * 128:(hc + 1) * 128]
                    order = (1, 0) if rev else (0, 1)
                    em.mm(po, Am[b_][:], vt, r=[Kk('Am%d' % b_), ('vtok', tt)], w=[kB0], start=True, stop=False)
                    yield
                    for oi, c in enumerate(order):
                        qh = qA if c == 0 else qB
                        cr = slice(c * 64, (c + 1) * 64)
                        em.mm(pds, ktok[b_][cr, :], vtok[cr, tt, hc * 128:(hc + 1) * 128], r=[Kk('ktok%d' % b_), ('vtok', tt)], w=[kB1])
                        em.mm(po, qh[:, tl], Sb[sbi][:], r=[Kk('qA'), Kk('qB'), Kk('Sb%d' % sbi)], w=[kB0], start=False, stop=(oi == 1))
                        em.tt('dve', Stmp[:], pds, S[:], ALU.add, r=[kB1, Kk('S')], w=[Kk('Stmp')])
                        cg = tt * 2 + c
                        ecol = cg * 64 + (0 if rev else 63)
                        em.ts('dve', S[:], Stmp[:], eD[:, ecol:ecol + 1], None, ALU.mult, None, r=[Kk('Stmp'), Kk('Gp')], w=[Kk('S')])
                        seg_end = (cg % 4 == 0) if rev else (cg % 4 == 3)
                        if seg_end:
                            seg = cg // 4
                            em.dma('sp', self.o_sh_d[seg, dr, hc], S[:], r=[Kk('S')], w=[('o_sh', seg, dr, hc)])
                            last = (cg == 0) if rev else (cg == 15)
                            if not last:
                                em.ts('dve', S[:], S[:], self.vcol('keep'), None, ALU.mult, None, r=[Kk('S')], w=[Kk('S')])
                        sbi = 1 - sbi
                        em.cp('pool', Sb[sbi][:], S[:], r=[Kk('S')], w=[Kk('Sb%d' % sbi)])
                        yield
                    if dr == 0:
                        em.cp('act', osum[:, tt, hc * 128:(hc + 1) * 128], po, r=[kB0], w=[('osum', tt, hc)])
                    else:
                        em.tt('dve', osum[:, tt, hc * 128:(hc + 1) * 128], po, osum[:, tt, hc * 128:(hc + 1) * 128], ALU.add,
                              r=[kB0, ('osum', tt, hc)], w=[('osum', tt, hc)])

            for dr in range(2):
                for hp in range(2):
                    gens = [hchain(0, dr, 2 * hp), hchain(1, dr, 2 * hp + 1)]
                    while gens:
                        for g_ in list(gens):
                            try:
                                next(g_)
                            except StopIteration:
                                gens.remove(g_)
            for tt in range(8):
                for hc in range(4):
                    u = tt * 4 + hc
                    em.stt(junk[:], osum[:, tt, hc * 128:(hc + 1) * 128], 1.0, osum[:, tt, hc * 128:(hc + 1) * 128], ALU.mult, ALU.mult,
                           r=[('osum', tt, hc)], w=['junkh', ('ssh', u)], accum=ssh[:, u:u + 1])
            em.act(ssh[:], ssh[:], AF.Ln, r=[('ssh', u) for u in range(32)], w=['rsh'], scale=1.0 / 128, bias=self.epsc[:, 0:1])
            em.act(ssh[:], ssh[:], AF.Exp, r=['rsh'], w=['rsh'], scale=-1.0 * 0.5)
            for tt in range(8):
                for hc in range(4):
                    u = tt * 4 + hc
                    em.stt(osum[:, tt, hc * 128:(hc + 1) * 128], osum[:, tt, hc * 128:(hc + 1) * 128], ssh[:, u:u + 1], bv1[:, 0:128],
                           ALU.mult, ALU.mult, r=[('osum', tt, hc), 'rsh', 'bv1'], w=[('osum', tt, hc)])
                em.tt('pool', ocat[:, tt, 0:512], osum[:, tt, :], gsil[:, tt, :], ALU.mult,
                      r=[('osum', tt, hc) for hc in range(4)] + [('gsil', tt)], w=[('ocat', tt, 0)])
            self.dump('ocat_h', ocat, [128, 8, D], [])
            P.flush()
        if self.stop == 'C1':
            return
        self.odd_rwkv(em, ocat, masks, bv1)
        with ExitStack() as e3:
            self.alloc_w(e3, 2, 4096)
            ptp = [self.ps(e3, "otp%d" % i, [128, 1024], BF16) for i in range(2)]
            pmx = [self.ps(e3, "pmx%d" % i, [128, 512]) for i in range(2)]
            n = 0
            for c in range(8):
                for gq in range(2):
                    pp = ptp[n % 2]
                    for j in range(4):
                        qt_ = gq * 4 + j
                        em.tr(pp[:, j * 128:(j + 1) * 128], ocat[:, qt_, c * 128:(c + 1) * 128], self.ident_b[:], r=[], w=[('otp', n % 2)])
                    em.cp('dve' if n % 2 == 0 else 'act', self.hT[:, c, gq * 512:(gq + 1) * 512], pp[:, 0:512], r=[('otp', n % 2)], w=[('hT', c)])
                    n += 1
            self.out_proj(self.od_w_out_d, pmx, l)
            self.dump('xm1', self.xT, [128, NCH, T], [])
            P.flush()


Builder.odd_mixer = _odd_mixer


def _odd_rwkv(self, em, ocat, masks, bv1):
    P, nc = self.P, self.nc
    wd = self.od_w_in_d
    idf = self.ident_f
    with ExitStack() as e2:
        sbf = lambda n, s, d=F32: self.sb(e2, n, s, d)
        e2b = ExitStack()
        sbb = lambda n, s, d=F32: self.sb(e2b, n, s, d)
        self.alloc_w(e2, 3, 1024)
        osum = sbf("osum_r", [128, 8, 512])
        bonus = sbf("bonus", [128, 8, 8])
        twd = sbf("twd", [128, T], BF16)
        adT = sbf("adT", [64, T], BF16)
        sgd = sbf("sgd", [128, T], BF16)
        wup = sbf("wup", [128, 512], BF16)
        aup = sbf("aup", [64, 512], BF16)
        gup = sbf("gup", [128, 512], BF16)
        hsel = sbf("hsel", [128, 2])
        omm = sbf("omm", [128, 15]); hmu = sbf("hmu", [128, 15]); hk = sbf("hk", [128, 15])
        zst = sbf("zst", [64, 16, 64]); sstg = [sbf("sstg%d" % i, [64, 64]) for i in range(2)]
        zst_b = sbf("zst_b", [64, 16, 64], BF16)

        m2 = sbf("m2", [128, 2, 256])
        pA2 = [self.ps(e2, "pA2_%d" % i, [128, T]) for i in range(2)]
        pBC = [self.ps(e2, "pBC_%d" % i, [128, T]) for i in range(2)]
        pj = pA2[0]
        pT = pj[:, 0:512]
        pS = pBC[0][:, 512:1024]
        P.excl.update(['pA', 'pB'])
        P.alias.update({('pj', 0): ('pA', 0, 0), ('pj', 1): ('pA', 0, 1), 'pT': ('pA', 0, 0), 'pS': ('pB', 0, 1)})
        vtok = sbb("vtok_p", [128, 8, 128], BF16)
        r_p = sbb("r_p", [128, T]); k_p = sbb("k_p", [128, T]); v_p = sbb("v_p", [128, T])
        a_p = sbb("a_p", [128, T]); kk_p = sbb("kk_p", [128, T]); kt_p = sbb("kt_p", [128, T]); b_p = sbb("b_p", [128, T])
        tmpf = sbb("tmpf", [128, T]); sqb = sbb("sqr", [128, T], BF16)
        Gp = sbb("Gpr", [128, T + 1]); Dd = self.rstd
        KR = [sbb("kr%d" % i, [128, 8, 256], BF16) for i in range(2)]; BE = [sbb("be%d" % i, [128, T], BF16) for i in range(2)]; TA = [sbb("ta%d" % i, [128, T], BF16) for i in range(2)]
        ED1 = sbb("eD1", [128, T])
        pd = tmpf; lw = v_p; Dp = a_p; enD = Gp; ED = [tmpf, ED1]
        P.alias.update({'pd': 'tmpf', 'lwraw': 'v_p', 'r_lw': 'v_p', 'Dp': 'a_p', 'enDr': 'r_Gp', ('eDr', 0): 'tmpf'})
        btk = [sbb("btk%d" % i, [128, 128], BF16) for i in range(2)]; ttk = [sbb("ttk%d" % i, [128, 128], BF16) for i in range(2)]
        nktk = [sbb("nktk%d" % i, [128, 128], BF16) for i in range(2)]
        M1b = [sbb("M1b%d" % i, [128, 2, 256], BF16) for i in range(2)]; M2b = [sbb("M2b%d" % i, [128, 2, 256], BF16) for i in range(2)]
        YA = [[sbb("YA%d_%d" % (s_, i), [128, 2, 128], BF16) for i in range(2)] for s_ in range(2)]
        YT_ = [[sbb("YT%d_%d" % (s_, i), [128, 2, 128], BF16) for i in range(2)] for s_ in range(2)]
        QQ = [[sbb("QQ%d_%d" % (s_, i), [128, 2, 128], BF16) for i in range(2)] for s_ in range(2)]
        rxb = [sbb("rxb%d" % i, [128, 2, 128], BF16) for i in range(2)]; xsb = [sbb("xsb%d" % i, [128, 2, 128], BF16) for i in range(2)]
        RAb = [sbb("RAb%d" % i, [64, 2, 128], BF16) for i in range(2)]; RBb = [sbb("RBb%d" % i, [64, 2, 128], BF16) for i in range(2)]
        GTb = [sbb("GTb%d" % i, [64, 2, 128]) for i in range(2)]; Z1b = [sbb("Z1b%d" % i, [64, 2, 128]) for i in range(2)]
        ztb = [sbb("ztb%d" % i, [64, 2, 64]) for i in range(2)]

        em.memset('pool', hsel[:], 0.0, ['hsel'])
        em.memset('pool', hsel[0:64, 0:1], 1.0, ['hsel'])
        em.memset('pool', hsel[64:128, 1:2], 1.0, ['hsel'])
        for s_ in range(2):
            em.memset('pool', RAb[s_][:], 0.0, [('t', s_, 'ra')])
            em.memset('pool', RBb[s_][:], 0.0, [('t', s_, 'rb')])
        em.memset('pool', Gp[:, 0:1], 0.0, ['r_Gp'])
        for dr in range(2):
            em.cp('pool', m2[:, dr, 0:128], masks[:, 2 + dr, :], r=['masks'], w=['m2'])
            em.cp('pool', m2[:, dr, 128:256], masks[:, dr, :], r=['masks'], w=['m2'])
        em.ts('dve', omm[:], self.vcol('mu', 0, 15), -1.0, 1.0, ALU.mult, ALU.add, r=[], w=['omm'])
        em.ts('dve', hmu[:], self.vcol('mu', 0, 15), 0.5, None, ALU.mult, None, r=[], w=['hmu'])
        em.ts('dve', hk[:], hmu[:], self.vcol('km1'), None, ALU.mult, None, r=['hmu'], w=['hk'])
        em.dma('pool', wup[:], self.w_up_d, [], ['wup'])
        em.dma('pool', aup[:], self.a_up_d, [], ['aup'])
        em.dma('pool', gup[:], self.g_up_d, [], ['gup'])
        sld = osum[0:64, 0:2, :].rearrange("p a (b k) -> p (a b) k", k=64)
        em.dma('sp', sld, self.st_r_d.rearrange("d h v k -> v (d h) k"), [], ['sld'])
        for i in range(16):
            em.tr(pS[0:64, 0:64], sld[:, i, :], idf[0:64, 0:64], r=['sld'], w=['pS'])
            em.cp('dve', zst[:, i, :], pS[0:64, 0:64], r=['pS'], w=[('zst', i)])
            em.cp('pool', zst_b[:, i, :], zst[:, i, :], r=[('zst', i)], w=[('zstb', i)])
        P.flush()

        def tshift(dst, j, nrows, rkeys, wkeys):
            pr = pj[0:nrows, :]
            em.act(dst, pr, AF.Identity, r=rkeys + ['omm'], w=wkeys, scale=omm[0:nrows, j:j + 1])
            em.stt(dst[:, 1:T], pr[:, 0:T - 1], hmu[0:nrows, j:j + 1], dst[:, 1:T], ALU.mult, ALU.add, r=rkeys + wkeys + ['hmu'], w=wkeys)
            em.stt(dst[:, 0:T - 1], pr[:, 1:T], hmu[0:nrows, j:j + 1], dst[:, 0:T - 1], ALU.mult, ALU.add, r=rkeys + wkeys + ['hmu'], w=wkeys)
            em.stt(dst[:, 256:T:256], pr[:, 255:T - 1:256], hk[0:nrows, j:j + 1], dst[:, 256:T:256], ALU.mult, ALU.add,
                   r=rkeys + wkeys + ['hk'], w=wkeys)
            em.stt(dst[:, 255:T - 1:256], pr[:, 256:T:256], hk[0:nrows, j:j + 1], dst[:, 255:T - 1:256], ALU.mult, ALU.add,
                   r=rkeys + wkeys + ['hk'], w=wkeys)

        PJ = [('pj', 0), ('pj', 1)]
        _proj_fm(self, em, wd, RW0 + 1536, 128, pj, 'pj')
        tshift(pd[:], 12, 128, PJ, ['pd'])
        em.act(twd[:], pd[:], AF.Tanh, r=['pd'], w=['twd'])
        _proj_fm(self, em, wd, RW0 + 1664, 64, pj, 'pj')
        tshift(pd[0:64, :], 13, 64, PJ, ['pd'])
        em.cp('pool', adT[:], pd[0:64, :], r=['pd'], w=['adT'])
        _proj_fm(self, em, wd, RW0 + 1728, 128, pj, 'pj')
        tshift(pd[:], 14, 128, PJ, ['pd'])
        em.act(sgd[:], pd[:], AF.Sigmoid, r=['pd'], w=['sgd'])
        if self.stop == 'C2a':
            P.flush(); e2b.close(); return
        for p in range(4):
            for nm, dst, j in (('r', r_p, p), ('k', k_p, 4 + p), ('v', v_p, 8 + p)):
                _proj_fm(self, em, wd, RW0 + j * 128, 128, pj, 'pj')
                tshift(dst[:], j, 128, PJ, [nm + '_p'])
            for tt in range(8):
                em.tr(pT[:, 0:128], v_p[:, tt * 128:(tt + 1) * 128], idf[:], r=['v_p'], w=['pT'])
                em.cp('act', vtok[:, tt, :], pT[:, 0:128], r=['pT'], w=[('vtok', tt)])
            for th in range(2):
                em.mm(pj[:, th * 512:(th + 1) * 512], aup[:, p * 128:(p + 1) * 128], adT[:, th * 512:(th + 1) * 512], r=['aup', 'adT'], w=[('pj', th)])
            em.act(a_p[:], pj[:], AF.Sigmoid, r=PJ, w=['a_p'], bias=self.vcol('a0', p))
            em.ts('dve', kk_p[:], k_p[:], self.vcol('k_k', p), None, ALU.mult, None, r=['k_p'], w=['kk_p'])
            em.act(sqb[:], kk_p[:], AF.Square, r=['kk_p'], w=['sqr'])
            for th in range(2):
                em.mm(pj[:, th * 512:(th + 1) * 512], self.bd_ones[:], sqb[:, th * 512:(th + 1) * 512], r=['sqr'], w=[('pj', th)])
            em.act(tmpf[:], pj[:], AF.Ln, r=PJ, w=['tmpf'], bias=self.epsc[:, 1:2])
            em.act(tmpf[:], tmpf[:], AF.Exp, r=['tmpf'], w=['tmpf'], scale=-0.5)
            em.tt('dve', kk_p[:], kk_p[:], tmpf[:], ALU.mult, r=['kk_p', 'tmpf'], w=['kk_p'])
            em.ts('dve', kt_p[:], a_p[:], -1.0, self.vcol('k_a', p), ALU.add, ALU.mult, r=['a_p'], w=['kt_p'])
            em.stt(kt_p[:], kt_p[:], 1.0, k_p[:], ALU.add, ALU.mult, r=['kt_p', 'k_p'], w=['kt_p'])
            em.tt('pool', b_p[:], a_p[:], kk_p[:], ALU.mult, r=['a_p', 'kk_p'], w=['b_p'])
            em.stt(tmpf[:], r_p[:], self.vcol('r_k', p), kt_p[:], ALU.mult, ALU.mult, r=['r_p', 'kt_p', 'tmpf'], w=['tmpf'])
            for tt in range(8):
                em.mm(pT[:, 0:2], tmpf[:, tt * 128:(tt + 1) * 128], hsel[:], r=['tmpf', 'hsel'], w=['pT'])
                em.cp('act', bonus[:, tt, 2 * p:2 * p + 2], pT[:, 0:2], r=['pT'], w=[('bonus', tt, p)])
                em.tt('pool', ocat[:, tt, 512 + p * 128:512 + (p + 1) * 128].rearrange('p (h e) -> p h e', e=64),
                      vtok[:, tt, :].rearrange('p (h e) -> p h e', e=64), bonus[:, tt, 2 * p:2 * p + 2].unsqueeze(2).broadcast_to([128, 2, 64]),
                      ALU.mult, r=[('vtok', tt), ('bonus', tt, p)], w=[('ocat', tt, 1)])
            def dir_prep(dr):
                rev = dr == 1
                kr, be, ta, eD = KR[dr], BE[dr], TA[dr], ED[dr]
                dslc = slice(dr * 64, (dr + 1) * 64)
                for th in range(2):
                    em.mm(pj[:, th * 512:(th + 1) * 512], wup[dslc, p * 128:(p + 1) * 128], twd[dslc, th * 512:(th + 1) * 512],
                          r=['wup', 'twd'], w=[('pj', th)])
                em.act(lw[:], pj[:], AF.Sigmoid, r=PJ, w=['lwraw'], bias=self.vcol('w0', dr * 4 + p))
                yield
                em.ts('dve', lw[:], lw[:], LWS, None, ALU.mult, None, r=['lwraw'], w=['r_lw'])
                yield
                em.memset('pool', Gp[:, 0:1], 0.0, ['r_Gp'])
                yield
                _decay(self, em, lw[:], Gp, Dd[:], rev, 'r')
                yield
                em.tt('pool', Dp[:], Dd[:], lw[:], ALU.subtract, r=['r_D', 'r_lw'], w=['Dp'])
                yield
                em.act(eD[:], Dd[:], AF.Exp, r=['r_D'], w=[('eDr', dr)])
                yield
                em.act(enD[:, 0:T], Dd[:], AF.Exp, r=['r_D'], w=['enDr'], scale=-1.0)
                yield
                em.act(Dp[:], Dp[:], AF.Exp, r=['Dp'], w=['Dp'])
                yield
                t3 = lambda ap: ap.rearrange("p (t l) -> p t l", l=128)
                em.tt('dve', kr[:, :, 0:128], t3(kk_p[:]), t3(Dp[:]), ALU.mult, r=['kk_p', 'Dp'], w=[('kr', dr)])
                yield
                em.tt('dve', kr[:, :, 128:256], t3(r_p[:]), t3(eD[:]), ALU.mult, r=['r_p', ('eDr', dr)], w=[('kr', dr)])
                yield
                em.tt('pool', be[:], b_p[:], enD[:, 0:T], ALU.mult, r=['b_p', 'enDr'], w=[('be', dr)])
                yield
                em.tt('pool', ta[:], kt_p[:], enD[:, 0:T], ALU.mult, r=['kt_p', 'enDr'], w=[('ta', dr)])
                yield

            def dir_tiles(dr, extra):
                rev = dr == 1
                kr, be, ta, eD = KR[dr], BE[dr], TA[dr], ED[dr]
                mk = m2[:, dr, :]
                mk3 = masks[:, 3 - dr, :]
                def tchain(slot, ti):
                    tt = 7 - ti if rev else ti
                    tl = slice(tt * 128, (tt + 1) * 128)
                    bb = slot
                    A2, BC = pA2[slot], pBC[slot]
                    kA = [('pA', slot, 0), ('pA', slot, 1)]
                    kB = [('pB', slot, 0), ('pB', slot, 1)]
                    m1, m2_, ya, yt, qq = M1b[slot], M2b[slot], YA[slot], YT_[slot], QQ[slot]
                    rx, xs, ra, rb, gt, z1, zt = rxb[slot], xsb[slot], RAb[slot], RBb[slot], GTb[slot], Z1b[slot], ztb[slot]
                    K_ = lambda n: ('t', slot, n)
                    zi0 = dr * 8 + 2 * p
                    HR = [slice(0, 64), slice(64, 128)]
                    v2 = lambda ap, n: ap.rearrange("p (h n) -> p h n", n=n)
                    vh = lambda ap: ap.rearrange("p (h n) -> p h n", n=512)
                    pTs = BC[:, 0:512].bitcast(BF16)
                    em.tr(pTs[:, 0:128], kr[:, tt, 0:128], self.ident_b[:], r=[('kr', dr)], w=[kB[0]])
                    em.tr(pTs[:, 128:256], be[:, tl], self.ident_b[:], r=[('be', dr)], w=[kB[0]])
                    em.tr(pTs[:, 256:384], ta[:, tl], self.ident_b[:], r=[('ta', dr)], w=[kB[0]])
                    em.act(nktk[bb][:], pTs[:, 0:128], AF.Identity, r=[kB[0]], w=[('nktk', bb)], scale=-1.0)
                    em.cp('dve', btk[bb][:], pTs[:, 128:256], r=[kB[0]], w=[('btk', bb)])
                    em.cp('act', ttk[bb][:], pTs[:, 256:384], r=[kB[0]], w=[('ttk', bb)])
                    for hh in range(2):
                        em.mm(A2[:, hh * 512:hh * 512 + 256], be[HR[hh], tl], kr[HR[hh], tt, :], r=[('be', dr), ('kr', dr)], w=[kA[hh]])
                    em.tt('dve', m1[:], vh(A2[:])[:, :, 0:256], mk.unsqueeze(1).broadcast_to([128, 2, 256]), ALU.mult, r=kA + ['m2'], w=[K_('m1')])
                    for hh in range(2):
                        em.mm(BC[:, hh * 512:hh * 512 + 128], kr[HR[hh], tt, 0:128], be[HR[hh], tl], r=[('kr', dr), ('be', dr)], w=[kB[hh]])
                    em.tt('dve', ya[0][:], vh(BC[:])[:, :, 0:128], mk3.unsqueeze(1).broadcast_to([128, 2, 128]), ALU.mult,
                          r=kB + ['masks'], w=[K_('yy0')])
                    yield
                    for hh in range(2):
                        em.mm(A2[:, hh * 512:hh * 512 + 256], ta[HR[hh], tl], kr[HR[hh], tt, :], r=[('ta', dr), ('kr', dr)], w=[kA[hh]])
                    em.tt('dve', m2_[:], vh(A2[:])[:, :, 0:256], mk.unsqueeze(1).broadcast_to([128, 2, 256]), ALU.mult, r=kA + ['m2'], w=[K_('m2')])
                    em.tt('pool', qq[0][:], idf[:].unsqueeze(1).broadcast_to([128, 2, 128]), m1[:, :, 0:128], ALU.subtract, r=[K_('m1')], w=[K_('qq0')])
                    yield
                    for hh in range(2):
                        em.mm(BC[:, 768 + hh * 64:832 + hh * 64], m2_[:, hh, 0:128], vtok[:, tt, hh * 64:(hh + 1) * 64], r=[K_('m2'), ('vtok', tt)], w=[kB[1]])
                    em.act(rx[:, :, 64:128], v2(BC[:, 768:896], 64), AF.Identity, r=[kB[1]], w=[K_('rx')], scale=-1.0)
                    em.cp('pool', rx[:, :, 0:64], v2(nktk[bb][:], 64), r=[('nktk', bb)], w=[K_('rx')])
                    yi, qi = 0, 0
                    for j in range(6):
                        yn = 1 - yi
                        for hh in range(2):
                            Yc = ya[yi][:, hh, :]
                            YTc = m1[:, hh, 0:128] if j == 0 else yt[yi][:, hh, :]
                            rk = [K_('yy%d' % yi)] + ([K_('m1')] if j == 0 else [])
                            if j < 5:
                                em.mm(BC[:, hh * 256:hh * 256 + 128], YTc, Yc, r=rk, w=[kB[0]])
                                if j < 4:
                                    em.mm(BC[:, hh * 256 + 128:hh * 256 + 256], Yc, YTc, r=rk, w=[kB[0]])
                            if j >= 1:
                                em.mm(BC[:, 512 + hh * 128:640 + hh * 128], Yc, qq[qi][:, hh, :], r=[K_('yy%d' % yi), K_('qq%d' % qi)], w=[kB[1]])
                        if j < 5:
                            em.cp('act', ya[yn][:], v2(BC[:, 0:512], 256)[:, :, 0:128], r=[kB[0]], w=[K_('yy%d' % yn)])
                            if j < 4:
                                em.cp('dve', yt[yn][:], v2(BC[:, 0:512], 256)[:, :, 128:256], r=[kB[0]], w=[K_('yy%d' % yn)])
                        if j >= 1:
                            em.tt('dve', qq[1 - qi][:], v2(BC[:, 512:768], 128), qq[qi][:], ALU.add, r=[kB[1], K_('qq%d' % qi)], w=[K_('qq%d' % (1 - qi))])
                            qi = 1 - qi
                        yi = yn
                        yield
                    for hh in range(2):
                        em.mm(BC[:, 512 + hh * 128:640 + hh * 128], qq[qi][:, hh, :], rx[:, hh, :], r=[K_('qq%d' % qi), K_('rx')], w=[kB[1]])
                    em.cp('act', xs[:], v2(BC[:, 512:768], 128), r=[kB[1]], w=[K_('xs')])
                    yield
                    for hh in range(2):
                        em.mm(BC[0:64, 768 + hh * 128:896 + hh * 128], xs[:, hh, 0:64], m1[:, hh, 128:256], r=[K_('xs'), K_('m1')], w=[kB[1]])
                    for hh in range(2):
                        em.tt('dve', ra[:, hh, 0:64], BC[0:64, 768 + hh * 128:832 + hh * 128], kr[HR[hh], tt, 128:192], ALU.add, r=[kB[1], ('kr', dr)], w=[K_('ra')])
                        em.tt('dve', rb[:, hh, 64:128], BC[0:64, 832 + hh * 128:896 + hh * 128], kr[HR[hh], tt, 192:256], ALU.add, r=[kB[1], ('kr', dr)], w=[K_('rb')])
                    for hh in range(2):
                        em.mm(A2[:, 256 + hh * 64:320 + hh * 64], m1[:, hh, 128:256], xs[:, hh, 64:128], r=[K_('m1'), K_('xs')], w=[kA[0]],
                              start=(hh == 0), stop=False, sgc=True)
                        em.mm(A2[:, 256 + hh * 64:320 + hh * 64], m2_[:, hh, 128:256], vtok[:, tt, hh * 64:(hh + 1) * 64], r=[K_('m2'), ('vtok', tt)], w=[kA[0]],
                              start=False, stop=False, sgc=True)
                    for hh in range(2):
                        for c in range(2):
                            cr = slice(c * 64, (c + 1) * 64)
                            o_ = c * 512 + hh * 64
                            em.mm(BC[0:64, o_:o_ + 64], xs[cr, hh, 0:64], btk[bb][cr, hh * 64:(hh + 1) * 64], r=[K_('xs'), ('btk', bb)], w=[kB[c]])
                    g4 = lambda ap: ap.rearrange("p c (h e) -> p c h e", e=64)
                    em.tt('dve', g4(gt[:]), g4(vh(BC[0:64, :])[:, :, 0:128]),
                          idf[0:64, 0:64].unsqueeze(1).unsqueeze(1).broadcast_to([64, 2, 2, 64]), ALU.add, r=kB, w=[K_('gt')])
                    yield
                    for hh in range(2):
                        for c in range(2):
                            cr = slice(c * 64, (c + 1) * 64)
                            o_ = c * 512 + 128 + hh * 64
                            em.mm(BC[0:64, o_:o_ + 64], btk[bb][cr, hh * 64:(hh + 1) * 64], xs[cr, hh, 64:128], r=[K_('xs'), ('btk', bb)], w=[kB[c]],
                                  start=True, stop=False, sgc=True)
                            em.mm(BC[0:64, o_:o_ + 64], ttk[bb][cr, hh * 64:(hh + 1) * 64], vtok[cr, tt, hh * 64:(hh + 1) * 64],
                                  r=[('ttk', bb), ('vtok', tt)], w=[kB[c]], start=False, stop=True, sgc=True)
                    em.cp('act', z1[:], vh(BC[0:64, :])[:, :, 128:256], r=kB, w=[K_('z1')])
                    yield
                    order = (1, 0) if rev else (0, 1)
                    for oi, c in enumerate(order):
                        for hh in range(2):
                            zi = zi0 + hh
                            em.mm(BC[0:64, 256 + hh * 64:320 + hh * 64], gt[:, c, hh * 64:(hh + 1) * 64], zst[:, zi, :], r=[K_('gt'), ('zst', zi)], w=[kB[0]])
                        for hh in range(2):
                            zi = zi0 + hh
                            Rh = ra if c == 0 else rb
                            em.mm(A2[:, 256 + hh * 64:320 + hh * 64], Rh[:, hh, :], zst_b[:, zi, :], r=[K_('ra'), K_('rb'), ('zstb', zi)], w=[kA[0]],
                                  start=False, stop=(oi == 1), sgc=True)
                        em.tt('dve', zt[:], v2(BC[0:64, 256:384], 64), v2(z1[:, c, :], 64), ALU.add, r=[kB[0], K_('z1')], w=[K_('zt')])
                        cg = tt * 2 + c
                        ecol = cg * 64 + (0 if rev else 63)
                        seg_end = (cg % 4 == 0) if rev else (cg % 4 == 3)
                        for hh in range(2):
                            zi = zi0 + hh
                            head = 2 * p + hh
                            em.ts('dve', zst[:, zi, :], zt[:, hh, :], eD[HR[hh], ecol:ecol + 1], None, ALU.mult, None, r=[K_('zt'), ('eDr', dr)], w=[('zst', zi)])
                            if seg_end:
                                seg = cg // 4
                                sg = sstg[hh]
                                sk = ('sstg', hh)
                                em.tr(BC[0:64, 640 + hh * 64:704 + hh * 64], zst[:, zi, :], idf[0:64, 0:64], r=[('zst', zi)], w=[kB[1]])
                                em.cp('act', sg[:], BC[0:64, 640 + hh * 64:704 + hh * 64], r=[kB[1]], w=[sk])
                                em.dma('sp', self.o_sr_d[seg, dr, head], sg[:], r=[sk], w=[('o_sr', seg, dr, head)])
                                last = (cg == 0) if rev else (cg == 15)
                                if not last:
                                    em.ts('dve', zst[:, zi, :], zst[:, zi, :], self.vcol('keep')[0:64, :], None, ALU.mult, None,
                                          r=[('zst', zi)], w=[('zst', zi)])
                        em.cp('pool', zst_b[:, zi0:zi0 + 2, :], zst[:, zi0:zi0 + 2, :], r=[('zst', zi0), ('zst', zi0 + 1)],
                              w=[('zstb', zi0), ('zstb', zi0 + 1)])
                        yield
                    ocols = slice(p * 128, (p + 1) * 128)
                    if dr == 0:
                        em.cp('act', osum[:, tt, ocols], A2[:, 256:384], r=[kA[0]], w=[('osum', tt, 2 * p), ('osum', tt, 2 * p + 1)])
                    else:
                        em.tt('dve', osum[:, tt, ocols], A2[:, 256:384], osum[:, tt, ocols], ALU.add,
                              r=[kA[0], ('osum', tt, 2 * p), ('osum', tt, 2 * p + 1)], w=[('osum', tt, 2 * p), ('osum', tt, 2 * p + 1)])

                NSTART = 2
                active = []
                nxt = 0
                while active or nxt < 8:
                    if nxt < 8 and len(active) < 2 and (not active or active[0][1] >= NSTART):
                        active.append([tchain(nxt % 2, nxt), 0])
                        nxt += 1
                    for ent in list(active):
                        try:
                            next(ent[0])
                            ent[1] += 1
                        except StopIteration:
                            active.remove(ent)
                    if extra is not None:
                        try:
                            next(extra)
                        except StopIteration:
                            extra = None
                if extra is not None:
                    for _ in extra:
                        pass
            for _ in dir_prep(0):
                pass
            dir_tiles(0, dir_prep(1))
            dir_tiles(1, None)
        P.flush()
        e2b.close()
        gtok = sbf("gtok", [128, 8, 512], BF16)
        for tt in range(8):
            em.mm(pT[:], sgd[:, tt * 128:(tt + 1) * 128], gup[:], r=['sgd', 'gup'], w=['pT'])
            em.cp('act', gtok[:, tt, :], pT[:], r=['pT'], w=[('gtok', tt)])
        mean = sbf("mean", [128, 8, 8]); var = sbf("var", [128, 8, 8]); sq2 = sbf("sq2", [128, 512]); cen = sbf("cen", [128, 8, 512])
        h4 = lambda ap: ap.rearrange("p (h e) -> p h e", e=64)
        for tt in range(8):
            OK = [('osum', tt, h) for h in range(8)]
            P.op('dve', lambda tt=tt: nc.vector.reduce_sum(out=mean[:, tt, :], in_=h4(osum[:, tt, :]), axis=AX.X), r=OK, w=[('mean', tt)])
            em.ts('dve', mean[:, tt, :], mean[:, tt, :], 1.0 / 64, None, ALU.mult, None, r=[('mean', tt)], w=[('mean', tt)])
            em.tt('dve', h4(cen[:, tt, :]), h4(osum[:, tt, :]), mean[:, tt, :].unsqueeze(2).broadcast_to([128, 8, 64]), ALU.subtract,
                  r=OK + [('mean', tt)], w=[('cen', tt)])
            em.tt('pool', sq2[:], cen[:, tt, :], cen[:, tt, :], ALU.mult, r=[('cen', tt)], w=['sq2'])
            P.op('dve', lambda tt=tt: nc.vector.reduce_sum(out=var[:, tt, :], in_=h4(sq2[:]), axis=AX.X), r=['sq2'], w=[('var', tt)])
        VK = [('var', tt) for tt in range(8)]
        em.act(var[:], var[:], AF.Ln, r=VK, w=['rstdr'], scale=1.0 / 64, bias=self.epsc[:, 2:3])
        em.act(var[:], var[:], AF.Exp, r=['rstdr'], w=['rstdr'], scale=-0.5)
        for tt in range(8):
            c3 = h4(cen[:, tt, :])
            em.tt('dve', c3, c3, var[:, tt, :].unsqueeze(2).broadcast_to([128, 8, 64]), ALU.mult, r=[('cen', tt), 'rstdr'], w=[('cen', tt)])
            em.tt('pool', cen[:, tt, :], cen[:, tt, :], bv1[:, 128:640], ALU.mult, r=[('cen', tt)], w=[('cen', tt)])
            em.tt('pool', cen[:, tt, :], cen[:, tt, :], bv1[:, 640:1152], ALU.add, r=[('cen', tt)], w=[('cen', tt)])
            em.tt('dve', cen[:, tt, :], cen[:, tt, :], ocat[:, tt, 512:1024], ALU.add, r=[('cen', tt), ('ocat', tt, 1)], w=[('cen', tt)])
            em.tt('pool', ocat[:, tt, 512:1024], cen[:, tt, :], gtok[:, tt, :], ALU.mult, r=[('cen', tt), ('gtok', tt)], w=[('ocat', tt, 1)])
        self.dump('ocat_r', ocat, [128, 8, D], [])
        P.flush()


Builder.odd_rwkv = _odd_rwkv


GRID_W = 64
def rope_tables(prompt):
    T = 1024
    if prompt:
        return np.stack([np.ones((128, T), np.float32), np.zeros((128, T), np.float32)])
    rows = (np.arange(T) // GRID_W).astype(np.float32)
    cols = (np.arange(T) % GRID_W).astype(np.float32)
    half = 16
    freq = np.power(np.float32(10000.0), -np.arange(half, dtype=np.float32) / half).astype(np.float32)
    cos = np.zeros((64, T), np.float32); sin = np.zeros((64, T), np.float32)
    for d in range(64):
        pos = rows if d < 32 else cols
        w = d % 32
        i = w % 16
        ang = (pos * freq[i]).astype(np.float32)
        cos[d] = np.cos(ang)
        sin[d] = -np.sin(ang) if w < 16 else np.sin(ang)
    return np.stack([np.concatenate([cos, cos]), np.concatenate([sin, sin])]).astype(np.float32)

def partner64():
    p = np.arange(64)
    w = p % 32
    return np.where(w < 16, p + 16, p - 16)

def ev_wx(ev_w_in):
    W = ev_w_in
    pr = partner64()
    qa = W[:, 0:512]; ka = W[:, 512:1024]; qb = W[:, 1536:2048]; kb = W[:, 2048:2176]
    hb_order = [0, 4, 1, 5, 2, 6, 3, 7]
    qbp = np.concatenate([qb[:, h * 64:(h + 1) * 64] for h in hb_order], axis=1)
    def sw(M):
        n = M.shape[1] // 64
        return np.concatenate([M[:, h * 64:(h + 1) * 64][:, pr] for h in range(n)], axis=1)
    return np.ascontiguousarray(np.concatenate([qbp, sw(qa), sw(ka), sw(qbp), sw(kb)], axis=1))

def amask(prompt):
    M = np.zeros((128, 48), np.float32)
    if prompt:
        for s in range(4):
            for kb in range(12):
                ok = kb >= 4 and (kb - 4) // 2 == s
                if not ok:
                    M[:, s * 12 + kb] = -30000.0
    return M

def host_vecs(vp, inp, cond, prompt):
    keep = 0.0 if prompt else 1.0
    V = np.zeros((128, vp.n), np.float32)
    def put(name, arr):
        c0, n = vp.cols[name]
        assert arr.shape == (128, n), (name, arr.shape, n)
        V[:, c0:c0+n] = arr
    put('cond', fm(cond))
    put('km1', np.full((128,1), keep - 1.0, np.float32))
    put('keep', np.full((128,1), keep, np.float32))
    for l in range(2):
        put('ada_b%d'%l, fm(inp['ada_b'][l]))
        put('nmg%d'%l, fm(inp['norm_mix_g'][l]))
        put('nfg%d'%l, fm(inp['norm_ffn_g'][l]))
        for i in range(3):
            put('cw%d_%d'%(i,l), fm(inp['ffn_conv_w'][l, i]))
        put('cb_%d'%l, fm(inp['ffn_conv_b'][l]))
    put('fng', fm(inp['final_norm_g']))
    pr = partner64()
    gq = inp['b_q_norm_g'][0]; gk = inp['b_k_norm_g'][0]
    put('gq', np.tile(gq, 2)[:, None]); put('gq_sw', np.tile(gq[pr], 2)[:, None])
    put('gk', np.tile(gk, 2)[:, None]); put('gk_sw', np.tile(gk[pr], 2)[:, None])
    put('amask', amask(prompt))
    return V

def bvec(inp):
    b = np.concatenate([inp['a_subln_g'][0], inp['b_k_norm_g'][0], inp['a_lambda'][0].reshape(-1)]).astype(np.float32)
    return np.ascontiguousarray(np.broadcast_to(b[None, :], (128, 448)))

def core_inputs(vp, inp, core):
    prompt = core < 4
    m = {}
    if prompt:
        m['x'] = np.ascontiguousarray(inp['x_prompt'][4 * core:4 * core + 4].reshape(1024, 1024))
        cond = inp['c_ctx']
        m['ctx_ak'] = np.zeros((4, 512, 128), np.float32); m['ctx_av'] = np.zeros((4, 512, 128), np.float32)
        m['ctx_bk'] = np.zeros((2, 512, 64), np.float32); m['ctx_bv'] = np.zeros((2, 512, 64), np.float32)
    else:
        b = core - 4
        m['x'] = np.ascontiguousarray(inp['x_sample'][b])
        cond = inp['c'][b]
        m['ctx_ak'] = np.ascontiguousarray(inp['cache_a_k'][b, 0]); m['ctx_av'] = np.ascontiguousarray(inp['cache_a_v'][b, 0])
        m['ctx_bk'] = np.ascontiguousarray(inp['cache_b_k'][b, 0]); m['ctx_bv'] = np.ascontiguousarray(inp['cache_b_v'][b, 0])
    m['vecs'] = host_vecs(vp, inp, cond, prompt)
    m['ident'] = np.eye(128, dtype=np.float32)
    m['rope'] = rope_tables(prompt)
    m['bvec'] = bvec(inp)
    m['ada_w'] = inp['ada_w']; m['ffn_w_up'] = inp['ffn_w_up']; m['ffn_w_down'] = inp['ffn_w_down']
    m['ev_w_in'] = np.ascontiguousarray(inp['ev_w_in'][0]); m['ev_wx'] = ev_wx(inp['ev_w_in'][0])
    m['ev_w_out'] = np.ascontiguousarray(inp['ev_w_out'][0])
    return m


def masks_const():
    idx = np.arange(128)
    blk = (idx[:, None] // 64) == (idx[None, :] // 64)
    s, t = idx[:, None], idx[None, :]
    M = np.stack([blk & (s <= t), blk & (s >= t), blk & (s < t), blk & (s > t)], axis=1)
    return np.ascontiguousarray(M.astype(np.float32))


def odd_inputs(vp, inp, core, m):
    prompt = core < 4
    V = m['vecs']
    def put(name, arr):
        c0, n = vp.cols[name]
        assert arr.shape == (128, n), (name, arr.shape, n)
        V[:, c0:c0+n] = arr
    lb = inp['hgrn_lb_logits']
    put('lb0', fm(lb[:, 0, :].reshape(-1)))
    put('lb1', fm(lb[:, 1, :].reshape(-1)))
    mu = inp['rwkv_mu'][0]
    MU = np.zeros((128, 15), np.float32)
    MU[:, 0:13] = fm(mu[0:1664])
    MU[0:64, 13] = mu[1664:1728]
    MU[:, 14] = mu[1728:1856]
    put('mu', MU)
    put('w0', fm(inp['rwkv_w0'][0].reshape(-1)))
    put('a0', fm(inp['rwkv_a0'][0]))
    put('k_k', fm(inp['rwkv_k_k'][0]))
    put('k_a', fm(inp['rwkv_k_a'][0]))
    put('r_k', fm(inp['rwkv_r_k'][0]))
    m['od_w_in'] = np.ascontiguousarray(inp['od_w_in'][0])
    m['od_w_out'] = np.ascontiguousarray(inp['od_w_out'][0])
    m['masks'] = masks_const()
    b = np.concatenate([inp['hgrn_norm_g'][0], inp['rwkv_ln_g'][0], inp['rwkv_ln_b'][0], np.zeros(128, np.float32)]).astype(np.float32)
    m['bv1'] = np.ascontiguousarray(np.broadcast_to(b[None, :], (128, 1280)))
    if prompt:
        m['st_h'] = np.zeros((2, 4, 128, 128), np.float32)
        m['st_r'] = np.zeros((2, 8, 64, 64), np.float32)
    else:
        bb = core - 4
        m['st_h'] = np.ascontiguousarray(inp['state_hgrn'][bb, 0])
        m['st_r'] = np.ascontiguousarray(inp['state_rwkv'][bb, 0])
    m['w_up'] = np.ascontiguousarray(inp['rwkv_w_up'][0].reshape(128, 512))
    m['a_up'] = np.ascontiguousarray(inp['rwkv_a_up'][0])
    m['g_up'] = np.ascontiguousarray(inp['rwkv_g_up'][0])
    return m


_BUILT = {}


def kernel(**inputs):
    inp = {k: np.asarray(v) for k, v in inputs.items()}
    B = Builder(debug=(), layers=(0, 1))
    B.build()
    maps = []
    for core in range(8):
        m = core_inputs(B.vp, inp, core)
        m = odd_inputs(B.vp, inp, core, m)
        maps.append(m)
    res = run_bass_kernel_spmd(B.nc, maps, core_ids=list(range(8)))
    R = res.results
    y_prompt = np.zeros((16, 256, 1024), np.float32)
    y_sample = np.zeros((4, 1024, 1024), np.float32)
    ak = np.zeros((16, 1, 4, 256, 128), np.float32)
    av = np.zeros((16, 1, 4, 256, 128), np.float32)
    bk = np.zeros((16, 1, 2, 256, 64), np.float32)
    bv = np.zeros((16, 1, 2, 256, 64), np.float32)
    sh = np.zeros((16, 1, 2, 4, 128, 128), np.float32)
    sr = np.zeros((16, 1, 2, 8, 64, 64), np.float32)
    for c in range(4):
        r = R[c]
        y_prompt[4 * c:4 * c + 4] = np.asarray(r['y']).reshape(4, 256, 1024)
        ak[4 * c:4 * c + 4, 0] = np.asarray(r['o_ak'])
        av[4 * c:4 * c + 4, 0] = np.asarray(r['o_av'])
        bk[4 * c:4 * c + 4, 0] = np.asarray(r['o_bk'])
        bv[4 * c:4 * c + 4, 0] = np.asarray(r['o_bv'])
        sh[4 * c:4 * c + 4, 0] = np.asarray(r['o_sh'])
        sr[4 * c:4 * c + 4, 0] = np.asarray(r['o_sr'])
    for b in range(4):
        y_sample[b] = np.asarray(R[4 + b]['y'])
    return (y_prompt, y_sample, ak, av, bk, bv, sh, sr)
```

```python
import numpy as np
from contextlib import ExitStack
import concourse.bass as bass
import concourse.mybir as mybir
from concourse.bass_utils import run_bass_kernel_spmd

F32 = mybir.dt.float32
BF16 = mybir.dt.bfloat16
AF = mybir.ActivationFunctionType
ALU = mybir.AluOpType
AX = mybir.AxisListType

ENGS = ('pe', 'dve', 'act', 'pool', 'sp')
NDS = 16


class Op:
    __slots__ = ('eng', 'fn', 'reads', 'writes', 'deps', 'need_inc', 'semkey', 'count', 'dma', 'prev_dma')

    def __init__(self, eng, fn, reads, writes, dma):
        self.eng = eng
        self.fn = fn
        self.reads = reads
        self.writes = writes
        self.deps = []
        self.need_inc = False
        self.semkey = None
        self.count = 0
        self.dma = dma
        self.prev_dma = None


class Prog:
    def __init__(self):
        self.nc = bass.Bass("TRN2", target_bir_lowering=False)
        nc = self.nc
        self.E = {'pe': nc.tensor, 'dve': nc.vector, 'act': nc.scalar, 'pool': nc.gpsimd, 'sp': nc.sync}
        self.es = ExitStack()
        self.sems = {}
        for e in ENGS:
            self.sems[e] = self.es.enter_context(nc.semaphore("sem_" + e))
        for q in ('sp', 'pool', 'act'):
            for i in range(NDS):
                self.sems[('dma', q, i)] = self.es.enter_context(nc.semaphore("dsem_%s_%d" % (q, i)))
        self.cnt = {k: 0 for k in self.sems}
        self.dma_rr = {'sp': 0, 'pool': 0, 'act': 0}
        self.last_dma_on_sem = {}
        self.seen = {e: {} for e in ENGS}
        self.pending = []
        self.last_writer = {}
        self.readers = {}
        self.n_ins = 0
        self.excl = set()
        self.alias = {}

    def op(self, eng, fn, r=(), w=()):
        self.pending.append(Op(eng, fn, tuple(r), tuple(w), False))

    def dma(self, q, fn, r=(), w=()):
        self.pending.append(Op(q, fn, tuple(r), tuple(w), True))

    def _wait(self, eng, key, val):
        if val <= 0:
            return
        if self.seen[eng].get(key, 0) >= val:
            return
        self.E[eng].wait_ge(self.sems[key], val)
        self.n_ins += 1
        self.seen[eng][key] = val

    def flush(self, barrier=True):
        ops = self.pending
        self.pending = []
        lw, rd = self.last_writer, self.readers
        for op in ops:
            deps = []
            if self.alias:
                op.reads = tuple(self.alias.get(b, b) for b in op.reads)
                op.writes = tuple(self.alias.get(b, b) for b in op.writes)
            ex = [b for b in op.reads if (b[0] if isinstance(b, tuple) else b) in self.excl]
            if ex:
                op.writes = tuple(op.writes) + tuple(b for b in ex if b not in op.writes)
                op.reads = tuple(b for b in op.reads if b not in ex)
            for b in op.reads:
                d = lw.get(b)
                if d is not None:
                    deps.append(d)
            for b in op.writes:
                d = lw.get(b)
                if d is not None:
                    deps.append(d)
                deps.extend(rd.get(b, ()))
            op.deps = [d for d in deps if d is not op]
            for d in op.deps:
                d.need_inc = True
            for b in op.reads:
                rd.setdefault(b, []).append(op)
            for b in op.writes:
                lw[b] = op
                rd[b] = []
        if barrier:
            last = {}
            for op in ops:
                last[op.eng] = op
            for op in last.values():
                op.need_inc = True
        for op in ops:
            if op.dma:
                i = self.dma_rr[op.eng] % NDS
                self.dma_rr[op.eng] += 1
                key = ('dma', op.eng, i)
                op.semkey = key
                self.cnt[key] += 16
                op.count = self.cnt[key]
                op.prev_dma = self.cnt[key] - 16
            elif op.need_inc:
                op.semkey = op.eng
                self.cnt[op.eng] += 1
                op.count = self.cnt[op.eng]
        for op in ops:
            need = {}
            for d in op.deps:
                if d.eng == 'pe' and op.eng == 'pe' and not d.dma and not op.dma:
                    continue
                k = d.semkey
                if need.get(k, 0) < d.count:
                    need[k] = d.count
            if op.dma and op.prev_dma:
                k = op.semkey
                if need.get(k, 0) < op.prev_dma:
                    need[k] = op.prev_dma
            for k, v in need.items():
                self._wait(op.eng, k, v)
            ins = op.fn()
            self.n_ins += 1
            if op.dma:
                ins.then_inc(self.sems[op.semkey], 16)
            elif op.need_inc:
                ins.then_inc(self.sems[op.eng], 1)
            op.fn = None
        if barrier:
            self.barrier()

    def barrier(self):
        for k, v in self.cnt.items():
            if k == 'sp':
                continue
            self._wait('sp', k, v)
        ins = self.E['sp'].nop()
        self.cnt['sp'] += 1
        ins.then_inc(self.sems['sp'], 1)
        for e in ENGS:
            if e != 'sp':
                self._wait(e, 'sp', self.cnt['sp'])
            for k, v in self.cnt.items():
                self.seen[e][k] = v
        self.last_writer = {}
        self.readers = {}


T = 1024
D = 1024
NCH = 8
DFF = 2816
NFF = 22
EPS = 1e-6


class VecPack:
    def __init__(self):
        self.cols = {}
        self.n = 0

    def add(self, name, ncols):
        self.cols[name] = (self.n, ncols)
        self.n += ncols
        return self.cols[name]


def build_vec_layout():
    vp = VecPack()
    vp.add('cond', 8)
    vp.add('km1', 1)
    vp.add('keep', 1)
    for l in range(2):
        vp.add('ada_b%d' % l, 48)
        vp.add('nmg%d' % l, 8)
        vp.add('nfg%d' % l, 8)
        vp.add('cw0_%d' % l, 44)
        vp.add('cw1_%d' % l, 44)
        vp.add('cw2_%d' % l, 44)
        vp.add('cb_%d' % l, 44)
    vp.add('fng', 8)
    vp.add('gq', 1); vp.add('gq_sw', 1); vp.add('gk', 1); vp.add('gk_sw', 1)
    vp.add('amask', 48)
    vp.add('lb0', 8); vp.add('lb1', 8); vp.add('mu', 15); vp.add('w0', 8)
    vp.add('a0', 4); vp.add('k_k', 4); vp.add('k_a', 4); vp.add('r_k', 4)
    return vp


def fm(v):
    v = np.asarray(v, dtype=np.float32)
    return np.ascontiguousarray(v.reshape(-1, 128).T)


class Builder:
    def __init__(self, debug=(), layers=(0, 1), do_mix=True):
        self.layers = layers
        import os
        self.stop = os.environ.get('KSTOP', '')
        self.skip = set(os.environ.get('KSKIP', '').split(','))
        self.do_mix = do_mix
        self.P = Prog()
        self.nc = self.P.nc
        self.debug = debug
        self.vp = build_vec_layout()
        self.dbg_outs = {}
        self.P.excl.update(['tp', 'mps', 'ssp', 'ups', 'ftp', 'pq', 'pqs', 'pss', 'ptm', 'ptp', 'sT', 'acc', 'otp', 'pmx'])

    def dram_in(self, name, shape, dt=F32):
        return self.nc.dram_tensor(name, list(shape), dt, kind="ExternalInput").ap()

    def dram_out(self, name, shape, dt=F32):
        return self.nc.dram_tensor(name, list(shape), dt, kind="ExternalOutput").ap()

    def sb(self, es, name, shape, dt):
        self.uid = getattr(self, 'uid', 0) + 1
        return es.enter_context(self.nc.sbuf_tensor("sb%d_%s" % (self.uid, name), list(shape), dt))

    def ps(self, es, name, shape, dt=F32):
        self.uid = getattr(self, 'uid', 0) + 1
        return es.enter_context(self.nc.psum_tensor("ps%d_%s" % (self.uid, name), list(shape), dt))

    def vcol(self, name, j=0, n=1):
        c0, nc_ = self.vp.cols[name]
        return self.vecs[:, c0 + j:c0 + j + n]

    def dump(self, name, sbt, shape, reads):
        if name not in self.debug:
            return
        P, nc = self.P, self.nc
        P.flush()
        dt = sbt.dtype if hasattr(sbt, 'dtype') else F32
        o = self.dram_out("dbg_" + name, shape, dt)
        self.dbg_outs[name] = (shape, dt)
        P.dma('sp', lambda: nc.sync.dma_start(out=o, in_=sbt[:]), r=reads, w=[('dbg', name)])

    def build(self):
        P, nc = self.P, self.nc
        top = ExitStack()
        self.top = top
        self.x_d = self.dram_in("x", [T, D])
        self.vecs_d = self.dram_in("vecs", [128, self.vp.n])
        self.ident_d = self.dram_in("ident", [128, 128])
        self.ada_w_d = self.dram_in("ada_w", [2, D, 6 * D])
        self.ffn_up_d = self.dram_in("ffn_w_up", [2, D, 2 * DFF])
        self.ffn_dn_d = self.dram_in("ffn_w_down", [2, DFF, D])
        self.y_d = self.dram_out("y", [T, D])
        self.ev_w_in_d = self.dram_in("ev_w_in", [D, 2304])
        self.ev_wx_d = self.dram_in("ev_wx", [D, 2176])
        self.ev_w_out_d = self.dram_in("ev_w_out", [D, D])
        self.rope_d = self.dram_in("rope", [2, 128, T])
        self.bvec_d = self.dram_in("bvec", [128, 448])
        self.ctx_ak_d = self.dram_in("ctx_ak", [4, 512, 128])
        self.ctx_av_d = self.dram_in("ctx_av", [4, 512, 128])
        self.ctx_bk_d = self.dram_in("ctx_bk", [2, 512, 64])
        self.ctx_bv_d = self.dram_in("ctx_bv", [2, 512, 64])
        self.o_ak_d = self.dram_out("o_ak", [4, 4, 256, 128])
        self.o_av_d = self.dram_out("o_av", [4, 4, 256, 128])
        self.o_bk_d = self.dram_out("o_bk", [4, 2, 256, 64])
        self.o_bv_d = self.dram_out("o_bv", [4, 2, 256, 64])
        self.od_w_in_d = self.dram_in("od_w_in", [D, 4416])
        self.od_w_out_d = self.dram_in("od_w_out", [D, D])
        self.masks_d = self.dram_in("masks", [128, 4, 128])
        self.bv1_d = self.dram_in("bv1", [128, 1280])
        self.st_h_d = self.dram_in("st_h", [2, 4, 128, 128])
        self.st_r_d = self.dram_in("st_r", [2, 8, 64, 64])
        self.w_up_d = self.dram_in("w_up", [128, 512])
        self.a_up_d = self.dram_in("a_up", [64, 512])
        self.g_up_d = self.dram_in("g_up", [128, 512])
        self.o_sh_d = self.dram_out("o_sh", [4, 2, 4, 128, 128])
        self.o_sr_d = self.dram_out("o_sr", [4, 2, 8, 64, 64])
        self.xT = self.sb(top, "xT", [128, NCH, T], F32)
        self.hT = self.sb(top, "hT", [128, NCH, T], BF16)
        self.vecs = self.sb(top, "vecs", [128, self.vp.n], F32)
        self.ident_f = self.sb(top, "ident_f", [128, 128], F32)
        self.ident_b = self.sb(top, "ident_b", [128, 128], BF16)
        self.ones_b = self.sb(top, "ones_b", [128, 128], BF16)
        self.mod = [self.sb(top, "mod%d" % l, [128, 48], F32) for l in range(2)]
        self.gm1 = [self.sb(top, "gm1_%d" % l, [128, 8], F32) for l in range(2)]
        self.gm2 = [self.sb(top, "gm2_%d" % l, [128, 8], F32) for l in range(2)]
        self.wk0 = [self.sb(top, "wk0_%d" % l, [128, 44], F32) for l in range(2)]
        self.wk2 = [self.sb(top, "wk2_%d" % l, [128, 44], F32) for l in range(2)]
        self.rstd = self.sb(top, "rstd", [128, T], F32)
        self.epsc = self.sb(top, "epsc", [128, 4], F32)
        self.bd_ones = self.sb(top, "bd_ones", [128, 128], BF16)
        self.NWB = 0
        self.wbuf = []
        self.wrr = 0

        self.phase_init()
        for l in range(2):
            if l in self.layers:
                if l == 0 and self.do_mix:
                    self.even_mixer()
                elif l == 1 and self.do_mix:
                    self.odd_mixer()
                else:
                    self.phase_mix(l)
                self.phase_ffn(l)
        self.phase_final()
        top.close()
        P.es.close()

    def phase_init(self):
        P, nc = self.P, self.nc
        with ExitStack() as es:
            xin = [self.sb(es, "xin%d" % i, [128, D], F32) for i in range(2)]
            scond = self.sb(es, "scond", [128, 8], BF16)
            abuf = [self.sb(es, "abuf%d" % i, [128, 6 * D], BF16) for i in range(2)]
            tp = [self.ps(es, "tp%d" % i, [128, 512]) for i in range(2)]
            mps = self.ps(es, "mps", [128, 48])

            P.dma('sp', lambda: nc.sync.dma_start(out=self.vecs[:], in_=self.vecs_d), w=['vecs'])
            P.dma('sp', lambda: nc.sync.dma_start(out=self.ident_f[:], in_=self.ident_d), w=['ident_f'])
            P.op('dve', lambda: nc.vector.tensor_copy(out=self.ident_b[:], in_=self.ident_f[:]), r=['ident_f'], w=['ident_b'])
            P.op('pool', lambda: nc.gpsimd.memset(self.ones_b[:], 1.0), w=['ones_b'])
            P.op('pool', lambda: nc.gpsimd.memset(self.epsc[:, 0:1], EPS), w=['epsc'])
            P.op('pool', lambda: nc.gpsimd.memset(self.epsc[:, 1:2], 1e-24), w=['epsc'])
            P.op('pool', lambda: nc.gpsimd.memset(self.epsc[:, 2:3], GN_EPS), w=['epsc'])
            P.op('pool', lambda: nc.gpsimd.memset(self.bd_ones[:], 0.0), w=['bd_ones'])
            P.op('pool', lambda: nc.gpsimd.memset(self.bd_ones[0:64, 0:64], 1.0), w=['bd_ones'])
            P.op('pool', lambda: nc.gpsimd.memset(self.bd_ones[64:128, 64:128], 1.0), w=['bd_ones'])
            for tt in range(8):
                xi = xin[tt % 2]
                P.dma('sp', lambda xi=xi, tt=tt: nc.sync.dma_start(out=xi[:], in_=self.x_d[tt * 128:(tt + 1) * 128, :]),
                      w=[('xin', tt % 2)])
                for g in range(2):
                    tpp = tp[g]
                    for cc in range(4):
                        c = g * 4 + cc
                        P.op('pe', lambda xi=xi, tpp=tpp, cc=cc, c=c: nc.tensor.transpose(
                            tpp[:, cc * 128:(cc + 1) * 128], xi[:, c * 128:(c + 1) * 128], self.ident_f[:]),
                            r=[('xin', tt % 2), 'ident_f'], w=[('tp', g)])
                    eng = 'act' if g == 0 else 'dve'
                    if g == 0:
                        P.op('act', lambda tpp=tpp, g=g, tt=tt: nc.scalar.copy(
                            out=self.xT[:, g * 4:(g + 1) * 4, tt * 128:(tt + 1) * 128],
                            in_=tpp[:].rearrange("p (c t) -> p c t", c=4)),
                            r=[('tp', g)], w=[('xT', c_) for c_ in range(g * 4, g * 4 + 4)])
                    else:
                        P.op('dve', lambda tpp=tpp, g=g, tt=tt: nc.vector.tensor_copy(
                            out=self.xT[:, g * 4:(g + 1) * 4, tt * 128:(tt + 1) * 128],
                            in_=tpp[:].rearrange("p (c t) -> p c t", c=4)),
                            r=[('tp', g)], w=[('xT', c_) for c_ in range(g * 4, g * 4 + 4)])
            P.op('act', lambda: nc.scalar.activation(out=scond[:], in_=self.vcol('cond', 0, 8), func=AF.Silu),
                 r=['vecs'], w=['scond'])
            for l in range(2):
                for kc in range(8):
                    ab = abuf[kc % 2]
                    P.dma('pool', lambda ab=ab, l=l, kc=kc: nc.gpsimd.dma_start(
                        out=ab[:], in_=self.ada_w_d[l, kc * 128:(kc + 1) * 128, :]), w=[('abuf', kc % 2)])

                    def mm(ab=ab, kc=kc):
                        ins = None
                        for col in range(48):
                            ins = nc.tensor.matmul(mps[:, col:col + 1], ab[:, col * 128:(col + 1) * 128], scond[:, kc:kc + 1],
                                                   start=(kc == 0 and col == 0), stop=(kc == 7), skip_group_check=True)
                        return ins
                    P.op('pe', mm, r=[('abuf', kc % 2), 'scond'], w=['mps'])
                md = self.mod[l]
                P.op('dve', lambda md=md, l=l: nc.vector.tensor_tensor(out=md[:], in0=mps[:], in1=self.vcol('ada_b%d' % l, 0, 48),
                                                                       op=ALU.add), r=['mps', 'vecs'], w=[('mod', l)])
                P.op('dve', lambda md=md, l=l: nc.vector.scalar_tensor_tensor(
                    out=self.gm1[l][:], in0=md[:, 8:16], scalar=1.0, in1=self.vcol('nmg%d' % l, 0, 8),
                    op0=ALU.add, op1=ALU.mult), r=[('mod', l), 'vecs'], w=[('gm1', l)])
                P.op('dve', lambda md=md, l=l: nc.vector.scalar_tensor_tensor(
                    out=self.gm2[l][:], in0=md[:, 32:40], scalar=1.0, in1=self.vcol('nfg%d' % l, 0, 8),
                    op0=ALU.add, op1=ALU.mult), r=[('mod', l), 'vecs'], w=[('gm2', l)])
                P.op('dve', lambda l=l: nc.vector.tensor_scalar(
                    out=self.wk0[l][:], in0=self.vcol('cw0_%d' % l, 0, 44), scalar1=self.vcol('km1'), scalar2=None,
                    op0=ALU.mult), r=['vecs'], w=[('wk0', l)])
                P.op('dve', lambda l=l: nc.vector.tensor_scalar(
                    out=self.wk2[l][:], in0=self.vcol('cw2_%d' % l, 0, 44), scalar1=self.vcol('km1'), scalar2=None,
                    op0=ALU.mult), r=['vecs'], w=[('wk2', l)])
            self.dump('mod0', self.mod[0], [128, 48], [('mod', 0)])
            self.dump('xT', self.xT, [128, NCH, T], [('xT', c) for c in range(8)])
            P.flush()

    def rmsnorm(self, es, gm, sh, out_fn):
        P, nc = self.P, self.nc
        sq = self.sb(es, "sq", [128, NCH, T], BF16)
        tmp = [self.sb(es, "ntmp%d" % i, [128, T], F32) for i in range(2)]
        ssp = [self.ps(es, "ssp%d" % i, [128, 512]) for i in range(2)]
        for c in range(8):
            P.op('act', lambda c=c: nc.scalar.activation(out=sq[:, c, :], in_=self.xT[:, c, :], func=AF.Square),
                 r=[('xT', c)], w=[('sq', c)])
        for th in range(2):
            def mm(th=th):
                ins = None
                for c in range(8):
                    ins = nc.tensor.matmul(ssp[th][:], self.ones_b[:], sq[:, c, th * 512:(th + 1) * 512],
                                           start=(c == 0), stop=(c == 7))
                return ins
            P.op('pe', mm, r=[('sq', c) for c in range(8)] + ['ones_b'], w=[('ssp', th)])
            P.op('act', lambda th=th: nc.scalar.activation(
                out=self.rstd[:, th * 512:(th + 1) * 512], in_=ssp[th][:], func=AF.Sqrt, scale=1.0 / D, bias=self.epsc[:, 0:1]),
                r=[('ssp', th), 'epsc'], w=[('rstd', th)])
            P.op('dve', lambda th=th: nc.vector.reciprocal(
                out=self.rstd[:, th * 512:(th + 1) * 512], in_=self.rstd[:, th * 512:(th + 1) * 512]),
                r=[('rstd', th)], w=[('rstd', th)])
        for c in range(8):
            tm = tmp[c % 2]
            P.op('dve', lambda c=c, tm=tm: nc.vector.tensor_tensor(out=tm[:], in0=self.xT[:, c, :], in1=self.rstd[:],
                                                                   op=ALU.mult),
                 r=[('xT', c), ('rstd', 0), ('rstd', 1)], w=[('ntmp', c % 2)])
            out_ap, wkeys = out_fn(c)
            bias = sh(c) if sh is not None else 0.0
            P.op('act', lambda c=c, tm=tm, out_ap=out_ap, bias=bias: nc.scalar.activation(
                out=out_ap, in_=tm[:], func=AF.Identity, scale=gm(c), bias=bias),
                r=[('ntmp', c % 2), 'gmsh'], w=wkeys)

    def alloc_w(self, es, n, elems):
        self.NWB = n
        self.wbuf = [self.sb(es, "wbuf%d" % i, [128, elems], BF16) for i in range(n)]
        self.wrr = 0

    def load_w(self, src_ap, view, key_extra=None):
        P, nc = self.P, self.nc
        i = self.wrr % self.NWB
        self.wrr += 1
        a, b = view
        dst = self.wbuf[i][:, 0:a * b].rearrange("p (a b) -> p a b", a=a)
        P.dma('pool', lambda: nc.gpsimd.dma_start(out=dst, in_=src_ap), w=[('wbuf', i)])
        return dst, ('wbuf', i)

    def phase_mix(self, l):
        P, nc = self.P, self.nc
        with ExitStack() as es:
            self.rmsnorm(es, lambda c: self.gm1[l][:, c:c + 1], lambda c: self.mod[l][:, c:c + 1],
                         lambda c: (self.hT[:, c, :], [('hT', c)]))
            if l == 0:
                self.dump('h0T', self.hT, [128, NCH, T], [('hT', c) for c in range(8)])
            P.flush()

    def phase_ffn(self, l):
        P, nc = self.P, self.nc
        with ExitStack() as es:
            self.alloc_w(es, 4, 4096)
            pre_w = []
            for half in range(2):
                src = self.ffn_up_d[l, :, half * DFF:half * DFF + 512].rearrange("(k p) n -> p k n", p=128)
                pre_w.append(self.load_w(src, (8, 512)))
            with ExitStack() as es2:
                self.rmsnorm(es2, lambda c: self.gm2[l][:, c:c + 1], lambda c: self.mod[l][:, 24 + c:25 + c],
                             lambda c: (self.hT[:, c, :], [('hT', c)]))
                P.flush()
            gT = self.sb(es, "gT", [128, NFF, T], BF16)
            cv = [self.sb(es, "cv%d" % i, [128, T], F32) for i in range(4)]
            sg = [self.sb(es, "sg%d" % i, [128, T], F32) for i in range(2)]
            ups = [self.ps(es, "ups%d" % i, [128, T]) for i in range(4)]
            cw = lambda nm, j: self.vcol('%s_%d' % (nm, l), j)
            for j in range(NFF):
                g_, jj_ = j // 4, j % 4
                if jj_ == 0 and g_ == 0:
                    wts = pre_w
                elif jj_ == 0:
                    ncol_ = 512 if g_ < 5 else 256
                    wts = []
                    for half in range(2):
                        c0 = half * DFF + g_ * 512
                        src = self.ffn_up_d[l, :, c0:c0 + ncol_].rearrange("(k p) n -> p k n", p=128)
                        wts.append(self.load_w(src, (8, ncol_)))
                for half in range(2):
                    wt, wkey = wts[half]
                    pi = (j % 2) * 2 + half
                    up = ups[pi]
                    cvb = cv[pi]
                    jj = half * NFF + j
                    for th in range(2):
                        def mm(wt=wt, up=up, th=th, jj_=jj_):
                            ins = None
                            for k in range(8):
                                ins = nc.tensor.matmul(up[:, th * 512:(th + 1) * 512], wt[:, k, jj_ * 128:(jj_ + 1) * 128],
                                                       self.hT[:, k, th * 512:(th + 1) * 512],
                                                       start=(k == 0), stop=(k == 7))
                            return ins
                        P.op('pe', mm, r=[wkey] + [('hT', k) for k in range(8)], w=[('ups', pi, th)])
                    ur = [('ups', pi, 0), ('ups', pi, 1)]
                    ck = ('cv', pi)
                    P.op('act', lambda up=up, cvb=cvb, jj=jj: nc.scalar.activation(
                        out=cvb[:], in_=up[:], func=AF.Identity, scale=cw('cw1', jj), bias=cw('cb', jj)),
                        r=ur + ['vecs'], w=[ck])
                    P.op('dve', lambda up=up, cvb=cvb, jj=jj: nc.vector.scalar_tensor_tensor(
                        out=cvb[:, 1:T], in0=up[:, 0:T - 1], scalar=cw('cw0', jj), in1=cvb[:, 1:T],
                        op0=ALU.mult, op1=ALU.add), r=ur + ['vecs', ck], w=[ck])
                    P.op('dve', lambda up=up, cvb=cvb, jj=jj: nc.vector.scalar_tensor_tensor(
                        out=cvb[:, 0:T - 1], in0=up[:, 1:T], scalar=cw('cw2', jj), in1=cvb[:, 0:T - 1],
                        op0=ALU.mult, op1=ALU.add), r=ur + ['vecs', ck], w=[ck])
                    P.op('dve', lambda up=up, cvb=cvb, jj=jj: nc.vector.scalar_tensor_tensor(
                        out=cvb[:, 256:T:256], in0=up[:, 255:T - 1:256], scalar=self.wk0[l][:, jj:jj + 1],
                        in1=cvb[:, 256:T:256], op0=ALU.mult, op1=ALU.add), r=ur + [('wk0', l), ck], w=[ck])
                    P.op('dve', lambda up=up, cvb=cvb, jj=jj: nc.vector.scalar_tensor_tensor(
                        out=cvb[:, 255:T - 1:256], in0=up[:, 256:T:256], scalar=self.wk2[l][:, jj:jj + 1],
                        in1=cvb[:, 255:T - 1:256], op0=ALU.mult, op1=ALU.add), r=ur + [('wk2', l), ck], w=[ck])
                pv = (j % 2) * 2
                sgb = sg[j % 2]
                P.op('act', lambda sgb=sgb, pv=pv: nc.scalar.activation(out=sgb[:], in_=cv[pv + 1][:], func=AF.Silu),
                     r=[('cv', pv + 1)], w=[('sg', j % 2)])
                P.op('dve', lambda sgb=sgb, pv=pv, j=j: nc.vector.tensor_tensor(
                    out=gT[:, j, :], in0=sgb[:], in1=cv[pv][:], op=ALU.mult),
                    r=[('sg', j % 2), ('cv', pv)], w=[('gT', j)])
            if l == 0:
                self.dump('gT0', gT, [128, NFF, T], [('gT', j) for j in range(NFF)])
            dps = [ups[0], ups[1]]
            for c in range(8):
                src = self.ffn_dn_d[l, :, c * 128:(c + 1) * 128].rearrange("(k p) n -> p k n", p=128)
                wt, wkey = self.load_w(src, (NFF, 128))
                for th in range(2):
                    pi = th
                    dp = ups[c % 2][:, th * 512:(th + 1) * 512]

                    def mm(wt=wt, dp=dp, th=th):
                        ins = None
                        for k in range(NFF):
                            ins = nc.tensor.matmul(dp, wt[:, k, :], gT[:, k, th * 512:(th + 1) * 512],
                                                   start=(k == 0), stop=(k == NFF - 1))
                        return ins
                    P.op('pe', mm, r=[wkey] + [('gT', k) for k in range(NFF)], w=[('ups', c % 2, th)])
                    P.op('dve', lambda dp=dp, c=c, th=th: nc.vector.scalar_tensor_tensor(
                        out=self.xT[:, c, th * 512:(th + 1) * 512], in0=dp, scalar=self.mod[l][:, 40 + c:41 + c],
                        in1=self.xT[:, c, th * 512:(th + 1) * 512], op0=ALU.mult, op1=ALU.add),
                        r=[('ups', c % 2, th), ('xT', c), ('mod', l)], w=[('xT', c)])
            P.flush()

    def phase_final(self):
        P, nc = self.P, self.nc
        with ExitStack() as es:
            yT = self.sb(es, "yT", [128, NCH, T], F32)
            self.rmsnorm(es, lambda c: self.vcol('fng', c), None, lambda c: (yT[:, c, :], [('yT', c)]))
            yo = [self.sb(es, "yo%d" % i, [128, D], F32) for i in range(2)]
            tp = [self.ps(es, "ftp%d" % i, [128, 512]) for i in range(2)]
            for tt in range(8):
                yb = yo[tt % 2]
                for g in range(2):
                    for cc in range(4):
                        c = g * 4 + cc
                        P.op('pe', lambda tt=tt, g=g, cc=cc, c=c: nc.tensor.transpose(
                            tp[g][:, cc * 128:(cc + 1) * 128], yT[:, c, tt * 128:(tt + 1) * 128], self.ident_f[:]),
                            r=[('yT', c), 'ident_f'], w=[('ftp', g)])
                    if g == 0:
                        P.op('act', lambda yb=yb, g=g: nc.scalar.copy(out=yb[:, g * 512:(g + 1) * 512], in_=tp[g][:]),
                             r=[('ftp', g)], w=[('yo', tt % 2, g)])
                    else:
                        P.op('dve', lambda yb=yb, g=g: nc.vector.tensor_copy(out=yb[:, g * 512:(g + 1) * 512], in_=tp[g][:]),
                             r=[('ftp', g)], w=[('yo', tt % 2, g)])
                P.dma('sp', lambda yb=yb, tt=tt: nc.sync.dma_start(out=self.y_d[tt * 128:(tt + 1) * 128, :], in_=yb[:]),
                      r=[('yo', tt % 2, 0), ('yo', tt % 2, 1)], w=[('y', tt)])
            P.flush()


def _even_mixer(self):
    P, nc = self.P, self.nc
    l = 0
    SC = 0.125
    with ExitStack() as es:
        qaT = self.sb(es, "qaT", [128, 4, T], BF16)
        kaT = self.sb(es, "kaT", [128, 4, 512 + T], BF16)
        qbT = self.sb(es, "qbT", [128, 4, T], BF16)
        kbT = self.sb(es, "kbT", [128, 512 + T], BF16)
        vA = self.sb(es, "vA", [128, 12, 4, 130], BF16)
        vB = self.sb(es, "vB", [128, 12, 2, 66], BF16)
        ocat = self.sb(es, "ocat", [128, 8, D], BF16)
        bvec = self.sb(es, "bvec", [128, 448], F32)
        nlam = self.sb(es, "nlam", [128, 1], F32)
        gsub8 = self.sb(es, "gsub8", [128, 128], F32)
        self.alloc_w(es, 4, 4096)
        pre_n = self.load_w(self.ev_w_in_d[:, 0:512].rearrange("(k p) n -> p k n", p=128), (8, 512))
        pre_s = self.load_w(self.ev_wx_d[:, 512:1024].rearrange("(k p) n -> p k n", p=128), (8, 512))
        with ExitStack() as e0:
            self.rmsnorm(e0, lambda c: self.gm1[l][:, c:c + 1], lambda c: self.mod[l][:, c:c + 1],
                         lambda c: (self.hT[:, c, :], [('hT', c)]))
            P.flush()
        with ExitStack() as e1:
            cosT = self.sb(e1, "cosT", [128, T], F32)
            sinT = self.sb(e1, "sinT", [128, T], F32)
            cak = self.sb(e1, "cak", [128, 4, 4, 128], BF16)
            cbk = self.sb(e1, "cbk", [128, 4, 128], BF16)
            t1 = [self.sb(e1, "rt1_%d" % i, [128, 512], F32) for i in range(2)]
            t2 = [self.sb(e1, "rt2_%d" % i, [128, 512], F32) for i in range(2)]
            sqb = self.sb(e1, "sqb", [128, 512], BF16)
            rsb = self.sb(e1, "rsb", [128, 512], F32)
            stg = [self.sb(e1, "stg%d" % i, [128, 512], F32) for i in range(2)]
            kbs = self.sb(e1, "kbs", [128, 128], F32)
            ssb = self.sb(e1, "ssb", [128, 2], F32)
            junk = self.sb(e1, "junk", [128, 128], F32)
            lpr = self.sb(e1, "lpr", [128, 2, 64], F32)
            lsum = self.sb(e1, "lsum", [128, 2], F32)
            pq = [self.ps(e1, "pq%d" % i, [128, 512]) for i in range(2)]
            pqs = [self.ps(e1, "pqs%d" % i, [128, 512]) for i in range(2)]
            pss = self.ps(e1, "pss", [128, 512])
            ptm = [self.ps(e1, "ptm%d" % i, [128, 512]) for i in range(2)]
            ptp = self.ps(e1, "ptp", [128, 1024], BF16)

            P.dma('sp', lambda: nc.sync.dma_start(out=cosT[:], in_=self.rope_d[0]), w=['cosT'])
            P.dma('sp', lambda: nc.sync.dma_start(out=sinT[:], in_=self.rope_d[1]), w=['sinT'])
            P.dma('sp', lambda: nc.sync.dma_start(out=bvec[:], in_=self.bvec_d), w=['bvec'])
            P.op('dve', lambda: nc.vector.tensor_tensor(
                out=lpr[:], in0=bvec[:, 192:448].rearrange("p (a b e) -> p a b e", a=2, b=2)[:, :, 0, :],
                in1=bvec[:, 192:448].rearrange("p (a b e) -> p a b e", a=2, b=2)[:, :, 1, :], op=ALU.mult),
                r=['bvec'], w=['lpr'])
            P.op('dve', lambda: nc.vector.reduce_sum(out=lsum[:], in_=lpr[:], axis=AX.X), r=['lpr'], w=['lsum'])
            P.op('act', lambda: nc.scalar.activation(out=lsum[:], in_=lsum[:], func=AF.Exp), r=['lsum'], w=['lsum'])
            P.op('dve', lambda: nc.vector.tensor_tensor(out=nlam[:], in0=lsum[:, 1:2], in1=lsum[:, 0:1], op=ALU.subtract),
                 r=['lsum'], w=['nlam'])
            P.op('dve', lambda: nc.vector.tensor_scalar_add(out=nlam[:], in0=nlam[:], scalar1=-0.2), r=['nlam'], w=['nlam'])
            P.op('dve', lambda: nc.vector.tensor_scalar_mul(out=gsub8[:], in0=bvec[:, 0:128], scalar1=0.8),
                 r=['bvec'], w=['gsub8'])
            P.op('pool', lambda: nc.gpsimd.memset(vA[:, :, :, 128:129], 1.0), w=['vA_ones'])
            P.op('pool', lambda: nc.gpsimd.memset(vB[:, :, :, 64:65], 1.0), w=['vB_ones'])
            if self.stop == 'B1l':
                P.flush(); return
            for kt in range(4):
                P.dma('pool', lambda kt=kt: nc.gpsimd.dma_start(
                    out=cak[:, kt], in_=self.ctx_ak_d[:, kt * 128:(kt + 1) * 128, :].rearrange("h p e -> p h e")),
                    w=[('cak', kt)])
                P.dma('pool', lambda kt=kt: nc.gpsimd.dma_start(
                    out=vA[:, kt, :, 0:128], in_=self.ctx_av_d[:, kt * 128:(kt + 1) * 128, :].rearrange("h p e -> p h e")),
                    w=[('vA', kt)])
                P.dma('pool', lambda kt=kt: nc.gpsimd.dma_start(
                    out=cbk[:, kt, :].rearrange("p (h e) -> p h e", h=2),
                    in_=self.ctx_bk_d[:, kt * 128:(kt + 1) * 128, :].rearrange("h p e -> p h e")), w=[('cbk', kt)])
                P.dma('pool', lambda kt=kt: nc.gpsimd.dma_start(
                    out=vB[:, kt, :, 0:64], in_=self.ctx_bv_d[:, kt * 128:(kt + 1) * 128, :].rearrange("h p e -> p h e")),
                    w=[('vB', kt)])
            for h in range(5):
                for kt in range(4):
                    src = cak[:, kt, h, :] if h < 4 else cbk[:, kt, :]
                    P.op('pe', lambda src=src, kt=kt: nc.tensor.transpose(ptp[:, kt * 128:(kt + 1) * 128], src, self.ident_b[:]),
                         r=[('cak', kt), ('cbk', kt), 'ident_b'], w=['ptp'])
                dst = kaT[:, h, 0:512] if h < 4 else kbT[:, 0:512]
                P.op('dve', lambda dst=dst: nc.vector.tensor_copy(out=dst, in_=ptp[:, 0:512]), r=['ptp'],
                     w=[('kaTc', h)])
            if self.stop == 'B1c':
                P.flush(); return
            chunks = []
            for a in range(4):
                chunks.append((lambda th, a=a: qaT[:, a, th * 512:(th + 1) * 512], (self.ev_w_in_d, a * 128),
                               (self.ev_wx_d, 512 + a * 128), None, ('qaT', a)))
            for a in range(4):
                chunks.append((lambda th, a=a: kaT[:, a, 512 + th * 512:512 + (th + 1) * 512], (self.ev_w_in_d, 512 + a * 128),
                               (self.ev_wx_d, 1024 + a * 128), None, ('kaT', a)))
            for c in range(4):
                chunks.append((lambda th, c=c: qbT[:, c, th * 512:(th + 1) * 512], (self.ev_wx_d, c * 128),
                               (self.ev_wx_d, 1536 + c * 128), ('gq', 'gq_sw'), ('qbT', c)))
            chunks.append((lambda th: kbT[:, 512 + th * 512:512 + (th + 1) * 512], (self.ev_w_in_d, 2048),
                           (self.ev_wx_d, 2048), ('gk', 'gk_sw'), ('kbT',)))
            it = 0
            for ci_, (dst_fn, (wd, c0), (wsd, cs0), gn, dkey) in enumerate(chunks):
                if ci_ == 0:
                    (wn_, wnk), (ws_, wsk) = pre_n, pre_s
                elif ci_ % 4 == 0:
                    nb_ = 512 if ci_ < 12 else 128
                    wn_, wnk = self.load_w(wd[:, c0:c0 + nb_].rearrange("(k p) n -> p k n", p=128), (8, nb_))
                    ws_, wsk = self.load_w(wsd[:, cs0:cs0 + nb_].rearrange("(k p) n -> p k n", p=128), (8, nb_))
                wo_ = (ci_ % 4) * 128
                wn = wn_[:, :, wo_:wo_ + 128]
                ws = ws_[:, :, wo_:wo_ + 128]
                for th in range(2):
                    b = it % 2
                    it += 1
                    for (pp, ww, wk_, nm) in ((pq[b], wn, wnk, 'pq'), (pqs[b], ws, wsk, 'pqs')):
                        def mm(pp=pp, ww=ww, th=th):
                            ins = None
                            for k in range(8):
                                ins = nc.tensor.matmul(pp[:], ww[:, k, :], self.hT[:, k, th * 512:(th + 1) * 512],
                                                       start=(k == 0), stop=(k == 7))
                            return ins
                        P.op('pe', mm, r=[wk_] + [('hT', k) for k in range(8)], w=[(nm, b)])
                    dst = dst_fn(th)
                    if gn is None:
                        P.op('dve', lambda b=b, th=th: nc.vector.tensor_tensor(
                            out=t1[b][:], in0=pq[b][:], in1=cosT[:, th * 512:(th + 1) * 512], op=ALU.mult),
                            r=[('pq', b), 'cosT'], w=[('t1', b)])
                        P.op('dve', lambda b=b, th=th: nc.vector.tensor_tensor(
                            out=t2[b][:], in0=pqs[b][:], in1=sinT[:, th * 512:(th + 1) * 512], op=ALU.mult),
                            r=[('pqs', b), 'sinT'], w=[('t2', b)])
                        P.op('dve', lambda b=b, dst=dst: nc.vector.tensor_tensor(out=dst, in0=t1[b][:], in1=t2[b][:], op=ALU.add),
                             r=[('t1', b), ('t2', b)], w=[dkey + (th,)])
                    else:
                        P.op('act', lambda b=b: nc.scalar.activation(out=sqb[:], in_=pq[b][:], func=AF.Square),
                             r=[('pq', b)], w=['sqb'])
                        P.op('pe', lambda: nc.tensor.matmul(pss[:], self.bd_ones[:], sqb[:], start=True, stop=True),
                             r=['sqb', 'bd_ones'], w=['pss'])
                        P.op('act', lambda: nc.scalar.activation(out=rsb[:], in_=pss[:], func=AF.Ln, scale=1.0 / 64,
                                                                 bias=self.epsc[:, 0:1]), r=['pss', 'epsc'], w=['rsb'])
                        P.op('act', lambda: nc.scalar.activation(out=rsb[:], in_=rsb[:], func=AF.Exp, scale=-0.5),
                             r=['rsb'], w=['rsb'])
                        P.op('dve', lambda b=b, gn=gn: nc.vector.scalar_tensor_tensor(
                            out=t1[b][:], in0=pq[b][:], scalar=self.vcol(gn[0]), in1=rsb[:], op0=ALU.mult, op1=ALU.mult),
                            r=[('pq', b), 'rsb', 'vecs'], w=[('t1', b)])
                        P.op('dve', lambda b=b, gn=gn: nc.vector.scalar_tensor_tensor(
                            out=t2[b][:], in0=pqs[b][:], scalar=self.vcol(gn[1]), in1=rsb[:], op0=ALU.mult, op1=ALU.mult),
                            r=[('pqs', b), 'rsb', 'vecs'], w=[('t2', b)])
                        P.op('dve', lambda b=b, th=th: nc.vector.tensor_tensor(
                            out=t1[b][:], in0=t1[b][:], in1=cosT[:, th * 512:(th + 1) * 512], op=ALU.mult),
                            r=[('t1', b), 'cosT'], w=[('t1', b)])
                        P.op('dve', lambda b=b, th=th: nc.vector.tensor_tensor(
                            out=t2[b][:], in0=t2[b][:], in1=sinT[:, th * 512:(th + 1) * 512], op=ALU.mult),
                            r=[('t2', b), 'sinT'], w=[('t2', b)])
                        P.op('dve', lambda b=b, dst=dst: nc.vector.tensor_tensor(out=dst, in0=t1[b][:], in1=t2[b][:], op=ALU.add),
                             r=[('t1', b), ('t2', b)], w=[dkey + (th,)])
            if self.stop == 'B1r':
                P.flush(); return
            wkv = []
            for (c0, n) in ((512, 512), (1024, 512), (2048, 256)):
                wkv.append(self.load_w(self.ev_w_in_d[:, c0:c0 + n].rearrange("(k p) n -> p k n", p=128), (8, n)) + (n,))
            si = 0
            for tt in range(8):
                seg, r0 = tt // 2, (tt % 2) * 128
                for bi, (wt, wkey, n) in enumerate(wkv):
                    pm = ptm[(tt * 3 + bi) % 2]
                    pk = ('ptm', (tt * 3 + bi) % 2)

                    def mm(wt=wt, pm=pm, n=n, tt=tt):
                        ins = None
                        for k in range(8):
                            ins = nc.tensor.matmul(pm[:, 0:n], self.hT[:, k, tt * 128:(tt + 1) * 128], wt[:, k, :],
                                                   start=(k == 0), stop=(k == 7))
                        return ins
                    P.op('pe', mm, r=[wkey] + [('hT', k) for k in range(8)], w=[pk])
                    if 'tm_evac' in self.skip: continue
                    if 'tm_evac2' in self.skip and bi == 2: continue
                    if bi < 2:
                        sg_ = stg[si % 2]
                        sk = ('stg', si % 2)
                        si += 1
                        P.op('act', lambda sg_=sg_, pm=pm: nc.scalar.copy(out=sg_[:], in_=pm[:]), r=[pk], w=[sk])
                        od = self.o_ak_d if bi == 0 else self.o_av_d
                        if 'outdma' not in self.skip: P.dma('sp', lambda sg_=sg_, od=od, seg=seg, r0=r0: nc.sync.dma_start(
                            out=od[seg, :, r0:r0 + 128, :].rearrange("h p e -> p h e"),
                            in_=sg_[:].rearrange("p (h e) -> p h e", h=4)), r=[sk], w=[('ocache', bi, tt)])
                        if bi == 1 and 'vcopy' not in self.skip:
                            P.op('dve', lambda pm=pm, tt=tt: nc.vector.tensor_copy(
                                out=vA[:, 4 + tt, :, 0:128], in_=pm[:].rearrange("p (h e) -> p h e", h=4)),
                                r=[pk], w=[('vA', 4 + tt)])
                    else:
                        sg_ = stg[si % 2]
                        sk = ('stg', si % 2)
                        si += 1
                        P.op('dve', lambda pm=pm, tt=tt: nc.vector.tensor_copy(
                            out=vB[:, 4 + tt, :, 0:64], in_=pm[:, 128:256].rearrange("p (h e) -> p h e", h=2)),
                            r=[pk], w=[('vB', 4 + tt)])
                        P.op('act', lambda sg_=sg_, pm=pm: nc.scalar.copy(out=sg_[:, 128:256], in_=pm[:, 128:256]), r=[pk], w=[sk])
                        P.op('act', lambda pm=pm: nc.scalar.copy(out=kbs[:], in_=pm[:, 0:128]), r=[pk], w=['kbs'])
                        for h in range(2):
                            if 'accum' in self.skip: continue
                            P.op('dve', lambda h=h: nc.vector.scalar_tensor_tensor(
                                out=junk[:, 0:64], in0=kbs[:, h * 64:(h + 1) * 64], scalar=1.0, in1=kbs[:, h * 64:(h + 1) * 64],
                                op0=ALU.mult, op1=ALU.mult, accum_out=ssb[:, h:h + 1]), r=['kbs'], w=['junk', ('ssb', h)])
                        P.op('act', lambda: nc.scalar.activation(out=ssb[:], in_=ssb[:], func=AF.Ln, scale=1.0 / 64,
                                                                 bias=self.epsc[:, 0:1]),
                             r=[('ssb', 0), ('ssb', 1), 'epsc'], w=[('ssb', 0), ('ssb', 1)])
                        P.op('act', lambda: nc.scalar.activation(out=ssb[:], in_=ssb[:], func=AF.Exp, scale=-0.5),
                             r=[('ssb', 0), ('ssb', 1)], w=[('ssb', 0), ('ssb', 1)])
                        for h in range(2):
                            P.op('dve', lambda h=h, sg_=sg_: nc.vector.scalar_tensor_tensor(
                                out=sg_[:, h * 64:(h + 1) * 64], in0=kbs[:, h * 64:(h + 1) * 64], scalar=ssb[:, h:h + 1],
                                in1=bvec[:, 128:192], op0=ALU.mult, op1=ALU.mult),
                                r=['kbs', ('ssb', h), 'bvec'], w=[sk])
                        if 'outdma2' not in self.skip: P.dma('sp', lambda sg_=sg_, seg=seg, r0=r0: nc.sync.dma_start(
                            out=self.o_bk_d[seg, :, r0:r0 + 128, :].rearrange("h p e -> p h e"),
                            in_=sg_[:, 0:128].rearrange("p (h e) -> p h e", h=2)), r=[sk], w=[('ocache', 2, tt)])
                        if 'outdma2' not in self.skip: P.dma('sp', lambda sg_=sg_, seg=seg, r0=r0: nc.sync.dma_start(
                            out=self.o_bv_d[seg, :, r0:r0 + 128, :].rearrange("h p e -> p h e"),
                            in_=sg_[:, 128:256].rearrange("p (h e) -> p h e", h=2)), r=[sk], w=[('ocache', 3, tt)])
            if self.stop == 'B1a':
                P.flush(); return
            self.dump('qaT', qaT, [128, 4, T], [])
            self.dump('kaT', kaT, [128, 4, 512 + T], [])
            self.dump('qbT', qbT, [128, 4, T], [])
            self.dump('kbT', kbT, [128, 512 + T], [])
            P.flush()
        if self.stop == 'B1':
            return
        with ExitStack() as e2:
            NPT = 6
            pt = [self.sb(e2, "pt%d" % i, [128, 256], BF16) for i in range(NPT)]
            dbuf = self.sb(e2, "dbuf", [128, 32, 128], F32)
            ssq = self.sb(e2, "ssq", [128, 32], F32)
            a1b = [self.sb(e2, "a1b%d" % i, [128, 128], F32) for i in range(2)]
            rr = self.sb(e2, "rr", [128, 8], F32)
            junk2 = self.sb(e2, "junk2", [128, 128], F32)
            sT = [self.ps(e2, "sT%d" % i, [128, 512]) for i in range(4)]
            acc = [self.ps(e2, "acc%d" % i, [128, 512]) for i in range(4)]
            steps = []
            g = 0
            for a in range(4):
                for s_ in range(4):
                    for kb in range(12):
                        for comp in range(2):
                            steps.append(dict(kind='A', a=a, s=s_, comp=comp, kb=kb, g=g + comp, last=(kb == 11)))
                    g += 2
            for c_ in range(4):
                for s_ in range(4):
                    for kb in range(12):
                        for hi in range(2):
                            steps.append(dict(kind='B', h=c_ + 4 * hi, s=s_, kb=kb, g=g + hi, last=(kb == 11)))
                    g += 2
            LA = 2
            nst = len(steps)

            def emit_S(i, st):
                slot = i % 4
                dstp = sT[slot][:, 0:256]
                s_, kb = st['s'], st['kb']
                if st['kind'] == 'A':
                    a, comp = st['a'], st['comp']
                    lhsT = kaT[comp * 64:(comp + 1) * 64, a, kb * 128:(kb + 1) * 128]
                    rhs = qaT[comp * 64:(comp + 1) * 64, a, s_ * 256:(s_ + 1) * 256]
                else:
                    h = st['h']
                    gk = h // 4
                    lhsT = kbT[gk * 64:(gk + 1) * 64, kb * 128:(kb + 1) * 128]
                    rhs = qbT[gk * 64:(gk + 1) * 64, h % 4, s_ * 256:(s_ + 1) * 256]
                P.op('pe', lambda: nc.tensor.matmul(dstp, lhsT, rhs, start=True, stop=True), r=[], w=[('sT', slot)])
                ptb = pt[i % NPT]
                col = s_ * 12 + kb
                P.op('act', lambda: nc.scalar.activation(out=ptb[:], in_=dstp, func=AF.Exp, scale=SC,
                                                         bias=self.vcol('amask', col)),
                     r=[('sT', slot)], w=[('pt', i % NPT)])

            def emit_PV(i, st):
                ptb = pt[i % NPT]
                kb = st['kb']
                ab = acc[st['g'] % 4]
                if st['kind'] == 'A':
                    rhs = vA[:, kb, st["a"], 0:129]
                    n = 129
                else:
                    rhs = vB[:, kb, st["h"] // 4, 0:65]
                    n = 65
                for qh in range(2):
                    P.op('pe', lambda qh=qh: nc.tensor.matmul(ab[:, qh * 256:qh * 256 + n], ptb[:, qh * 128:(qh + 1) * 128], rhs,
                                                              start=(kb == 0 and qh == 0), stop=(kb == 11),
                                                              skip_group_check=True),
                         r=[('pt', i % NPT)], w=[('acc', st['g'] % 4)])
                if not st['last']:
                    return
                s_ = st['s']
                if st['kind'] == 'A':
                    if st['comp'] == 0:
                        return
                    a = st['a']
                    ab0, ab1 = acc[(st['g'] - 1) % 4], acc[st['g'] % 4]
                    k0, k1 = ('acc', (st['g'] - 1) % 4), ('acc', st['g'] % 4)
                    for qh in range(2):
                        u = (a * 4 + s_) * 2 + qh
                        o0 = qh * 256
                        P.op('dve', lambda o0=o0: nc.vector.reciprocal(out=rr[:, 0:1], in_=ab0[:, o0 + 128:o0 + 129]), r=[k0], w=['rr0'])
                        P.op('dve', lambda o0=o0: nc.vector.reciprocal(out=rr[:, 1:2], in_=ab1[:, o0 + 128:o0 + 129]), r=[k1], w=['rr1'])
                        P.op('dve', lambda: nc.vector.tensor_tensor(out=rr[:, 2:3], in0=rr[:, 1:2], in1=nlam[:], op=ALU.mult),
                             r=['rr1'], w=['rr2'])
                        a1 = a1b[u % 2]
                        P.op('dve', lambda o0=o0, a1=a1: nc.vector.tensor_scalar_mul(out=a1[:], in0=ab0[:, o0:o0 + 128], scalar1=rr[:, 0:1]),
                             r=[k0, 'rr0'], w=[('a1b', u % 2)])
                        P.op('dve', lambda o0=o0, a1=a1, u=u: nc.vector.scalar_tensor_tensor(
                            out=dbuf[:, u, :], in0=ab1[:, o0:o0 + 128], scalar=rr[:, 2:3], in1=a1[:], op0=ALU.mult, op1=ALU.add),
                            r=[k1, 'rr2', ('a1b', u % 2)], w=[('dbuf', u)])
                        P.op('dve', lambda u=u: nc.vector.scalar_tensor_tensor(
                            out=junk2[:], in0=dbuf[:, u, :], scalar=1.0, in1=dbuf[:, u, :], op0=ALU.mult, op1=ALU.mult,
                            accum_out=ssq[:, u:u + 1]), r=[('dbuf', u)], w=['junk2', ('ssq', u)])
                else:
                    h = st['h']
                    ab0 = acc[st['g'] % 4]
                    k0 = ('acc', st['g'] % 4)
                    for qh in range(2):
                        o0 = qh * 256
                        qt = s_ * 2 + qh
                        P.op('dve', lambda o0=o0: nc.vector.reciprocal(out=rr[:, 4:5], in_=ab0[:, o0 + 64:o0 + 65]), r=[k0], w=['rr4'])
                        P.op('dve', lambda o0=o0, qt=qt, h=h: nc.vector.tensor_scalar_mul(
                            out=ocat[:, qt, 512 + h * 64:512 + (h + 1) * 64], in0=ab0[:, o0:o0 + 64], scalar1=rr[:, 4:5]),
                            r=[k0, 'rr4'], w=[('ocat', qt, 4 + h // 2)])

            for j in range(nst // 2 + 1):
                if 2 * j < nst:
                    emit_S(2 * j, steps[2 * j])
                    emit_S(2 * j + 1, steps[2 * j + 1])
                if j >= 1:
                    emit_PV(2 * j - 2, steps[2 * j - 2])
                    emit_PV(2 * j - 1, steps[2 * j - 1])
            P.op('act', lambda: nc.scalar.activation(out=ssq[:], in_=ssq[:], func=AF.Ln, scale=1.0 / 128, bias=self.epsc[:, 0:1]),
                 r=[('ssq', u) for u in range(32)], w=['rstdA'])
            P.op('act', lambda: nc.scalar.activation(out=ssq[:], in_=ssq[:], func=AF.Exp, scale=-0.5), r=['rstdA'], w=['rstdA'])
            for a in range(4):
                for s_ in range(4):
                    for qh in range(2):
                        u = (a * 4 + s_) * 2 + qh
                        qt = s_ * 2 + qh
                        eng = 'dve'
                        E = nc.vector
                        P.op(eng, lambda E=E, u=u, qt=qt, a=a: E.scalar_tensor_tensor(
                            out=ocat[:, qt, a * 128:(a + 1) * 128], in0=dbuf[:, u, :], scalar=ssq[:, u:u + 1], in1=gsub8[:],
                            op0=ALU.mult, op1=ALU.mult), r=[('dbuf', u), 'rstdA', 'gsub8'], w=[('ocat', qt, a)])
            self.dump('ocat', ocat, [128, 8, D], [])
            P.flush()
        if self.stop == 'B2':
            return
        with ExitStack() as e3:
            self.alloc_w(e3, 2, 4096)
            ptp = [self.ps(e3, "otp%d" % i, [128, 1024], BF16) for i in range(2)]
            pmx = [self.ps(e3, "pmx%d" % i, [128, 512]) for i in range(2)]
            n = 0
            for c in range(8):
                for gq in range(2):
                    pp = ptp[n % 2]
                    for j in range(4):
                        qt = gq * 4 + j
                        P.op('pe', lambda pp=pp, j=j, qt=qt, c=c: nc.tensor.transpose(
                            pp[:, j * 128:(j + 1) * 128], ocat[:, qt, c * 128:(c + 1) * 128], self.ident_b[:]),
                            r=[], w=[('otp', n % 2)])
                    if n % 2 == 0:
                        P.op('dve', lambda pp=pp, c=c, gq=gq: nc.vector.tensor_copy(out=self.hT[:, c, gq * 512:(gq + 1) * 512], in_=pp[:, 0:512]),
                             r=[('otp', n % 2)], w=[('hT', c)])
                    else:
                        P.op('act', lambda pp=pp, c=c, gq=gq: nc.scalar.copy(out=self.hT[:, c, gq * 512:(gq + 1) * 512], in_=pp[:, 0:512]),
                             r=[('otp', n % 2)], w=[('hT', c)])
                    n += 1
            self.out_proj(self.ev_w_out_d, pmx, l)
            self.dump('xm0', self.xT, [128, NCH, T], [('xT', c) for c in range(8)])
            P.flush()


def _out_proj(self, w_d, pmx, l):
    P, nc = self.P, self.nc
    n = 0
    for c in range(8):
        if c % 4 == 0:
            wt, wkey = self.load_w(w_d[:, c * 128:c * 128 + 512].rearrange("(k p) n -> p k n", p=128), (8, 512))
        co = (c % 4) * 128
        for th in range(2):
            pm = pmx[n % 2]
            pk = ('pmx', n % 2)
            n += 1

            def mm(wt=wt, pm=pm, th=th, co=co):
                ins = None
                for k in range(8):
                    ins = nc.tensor.matmul(pm[:], wt[:, k, co:co + 128], self.hT[:, k, th * 512:(th + 1) * 512],
                                           start=(k == 0), stop=(k == 7))
                return ins
            P.op('pe', mm, r=[wkey] + [('hT', k) for k in range(8)], w=[pk])
            P.op('dve', lambda pm=pm, c=c, th=th: nc.vector.scalar_tensor_tensor(
                out=self.xT[:, c, th * 512:(th + 1) * 512], in0=pm[:], scalar=self.mod[l][:, 16 + c:17 + c],
                in1=self.xT[:, c, th * 512:(th + 1) * 512], op0=ALU.mult, op1=ALU.add),
                r=[pk, ('xT', c)], w=[('xT', c)])


Builder.even_mixer = _even_mixer
Builder.out_proj = _out_proj


class Em:
    def __init__(self, B):
        self.B, self.P, self.nc = B, B.P, B.nc

    def V(self, eng):
        return self.nc.vector if eng == 'dve' else self.nc.gpsimd

    def tt(self, eng, out, in0, in1, op, r, w):
        self.P.op(eng, lambda: self.V(eng).tensor_tensor(out=out, in0=in0, in1=in1, op=op), r=r, w=w)

    def ts(self, eng, out, in0, s1, s2, op0, op1, r, w):
        if s2 is None:
            self.P.op(eng, lambda: self.V(eng).tensor_scalar(out=out, in0=in0, scalar1=s1, scalar2=None, op0=op0), r=r, w=w)
        else:
            self.P.op(eng, lambda: self.V(eng).tensor_scalar(out=out, in0=in0, scalar1=s1, scalar2=s2, op0=op0, op1=op1), r=r, w=w)

    def stt(self, out, in0, scalar, in1, op0, op1, r, w, accum=None):
        if accum is None:
            self.P.op('dve', lambda: self.nc.vector.scalar_tensor_tensor(out=out, in0=in0, scalar=scalar, in1=in1, op0=op0, op1=op1), r=r, w=w)
        else:
            self.P.op('dve', lambda: self.nc.vector.scalar_tensor_tensor(out=out, in0=in0, scalar=scalar, in1=in1, op0=op0, op1=op1,
                                                                         accum_out=accum), r=r, w=w)

    def act(self, out, in_, func, r, w, scale=1.0, bias=0.0):
        self.P.op('act', lambda: self.nc.scalar.activation(out=out, in_=in_, func=func, scale=scale, bias=bias), r=r, w=w)

    def cp(self, eng, out, in_, r, w):
        if eng == 'act':
            self.P.op('act', lambda: self.nc.scalar.copy(out=out, in_=in_), r=r, w=w)
        else:
            self.P.op(eng, lambda: self.V(eng).tensor_copy(out=out, in_=in_), r=r, w=w)

    def mm(self, out, lhsT, rhs, r, w, start=True, stop=True, sgc=False):
        if sgc:
            self.P.op('pe', lambda: self.nc.tensor.matmul(out, lhsT, rhs, start=start, stop=stop, skip_group_check=True), r=r, w=w)
        else:
            self.P.op('pe', lambda: self.nc.tensor.matmul(out, lhsT, rhs, start=start, stop=stop), r=r, w=w)

    def tr(self, out, in_, ident, r, w):
        self.P.op('pe', lambda: self.nc.tensor.transpose(out, in_, ident), r=r, w=w)

    def dma(self, q, out, in_, r, w):
        E = self.nc.sync if q == 'sp' else self.nc.gpsimd
        self.P.dma(q, lambda: E.dma_start(out=out, in_=in_), r=r, w=w)

    def memset(self, eng, ap, val, w):
        self.P.op(eng, lambda: self.V(eng).memset(ap, val), w=w)


HG0 = 0
RW0 = 2560
LWS = -0.6065306597126334
GN_EPS = 64e-5


def _proj_fm(self, em, w_d, c0, ncols, pp, pkey, pkeys=None):
    nc = self.nc
    wt, wkey = self.load_w(w_d[:, c0:c0 + ncols].rearrange("(k p) n -> p k n", p=128), (8, ncols))
    for th in range(2):
        def mm(wt=wt, th=th):
            ins = None
            for k in range(8):
                ins = nc.tensor.matmul(pp[0:ncols, th * 512:(th + 1) * 512], wt[:, k, :], self.hT[:, k, th * 512:(th + 1) * 512],
                                       start=(k == 0), stop=(k == 7))
            return ins
        self.P.op('pe', mm, r=[wkey] + [('hT', k) for k in range(8)], w=[pkeys[th] if pkeys else (pkey, th)])


def _decay(self, em, lw, Gp, D, rev, key):
    nc = self.nc
    self.P.op('dve', lambda: nc.vector.tensor_tensor_scan(out=Gp[:, 1:T + 1], data0=self.onesT[:], data1=lw, initial=0.0,
                                                          op0=ALU.mult, op1=ALU.add), r=[key + '_lw', 'onesT'], w=[key + '_Gp'])
    v3 = lambda ap: ap.rearrange("p (c l) -> p c l", l=64)
    if not rev:
        em.tt('dve', v3(D), v3(Gp[:, 1:T + 1]), Gp[:, 0:T:64].unsqueeze(2).broadcast_to([128, 16, 64]), ALU.subtract,
              r=[key + '_Gp'], w=[key + '_D'])
    else:
        em.tt('dve', v3(D), Gp[:, 64:T + 1:64].unsqueeze(2).broadcast_to([128, 16, 64]), v3(Gp[:, 0:T]), ALU.subtract,
              r=[key + '_Gp'], w=[key + '_D'])


def _odd_mixer(self):
    P, nc = self.P, self.nc
    em = Em(self)
    l = 1
    wd = self.od_w_in_d
    with ExitStack() as es:
        ocat = self.sb(es, "ocat1", [128, 8, D], BF16)
        self.onesT = self.sb(es, "onesT", [128, T], F32)
        masks = self.sb(es, "masks", [128, 4, 128], F32)
        bv1 = self.sb(es, "bv1", [128, 1280], F32)
        with ExitStack() as e0:
            self.rmsnorm(e0, lambda c: self.gm1[l][:, c:c + 1], lambda c: self.mod[l][:, c:c + 1],
                         lambda c: (self.hT[:, c, :], [('hT', c)]))
            em.memset('pool', self.onesT[:], 1.0, ['onesT'])
            em.dma('sp', masks[:], self.masks_d, [], ['masks'])
            em.dma('sp', bv1[:], self.bv1_d, [], ['bv1'])
            P.flush()
        self.dump('h1T', self.hT, [128, NCH, T], [])
        with ExitStack() as e1:
            self.alloc_w(e1, 3, 4096)
            osum = self.sb(e1, "osum_h", [128, 8, 512], F32)
            vtok = self.sb(e1, "vtok_h", [128, 8, 512], BF16)
            gsil = self.sb(e1, "gsil", [128, 8, 512], BF16)
            lbv = self.sb(e1, "lbv", [128, 8], F32)
            omlb = self.sb(e1, "omlb", [128, 8], F32)
            ssh = self.sb(e1, "ssh", [128, 32], F32)
            junk = self.sb(e1, "junkh", [128, 128], F32)
            HB = []
            for s_ in range(2):
                hb = {}
                for nm in ('fl', 'kf', 'lg', 'qs'):
                    hb[nm] = self.sb(e1, "h%s%d" % (nm, s_), [128, T], F32)
                hb['Gp'] = self.sb(e1, "hGp%d" % s_, [128, T + 1], F32)
                for nm in ('qt', 'qA', 'qB', 'ktl'):
                    hb[nm] = self.sb(e1, "h%s%d" % (nm, s_), [128, T], BF16)
                hb['Am'] = [self.sb(e1, "hAm%d_%d" % (s_, i), [128, 128], BF16) for i in range(2)]
                hb['ktok'] = [self.sb(e1, "hktok%d_%d" % (s_, i), [128, 128], BF16) for i in range(2)]
                hb['S'] = self.sb(e1, "hS%d" % s_, [128, 128], F32)
                hb['Stmp'] = self.sb(e1, "hStmp%d" % s_, [128, 128], F32)
                hb['Sb'] = [self.sb(e1, "hSb%d_%d" % (s_, i), [128, 128], BF16) for i in range(2)]
                hb['hbA'] = self.ps(e1, "hbA%d" % s_, [128, T])
                hb['hbB'] = self.ps(e1, "hbB%d" % s_, [128, T])
                HB.append(hb)
            ptk = [HB[0]['hbB'][:, 0:512], HB[0]['hbB'][:, 512:1024]]
            P.excl.update(['hbA', 'hbB'])
            P.alias.update({('ptk', 0): ('hbB', 0, 0), ('ptk', 1): ('hbB', 0, 1)})
            for s_ in range(2):
                em.memset('pool', HB[s_]['qA'][:], 0.0, [('h', s_, 'qA')])
                em.memset('pool', HB[s_]['qB'][:], 0.0, [('h', s_, 'qB')])
            em.tt('dve', lbv[:], self.vcol('lb1', 0, 8), self.vcol('lb0', 0, 8), ALU.subtract, r=[], w=['lbv'])
            em.act(lbv[:], lbv[:], AF.Sigmoid, r=['lbv'], w=['lbv'])
            em.ts('dve', omlb[:], lbv[:], -1.0, 1.0, ALU.mult, ALU.add, r=['lbv'], w=['omlb'])
            wv = self.load_w(wd[:, 1536:2048].rearrange("(k p) n -> p k n", p=128), (8, 512))
            wg = self.load_w(wd[:, 2048:2560].rearrange("(k p) n -> p k n", p=128), (8, 512))
            for tt in range(8):
                for bi, (wt, wkey) in enumerate((wv, wg)):
                    pm = ptk[bi]

                    def mm(wt=wt, pm=pm, tt=tt):
                        ins = None
                        for k in range(8):
                            ins = nc.tensor.matmul(pm[:], self.hT[:, k, tt * 128:(tt + 1) * 128], wt[:, k, :], start=(k == 0), stop=(k == 7))
                        return ins
                    P.op('pe', mm, r=[wkey], w=[('ptk', bi)])
                    if bi == 0:
                        em.cp('dve', vtok[:, tt, :], pm[:], r=[('ptk', bi)], w=[('vtok', tt)])
                    else:
                        em.act(gsil[:, tt, :], pm[:], AF.Silu, r=[('ptk', bi)], w=[('gsil', tt)])
            def hchain(slot, dr, hc):
                rev = dr == 1
                mask = masks[:, 1 if rev else 0, :]
                ci = dr * 4 + hc
                Kk = lambda n: ('h', slot, n)
                hb = HB[slot]
                fl, kf, lg, Gp, qsc = hb['fl'], hb['kf'], hb['lg'], hb['Gp'], hb['qs']
                qt, qA, qB, ktl = hb['qt'], hb['qA'], hb['qB'], hb['ktl']
                Am, ktok, S, Stmp, Sb = hb['Am'], hb['ktok'], hb['S'], hb['Stmp'], hb['Sb']
                hbA, hbB = hb['hbA'], hb['hbB']
                kA0, kA1, kB0, kB1 = ('hbA', slot, 0), ('hbA', slot, 1), ('hbB', slot, 0), ('hbB', slot, 1)
                Dd = lg
                enD = fl
                eD = Gp
                psc = hbA[:, 0:128]
                pktr = hbA[:, 512:1024].bitcast(BF16)[:, 0:128]
                po = hbB[:, 0:128]
                pds = hbB[:, 512:640]
                _proj_fm(self, em, wd, hc * 128, 128, hbA, None, pkeys=[kA0, kA1])
                em.act(qsc[:], hbA[:], AF.Silu, r=[kA0, kA1], w=[Kk('qs')])
                yield
                _proj_fm(self, em, wd, 512 + dr * 512 + hc * 128, 128, hbA, None, pkeys=[kA0, kA1])
                em.act(fl[:], hbA[:], AF.Sigmoid, r=[kA0, kA1], w=[Kk('fl')])
                em.ts('dve', fl[:], fl[:], omlb[:, ci:ci + 1], lbv[:, ci:ci + 1], ALU.mult, ALU.add, r=[Kk('fl'), 'omlb', 'lbv'], w=[Kk('fl')])
                yield
                em.ts('pool', kf[:], fl[:], -1.0, 1.0, ALU.mult, ALU.add, r=[Kk('fl')], w=[Kk('kf')])
                em.act(lg[:], fl[:], AF.Ln, r=[Kk('fl')], w=[Kk('lg')])
                em.memset('pool', Gp[:, 0:1], 0.0, [Kk('Gp')])
                P.op('dve', lambda: nc.vector.tensor_tensor_scan(out=Gp[:, 1:T + 1], data0=self.onesT[:], data1=lg[:], initial=0.0,
                                                                 op0=ALU.mult, op1=ALU.add), r=[Kk('lg'), 'onesT'], w=[Kk('Gp')])
                yield
                v3 = lambda ap: ap.rearrange("p (c l) -> p c l", l=64)
                if not rev:
                    em.tt('dve', v3(Dd[:]), v3(Gp[:, 1:T + 1]), Gp[:, 0:T:64].unsqueeze(2).broadcast_to([128, 16, 64]), ALU.subtract,
                          r=[Kk('Gp'), Kk('lg')], w=[Kk('lg')])
                else:
                    em.tt('dve', v3(Dd[:]), Gp[:, 64:T + 1:64].unsqueeze(2).broadcast_to([128, 16, 64]), v3(Gp[:, 0:T]), ALU.subtract,
                          r=[Kk('Gp'), Kk('lg')], w=[Kk('lg')])
                em.act(eD[:, 0:T], Dd[:], AF.Exp, r=[Kk('lg'), Kk('Gp')], w=[Kk('Gp')])
                em.act(enD[:], Dd[:], AF.Exp, r=[Kk('lg'), Kk('fl')], w=[Kk('fl')], scale=-1.0)
                yield
                em.tt('dve', qt[:], qsc[:], eD[:, 0:T], ALU.mult, r=[Kk('qs'), Kk('Gp')], w=[Kk('qt')])
                h3 = lambda ap: ap.rearrange("p (t l) -> p t l", l=128)
                em.cp('pool', h3(qA[:])[:, :, 0:64], h3(qt[:])[:, :, 0:64], r=[Kk('qt')], w=[Kk('qA')])
                em.cp('pool', h3(qB[:])[:, :, 64:128], h3(qt[:])[:, :, 64:128], r=[Kk('qt')], w=[Kk('qB')])
                em.tt('dve', ktl[:], kf[:], enD[:], ALU.mult, r=[Kk('kf'), Kk('fl')], w=[Kk('ktl')])
                em.dma('sp', S[:], self.st_h_d[dr, hc], [], [Kk('S')])
                em.cp('pool', Sb[0][:], S[:], r=[Kk('S')], w=[Kk('Sb0')])
                yield
                sbi = 0
                for ti in range(8):
                    tt = 7 - ti if rev else ti
                    tl = slice(tt * 128, (tt + 1) * 128)
                    b_ = ti % 2
                    em.mm(psc, ktl[:, tl], qt[:, tl], r=[Kk('ktl'), Kk('qt')], w=[kA0])
                    em.tt('dve', Am[b_][:], psc, mask, ALU.mult, r=[kA0, 'masks'], w=[Kk('Am%d' % b_)])
                    em.tr(pktr, ktl[:, tl], self.ident_b[:], r=[Kk('ktl')], w=[kA1])
                    em.cp('act', ktok[b_][:], pktr, r=[kA1], w=[Kk('ktok%d' % b_)])
                    vt = vtok[:, tt, hc * 128:(hc + 1) * 128]
                    order = (1, 0) if rev else (0, 1)
                    em.mm(po, Am[b_][:], vt, r=[Kk('Am%d' % b_), ('vtok', tt)], w=[kB0], start=True, stop=False)
                    yield
                    for oi, c in enumerate(order):
                        qh = qA if c == 0 else qB
                        cr = slice(c * 64, (c + 1) * 64)
                        em.mm(pds, ktok[b_][cr, :], vtok[cr, tt, hc * 128:(hc + 1) * 128], r=[Kk('ktok%d' % b_), ('vtok', tt)], w=[kB1])
                        em.mm(po, qh[:, tl], Sb[sbi][:], r=[Kk('qA'), Kk('qB'), Kk('Sb%d' % sbi)], w=[kB0], start=False, stop=(oi == 1))
                        em.tt('dve', Stmp[:], pds, S[:], ALU.add, r=[kB1, Kk('S')], w=[Kk('Stmp')])
                        cg = tt * 2 + c
                        ecol = cg * 64 + (0 if rev else 63)
                        em.ts('dve', S[:], Stmp[:], eD[:, ecol:ecol + 1], None, ALU.mult, None, r=[Kk('Stmp'), Kk('Gp')], w=[Kk('S')])
                        seg_end = (cg % 4 == 0) if rev else (cg % 4 == 3)
                        if seg_end:
                            seg = cg // 4
                            em.dma('sp', self.o_sh_d[seg, dr, hc], S[:], r=[Kk('S')], w=[('o_sh', seg, dr, hc)])
                            last = (cg == 0) if rev else (cg == 15)
                            if not last:
                                em.ts('dve', S[:], S[:], self.vcol('keep'), None, ALU.mult, None, r=[Kk('S')], w=[Kk('S')])
                        sbi = 1 - sbi
                        em.cp('pool', Sb[sbi][:], S[:], r=[Kk('S')], w=[Kk('Sb%d' % sbi)])
                        yield
                    if dr == 0:
                        em.cp('act', osum[:, tt, hc * 128:(hc + 1) * 128], po, r=[kB0], w=[('osum', tt, hc)])
                    else:
                        em.tt('dve', osum[:, tt, hc * 128:(hc + 1) * 128], po, osum[:, tt, hc * 128:(hc + 1) * 128], ALU.add,
                              r=[kB0, ('osum', tt, hc)], w=[('osum', tt, hc)])

            for dr in range(2):
                for hp in range(2):
                    gens = [hchain(0, dr, 2 * hp), hchain(1, dr, 2 * hp + 1)]
                    while gens:
                        for g_ in list(gens):
                            try:
                                next(g_)
                            except StopIteration:
                                gens.remove(g_)
            for tt in range(8):
                for hc in range(4):
                    u = tt * 4 + hc
                    em.stt(junk[:], osum[:, tt, hc * 128:(hc + 1) * 128], 1.0, osum[:, tt, hc * 128:(hc + 1) * 128], ALU.mult, ALU.mult,
                           r=[('osum', tt, hc)], w=['junkh', ('ssh', u)], accum=ssh[:, u:u + 1])
            em.act(ssh[:], ssh[:], AF.Ln, r=[('ssh', u) for u in range(32)], w=['rsh'], scale=1.0 / 128, bias=self.epsc[:, 0:1])
            em.act(ssh[:], ssh[:], AF.Exp, r=['rsh'], w=['rsh'], scale=-1.0 * 0.5)
            for tt in range(8):
                for hc in range(4):
                    u = tt * 4 + hc
                    em.stt(osum[:, tt, hc * 128:(hc + 1) * 128], osum[:, tt, hc * 128:(hc + 1) * 128], ssh[:, u:u + 1], bv1[:, 0:128],
                           ALU.mult, ALU.mult, r=[('osum', tt, hc), 'rsh', 'bv1'], w=[('osum', tt, hc)])
                em.tt('pool', ocat[:, tt, 0:512], osum[:, tt, :], gsil[:, tt, :], ALU.mult,
                      r=[('osum', tt, hc) for hc in range(4)] + [('gsil', tt)], w=[('ocat', tt, 0)])
            self.dump('ocat_h', ocat, [128, 8, D], [])
            P.flush()
        if self.stop == 'C1':
            return
        self.odd_rwkv(em, ocat, masks, bv1)
        with ExitStack() as e3:
            self.alloc_w(e3, 2, 4096)
            ptp = [self.ps(e3, "otp%d" % i, [128, 1024], BF16) for i in range(2)]
            pmx = [self.ps(e3, "pmx%d" % i, [128, 512]) for i in range(2)]
            n = 0
            for c in range(8):
                for gq in range(2):
                    pp = ptp[n % 2]
                    for j in range(4):
                        qt_ = gq * 4 + j
                        em.tr(pp[:, j * 128:(j + 1) * 128], ocat[:, qt_, c * 128:(c + 1) * 128], self.ident_b[:], r=[], w=[('otp', n % 2)])
                    em.cp('dve' if n % 2 == 0 else 'act', self.hT[:, c, gq * 512:(gq + 1) * 512], pp[:, 0:512], r=[('otp', n % 2)], w=[('hT', c)])
                    n += 1
            self.out_proj(self.od_w_out_d, pmx, l)
            self.dump('xm1', self.xT, [128, NCH, T], [])
            P.flush()


Builder.odd_mixer = _odd_mixer


def _odd_rwkv(self, em, ocat, masks, bv1):
    P, nc = self.P, self.nc
    wd = self.od_w_in_d
    idf = self.ident_f
    with ExitStack() as e2:
        sbf = lambda n, s, d=F32: self.sb(e2, n, s, d)
        e2b = ExitStack()
        sbb = lambda n, s, d=F32: self.sb(e2b, n, s, d)
        self.alloc_w(e2, 3, 1024)
        osum = sbf("osum_r", [128, 8, 512])
        bonus = sbf("bonus", [128, 8, 8])
        twd = sbf("twd", [128, T], BF16)
        adT = sbf("adT", [64, T], BF16)
        sgd = sbf("sgd", [128, T], BF16)
        wup = sbf("wup", [128, 512], BF16)
        aup = sbf("aup", [64, 512], BF16)
        gup = sbf("gup", [128, 512], BF16)
        hsel = sbf("hsel", [128, 2])
        omm = sbf("omm", [128, 15]); hmu = sbf("hmu", [128, 15]); hk = sbf("hk", [128, 15])
        zst = sbf("zst", [64, 16, 64]); sstg = [sbf("sstg%d" % i, [64, 64]) for i in range(2)]
        zst_b = sbf("zst_b", [64, 16, 64], BF16)

        m2 = sbf("m2", [128, 2, 256])
        pA2 = [self.ps(e2, "pA2_%d" % i, [128, T]) for i in range(2)]
        pBC = [self.ps(e2, "pBC_%d" % i, [128, T]) for i in range(2)]
        pj = pA2[0]
        pT = pj[:, 0:512]
        pS = pBC[0][:, 512:1024]
        P.excl.update(['pA', 'pB'])
        P.alias.update({('pj', 0): ('pA', 0, 0), ('pj', 1): ('pA', 0, 1), 'pT': ('pA', 0, 0), 'pS': ('pB', 0, 1)})
        vtok = sbb("vtok_p", [128, 8, 128], BF16)
        r_p = sbb("r_p", [128, T]); k_p = sbb("k_p", [128, T]); v_p = sbb("v_p", [128, T])
        a_p = sbb("a_p", [128, T]); kk_p = sbb("kk_p", [128, T]); kt_p = sbb("kt_p", [128, T]); b_p = sbb("b_p", [128, T])
        tmpf = sbb("tmpf", [128, T]); sqb = sbb("sqr", [128, T], BF16)
        Gp = sbb("Gpr", [128, T + 1]); Dd = self.rstd
        KR = [sbb("kr%d" % i, [128, 8, 256], BF16) for i in range(2)]; BE = [sbb("be%d" % i, [128, T], BF16) for i in range(2)]; TA = [sbb("ta%d" % i, [128, T], BF16) for i in range(2)]
        ED1 = sbb("eD1", [128, T])
        pd = tmpf; lw = v_p; Dp = a_p; enD = Gp; ED = [tmpf, ED1]
        P.alias.update({'pd': 'tmpf', 'lwraw': 'v_p', 'r_lw': 'v_p', 'Dp': 'a_p', 'enDr': 'r_Gp', ('eDr', 0): 'tmpf'})
        btk = [sbb("btk%d" % i, [128, 128], BF16) for i in range(2)]; ttk = [sbb("ttk%d" % i, [128, 128], BF16) for i in range(2)]
        nktk = [sbb("nktk%d" % i, [128, 128], BF16) for i in range(2)]
        M1b = [sbb("M1b%d" % i, [128, 2, 256], BF16) for i in range(2)]; M2b = [sbb("M2b%d" % i, [128, 2, 256], BF16) for i in range(2)]
        YA = [[sbb("YA%d_%d" % (s_, i), [128, 2, 128], BF16) for i in range(2)] for s_ in range(2)]
        YT_ = [[sbb("YT%d_%d" % (s_, i), [128, 2, 128], BF16) for i in range(2)] for s_ in range(2)]
        QQ = [[sbb("QQ%d_%d" % (s_, i), [128, 2, 128], BF16) for i in range(2)] for s_ in range(2)]
        rxb = [sbb("rxb%d" % i, [128, 2, 128], BF16) for i in range(2)]; xsb = [sbb("xsb%d" % i, [128, 2, 128], BF16) for i in range(2)]
        RAb = [sbb("RAb%d" % i, [64, 2, 128], BF16) for i in range(2)]; RBb = [sbb("RBb%d" % i, [64, 2, 128], BF16) for i in range(2)]
        GTb = [sbb("GTb%d" % i, [64, 2, 128]) for i in range(2)]; Z1b = [sbb("Z1b%d" % i, [64, 2, 128]) for i in range(2)]
        ztb = [sbb("ztb%d" % i, [64, 2, 64]) for i in range(2)]

        em.memset('pool', hsel[:], 0.0, ['hsel'])
        em.memset('pool', hsel[0:64, 0:1], 1.0, ['hsel'])
        em.memset('pool', hsel[64:128, 1:2], 1.0, ['hsel'])
        for s_ in range(2):
            em.memset('pool', RAb[s_][:], 0.0, [('t', s_, 'ra')])
            em.memset('pool', RBb[s_][:], 0.0, [('t', s_, 'rb')])
        em.memset('pool', Gp[:, 0:1], 0.0, ['r_Gp'])
        for dr in range(2):
            em.cp('pool', m2[:, dr, 0:128], masks[:, 2 + dr, :], r=['masks'], w=['m2'])
            em.cp('pool', m2[:, dr, 128:256], masks[:, dr, :], r=['masks'], w=['m2'])
        em.ts('dve', omm[:], self.vcol('mu', 0, 15), -1.0, 1.0, ALU.mult, ALU.add, r=[], w=['omm'])
        em.ts('dve', hmu[:], self.vcol('mu', 0, 15), 0.5, None, ALU.mult, None, r=[], w=['hmu'])
        em.ts('dve', hk[:], hmu[:], self.vcol('km1'), None, ALU.mult, None, r=['hmu'], w=['hk'])
        em.dma('pool', wup[:], self.w_up_d, [], ['wup'])
        em.dma('pool', aup[:], self.a_up_d, [], ['aup'])
        em.dma('pool', gup[:], self.g_up_d, [], ['gup'])
        sld = osum[0:64, 0:2, :].rearrange("p a (b k) -> p (a b) k", k=64)
        em.dma('sp', sld, self.st_r_d.rearrange("d h v k -> v (d h) k"), [], ['sld'])
        for i in range(16):
            em.tr(pS[0:64, 0:64], sld[:, i, :], idf[0:64, 0:64], r=['sld'], w=['pS'])
            em.cp('dve', zst[:, i, :], pS[0:64, 0:64], r=['pS'], w=[('zst', i)])
            em.cp('pool', zst_b[:, i, :], zst[:, i, :], r=[('zst', i)], w=[('zstb', i)])
        P.flush()

        def tshift(dst, j, nrows, rkeys, wkeys):
            pr = pj[0:nrows, :]
            em.act(dst, pr, AF.Identity, r=rkeys + ['omm'], w=wkeys, scale=omm[0:nrows, j:j + 1])
            em.stt(dst[:, 1:T], pr[:, 0:T - 1], hmu[0:nrows, j:j + 1], dst[:, 1:T], ALU.mult, ALU.add, r=rkeys + wkeys + ['hmu'], w=wkeys)
            em.stt(dst[:, 0:T - 1], pr[:, 1:T], hmu[0:nrows, j:j + 1], dst[:, 0:T - 1], ALU.mult, ALU.add, r=rkeys + wkeys + ['hmu'], w=wkeys)
            em.stt(dst[:, 256:T:256], pr[:, 255:T - 1:256], hk[0:nrows, j:j + 1], dst[:, 256:T:256], ALU.mult, ALU.add,
                   r=rkeys + wkeys + ['hk'], w=wkeys)
            em.stt(dst[:, 255:T - 1:256], pr[:, 256:T:256], hk[0:nrows, j:j + 1], dst[:, 255:T - 1:256], ALU.mult, ALU.add,
                   r=rkeys + wkeys + ['hk'], w=wkeys)

        PJ = [('pj', 0), ('pj', 1)]
        _proj_fm(self, em, wd, RW0 + 1536, 128, pj, 'pj')
        tshift(pd[:], 12, 128, PJ, ['pd'])
        em.act(twd[:], pd[:], AF.Tanh, r=['pd'], w=['twd'])
        _proj_fm(self, em, wd, RW0 + 1664, 64, pj, 'pj')
        tshift(pd[0:64, :], 13, 64, PJ, ['pd'])
        em.cp('pool', adT[:], pd[0:64, :], r=['pd'], w=['adT'])
        _proj_fm(self, em, wd, RW0 + 1728, 128, pj, 'pj')
        tshift(pd[:], 14, 128, PJ, ['pd'])
        em.act(sgd[:], pd[:], AF.Sigmoid, r=['pd'], w=['sgd'])
        if self.stop == 'C2a':
            P.flush(); e2b.close(); return
        for p in range(4):
            for nm, dst, j in (('r', r_p, p), ('k', k_p, 4 + p), ('v', v_p, 8 + p)):
                _proj_fm(self, em, wd, RW0 + j * 128, 128, pj, 'pj')
                tshift(dst[:], j, 128, PJ, [nm + '_p'])
            for tt in range(8):
                em.tr(pT[:, 0:128], v_p[:, tt * 128:(tt + 1) * 128], idf[:], r=['v_p'], w=['pT'])
                em.cp('act', vtok[:, tt, :], pT[:, 0:128], r=['pT'], w=[('vtok', tt)])
            for th in range(2):
                em.mm(pj[:, th * 512:(th + 1) * 512], aup[:, p * 128:(p + 1) * 128], adT[:, th * 512:(th + 1) * 512], r=['aup', 'adT'], w=[('pj', th)])
            em.act(a_p[:], pj[:], AF.Sigmoid, r=PJ, w=['a_p'], bias=self.vcol('a0', p))
            em.ts('dve', kk_p[:], k_p[:], self.vcol('k_k', p), None, ALU.mult, None, r=['k_p'], w=['kk_p'])
            em.act(sqb[:], kk_p[:], AF.Square, r=['kk_p'], w=['sqr'])
            for th in range(2):
                em.mm(pj[:, th * 512:(th + 1) * 512], self.bd_ones[:], sqb[:, th * 512:(th + 1) * 512], r=['sqr'], w=[('pj', th)])
            em.act(tmpf[:], pj[:], AF.Ln, r=PJ, w=['tmpf'], bias=self.epsc[:, 1:2])
            em.act(tmpf[:], tmpf[:], AF.Exp, r=['tmpf'], w=['tmpf'], scale=-0.5)
            em.tt('dve', kk_p[:], kk_p[:], tmpf[:], ALU.mult, r=['kk_p', 'tmpf'], w=['kk_p'])
            em.ts('dve', kt_p[:], a_p[:], -1.0, self.vcol('k_a', p), ALU.add, ALU.mult, r=['a_p'], w=['kt_p'])
            em.stt(kt_p[:], kt_p[:], 1.0, k_p[:], ALU.add, ALU.mult, r=['kt_p', 'k_p'], w=['kt_p'])
            em.tt('pool', b_p[:], a_p[:], kk_p[:], ALU.mult, r=['a_p', 'kk_p'], w=['b_p'])
            em.stt(tmpf[:], r_p[:], self.vcol('r_k', p), kt_p[:], ALU.mult, ALU.mult, r=['r_p', 'kt_p', 'tmpf'], w=['tmpf'])
            for tt in range(8):
                em.mm(pT[:, 0:2], tmpf[:, tt * 128:(tt + 1) * 128], hsel[:], r=['tmpf', 'hsel'], w=['pT'])
                em.cp('act', bonus[:, tt, 2 * p:2 * p + 2], pT[:, 0:2], r=['pT'], w=[('bonus', tt, p)])
                em.tt('pool', ocat[:, tt, 512 + p * 128:512 + (p + 1) * 128].rearrange('p (h e) -> p h e', e=64),
                      vtok[:, tt, :].rearrange('p (h e) -> p h e', e=64), bonus[:, tt, 2 * p:2 * p + 2].unsqueeze(2).broadcast_to([128, 2, 64]),
                      ALU.mult, r=[('vtok', tt), ('bonus', tt, p)], w=[('ocat', tt, 1)])
            def dir_prep(dr):
                rev = dr == 1
                kr, be, ta, eD = KR[dr], BE[dr], TA[dr], ED[dr]
                dslc = slice(dr * 64, (dr + 1) * 64)
                for th in range(2):
                    em.mm(pj[:, th * 512:(th + 1) * 512], wup[dslc, p * 128:(p + 1) * 128], twd[dslc, th * 512:(th + 1) * 512],
                          r=['wup', 'twd'], w=[('pj', th)])
                em.act(lw[:], pj[:], AF.Sigmoid, r=PJ, w=['lwraw'], bias=self.vcol('w0', dr * 4 + p))
                yield
                em.ts('dve', lw[:], lw[:], LWS, None, ALU.mult, None, r=['lwraw'], w=['r_lw'])
                yield
                em.memset('pool', Gp[:, 0:1], 0.0, ['r_Gp'])
                yield
                _decay(self, em, lw[:], Gp, Dd[:], rev, 'r')
                yield
                em.tt('pool', Dp[:], Dd[:], lw[:], ALU.subtract, r=['r_D', 'r_lw'], w=['Dp'])
                yield
                em.act(eD[:], Dd[:], AF.Exp, r=['r_D'], w=[('eDr', dr)])
                yield
                em.act(enD[:, 0:T], Dd[:], AF.Exp, r=['r_D'], w=['enDr'], scale=-1.0)
                yield
                em.act(Dp[:], Dp[:], AF.Exp, r=['Dp'], w=['Dp'])
                yield
                t3 = lambda ap: ap.rearrange("p (t l) -> p t l", l=128)
                em.tt('dve', kr[:, :, 0:128], t3(kk_p[:]), t3(Dp[:]), ALU.mult, r=['kk_p', 'Dp'], w=[('kr', dr)])
                yield
                em.tt('dve', kr[:, :, 128:256], t3(r_p[:]), t3(eD[:]), ALU.mult, r=['r_p', ('eDr', dr)], w=[('kr', dr)])
                yield
                em.tt('pool', be[:], b_p[:], enD[:, 0:T], ALU.mult, r=['b_p', 'enDr'], w=[('be', dr)])
                yield
                em.tt('pool', ta[:], kt_p[:], enD[:, 0:T], ALU.mult, r=['kt_p', 'enDr'], w=[('ta', dr)])
                yield

            def dir_tiles(dr, extra):
                rev = dr == 1
                kr, be, ta, eD = KR[dr], BE[dr], TA[dr], ED[dr]
                mk = m2[:, dr, :]
                mk3 = masks[:, 3 - dr, :]
                def tchain(slot, ti):
                    tt = 7 - ti if rev else ti
                    tl = slice(tt * 128, (tt + 1) * 128)
                    bb = slot
                    A2, BC = pA2[slot], pBC[slot]
                    kA = [('pA', slot, 0), ('pA', slot, 1)]
                    kB = [('pB', slot, 0), ('pB', slot, 1)]
                    m1, m2_, ya, yt, qq = M1b[slot], M2b[slot], YA[slot], YT_[slot], QQ[slot]
                    rx, xs, ra, rb, gt, z1, zt = rxb[slot], xsb[slot], RAb[slot], RBb[slot], GTb[slot], Z1b[slot], ztb[slot]
                    K_ = lambda n: ('t', slot, n)
                    zi0 = dr * 8 + 2 * p
                    HR = [slice(0, 64), slice(64, 128)]
                    v2 = lambda ap, n: ap.rearrange("p (h n) -> p h n", n=n)
                    vh = lambda ap: ap.rearrange("p (h n) -> p h n", n=512)
                    pTs = BC[:, 0:512].bitcast(BF16)
                    em.tr(pTs[:, 0:128], kr[:, tt, 0:128], self.ident_b[:], r=[('kr', dr)], w=[kB[0]])
                    em.tr(pTs[:, 128:256], be[:, tl], self.ident_b[:], r=[('be', dr)], w=[kB[0]])
                    em.tr(pTs[:, 256:384], ta[:, tl], self.ident_b[:], r=[('ta', dr)], w=[kB[0]])
                    em.act(nktk[bb][:], pTs[:, 0:128], AF.Identity, r=[kB[0]], w=[('nktk', bb)], scale=-1.0)
                    em.cp('dve', btk[bb][:], pTs[:, 128:256], r=[kB[0]], w=[('btk', bb)])
                    em.cp('act', ttk[bb][:], pTs[:, 256:384], r=[kB[0]], w=[('ttk', bb)])
                    for hh in range(2):
                        em.mm(A2[:, hh * 512:hh * 512 + 256], be[HR[hh], tl], kr[HR[hh], tt, :], r=[('be', dr), ('kr', dr)], w=[kA[hh]])
                    em.tt('dve', m1[:], vh(A2[:])[:, :, 0:256], mk.unsqueeze(1).broadcast_to([128, 2, 256]), ALU.mult, r=kA + ['m2'], w=[K_('m1')])
                    for hh in range(2):
                        em.mm(BC[:, hh * 512:hh * 512 + 128], kr[HR[hh], tt, 0:128], be[HR[hh], tl], r=[('kr', dr), ('be', dr)], w=[kB[hh]])
                    em.tt('dve', ya[0][:], vh(BC[:])[:, :, 0:128], mk3.unsqueeze(1).broadcast_to([128, 2, 128]), ALU.mult,
                          r=kB + ['masks'], w=[K_('yy0')])
                    yield
                    for hh in range(2):
                        em.mm(A2[:, hh * 512:hh * 512 + 256], ta[HR[hh], tl], kr[HR[hh], tt, :], r=[('ta', dr), ('kr', dr)], w=[kA[hh]])
                    em.tt('dve', m2_[:], vh(A2[:])[:, :, 0:256], mk.unsqueeze(1).broadcast_to([128, 2, 256]), ALU.mult, r=kA + ['m2'], w=[K_('m2')])
                    em.tt('pool', qq[0][:], idf[:].unsqueeze(1).broadcast_to([128, 2, 128]), m1[:, :, 0:128], ALU.subtract, r=[K_('m1')], w=[K_('qq0')])
                    yield
                    for hh in range(2):
                        em.mm(BC[:, 768 + hh * 64:832 + hh * 64], m2_[:, hh, 0:128], vtok[:, tt, hh * 64:(hh + 1) * 64], r=[K_('m2'), ('vtok', tt)], w=[kB[1]])
                    em.act(rx[:, :, 64:128], v2(BC[:, 768:896], 64), AF.Identity, r=[kB[1]], w=[K_('rx')], scale=-1.0)
                    em.cp('pool', rx[:, :, 0:64], v2(nktk[bb][:], 64), r=[('nktk', bb)], w=[K_('rx')])
                    yi, qi = 0, 0
                    for j in range(6):
                        yn = 1 - yi
                        for hh in range(2):
                            Yc = ya[yi][:, hh, :]
                            YTc = m1[:, hh, 0:128] if j == 0 else yt[yi][:, hh, :]
                            rk = [K_('yy%d' % yi)] + ([K_('m1')] if j == 0 else [])
                            if j < 5:
                                em.mm(BC[:, hh * 256:hh * 256 + 128], YTc, Yc, r=rk, w=[kB[0]])
                                if j < 4:
                                    em.mm(BC[:, hh * 256 + 128:hh * 256 + 256], Yc, YTc, r=rk, w=[kB[0]])
                            if j >= 1:
                                em.mm(BC[:, 512 + hh * 128:640 + hh * 128], Yc, qq[qi][:, hh, :], r=[K_('yy%d' % yi), K_('qq%d' % qi)], w=[kB[1]])
                        if j < 5:
                            em.cp('act', ya[yn][:], v2(BC[:, 0:512], 256)[:, :, 0:128], r=[kB[0]], w=[K_('yy%d' % yn)])
                            if j < 4:
                                em.cp('dve', yt[yn][:], v2(BC[:, 0:512], 256)[:, :, 128:256], r=[kB[0]], w=[K_('yy%d' % yn)])
                        if j >= 1:
                            em.tt('dve', qq[1 - qi][:], v2(BC[:, 512:768], 128), qq[qi][:], ALU.add, r=[kB[1], K_('qq%d' % qi)], w=[K_('qq%d' % (1 - qi))])
                            qi = 1 - qi
                        yi = yn
                        yield
                    for hh in range(2):
                        em.mm(BC[:, 512 + hh * 128:640 + hh * 128], qq[qi][:, hh, :], rx[:, hh, :], r=[K_('qq%d' % qi), K_('rx')], w=[kB[1]])
                    em.cp('act', xs[:], v2(BC[:, 512:768], 128), r=[kB[1]], w=[K_('xs')])
                    yield
                    for hh in range(2):
                        em.mm(BC[0:64, 768 + hh * 128:896 + hh * 128], xs[:, hh, 0:64], m1[:, hh, 128:256], r=[K_('xs'), K_('m1')], w=[kB[1]])
                    for hh in range(2):
                        em.tt('dve', ra[:, hh, 0:64], BC[0:64, 768 + hh * 128:832 + hh * 128], kr[HR[hh], tt, 128:192], ALU.add, r=[kB[1], ('kr', dr)], w=[K_('ra')])
                        em.tt('dve', rb[:, hh, 64:128], BC[0:64, 832 + hh * 128:896 + hh * 128], kr[HR[hh], tt, 192:256], ALU.add, r=[kB[1], ('kr', dr)], w=[K_('rb')])
                    for hh in range(2):
                        em.mm(A2[:, 256 + hh * 64:320 + hh * 64], m1[:, hh, 128:256], xs[:, hh, 64:128], r=[K_('m1'), K_('xs')], w=[kA[0]],
                              start=(hh == 0), stop=False, sgc=True)
                        em.mm(A2[:, 256 + hh * 64:320 + hh * 64], m2_[:, hh, 128:256], vtok[:, tt, hh * 64:(hh + 1) * 64], r=[K_('m2'), ('vtok', tt)], w=[kA[0]],
                              start=False, stop=False, sgc=True)
                    for hh in range(2):
                        for c in range(2):
                            cr = slice(c * 64, (c + 1) * 64)
                            o_ = c * 512 + hh * 64
                            em.mm(BC[0:64, o_:o_ + 64], xs[cr, hh, 0:64], btk[bb][cr, hh * 64:(hh + 1) * 64], r=[K_('xs'), ('btk', bb)], w=[kB[c]])
                    g4 = lambda ap: ap.rearrange("p c (h e) -> p c h e", e=64)
                    em.tt('dve', g4(gt[:]), g4(vh(BC[0:64, :])[:, :, 0:128]),
                          idf[0:64, 0:64].unsqueeze(1).unsqueeze(1).broadcast_to([64, 2, 2, 64]), ALU.add, r=kB, w=[K_('gt')])
                    yield
                    for hh in range(2):
                        for c in range(2):
                            cr = slice(c * 64, (c + 1) * 64)
                            o_ = c * 512 + 128 + hh * 64
                            em.mm(BC[0:64, o_:o_ + 64], btk[bb][cr, hh * 64:(hh + 1) * 64], xs[cr, hh, 64:128], r=[K_('xs'), ('btk', bb)], w=[kB[c]],
                                  start=True, stop=False, sgc=True)
                            em.mm(BC[0:64, o_:o_ + 64], ttk[bb][cr, hh * 64:(hh + 1) * 64], vtok[cr, tt, hh * 64:(hh + 1) * 64],
                                  r=[('ttk', bb), ('vtok', tt)], w=[kB[c]], start=False, stop=True, sgc=True)
                    em.cp('act', z1[:], vh(BC[0:64, :])[:, :, 128:256], r=kB, w=[K_('z1')])
                    yield
                    order = (1, 0) if rev else (0, 1)
                    for oi, c in enumerate(order):
                        for hh in range(2):
                            zi = zi0 + hh
                            em.mm(BC[0:64, 256 + hh * 64:320 + hh * 64], gt[:, c, hh * 64:(hh + 1) * 64], zst[:, zi, :], r=[K_('gt'), ('zst', zi)], w=[kB[0]])
                        for hh in range(2):
                            zi = zi0 + hh
                            Rh = ra if c == 0 else rb
                            em.mm(A2[:, 256 + hh * 64:320 + hh * 64], Rh[:, hh, :], zst_b[:, zi, :], r=[K_('ra'), K_('rb'), ('zstb', zi)], w=[kA[0]],
                                  start=False, stop=(oi == 1), sgc=True)
                        em.tt('dve', zt[:], v2(BC[0:64, 256:384], 64), v2(z1[:, c, :], 64), ALU.add, r=[kB[0], K_('z1')], w=[K_('zt')])
                        cg = tt * 2 + c
                        ecol = cg * 64 + (0 if rev else 63)
                        seg_end = (cg % 4 == 0) if rev else (cg % 4 == 3)
                        for hh in range(2):
                            zi = zi0 + hh
                            head = 2 * p + hh
                            em.ts('dve', zst[:, zi, :], zt[:, hh, :], eD[HR[hh], ecol:ecol + 1], None, ALU.mult, None, r=[K_('zt'), ('eDr', dr)], w=[('zst', zi)])
                            if seg_end:
                                seg = cg // 4
                                sg = sstg[hh]
                                sk = ('sstg', hh)
                                em.tr(BC[0:64, 640 + hh * 64:704 + hh * 64], zst[:, zi, :], idf[0:64, 0:64], r=[('zst', zi)], w=[kB[1]])
                                em.cp('act', sg[:], BC[0:64, 640 + hh * 64:704 + hh * 64], r=[kB[1]], w=[sk])
                                em.dma('sp', self.o_sr_d[seg, dr, head], sg[:], r=[sk], w=[('o_sr', seg, dr, head)])
                                last = (cg == 0) if rev else (cg == 15)
                                if not last:
                                    em.ts('dve', zst[:, zi, :], zst[:, zi, :], self.vcol('keep')[0:64, :], None, ALU.mult, None,
                                          r=[('zst', zi)], w=[('zst', zi)])
                        em.cp('pool', zst_b[:, zi0:zi0 + 2, :], zst[:, zi0:zi0 + 2, :], r=[('zst', zi0), ('zst', zi0 + 1)],
                              w=[('zstb', zi0), ('zstb', zi0 + 1)])
                        yield
                    ocols = slice(p * 128, (p + 1) * 128)
                    if dr == 0:
                        em.cp('act', osum[:, tt, ocols], A2[:, 256:384], r=[kA[0]], w=[('osum', tt, 2 * p), ('osum', tt, 2 * p + 1)])
                    else:
                        em.tt('dve', osum[:, tt, ocols], A2[:, 256:384], osum[:, tt, ocols], ALU.add,
                              r=[kA[0], ('osum', tt, 2 * p), ('osum', tt, 2 * p + 1)], w=[('osum', tt, 2 * p), ('osum', tt, 2 * p + 1)])

                NSTART = 2
                active = []
                nxt = 0
                while active or nxt < 8:
                    if nxt < 8 and len(active) < 2 and (not active or active[0][1] >= NSTART):
                        active.append([tchain(nxt % 2, nxt), 0])
                        nxt += 1
                    for ent in list(active):
                        try:
                            next(ent[0])
                            ent[1] += 1
                        except StopIteration:
                            active.remove(ent)
                    if extra is not None:
                        try:
                            next(extra)
                        except StopIteration:
                            extra = None
                if extra is not None:
                    for _ in extra:
                        pass
            for _ in dir_prep(0):
                pass
            dir_tiles(0, dir_prep(1))
            dir_tiles(1, None)
        P.flush()
        e2b.close()
        gtok = sbf("gtok", [128, 8, 512], BF16)
        for tt in range(8):
            em.mm(pT[:], sgd[:, tt * 128:(tt + 1) * 128], gup[:], r=['sgd', 'gup'], w=['pT'])
            em.cp('act', gtok[:, tt, :], pT[:], r=['pT'], w=[('gtok', tt)])
        mean = sbf("mean", [128, 8, 8]); var = sbf("var", [128, 8, 8]); sq2 = sbf("sq2", [128, 512]); cen = sbf("cen", [128, 8, 512])
        h4 = lambda ap: ap.rearrange("p (h e) -> p h e", e=64)
        for tt in range(8):
            OK = [('osum', tt, h) for h in range(8)]
            P.op('dve', lambda tt=tt: nc.vector.reduce_sum(out=mean[:, tt, :], in_=h4(osum[:, tt, :]), axis=AX.X), r=OK, w=[('mean', tt)])
            em.ts('dve', mean[:, tt, :], mean[:, tt, :], 1.0 / 64, None, ALU.mult, None, r=[('mean', tt)], w=[('mean', tt)])
            em.tt('dve', h4(cen[:, tt, :]), h4(osum[:, tt, :]), mean[:, tt, :].unsqueeze(2).broadcast_to([128, 8, 64]), ALU.subtract,
                  r=OK + [('mean', tt)], w=[('cen', tt)])
            em.tt('pool', sq2[:], cen[:, tt, :], cen[:, tt, :], ALU.mult, r=[('cen', tt)], w=['sq2'])
            P.op('dve', lambda tt=tt: nc.vector.reduce_sum(out=var[:, tt, :], in_=h4(sq2[:]), axis=AX.X), r=['sq2'], w=[('var', tt)])
        VK = [('var', tt) for tt in range(8)]
        em.act(var[:], var[:], AF.Ln, r=VK, w=['rstdr'], scale=1.0 / 64, bias=self.epsc[:, 2:3])
        em.act(var[:], var[:], AF.Exp, r=['rstdr'], w=['rstdr'], scale=-0.5)
        for tt in range(8):
            c3 = h4(cen[:, tt, :])
            em.tt('dve', c3, c3, var[:, tt, :].unsqueeze(2).broadcast_to([128, 8, 64]), ALU.mult, r=[('cen', tt), 'rstdr'], w=[('cen', tt)])
            em.tt('pool', cen[:, tt, :], cen[:, tt, :], bv1[:, 128:640], ALU.mult, r=[('cen', tt)], w=[('cen', tt)])
            em.tt('pool', cen[:, tt, :], cen[:, tt, :], bv1[:, 640:1152], ALU.add, r=[('cen', tt)], w=[('cen', tt)])
            em.tt('dve', cen[:, tt, :], cen[:, tt, :], ocat[:, tt, 512:1024], ALU.add, r=[('cen', tt), ('ocat', tt, 1)], w=[('cen', tt)])
            em.tt('pool', ocat[:, tt, 512:1024], cen[:, tt, :], gtok[:, tt, :], ALU.mult, r=[('cen', tt), ('gtok', tt)], w=[('ocat', tt, 1)])
        self.dump('ocat_r', ocat, [128, 8, D], [])
        P.flush()


Builder.odd_rwkv = _odd_rwkv


GRID_W = 64
def rope_tables(prompt):
    T = 1024
    if prompt:
        return np.stack([np.ones((128, T), np.float32), np.zeros((128, T), np.float32)])
    rows = (np.arange(T) // GRID_W).astype(np.float32)
    cols = (np.arange(T) % GRID_W).astype(np.float32)
    half = 16
    freq = np.power(np.float32(10000.0), -np.arange(half, dtype=np.float32) / half).astype(np.float32)
    cos = np.zeros((64, T), np.float32); sin = np.zeros((64, T), np.float32)
    for d in range(64):
        pos = rows if d < 32 else cols
        w = d % 32
        i = w % 16
        ang = (pos * freq[i]).astype(np.float32)
        cos[d] = np.cos(ang)
        sin[d] = -np.sin(ang) if w < 16 else np.sin(ang)
    return np.stack([np.concatenate([cos, cos]), np.concatenate([sin, sin])]).astype(np.float32)

def partner64():
    p = np.arange(64)
    w = p % 32
    return np.where(w < 16, p + 16, p - 16)

def ev_wx(ev_w_in):
    W = ev_w_in
    pr = partner64()
    qa = W[:, 0:512]; ka = W[:, 512:1024]; qb = W[:, 1536:2048]; kb = W[:, 2048:2176]
    hb_order = [0, 4, 1, 5, 2, 6, 3, 7]
    qbp = np.concatenate([qb[:, h * 64:(h + 1) * 64] for h in hb_order], axis=1)
    def sw(M):
        n = M.shape[1] // 64
        return np.concatenate([M[:, h * 64:(h + 1) * 64][:, pr] for h in range(n)], axis=1)
    return np.ascontiguousarray(np.concatenate([qbp, sw(qa), sw(ka), sw(qbp), sw(kb)], axis=1))

def amask(prompt):
    M = np.zeros((128, 48), np.float32)
    if prompt:
        for s in range(4):
            for kb in range(12):
                ok = kb >= 4 and (kb - 4) // 2 == s
                if not ok:
                    M[:, s * 12 + kb] = -30000.0
    return M

def host_vecs(vp, inp, cond, prompt):
    keep = 0.0 if prompt else 1.0
    V = np.zeros((128, vp.n), np.float32)
    def put(name, arr):
        c0, n = vp.cols[name]
        assert arr.shape == (128, n), (name, arr.shape, n)
        V[:, c0:c0+n] = arr
    put('cond', fm(cond))
    put('km1', np.full((128,1), keep - 1.0, np.float32))
    put('keep', np.full((128,1), keep, np.float32))
    for l in range(2):
        put('ada_b%d'%l, fm(inp['ada_b'][l]))
        put('nmg%d'%l, fm(inp['norm_mix_g'][l]))
        put('nfg%d'%l, fm(inp['norm_ffn_g'][l]))
        for i in range(3):
            put('cw%d_%d'%(i,l), fm(inp['ffn_conv_w'][l, i]))
        put('cb_%d'%l, fm(inp['ffn_conv_b'][l]))
    put('fng', fm(inp['final_norm_g']))
    pr = partner64()
    gq = inp['b_q_norm_g'][0]; gk = inp['b_k_norm_g'][0]
    put('gq', np.tile(gq, 2)[:, None]); put('gq_sw', np.tile(gq[pr], 2)[:, None])
    put('gk', np.tile(gk, 2)[:, None]); put('gk_sw', np.tile(gk[pr], 2)[:, None])
    put('amask', amask(prompt))
    return V

def bvec(inp):
    b = np.concatenate([inp['a_subln_g'][0], inp['b_k_norm_g'][0], inp['a_lambda'][0].reshape(-1)]).astype(np.float32)
    return np.ascontiguousarray(np.broadcast_to(b[None, :], (128, 448)))

def core_inputs(vp, inp, core):
    prompt = core < 4
    m = {}
    if prompt:
        m['x'] = np.ascontiguousarray(inp['x_prompt'][4 * core:4 * core + 4].reshape(1024, 1024))
        cond = inp['c_ctx']
        m['ctx_ak'] = np.zeros((4, 512, 128), np.float32); m['ctx_av'] = np.zeros((4, 512, 128), np.float32)
        m['ctx_bk'] = np.zeros((2, 512, 64), np.float32); m['ctx_bv'] = np.zeros((2, 512, 64), np.float32)
    else:
        b = core - 4
        m['x'] = np.ascontiguousarray(inp['x_sample'][b])
        cond = inp['c'][b]
        m['ctx_ak'] = np.ascontiguousarray(inp['cache_a_k'][b, 0]); m['ctx_av'] = np.ascontiguousarray(inp['cache_a_v'][b, 0])
        m['ctx_bk'] = np.ascontiguousarray(inp['cache_b_k'][b, 0]); m['ctx_bv'] = np.ascontiguousarray(inp['cache_b_v'][b, 0])
    m['vecs'] = host_vecs(vp, inp, cond, prompt)
    m['ident'] = np.eye(128, dtype=np.float32)
    m['rope'] = rope_tables(prompt)
    m['bvec'] = bvec(inp)
    m['ada_w'] = inp['ada_w']; m['ffn_w_up'] = inp['ffn_w_up']; m['ffn_w_down'] = inp['ffn_w_down']
    m['ev_w_in'] = np.ascontiguousarray(inp['ev_w_in'][0]); m['ev_wx'] = ev_wx(inp['ev_w_in'][0])
    m['ev_w_out'] = np.ascontiguousarray(inp['ev_w_out'][0])
    return m


def masks_const():
    idx = np.arange(128)
    blk = (idx[:, None] // 64) == (idx[None, :] // 64)
    s, t = idx[:, None], idx[None, :]
    M = np.stack([blk & (s <= t), blk & (s >= t), blk & (s < t), blk & (s > t)], axis=1)
    return np.ascontiguousarray(M.astype(np.float32))


def odd_inputs(vp, inp, core, m):
    prompt = core < 4
    V = m['vecs']
    def put(name, arr):
        c0, n = vp.cols[name]
        assert arr.shape == (128, n), (name, arr.shape, n)
        V[:, c0:c0+n] = arr
    lb = inp['hgrn_lb_logits']
    put('lb0', fm(lb[:, 0, :].reshape(-1)))
    put('lb1', fm(lb[:, 1, :].reshape(-1)))
    mu = inp['rwkv_mu'][0]
    MU = np.zeros((128, 15), np.float32)
    MU[:, 0:13] = fm(mu[0:1664])
    MU[0:64, 13] = mu[1664:1728]
    MU[:, 14] = mu[1728:1856]
    put('mu', MU)
    put('w0', fm(inp['rwkv_w0'][0].reshape(-1)))
    put('a0', fm(inp['rwkv_a0'][0]))
    put('k_k', fm(inp['rwkv_k_k'][0]))
    put('k_a', fm(inp['rwkv_k_a'][0]))
    put('r_k', fm(inp['rwkv_r_k'][0]))
    m['od_w_in'] = np.ascontiguousarray(inp['od_w_in'][0])
    m['od_w_out'] = np.ascontiguousarray(inp['od_w_out'][0])
    m['masks'] = masks_const()
    b = np.concatenate([inp['hgrn_norm_g'][0], inp['rwkv_ln_g'][0], inp['rwkv_ln_b'][0], np.zeros(128, np.float32)]).astype(np.float32)
    m['bv1'] = np.ascontiguousarray(np.broadcast_to(b[None, :], (128, 1280)))
    if prompt:
        m['st_h'] = np.zeros((2, 4, 128, 128), np.float32)
        m['st_r'] = np.zeros((2, 8, 64, 64), np.float32)
    else:
        bb = core - 4
        m['st_h'] = np.ascontiguousarray(inp['state_hgrn'][bb, 0])
        m['st_r'] = np.ascontiguousarray(inp['state_rwkv'][bb, 0])
    m['w_up'] = np.ascontiguousarray(inp['rwkv_w_up'][0].reshape(128, 512))
    m['a_up'] = np.ascontiguousarray(inp['rwkv_a_up'][0])
    m['g_up'] = np.ascontiguousarray(inp['rwkv_g_up'][0])
    return m


_BUILT = {}


def kernel(**inputs):
    inp = {k: np.asarray(v) for k, v in inputs.items()}
    B = Builder(debug=(), layers=(0, 1))
    B.build()
    maps = []
    for core in range(8):
        m = core_inputs(B.vp, inp, core)
        m = odd_inputs(B.vp, inp, core, m)
        maps.append(m)
    res = run_bass_kernel_spmd(B.nc, maps, core_ids=list(range(8)))
    R = res.results
    y_prompt = np.zeros((16, 256, 1024), np.float32)
    y_sample = np.zeros((4, 1024, 1024), np.float32)
    ak = np.zeros((16, 1, 4, 256, 128), np.float32)
    av = np.zeros((16, 1, 4, 256, 128), np.float32)
    bk = np.zeros((16, 1, 2, 256, 64), np.float32)
    bv = np.zeros((16, 1, 2, 256, 64), np.float32)
    sh = np.zeros((16, 1, 2, 4, 128, 128), np.float32)
    sr = np.zeros((16, 1, 2, 8, 64, 64), np.float32)
    for c in range(4):
        r = R[c]
        y_prompt[4 * c:4 * c + 4] = np.asarray(r['y']).reshape(4, 256, 1024)
        ak[4 * c:4 * c + 4, 0] = np.asarray(r['o_ak'])
        av[4 * c:4 * c + 4, 0] = np.asarray(r['o_av'])
        bk[4 * c:4 * c + 4, 0] = np.asarray(r['o_bk'])
        bv[4 * c:4 * c + 4, 0] = np.asarray(r['o_bv'])
        sh[4 * c:4 * c + 4, 0] = np.asarray(r['o_sh'])
        sr[4 * c:4 * c + 4, 0] = np.asarray(r['o_sr'])
    for b in range(4):
        y_sample[b] = np.asarray(R[4 + b]['y'])
    return (y_prompt, y_sample, ak, av, bk, bv, sh, sr)
```

```python
import numpy as np
from contextlib import ExitStack
import concourse.bass as bass
import concourse.mybir as mybir
from concourse.bass_utils import run_bass_kernel_spmd

F32 = mybir.dt.float32
BF16 = mybir.dt.bfloat16
AF = mybir.ActivationFunctionType
ALU = mybir.AluOpType
AX = mybir.AxisListType

ENGS = ('pe', 'dve', 'act', 'pool', 'sp')
NDS = 16


class Op:
    __slots__ = ('eng', 'fn', 'reads', 'writes', 'deps', 'need_inc', 'semkey', 'count', 'dma', 'prev_dma')

    def __init__(self, eng, fn, reads, writes, dma):
        self.eng = eng
        self.fn = fn
        self.reads = reads
        self.writes = writes
        self.deps = []
        self.need_inc = False
        self.semkey = None
        self.count = 0
        self.dma = dma
        self.prev_dma = None


class Prog:
    def __init__(self):
        self.nc = bass.Bass("TRN2", target_bir_lowering=False)
        nc = self.nc
        self.E = {'pe': nc.tensor, 'dve': nc.vector, 'act': nc.scalar, 'pool': nc.gpsimd, 'sp': nc.sync}
        self.es = ExitStack()
        self.sems = {}
        for e in ENGS:
            self.sems[e] = self.es.enter_context(nc.semaphore("sem_" + e))
        for q in ('sp', 'pool', 'act'):
            for i in range(NDS):
                self.sems[('dma', q, i)] = self.es.enter_context(nc.semaphore("dsem_%s_%d" % (q, i)))
        self.cnt = {k: 0 for k in self.sems}
        self.dma_rr = {'sp': 0, 'pool': 0, 'act': 0}
        self.last_dma_on_sem = {}
        self.seen = {e: {} for e in ENGS}
        self.pending = []
        self.last_writer = {}
        self.readers = {}
        self.n_ins = 0
        self.excl = set()
        self.alias = {}

    def op(self, eng, fn, r=(), w=()):
        self.pending.append(Op(eng, fn, tuple(r), tuple(w), False))

    def dma(self, q, fn, r=(), w=()):
        self.pending.append(Op(q, fn, tuple(r), tuple(w), True))

    def _wait(self, eng, key, val):
        if val <= 0:
            return
        if self.seen[eng].get(key, 0) >= val:
            return
        self.E[eng].wait_ge(self.sems[key], val)
        self.n_ins += 1
        self.seen[eng][key] = val

    def flush(self, barrier=True):
        ops = self.pending
        self.pending = []
        lw, rd = self.last_writer, self.readers
        for op in ops:
            deps = []
            if self.alias:
                op.reads = tuple(self.alias.get(b, b) for b in op.reads)
                op.writes = tuple(self.alias.get(b, b) for b in op.writes)
            ex = [b for b in op.reads if (b[0] if isinstance(b, tuple) else b) in self.excl]
            if ex:
                op.writes = tuple(op.writes) + tuple(b for b in ex if b not in op.writes)
                op.reads = tuple(b for b in op.reads if b not in ex)
            for b in op.reads:
                d = lw.get(b)
                if d is not None:
                    deps.append(d)
            for b in op.writes:
                d = lw.get(b)
                if d is not None:
                    deps.append(d)
                deps.extend(rd.get(b, ()))
            op.deps = [d for d in deps if d is not op]
            for d in op.deps:
                d.need_inc = True
            for b in op.reads:
                rd.setdefault(b, []).append(op)
            for b in op.writes:
                lw[b] = op
                rd[b] = []
        if barrier:
            last = {}
            for op in ops:
                last[op.eng] = op
            for op in last.values():
                op.need_inc = True
        for op in ops:
            if op.dma:
                i = self.dma_rr[op.eng] % NDS
                self.dma_rr[op.eng] += 1
                key = ('dma', op.eng, i)
                op.semkey = key
                self.cnt[key] += 16
                op.count = self.cnt[key]
                op.prev_dma = self.cnt[key] - 16
            elif op.need_inc:
                op.semkey = op.eng
                self.cnt[op.eng] += 1
                op.count = self.cnt[op.eng]
        for op in ops:
            need = {}
            for d in op.deps:
                if d.eng == 'pe' and op.eng == 'pe' and not d.dma and not op.dma:
                    continue
                k = d.semkey
                if need.get(k, 0) < d.count:
                    need[k] = d.count
            if op.dma and op.prev_dma:
                k = op.semkey
                if need.get(k, 0) < op.prev_dma:
                    need[k] = op.prev_dma
            for k, v in need.items():
                self._wait(op.eng, k, v)
            ins = op.fn()
            self.n_ins += 1
            if op.dma:
                ins.then_inc(self.sems[op.semkey], 16)
            elif op.need_inc:
                ins.then_inc(self.sems[op.eng], 1)
            op.fn = None
        if barrier:
            self.barrier()

    def barrier(self):
        for k, v in self.cnt.items():
            if k == 'sp':
                continue
            self._wait('sp', k, v)
        ins = self.E['sp'].nop()
        self.cnt['sp'] += 1
        ins.then_inc(self.sems['sp'], 1)
        for e in ENGS:
            if e != 'sp':
                self._wait(e, 'sp', self.cnt['sp'])
            for k, v in self.cnt.items():
                self.seen[e][k] = v
        self.last_writer = {}
        self.readers = {}


T = 1024
D = 1024
NCH = 8
DFF = 2816
NFF = 22
EPS = 1e-6


class VecPack:
    def __init__(self):
        self.cols = {}
        self.n = 0

    def add(self, name, ncols):
        self.cols[name] = (self.n, ncols)
        self.n += ncols
        return self.cols[name]


def build_vec_layout():
    vp = VecPack()
    vp.add('cond', 8)
    vp.add('km1', 1)
    vp.add('keep', 1)
    for l in range(2):
        vp.add('ada_b%d' % l, 48)
        vp.add('nmg%d' % l, 8)
        vp.add('nfg%d' % l, 8)
        vp.add('cw0_%d' % l, 44)
        vp.add('cw1_%d' % l, 44)
        vp.add('cw2_%d' % l, 44)
        vp.add('cb_%d' % l, 44)
    vp.add('fng', 8)
    vp.add('gq', 1); vp.add('gq_sw', 1); vp.add('gk', 1); vp.add('gk_sw', 1)
    vp.add('amask', 48)
    vp.add('lb0', 8); vp.add('lb1', 8); vp.add('mu', 15); vp.add('w0', 8)
    vp.add('a0', 4); vp.add('k_k', 4); vp.add('k_a', 4); vp.add('r_k', 4)
    return vp


def fm(v):
    v = np.asarray(v, dtype=np.float32)
    return np.ascontiguousarray(v.reshape(-1, 128).T)


class Builder:
    def __init__(self, debug=(), layers=(0, 1), do_mix=True):
        self.layers = layers
        import os
        self.stop = os.environ.get('KSTOP', '')
        self.skip = set(os.environ.get('KSKIP', '').split(','))
        self.do_mix = do_mix
        self.P = Prog()
        self.nc = self.P.nc
        self.debug = debug
        self.vp = build_vec_layout()
        self.dbg_outs = {}
        self.P.excl.update(['tp', 'mps', 'ssp', 'ups', 'ftp', 'pq', 'pqs', 'pss', 'ptm', 'ptp', 'sT', 'acc', 'otp', 'pmx'])

    def dram_in(self, name, shape, dt=F32):
        return self.nc.dram_tensor(name, list(shape), dt, kind="ExternalInput").ap()

    def dram_out(self, name, shape, dt=F32):
        return self.nc.dram_tensor(name, list(shape), dt, kind="ExternalOutput").ap()

    def sb(self, es, name, shape, dt):
        self.uid = getattr(self, 'uid', 0) + 1
        return es.enter_context(self.nc.sbuf_tensor("sb%d_%s" % (self.uid, name), list(shape), dt))

    def ps(self, es, name, shape, dt=F32):
        self.uid = getattr(self, 'uid', 0) + 1
        return es.enter_context(self.nc.psum_tensor("ps%d_%s" % (self.uid, name), list(shape), dt))

    def vcol(self, name, j=0, n=1):
        c0, nc_ = self.vp.cols[name]
        return self.vecs[:, c0 + j:c0 + j + n]

    def dump(self, name, sbt, shape, reads):
        if name not in self.debug:
            return
        P, nc = self.P, self.nc
        P.flush()
        dt = sbt.dtype if hasattr(sbt, 'dtype') else F32
        o = self.dram_out("dbg_" + name, shape, dt)
        self.dbg_outs[name] = (shape, dt)
        P.dma('sp', lambda: nc.sync.dma_start(out=o, in_=sbt[:]), r=reads, w=[('dbg', name)])

    def build(self):
        P, nc = self.P, self.nc
        top = ExitStack()
        self.top = top
        self.x_d = self.dram_in("x", [T, D])
        self.vecs_d = self.dram_in("vecs", [128, self.vp.n])
        self.ident_d = self.dram_in("ident", [128, 128])
        self.ada_w_d = self.dram_in("ada_w", [2, D, 6 * D])
        self.ffn_up_d = self.dram_in("ffn_w_up", [2, D, 2 * DFF])
        self.ffn_dn_d = self.dram_in("ffn_w_down", [2, DFF, D])
        self.y_d = self.dram_out("y", [T, D])
        self.ev_w_in_d = self.dram_in("ev_w_in", [D, 2304])
        self.ev_wx_d = self.dram_in("ev_wx", [D, 2176])
        self.ev_w_out_d = self.dram_in("ev_w_out", [D, D])
        self.rope_d = self.dram_in("rope", [2, 128, T])
        self.bvec_d = self.dram_in("bvec", [128, 448])
        self.ctx_ak_d = self.dram_in("ctx_ak", [4, 512, 128])
        self.ctx_av_d = self.dram_in("ctx_av", [4, 512, 128])
        self.ctx_bk_d = self.dram_in("ctx_bk", [2, 512, 64])
        self.ctx_bv_d = self.dram_in("ctx_bv", [2, 512, 64])
        self.o_ak_d = self.dram_out("o_ak", [4, 4, 256, 128])
        self.o_av_d = self.dram_out("o_av", [4, 4, 256, 128])
        self.o_bk_d = self.dram_out("o_bk", [4, 2, 256, 64])
        self.o_bv_d = self.dram_out("o_bv", [4, 2, 256, 64])
        self.od_w_in_d = self.dram_in("od_w_in", [D, 4416])
        self.od_w_out_d = self.dram_in("od_w_out", [D, D])
        self.masks_d = self.dram_in("masks", [128, 4, 128])
        self.bv1_d = self.dram_in("bv1", [128, 1280])
        self.st_h_d = self.dram_in("st_h", [2, 4, 128, 128])
        self.st_r_d = self.dram_in("st_r", [2, 8, 64, 64])
        self.w_up_d = self.dram_in("w_up", [128, 512])
        self.a_up_d = self.dram_in("a_up", [64, 512])
        self.g_up_d = self.dram_in("g_up", [128, 512])
        self.o_sh_d = self.dram_out("o_sh", [4, 2, 4, 128, 128])
        self.o_sr_d = self.dram_out("o_sr", [4, 2, 8, 64, 64])
        self.xT = self.sb(top, "xT", [128, NCH, T], F32)
        self.hT = self.sb(top, "hT", [128, NCH, T], BF16)
        self.vecs = self.sb(top, "vecs", [128, self.vp.n], F32)
        self.ident_f = self.sb(top, "ident_f", [128, 128], F32)
        self.ident_b = self.sb(top, "ident_b", [128, 128], BF16)
        self.ones_b = self.sb(top, "ones_b", [128, 128], BF16)
        self.mod = [self.sb(top, "mod%d" % l, [128, 48], F32) for l in range(2)]
        self.gm1 = [self.sb(top, "gm1_%d" % l, [128, 8], F32) for l in range(2)]
        self.gm2 = [self.sb(top, "gm2_%d" % l, [128, 8], F32) for l in range(2)]
        self.wk0 = [self.sb(top, "wk0_%d" % l, [128, 44], F32) for l in range(2)]
        self.wk2 = [self.sb(top, "wk2_%d" % l, [128, 44], F32) for l in range(2)]
        self.rstd = self.sb(top, "rstd", [128, T], F32)
        self.epsc = self.sb(top, "epsc", [128, 4], F32)
        self.bd_ones = self.sb(top, "bd_ones", [128, 128], BF16)
        self.NWB = 0
        self.wbuf = []
        self.wrr = 0

        self.phase_init()
        for l in range(2):
            if l in self.layers:
                if l == 0 and self.do_mix:
                    self.even_mixer()
                elif l == 1 and self.do_mix:
                    self.odd_mixer()
                else:
                    self.phase_mix(l)
                self.phase_ffn(l)
        self.phase_final()
        top.close()
        P.es.close()

    def phase_init(self):
        P, nc = self.P, self.nc
        with ExitStack() as es:
            xin = [self.sb(es, "xin%d" % i, [128, D], F32) for i in range(2)]
            scond = self.sb(es, "scond", [128, 8], BF16)
            abuf = [self.sb(es, "abuf%d" % i, [128, 6 * D], BF16) for i in range(2)]
            tp = [self.ps(es, "tp%d" % i, [128, 512]) for i in range(2)]
            mps = self.ps(es, "mps", [128, 48])

            P.dma('sp', lambda: nc.sync.dma_start(out=self.vecs[:], in_=self.vecs_d), w=['vecs'])
            P.dma('sp', lambda: nc.sync.dma_start(out=self.ident_f[:], in_=self.ident_d), w=['ident_f'])
            P.op('dve', lambda: nc.vector.tensor_copy(out=self.ident_b[:], in_=self.ident_f[:]), r=['ident_f'], w=['ident_b'])
            P.op('pool', lambda: nc.gpsimd.memset(self.ones_b[:], 1.0), w=['ones_b'])
            P.op('pool', lambda: nc.gpsimd.memset(self.epsc[:, 0:1], EPS), w=['epsc'])
            P.op('pool', lambda: nc.gpsimd.memset(self.epsc[:, 1:2], 1e-24), w=['epsc'])
            P.op('pool', lambda: nc.gpsimd.memset(self.epsc[:, 2:3], GN_EPS), w=['epsc'])
            P.op('pool', lambda: nc.gpsimd.memset(self.bd_ones[:], 0.0), w=['bd_ones'])
            P.op('pool', lambda: nc.gpsimd.memset(self.bd_ones[0:64, 0:64], 1.0), w=['bd_ones'])
            P.op('pool', lambda: nc.gpsimd.memset(self.bd_ones[64:128, 64:128], 1.0), w=['bd_ones'])
            for tt in range(8):
                xi = xin[tt % 2]
                P.dma('sp', lambda xi=xi, tt=tt: nc.sync.dma_start(out=xi[:], in_=self.x_d[tt * 128:(tt + 1) * 128, :]),
                      w=[('xin', tt % 2)])
                for g in range(2):
                    tpp = tp[g]
                    for cc in range(4):
                        c = g * 4 + cc
                        P.op('pe', lambda xi=xi, tpp=tpp, cc=cc, c=c: nc.tensor.transpose(
                            tpp[:, cc * 128:(cc + 1) * 128], xi[:, c * 128:(c + 1) * 128], self.ident_f[:]),
                            r=[('xin', tt % 2), 'ident_f'], w=[('tp', g)])
                    eng = 'act' if g == 0 else 'dve'
                    if g == 0:
                        P.op('act', lambda tpp=tpp, g=g, tt=tt: nc.scalar.copy(
                            out=self.xT[:, g * 4:(g + 1) * 4, tt * 128:(tt + 1) * 128],
                            in_=tpp[:].rearrange("p (c t) -> p c t", c=4)),
                            r=[('tp', g)], w=[('xT', c_) for c_ in range(g * 4, g * 4 + 4)])
                    else:
                        P.op('dve', lambda tpp=tpp, g=g, tt=tt: nc.vector.tensor_copy(
                            out=self.xT[:, g * 4:(g + 1) * 4, tt * 128:(tt + 1) * 128],
                            in_=tpp[:].rearrange("p (c t) -> p c t", c=4)),
                            r=[('tp', g)], w=[('xT', c_) for c_ in range(g * 4, g * 4 + 4)])
            P.op('act', lambda: nc.scalar.activation(out=scond[:], in_=self.vcol('cond', 0, 8), func=AF.Silu),
                 r=['vecs'], w=['scond'])
            for l in range(2):
                for kc in range(8):
                    ab = abuf[kc % 2]
                    P.dma('pool', lambda ab=ab, l=l, kc=kc: nc.gpsimd.dma_start(
                        out=ab[:], in_=self.ada_w_d[l, kc * 128:(kc + 1) * 128, :]), w=[('abuf', kc % 2)])

                    def mm(ab=ab, kc=kc):
                        ins = None
                        for col in range(48):
                            ins = nc.tensor.matmul(mps[:, col:col + 1], ab[:, col * 128:(col + 1) * 128], scond[:, kc:kc + 1],
                                                   start=(kc == 0 and col == 0), stop=(kc == 7), skip_group_check=True)
                        return ins
                    P.op('pe', mm, r=[('abuf', kc % 2), 'scond'], w=['mps'])
                md = self.mod[l]
                P.op('dve', lambda md=md, l=l: nc.vector.tensor_tensor(out=md[:], in0=mps[:], in1=self.vcol('ada_b%d' % l, 0, 48),
                                                                       op=ALU.add), r=['mps', 'vecs'], w=[('mod', l)])
                P.op('dve', lambda md=md, l=l: nc.vector.scalar_tensor_tensor(
                    out=self.gm1[l][:], in0=md[:, 8:16], scalar=1.0, in1=self.vcol('nmg%d' % l, 0, 8),
                    op0=ALU.add, op1=ALU.mult), r=[('mod', l), 'vecs'], w=[('gm1', l)])
                P.op('dve', lambda md=md, l=l: nc.vector.scalar_tensor_tensor(
                    out=self.gm2[l][:], in0=md[:, 32:40], scalar=1.0, in1=self.vcol('nfg%d' % l, 0, 8),
                    op0=ALU.add, op1=ALU.mult), r=[('mod', l), 'vecs'], w=[('gm2', l)])
                P.op('dve', lambda l=l: nc.vector.tensor_scalar(
                    out=self.wk0[l][:], in0=self.vcol('cw0_%d' % l, 0, 44), scalar1=self.vcol('km1'), scalar2=None,
                    op0=ALU.mult), r=['vecs'], w=[('wk0', l)])
                P.op('dve', lambda l=l: nc.vector.tensor_scalar(
                    out=self.wk2[l][:], in0=self.vcol('cw2_%d' % l, 0, 44), scalar1=self.vcol('km1'), scalar2=None,
                    op0=ALU.mult), r=['vecs'], w=[('wk2', l)])
            self.dump('mod0', self.mod[0], [128, 48], [('mod', 0)])
            self.dump('xT', self.xT, [128, NCH, T], [('xT', c) for c in range(8)])
            P.flush()

    def rmsnorm(self, es, gm, sh, out_fn):
        P, nc = self.P, self.nc
        sq = self.sb(es, "sq", [128, NCH, T], BF16)
        tmp = [self.sb(es, "ntmp%d" % i, [128, T], F32) for i in range(2)]
        ssp = [self.ps(es, "ssp%d" % i, [128, 512]) for i in range(2)]
        for c in range(8):
            P.op('act', lambda c=c: nc.scalar.activation(out=sq[:, c, :], in_=self.xT[:, c, :], func=AF.Square),
                 r=[('xT', c)], w=[('sq', c)])
        for th in range(2):
            def mm(th=th):
                ins = None
                for c in range(8):
                    ins = nc.tensor.matmul(ssp[th][:], self.ones_b[:], sq[:, c, th * 512:(th + 1) * 512],
                                           start=(c == 0), stop=(c == 7))
                return ins
            P.op('pe', mm, r=[('sq', c) for c in range(8)] + ['ones_b'], w=[('ssp', th)])
            P.op('act', lambda th=th: nc.scalar.activation(
                out=self.rstd[:, th * 512:(th + 1) * 512], in_=ssp[th][:], func=AF.Sqrt, scale=1.0 / D, bias=self.epsc[:, 0:1]),
                r=[('ssp', th), 'epsc'], w=[('rstd', th)])
            P.op('dve', lambda th=th: nc.vector.reciprocal(
                out=self.rstd[:, th * 512:(th + 1) * 512], in_=self.rstd[:, th * 512:(th + 1) * 512]),
                r=[('rstd', th)], w=[('rstd', th)])
        for c in range(8):
            tm = tmp[c % 2]
            P.op('dve', lambda c=c, tm=tm: nc.vector.tensor_tensor(out=tm[:], in0=self.xT[:, c, :], in1=self.rstd[:],
                                                                   op=ALU.mult),
                 r=[('xT', c), ('rstd', 0), ('rstd', 1)], w=[('ntmp', c % 2)])
            out_ap, wkeys = out_fn(c)
            bias = sh(c) if sh is not None else 0.0
            P.op('act', lambda c=c, tm=tm, out_ap=out_ap, bias=bias: nc.scalar.activation(
                out=out_ap, in_=tm[:], func=AF.Identity, scale=gm(c), bias=bias),
                r=[('ntmp', c % 2), 'gmsh'], w=wkeys)

    def alloc_w(self, es, n, elems):
        self.NWB = n
        self.wbuf = [self.sb(es, "wbuf%d" % i, [128, elems], BF16) for i in range(n)]
        self.wrr = 0

    def load_w(self, src_ap, view, key_extra=None):
        P, nc = self.P, self.nc
        i = self.wrr % self.NWB
        self.wrr += 1
        a, b = view
        dst = self.wbuf[i][:, 0:a * b].rearrange("p (a b) -> p a b", a=a)
        P.dma('pool', lambda: nc.gpsimd.dma_start(out=dst, in_=src_ap), w=[('wbuf', i)])
        return dst, ('wbuf', i)

    def phase_mix(self, l):
        P, nc = self.P, self.nc
        with ExitStack() as es:
            self.rmsnorm(es, lambda c: self.gm1[l][:, c:c + 1], lambda c: self.mod[l][:, c:c + 1],
                         lambda c: (self.hT[:, c, :], [('hT', c)]))
            if l == 0:
                self.dump('h0T', self.hT, [128, NCH, T], [('hT', c) for c in range(8)])
            P.flush()

    def phase_ffn(self, l):
        P, nc = self.P, self.nc
        with ExitStack() as es:
            self.alloc_w(es, 6, 4096)
            pre_w = []
            for half in range(2):
                src = self.ffn_up_d[l, :, half * DFF:half * DFF + 512].rearrange("(k p) n -> p k n", p=128)
                pre_w.append(self.load_w(src, (8, 512)))
            with ExitStack() as es2:
                self.rmsnorm(es2, lambda c: self.gm2[l][:, c:c + 1], lambda c: self.mod[l][:, 24 + c:25 + c],
                             lambda c: (self.hT[:, c, :], [('hT', c)]))
                P.flush()
            gT = self.sb(es, "gT", [128, NFF, T], BF16)
            cv = [self.sb(es, "cv%d" % i, [128, T], F32) for i in range(4)]
            sg = [self.sb(es, "sg%d" % i, [128, T], F32) for i in range(2)]
            ups = [self.ps(es, "ups%d" % i, [128, T]) for i in range(4)]
            cw = lambda nm, j: self.vcol('%s_%d' % (nm, l), j)
            for j in range(NFF):
                g_, jj_ = j // 4, j % 4
                if jj_ == 0 and g_ == 0:
                    wts = pre_w
                elif jj_ == 0:
                    ncol_ = 512 if g_ < 5 else 256
                    wts = []
                    for half in range(2):
                        c0 = half * DFF + g_ * 512
                        src = self.ffn_up_d[l, :, c0:c0 + ncol_].rearrange("(k p) n -> p k n", p=128)
                        wts.append(self.load_w(src, (8, ncol_)))
                for half in range(2):
                    wt, wkey = wts[half]
                    pi = (j % 2) * 2 + half
                    up = ups[pi]
                    cvb = cv[pi]
                    jj = half * NFF + j
                    for th in range(2):
                        def mm(wt=wt, up=up, th=th, jj_=jj_):
                            ins = None
                            for k in range(8):
                                ins = nc.tensor.matmul(up[:, th * 512:(th + 1) * 512], wt[:, k, jj_ * 128:(jj_ + 1) * 128],
                                                       self.hT[:, k, th * 512:(th + 1) * 512],
                                                       start=(k == 0), stop=(k == 7))
                            return ins
                        P.op('pe', mm, r=[wkey] + [('hT', k) for k in range(8)], w=[('ups', pi, th)])
                    ur = [('ups', pi, 0), ('ups', pi, 1)]
                    ck = ('cv', pi)
                    P.op('act', lambda up=up, cvb=cvb, jj=jj: nc.scalar.activation(
                        out=cvb[:], in_=up[:], func=AF.Identity, scale=cw('cw1', jj), bias=cw('cb', jj)),
                        r=ur + ['vecs'], w=[ck])
                    P.op('dve', lambda up=up, cvb=cvb, jj=jj: nc.vector.scalar_tensor_tensor(
                        out=cvb[:, 1:T], in0=up[:, 0:T - 1], scalar=cw('cw0', jj), in1=cvb[:, 1:T],
                        op0=ALU.mult, op1=ALU.add), r=ur + ['vecs', ck], w=[ck])
                    P.op('dve', lambda up=up, cvb=cvb, jj=jj: nc.vector.scalar_tensor_tensor(
                        out=cvb[:, 0:T - 1], in0=up[:, 1:T], scalar=cw('cw2', jj), in1=cvb[:, 0:T - 1],
                        op0=ALU.mult, op1=ALU.add), r=ur + ['vecs', ck], w=[ck])
                    P.op('dve', lambda up=up, cvb=cvb, jj=jj: nc.vector.scalar_tensor_tensor(
                        out=cvb[:, 256:T:256], in0=up[:, 255:T - 1:256], scalar=self.wk0[l][:, jj:jj + 1],
                        in1=cvb[:, 256:T:256], op0=ALU.mult, op1=ALU.add), r=ur + [('wk0', l), ck], w=[ck])
                    P.op('dve', lambda up=up, cvb=cvb, jj=jj: nc.vector.scalar_tensor_tensor(
                        out=cvb[:, 255:T - 1:256], in0=up[:, 256:T:256], scalar=self.wk2[l][:, jj:jj + 1],
                        in1=cvb[:, 255:T - 1:256], op0=ALU.mult, op1=ALU.add), r=ur + [('wk2', l), ck], w=[ck])
                pv = (j % 2) * 2
                sgb = sg[j % 2]
                P.op('act', lambda sgb=sgb, pv=pv: nc.scalar.activation(out=sgb[:], in_=cv[pv + 1][:], func=AF.Silu),
                     r=[('cv', pv + 1)], w=[('sg', j % 2)])
                P.op('dve', lambda sgb=sgb, pv=pv, j=j: nc.vector.tensor_tensor(
                    out=gT[:, j, :], in0=sgb[:], in1=cv[pv][:], op=ALU.mult),
                    r=[('sg', j % 2), ('cv', pv)], w=[('gT', j)])
            if l == 0:
                self.dump('gT0', gT, [128, NFF, T], [('gT', j) for j in range(NFF)])
            dps = [ups[0], ups[1]]
            for c in range(8):
                src = self.ffn_dn_d[l, :, c * 128:(c + 1) * 128].rearrange("(k p) n -> p k n", p=128)
                wt, wkey = self.load_w(src, (NFF, 128))
                for th in range(2):
                    pi = th
                    dp = ups[c % 2][:, th * 512:(th + 1) * 512]

                    def mm(wt=wt, dp=dp, th=th):
                        ins = None
                        for k in range(NFF):
                            ins = nc.tensor.matmul(dp, wt[:, k, :], gT[:, k, th * 512:(th + 1) * 512],
                                                   start=(k == 0), stop=(k == NFF - 1))
                        return ins
                    P.op('pe', mm, r=[wkey] + [('gT', k) for k in range(NFF)], w=[('ups', c % 2, th)])
                    P.op('dve', lambda dp=dp, c=c, th=th: nc.vector.scalar_tensor_tensor(
                        out=self.xT[:, c, th * 512:(th + 1) * 512], in0=dp, scalar=self.mod[l][:, 40 + c:41 + c],
                        in1=self.xT[:, c, th * 512:(th + 1) * 512], op0=ALU.mult, op1=ALU.add),
                        r=[('ups', c % 2, th), ('xT', c), ('mod', l)], w=[('xT', c)])
            P.flush()

    def phase_final(self):
        P, nc = self.P, self.nc
        with ExitStack() as es:
            yT = self.sb(es, "yT", [128, NCH, T], F32)
            self.rmsnorm(es, lambda c: self.vcol('fng', c), None, lambda c: (yT[:, c, :], [('yT', c)]))
            yo = [self.sb(es, "yo%d" % i, [128, D], F32) for i in range(2)]
            tp = [self.ps(es, "ftp%d" % i, [128, 512]) for i in range(2)]
            for tt in range(8):
                yb = yo[tt % 2]
                for g in range(2):
                    for cc in range(4):
                        c = g * 4 + cc
                        P.op('pe', lambda tt=tt, g=g, cc=cc, c=c: nc.tensor.transpose(
                            tp[g][:, cc * 128:(cc + 1) * 128], yT[:, c, tt * 128:(tt + 1) * 128], self.ident_f[:]),
                            r=[('yT', c), 'ident_f'], w=[('ftp', g)])
                    if g == 0:
                        P.op('act', lambda yb=yb, g=g: nc.scalar.copy(out=yb[:, g * 512:(g + 1) * 512], in_=tp[g][:]),
                             r=[('ftp', g)], w=[('yo', tt % 2, g)])
                    else:
                        P.op('dve', lambda yb=yb, g=g: nc.vector.tensor_copy(out=yb[:, g * 512:(g + 1) * 512], in_=tp[g][:]),
                             r=[('ftp', g)], w=[('yo', tt % 2, g)])
                P.dma('sp', lambda yb=yb, tt=tt: nc.sync.dma_start(out=self.y_d[tt * 128:(tt + 1) * 128, :], in_=yb[:]),
                      r=[('yo', tt % 2, 0), ('yo', tt % 2, 1)], w=[('y', tt)])
            P.flush()


def _even_mixer(self):
    P, nc = self.P, self.nc
    l = 0
    SC = 0.125
    with ExitStack() as es:
        qaT = self.sb(es, "qaT", [128, 4, T], BF16)
        kaT = self.sb(es, "kaT", [128, 4, 512 + T], BF16)
        qbT = self.sb(es, "qbT", [128, 4, T], BF16)
        kbT = self.sb(es, "kbT", [128, 512 + T], BF16)
        vA = self.sb(es, "vA", [128, 12, 4, 130], BF16)
        vB = self.sb(es, "vB", [128, 12, 2, 66], BF16)
        ocat = self.sb(es, "ocat", [128, 8, D], BF16)
        bvec = self.sb(es, "bvec", [128, 448], F32)
        nlam = self.sb(es, "nlam", [128, 1], F32)
        gsub8 = self.sb(es, "gsub8", [128, 128], F32)
        with ExitStack() as e0:
            self.rmsnorm(e0, lambda c: self.gm1[l][:, c:c + 1], lambda c: self.mod[l][:, c:c + 1],
                         lambda c: (self.hT[:, c, :], [('hT', c)]))
            P.flush()
        with ExitStack() as e1:
            self.alloc_w(e1, 4, 4096)
            cosT = self.sb(e1, "cosT", [128, T], F32)
            sinT = self.sb(e1, "sinT", [128, T], F32)
            cak = self.sb(e1, "cak", [128, 4, 4, 128], BF16)
            cbk = self.sb(e1, "cbk", [128, 4, 128], BF16)
            t1 = [self.sb(e1, "rt1_%d" % i, [128, 512], F32) for i in range(2)]
            t2 = [self.sb(e1, "rt2_%d" % i, [128, 512], F32) for i in range(2)]
            sqb = self.sb(e1, "sqb", [128, 512], BF16)
            rsb = self.sb(e1, "rsb", [128, 512], F32)
            stg = [self.sb(e1, "stg%d" % i, [128, 512], F32) for i in range(2)]
            kbs = self.sb(e1, "kbs", [128, 128], F32)
            ssb = self.sb(e1, "ssb", [128, 2], F32)
            junk = self.sb(e1, "junk", [128, 128], F32)
            lpr = self.sb(e1, "lpr", [128, 2, 64], F32)
            lsum = self.sb(e1, "lsum", [128, 2], F32)
            pq = [self.ps(e1, "pq%d" % i, [128, 512]) for i in range(2)]
            pqs = [self.ps(e1, "pqs%d" % i, [128, 512]) for i in range(2)]
            pss = self.ps(e1, "pss", [128, 512])
            ptm = [self.ps(e1, "ptm%d" % i, [128, 512]) for i in range(2)]
            ptp = self.ps(e1, "ptp", [128, 1024], BF16)

            P.dma('sp', lambda: nc.sync.dma_start(out=cosT[:], in_=self.rope_d[0]), w=['cosT'])
            P.dma('sp', lambda: nc.sync.dma_start(out=sinT[:], in_=self.rope_d[1]), w=['sinT'])
            P.dma('sp', lambda: nc.sync.dma_start(out=bvec[:], in_=self.bvec_d), w=['bvec'])
            P.op('dve', lambda: nc.vector.tensor_tensor(
                out=lpr[:], in0=bvec[:, 192:448].rearrange("p (a b e) -> p a b e", a=2, b=2)[:, :, 0, :],
                in1=bvec[:, 192:448].rearrange("p (a b e) -> p a b e", a=2, b=2)[:, :, 1, :], op=ALU.mult),
                r=['bvec'], w=['lpr'])
            P.op('dve', lambda: nc.vector.reduce_sum(out=lsum[:], in_=lpr[:], axis=AX.X), r=['lpr'], w=['lsum'])
            P.op('act', lambda: nc.scalar.activation(out=lsum[:], in_=lsum[:], func=AF.Exp), r=['lsum'], w=['lsum'])
            P.op('dve', lambda: nc.vector.tensor_tensor(out=nlam[:], in0=lsum[:, 1:2], in1=lsum[:, 0:1], op=ALU.subtract),
                 r=['lsum'], w=['nlam'])
            P.op('dve', lambda: nc.vector.tensor_scalar_add(out=nlam[:], in0=nlam[:], scalar1=-0.2), r=['nlam'], w=['nlam'])
            P.op('dve', lambda: nc.vector.tensor_scalar_mul(out=gsub8[:], in0=bvec[:, 0:128], scalar1=0.8),
                 r=['bvec'], w=['gsub8'])
            P.op('pool', lambda: nc.gpsimd.memset(vA[:, :, :, 128:129], 1.0), w=['vA_ones'])
            P.op('pool', lambda: nc.gpsimd.memset(vB[:, :, :, 64:65], 1.0), w=['vB_ones'])
            if self.stop == 'B1l':
                P.flush(); return
            for kt in range(4):
                P.dma('pool', lambda kt=kt: nc.gpsimd.dma_start(
                    out=cak[:, kt], in_=self.ctx_ak_d[:, kt * 128:(kt + 1) * 128, :].rearrange("h p e -> p h e")),
                    w=[('cak', kt)])
                P.dma('pool', lambda kt=kt: nc.gpsimd.dma_start(
                    out=vA[:, kt, :, 0:128], in_=self.ctx_av_d[:, kt * 128:(kt + 1) * 128, :].rearrange("h p e -> p h e")),
                    w=[('vA', kt)])
                P.dma('pool', lambda kt=kt: nc.gpsimd.dma_start(
                    out=cbk[:, kt, :].rearrange("p (h e) -> p h e", h=2),
                    in_=self.ctx_bk_d[:, kt * 128:(kt + 1) * 128, :].rearrange("h p e -> p h e")), w=[('cbk', kt)])
                P.dma('pool', lambda kt=kt: nc.gpsimd.dma_start(
                    out=vB[:, kt, :, 0:64], in_=self.ctx_bv_d[:, kt * 128:(kt + 1) * 128, :].rearrange("h p e -> p h e")),
                    w=[('vB', kt)])
            for h in range(5):
                for kt in range(4):
                    src = cak[:, kt, h, :] if h < 4 else cbk[:, kt, :]
                    P.op('pe', lambda src=src, kt=kt: nc.tensor.transpose(ptp[:, kt * 128:(kt + 1) * 128], src, self.ident_b[:]),
                         r=[('cak', kt), ('cbk', kt), 'ident_b'], w=['ptp'])
                dst = kaT[:, h, 0:512] if h < 4 else kbT[:, 0:512]
                P.op('dve', lambda dst=dst: nc.vector.tensor_copy(out=dst, in_=ptp[:, 0:512]), r=['ptp'],
                     w=[('kaTc', h)])
            if self.stop == 'B1c':
                P.flush(); return
            chunks = []
            for a in range(4):
                chunks.append((lambda th, a=a: qaT[:, a, th * 512:(th + 1) * 512], (self.ev_w_in_d, a * 128),
                               (self.ev_wx_d, 512 + a * 128), None, ('qaT', a)))
            for a in range(4):
                chunks.append((lambda th, a=a: kaT[:, a, 512 + th * 512:512 + (th + 1) * 512], (self.ev_w_in_d, 512 + a * 128),
                               (self.ev_wx_d, 1024 + a * 128), None, ('kaT', a)))
            for c in range(4):
                chunks.append((lambda th, c=c: qbT[:, c, th * 512:(th + 1) * 512], (self.ev_wx_d, c * 128),
                               (self.ev_wx_d, 1536 + c * 128), ('gq', 'gq_sw'), ('qbT', c)))
            chunks.append((lambda th: kbT[:, 512 + th * 512:512 + (th + 1) * 512], (self.ev_w_in_d, 2048),
                           (self.ev_wx_d, 2048), ('gk', 'gk_sw'), ('kbT',)))
            it = 0
            for ci_, (dst_fn, (wd, c0), (wsd, cs0), gn, dkey) in enumerate(chunks):
                if ci_ % 4 == 0:
                    nb_ = 512 if ci_ < 12 else 128
                    wn_, wnk = self.load_w(wd[:, c0:c0 + nb_].rearrange("(k p) n -> p k n", p=128), (8, nb_))
                    ws_, wsk = self.load_w(wsd[:, cs0:cs0 + nb_].rearrange("(k p) n -> p k n", p=128), (8, nb_))
                wo_ = (ci_ % 4) * 128
                wn = wn_[:, :, wo_:wo_ + 128]
                ws = ws_[:, :, wo_:wo_ + 128]
                for th in range(2):
                    b = it % 2
                    it += 1
                    for (pp, ww, wk_, nm) in ((pq[b], wn, wnk, 'pq'), (pqs[b], ws, wsk, 'pqs')):
                        def mm(pp=pp, ww=ww, th=th):
                            ins = None
                            for k in range(8):
                                ins = nc.tensor.matmul(pp[:], ww[:, k, :], self.hT[:, k, th * 512:(th + 1) * 512],
                                                       start=(k == 0), stop=(k == 7))
                            return ins
                        P.op('pe', mm, r=[wk_] + [('hT', k) for k in range(8)], w=[(nm, b)])
                    dst = dst_fn(th)
                    if gn is None:
                        P.op('dve', lambda b=b, th=th: nc.vector.tensor_tensor(
                            out=t1[b][:], in0=pq[b][:], in1=cosT[:, th * 512:(th + 1) * 512], op=ALU.mult),
                            r=[('pq', b), 'cosT'], w=[('t1', b)])
                        P.op('dve', lambda b=b, th=th: nc.vector.tensor_tensor(
                            out=t2[b][:], in0=pqs[b][:], in1=sinT[:, th * 512:(th + 1) * 512], op=ALU.mult),
                            r=[('pqs', b), 'sinT'], w=[('t2', b)])
                        P.op('dve', lambda b=b, dst=dst: nc.vector.tensor_tensor(out=dst, in0=t1[b][:], in1=t2[b][:], op=ALU.add),
                             r=[('t1', b), ('t2', b)], w=[dkey + (th,)])
                    else:
                        P.op('act', lambda b=b: nc.scalar.activation(out=sqb[:], in_=pq[b][:], func=AF.Square),
                             r=[('pq', b)], w=['sqb'])
                        P.op('pe', lambda: nc.tensor.matmul(pss[:], self.bd_ones[:], sqb[:], start=True, stop=True),
                             r=['sqb', 'bd_ones'], w=['pss'])
                        P.op('act', lambda: nc.scalar.activation(out=rsb[:], in_=pss[:], func=AF.Ln, scale=1.0 / 64,
                                                                 bias=self.epsc[:, 0:1]), r=['pss', 'epsc'], w=['rsb'])
                        P.op('act', lambda: nc.scalar.activation(out=rsb[:], in_=rsb[:], func=AF.Exp, scale=-0.5),
                             r=['rsb'], w=['rsb'])
                        P.op('dve', lambda b=b, gn=gn: nc.vector.scalar_tensor_tensor(
                            out=t1[b][:], in0=pq[b][:], scalar=self.vcol(gn[0]), in1=rsb[:], op0=ALU.mult, op1=ALU.mult),
                            r=[('pq', b), 'rsb', 'vecs'], w=[('t1', b)])
                        P.op('dve', lambda b=b, gn=gn: nc.vector.scalar_tensor_tensor(
                            out=t2[b][:], in0=pqs[b][:], scalar=self.vcol(gn[1]), in1=rsb[:], op0=ALU.mult, op1=ALU.mult),
                            r=[('pqs', b), 'rsb', 'vecs'], w=[('t2', b)])
                        P.op('dve', lambda b=b, th=th: nc.vector.tensor_tensor(
                            out=t1[b][:], in0=t1[b][:], in1=cosT[:, th * 512:(th + 1) * 512], op=ALU.mult),
                            r=[('t1', b), 'cosT'], w=[('t1', b)])
                        P.op('dve', lambda b=b, th=th: nc.vector.tensor_tensor(
                            out=t2[b][:], in0=t2[b][:], in1=sinT[:, th * 512:(th + 1) * 512], op=ALU.mult),
                            r=[('t2', b), 'sinT'], w=[('t2', b)])
                        P.op('dve', lambda b=b, dst=dst: nc.vector.tensor_tensor(out=dst, in0=t1[b][:], in1=t2[b][:], op=ALU.add),
                             r=[('t1', b), ('t2', b)], w=[dkey + (th,)])
            if self.stop == 'B1r':
                P.flush(); return
            wkv = []
            for (c0, n) in ((512, 512), (1024, 512), (2048, 256)):
                wkv.append(self.load_w(self.ev_w_in_d[:, c0:c0 + n].rearrange("(k p) n -> p k n", p=128), (8, n)) + (n,))
            si = 0
            for tt in range(8):
                seg, r0 = tt // 2, (tt % 2) * 128
                for bi, (wt, wkey, n) in enumerate(wkv):
                    pm = ptm[(tt * 3 + bi) % 2]
                    pk = ('ptm', (tt * 3 + bi) % 2)

                    def mm(wt=wt, pm=pm, n=n, tt=tt):
                        ins = None
                        for k in range(8):
                            ins = nc.tensor.matmul(pm[:, 0:n], self.hT[:, k, tt * 128:(tt + 1) * 128], wt[:, k, :],
                                                   start=(k == 0), stop=(k == 7))
                        return ins
                    P.op('pe', mm, r=[wkey] + [('hT', k) for k in range(8)], w=[pk])
                    if 'tm_evac' in self.skip: continue
                    if 'tm_evac2' in self.skip and bi == 2: continue
                    if bi < 2:
                        sg_ = stg[si % 2]
                        sk = ('stg', si % 2)
                        si += 1
                        P.op('act', lambda sg_=sg_, pm=pm: nc.scalar.copy(out=sg_[:], in_=pm[:]), r=[pk], w=[sk])
                        od = self.o_ak_d if bi == 0 else self.o_av_d
                        if 'outdma' not in self.skip: P.dma('sp', lambda sg_=sg_, od=od, seg=seg, r0=r0: nc.sync.dma_start(
                            out=od[seg, :, r0:r0 + 128, :].rearrange("h p e -> p h e"),
                            in_=sg_[:].rearrange("p (h e) -> p h e", h=4)), r=[sk], w=[('ocache', bi, tt)])
                        if bi == 1 and 'vcopy' not in self.skip:
                            P.op('dve', lambda pm=pm, tt=tt: nc.vector.tensor_copy(
                                out=vA[:, 4 + tt, :, 0:128], in_=pm[:].rearrange("p (h e) -> p h e", h=4)),
                                r=[pk], w=[('vA', 4 + tt)])
                    else:
                        sg_ = stg[si % 2]
                        sk = ('stg', si % 2)
                        si += 1
                        P.op('dve', lambda pm=pm, tt=tt: nc.vector.tensor_copy(
                            out=vB[:, 4 + tt, :, 0:64], in_=pm[:, 128:256].rearrange("p (h e) -> p h e", h=2)),
                            r=[pk], w=[('vB', 4 + tt)])
                        P.op('act', lambda sg_=sg_, pm=pm: nc.scalar.copy(out=sg_[:, 128:256], in_=pm[:, 128:256]), r=[pk], w=[sk])
                        P.op('act', lambda pm=pm: nc.scalar.copy(out=kbs[:], in_=pm[:, 0:128]), r=[pk], w=['kbs'])
                        for h in range(2):
                            if 'accum' in self.skip: continue
                            P.op('dve', lambda h=h: nc.vector.scalar_tensor_tensor(
                                out=junk[:, 0:64], in0=kbs[:, h * 64:(h + 1) * 64], scalar=1.0, in1=kbs[:, h * 64:(h + 1) * 64],
                                op0=ALU.mult, op1=ALU.mult, accum_out=ssb[:, h:h + 1]), r=['kbs'], w=['junk', ('ssb', h)])
                        P.op('act', lambda: nc.scalar.activation(out=ssb[:], in_=ssb[:], func=AF.Ln, scale=1.0 / 64,
                                                                 bias=self.epsc[:, 0:1]),
                             r=[('ssb', 0), ('ssb', 1), 'epsc'], w=[('ssb', 0), ('ssb', 1)])
                        P.op('act', lambda: nc.scalar.activation(out=ssb[:], in_=ssb[:], func=AF.Exp, scale=-0.5),
                             r=[('ssb', 0), ('ssb', 1)], w=[('ssb', 0), ('ssb', 1)])
                        for h in range(2):
                            P.op('dve', lambda h=h, sg_=sg_: nc.vector.scalar_tensor_tensor(
                                out=sg_[:, h * 64:(h + 1) * 64], in0=kbs[:, h * 64:(h + 1) * 64], scalar=ssb[:, h:h + 1],
                                in1=bvec[:, 128:192], op0=ALU.mult, op1=ALU.mult),
                                r=['kbs', ('ssb', h), 'bvec'], w=[sk])
                        if 'outdma2' not in self.skip: P.dma('sp', lambda sg_=sg_, seg=seg, r0=r0: nc.sync.dma_start(
                            out=self.o_bk_d[seg, :, r0:r0 + 128, :].rearrange("h p e -> p h e"),
                            in_=sg_[:, 0:128].rearrange("p (h e) -> p h e", h=2)), r=[sk], w=[('ocache', 2, tt)])
                        if 'outdma2' not in self.skip: P.dma('sp', lambda sg_=sg_, seg=seg, r0=r0: nc.sync.dma_start(
                            out=self.o_bv_d[seg, :, r0:r0 + 128, :].rearrange("h p e -> p h e"),
                            in_=sg_[:, 128:256].rearrange("p (h e) -> p h e", h=2)), r=[sk], w=[('ocache', 3, tt)])
            if self.stop == 'B1a':
                P.flush(); return
            self.dump('qaT', qaT, [128, 4, T], [])
            self.dump('kaT', kaT, [128, 4, 512 + T], [])
            self.dump('qbT', qbT, [128, 4, T], [])
            self.dump('kbT', kbT, [128, 512 + T], [])
            P.flush()
        if self.stop == 'B1':
            return
        with ExitStack() as e2:
            NPT = 6
            pt = [self.sb(e2, "pt%d" % i, [128, 256], BF16) for i in range(NPT)]
            dbuf = self.sb(e2, "dbuf", [128, 32, 128], F32)
            ssq = self.sb(e2, "ssq", [128, 32], F32)
            a1b = [self.sb(e2, "a1b%d" % i, [128, 128], F32) for i in range(2)]
            rr = self.sb(e2, "rr", [128, 8], F32)
            junk2 = self.sb(e2, "junk2", [128, 128], F32)
            sT = [self.ps(e2, "sT%d" % i, [128, 512]) for i in range(4)]
            acc = [self.ps(e2, "acc%d" % i, [128, 512]) for i in range(4)]
            steps = []
            g = 0
            for a in range(4):
                for s_ in range(4):
                    for kb in range(12):
                        for comp in range(2):
                            steps.append(dict(kind='A', a=a, s=s_, comp=comp, kb=kb, g=g + comp, last=(kb == 11)))
                    g += 2
            for c_ in range(4):
                for s_ in range(4):
                    for kb in range(12):
                        for hi in range(2):
                            steps.append(dict(kind='B', h=c_ + 4 * hi, s=s_, kb=kb, g=g + hi, last=(kb == 11)))
                    g += 2
            LA = 2
            nst = len(steps)

            def emit_S(i, st):
                slot = i % 4
                dstp = sT[slot][:, 0:256]
                s_, kb = st['s'], st['kb']
                if st['kind'] == 'A':
                    a, comp = st['a'], st['comp']
                    lhsT = kaT[comp * 64:(comp + 1) * 64, a, kb * 128:(kb + 1) * 128]
                    rhs = qaT[comp * 64:(comp + 1) * 64, a, s_ * 256:(s_ + 1) * 256]
                else:
                    h = st['h']
                    gk = h // 4
                    lhsT = kbT[gk * 64:(gk + 1) * 64, kb * 128:(kb + 1) * 128]
                    rhs = qbT[gk * 64:(gk + 1) * 64, h % 4, s_ * 256:(s_ + 1) * 256]
                P.op('pe', lambda: nc.tensor.matmul(dstp, lhsT, rhs, start=True, stop=True), r=[], w=[('sT', slot)])
                ptb = pt[i % NPT]
                col = s_ * 12 + kb
                P.op('act', lambda: nc.scalar.activation(out=ptb[:], in_=dstp, func=AF.Exp, scale=SC,
                                                         bias=self.vcol('amask', col)),
                     r=[('sT', slot)], w=[('pt', i % NPT)])

            def emit_PV(i, st):
                ptb = pt[i % NPT]
                kb = st['kb']
                ab = acc[st['g'] % 4]
                if st['kind'] == 'A':
                    rhs = vA[:, kb, st["a"], 0:129]
                    n = 129
                else:
                    rhs = vB[:, kb, st["h"] // 4, 0:65]
                    n = 65
                for qh in range(2):
                    P.op('pe', lambda qh=qh: nc.tensor.matmul(ab[:, qh * 256:qh * 256 + n], ptb[:, qh * 128:(qh + 1) * 128], rhs,
                                                              start=(kb == 0 and qh == 0), stop=(kb == 11),
                                                              skip_group_check=True),
                         r=[('pt', i % NPT)], w=[('acc', st['g'] % 4)])
                if not st['last']:
                    return
                s_ = st['s']
                if st['kind'] == 'A':
                    if st['comp'] == 0:
                        return
                    a = st['a']
                    ab0, ab1 = acc[(st['g'] - 1) % 4], acc[st['g'] % 4]
                    k0, k1 = ('acc', (st['g'] - 1) % 4), ('acc', st['g'] % 4)
                    for qh in range(2):
                        u = (a * 4 + s_) * 2 + qh
                        o0 = qh * 256
                        P.op('dve', lambda o0=o0: nc.vector.reciprocal(out=rr[:, 0:1], in_=ab0[:, o0 + 128:o0 + 129]), r=[k0], w=['rr0'])
                        P.op('dve', lambda o0=o0: nc.vector.reciprocal(out=rr[:, 1:2], in_=ab1[:, o0 + 128:o0 + 129]), r=[k1], w=['rr1'])
                        P.op('dve', lambda: nc.vector.tensor_tensor(out=rr[:, 2:3], in0=rr[:, 1:2], in1=nlam[:], op=ALU.mult),
                             r=['rr1'], w=['rr2'])
                        a1 = a1b[u % 2]
                        P.op('dve', lambda o0=o0, a1=a1: nc.vector.tensor_scalar_mul(out=a1[:], in0=ab0[:, o0:o0 + 128], scalar1=rr[:, 0:1]),
                             r=[k0, 'rr0'], w=[('a1b', u % 2)])
                        P.op('dve', lambda o0=o0, a1=a1, u=u: nc.vector.scalar_tensor_tensor(
                            out=dbuf[:, u, :], in0=ab1[:, o0:o0 + 128], scalar=rr[:, 2:3], in1=a1[:], op0=ALU.mult, op1=ALU.add),
                            r=[k1, 'rr2', ('a1b', u % 2)], w=[('dbuf', u)])
                        P.op('dve', lambda u=u: nc.vector.scalar_tensor_tensor(
                            out=junk2[:], in0=dbuf[:, u, :], scalar=1.0, in1=dbuf[:, u, :], op0=ALU.mult, op1=ALU.mult,
                            accum_out=ssq[:, u:u + 1]), r=[('dbuf', u)], w=['junk2', ('ssq', u)])
                else:
                    h = st['h']
                    ab0 = acc[st['g'] % 4]
                    k0 = ('acc', st['g'] % 4)
                    for qh in range(2):
                        o0 = qh * 256
                        qt = s_ * 2 + qh
                        P.op('dve', lambda o0=o0: nc.vector.reciprocal(out=rr[:, 4:5], in_=ab0[:, o0 + 64:o0 + 65]), r=[k0], w=['rr4'])
                        P.op('dve', lambda o0=o0, qt=qt, h=h: nc.vector.tensor_scalar_mul(
                            out=ocat[:, qt, 512 + h * 64:512 + (h + 1) * 64], in0=ab0[:, o0:o0 + 64], scalar1=rr[:, 4:5]),
                            r=[k0, 'rr4'], w=[('ocat', qt, 4 + h // 2)])

            for j in range(nst // 2 + 1):
                if 2 * j < nst:
                    emit_S(2 * j, steps[2 * j])
                    emit_S(2 * j + 1, steps[2 * j + 1])
                if j >= 1:
                    emit_PV(2 * j - 2, steps[2 * j - 2])
                    emit_PV(2 * j - 1, steps[2 * j - 1])
            P.op('act', lambda: nc.scalar.activation(out=ssq[:], in_=ssq[:], func=AF.Ln, scale=1.0 / 128, bias=self.epsc[:, 0:1]),
                 r=[('ssq', u) for u in range(32)], w=['rstdA'])
            P.op('act', lambda: nc.scalar.activation(out=ssq[:], in_=ssq[:], func=AF.Exp, scale=-0.5), r=['rstdA'], w=['rstdA'])
            for a in range(4):
                for s_ in range(4):
                    for qh in range(2):
                        u = (a * 4 + s_) * 2 + qh
                        qt = s_ * 2 + qh
                        eng = 'dve'
                        E = nc.vector
                        P.op(eng, lambda E=E, u=u, qt=qt, a=a: E.scalar_tensor_tensor(
                            out=ocat[:, qt, a * 128:(a + 1) * 128], in0=dbuf[:, u, :], scalar=ssq[:, u:u + 1], in1=gsub8[:],
                            op0=ALU.mult, op1=ALU.mult), r=[('dbuf', u), 'rstdA', 'gsub8'], w=[('ocat', qt, a)])
            self.dump('ocat', ocat, [128, 8, D], [])
            P.flush()
        if self.stop == 'B2':
            return
        with ExitStack() as e3:
            self.alloc_w(e3, 2, 4096)
            ptp = [self.ps(e3, "otp%d" % i, [128, 1024], BF16) for i in range(2)]
            pmx = [self.ps(e3, "pmx%d" % i, [128, 512]) for i in range(2)]
            n = 0
            for c in range(8):
                for gq in range(2):
                    pp = ptp[n % 2]
                    for j in range(4):
                        qt = gq * 4 + j
                        P.op('pe', lambda pp=pp, j=j, qt=qt, c=c: nc.tensor.transpose(
                            pp[:, j * 128:(j + 1) * 128], ocat[:, qt, c * 128:(c + 1) * 128], self.ident_b[:]),
                            r=[], w=[('otp', n % 2)])
                    if n % 2 == 0:
                        P.op('dve', lambda pp=pp, c=c, gq=gq: nc.vector.tensor_copy(out=self.hT[:, c, gq * 512:(gq + 1) * 512], in_=pp[:, 0:512]),
                             r=[('otp', n % 2)], w=[('hT', c)])
                    else:
                        P.op('act', lambda pp=pp, c=c, gq=gq: nc.scalar.copy(out=self.hT[:, c, gq * 512:(gq + 1) * 512], in_=pp[:, 0:512]),
                             r=[('otp', n % 2)], w=[('hT', c)])
                    n += 1
            self.out_proj(self.ev_w_out_d, pmx, l)
            self.dump('xm0', self.xT, [128, NCH, T], [('xT', c) for c in range(8)])
            P.flush()


def _out_proj(self, w_d, pmx, l):
    P, nc = self.P, self.nc
    n = 0
    for c in range(8):
        if c % 4 == 0:
            wt, wkey = self.load_w(w_d[:, c * 128:c * 128 + 512].rearrange("(k p) n -> p k n", p=128), (8, 512))
        co = (c % 4) * 128
        for th in range(2):
            pm = pmx[n % 2]
            pk = ('pmx', n % 2)
            n += 1

            def mm(wt=wt, pm=pm, th=th, co=co):
                ins = None
                for k in range(8):
                    ins = nc.tensor.matmul(pm[:], wt[:, k, co:co + 128], self.hT[:, k, th * 512:(th + 1) * 512],
                                           start=(k == 0), stop=(k == 7))
                return ins
            P.op('pe', mm, r=[wkey] + [('hT', k) for k in range(8)], w=[pk])
            P.op('dve', lambda pm=pm, c=c, th=th: nc.vector.scalar_tensor_tensor(
                out=self.xT[:, c, th * 512:(th + 1) * 512], in0=pm[:], scalar=self.mod[l][:, 16 + c:17 + c],
                in1=self.xT[:, c, th * 512:(th + 1) * 512], op0=ALU.mult, op1=ALU.add),
                r=[pk, ('xT', c)], w=[('xT', c)])


Builder.even_mixer = _even_mixer
Builder.out_proj = _out_proj


class Em:
    def __init__(self, B):
        self.B, self.P, self.nc = B, B.P, B.nc

    def V(self, eng):
        return self.nc.vector if eng == 'dve' else self.nc.gpsimd

    def tt(self, eng, out, in0, in1, op, r, w):
        self.P.op(eng, lambda: self.V(eng).tensor_tensor(out=out, in0=in0, in1=in1, op=op), r=r, w=w)

    def ts(self, eng, out, in0, s1, s2, op0, op1, r, w):
        if s2 is None:
            self.P.op(eng, lambda: self.V(eng).tensor_scalar(out=out, in0=in0, scalar1=s1, scalar2=None, op0=op0), r=r, w=w)
        else:
            self.P.op(eng, lambda: self.V(eng).tensor_scalar(out=out, in0=in0, scalar1=s1, scalar2=s2, op0=op0, op1=op1), r=r, w=w)

    def stt(self, out, in0, scalar, in1, op0, op1, r, w, accum=None):
        if accum is None:
            self.P.op('dve', lambda: self.nc.vector.scalar_tensor_tensor(out=out, in0=in0, scalar=scalar, in1=in1, op0=op0, op1=op1), r=r, w=w)
        else:
            self.P.op('dve', lambda: self.nc.vector.scalar_tensor_tensor(out=out, in0=in0, scalar=scalar, in1=in1, op0=op0, op1=op1,
                                                                         accum_out=accum), r=r, w=w)

    def act(self, out, in_, func, r, w, scale=1.0, bias=0.0):
        self.P.op('act', lambda: self.nc.scalar.activation(out=out, in_=in_, func=func, scale=scale, bias=bias), r=r, w=w)

    def cp(self, eng, out, in_, r, w):
        if eng == 'act':
            self.P.op('act', lambda: self.nc.scalar.copy(out=out, in_=in_), r=r, w=w)
        else:
            self.P.op(eng, lambda: self.V(eng).tensor_copy(out=out, in_=in_), r=r, w=w)

    def mm(self, out, lhsT, rhs, r, w, start=True, stop=True, sgc=False):
        if sgc:
            self.P.op('pe', lambda: self.nc.tensor.matmul(out, lhsT, rhs, start=start, stop=stop, skip_group_check=True), r=r, w=w)
        else:
            self.P.op('pe', lambda: self.nc.tensor.matmul(out, lhsT, rhs, start=start, stop=stop), r=r, w=w)

    def tr(self, out, in_, ident, r, w):
        self.P.op('pe', lambda: self.nc.tensor.transpose(out, in_, ident), r=r, w=w)

    def dma(self, q, out, in_, r, w):
        E = self.nc.sync if q == 'sp' else self.nc.gpsimd
        self.P.dma(q, lambda: E.dma_start(out=out, in_=in_), r=r, w=w)

    def memset(self, eng, ap, val, w):
        self.P.op(eng, lambda: self.V(eng).memset(ap, val), w=w)


HG0 = 0
RW0 = 2560
LWS = -0.6065306597126334
GN_EPS = 64e-5


def _proj_fm(self, em, w_d, c0, ncols, pp, pkey, pkeys=None):
    nc = self.nc
    wt, wkey = self.load_w(w_d[:, c0:c0 + ncols].rearrange("(k p) n -> p k n", p=128), (8, ncols))
    for th in range(2):
        def mm(wt=wt, th=th):
            ins = None
            for k in range(8):
                ins = nc.tensor.matmul(pp[0:ncols, th * 512:(th + 1) * 512], wt[:, k, :], self.hT[:, k, th * 512:(th + 1) * 512],
                                       start=(k == 0), stop=(k == 7))
            return ins
        self.P.op('pe', mm, r=[wkey] + [('hT', k) for k in range(8)], w=[pkeys[th] if pkeys else (pkey, th)])


def _decay(self, em, lw, Gp, D, rev, key):
    nc = self.nc
    self.P.op('dve', lambda: nc.vector.tensor_tensor_scan(out=Gp[:, 1:T + 1], data0=self.onesT[:], data1=lw, initial=0.0,
                                                          op0=ALU.mult, op1=ALU.add), r=[key + '_lw', 'onesT'], w=[key + '_Gp'])
    v3 = lambda ap: ap.rearrange("p (c l) -> p c l", l=64)
    if not rev:
        em.tt('dve', v3(D), v3(Gp[:, 1:T + 1]), Gp[:, 0:T:64].unsqueeze(2).broadcast_to([128, 16, 64]), ALU.subtract,
              r=[key + '_Gp'], w=[key + '_D'])
    else:
        em.tt('dve', v3(D), Gp[:, 64:T + 1:64].unsqueeze(2).broadcast_to([128, 16, 64]), v3(Gp[:, 0:T]), ALU.subtract,
              r=[key + '_Gp'], w=[key + '_D'])


def _odd_mixer(self):
    P, nc = self.P, self.nc
    em = Em(self)
    l = 1
    wd = self.od_w_in_d
    with ExitStack() as es:
        ocat = self.sb(es, "ocat1", [128, 8, D], BF16)
        self.onesT = self.sb(es, "onesT", [128, T], F32)
        masks = self.sb(es, "masks", [128, 4, 128], F32)
        bv1 = self.sb(es, "bv1", [128, 1280], F32)
        with ExitStack() as e0:
            self.rmsnorm(e0, lambda c: self.gm1[l][:, c:c + 1], lambda c: self.mod[l][:, c:c + 1],
                         lambda c: (self.hT[:, c, :], [('hT', c)]))
            em.memset('pool', self.onesT[:], 1.0, ['onesT'])
            em.dma('sp', masks[:], self.masks_d, [], ['masks'])
            em.dma('sp', bv1[:], self.bv1_d, [], ['bv1'])
            P.flush()
        self.dump('h1T', self.hT, [128, NCH, T], [])
        with ExitStack() as e1:
            self.alloc_w(e1, 3, 4096)
            osum = self.sb(e1, "osum_h", [128, 8, 512], F32)
            vtok = self.sb(e1, "vtok_h", [128, 8, 512], BF16)
            gsil = self.sb(e1, "gsil", [128, 8, 512], BF16)
            lbv = self.sb(e1, "lbv", [128, 8], F32)
            omlb = self.sb(e1, "omlb", [128, 8], F32)
            ssh = self.sb(e1, "ssh", [128, 32], F32)
            junk = self.sb(e1, "junkh", [128, 128], F32)
            HB = []
            for s_ in range(2):
                hb = {}
                for nm in ('fl', 'kf', 'lg', 'qs'):
                    hb[nm] = self.sb(e1, "h%s%d" % (nm, s_), [128, T], F32)
                hb['Gp'] = self.sb(e1, "hGp%d" % s_, [128, T + 1], F32)
                for nm in ('qt', 'qA', 'qB', 'ktl'):
                    hb[nm] = self.sb(e1, "h%s%d" % (nm, s_), [128, T], BF16)
                hb['Am'] = [self.sb(e1, "hAm%d_%d" % (s_, i), [128, 128], BF16) for i in range(2)]
                hb['ktok'] = [self.sb(e1, "hktok%d_%d" % (s_, i), [128, 128], BF16) for i in range(2)]
                hb['S'] = self.sb(e1, "hS%d" % s_, [128, 128], F32)
                hb['Stmp'] = self.sb(e1, "hStmp%d" % s_, [128, 128], F32)
                hb['Sb'] = [self.sb(e1, "hSb%d_%d" % (s_, i), [128, 128], BF16) for i in range(2)]
                hb['hbA'] = self.ps(e1, "hbA%d" % s_, [128, T])
                hb['hbB'] = self.ps(e1, "hbB%d" % s_, [128, T])
                HB.append(hb)
            ptk = [HB[0]['hbB'][:, 0:512], HB[0]['hbB'][:, 512:1024]]
            P.excl.update(['hbA', 'hbB'])
            P.alias.update({('ptk', 0): ('hbB', 0, 0), ('ptk', 1): ('hbB', 0, 1)})
            for s_ in range(2):
                em.memset('pool', HB[s_]['qA'][:], 0.0, [('h', s_, 'qA')])
                em.memset('pool', HB[s_]['qB'][:], 0.0, [('h', s_, 'qB')])
            em.tt('dve', lbv[:], self.vcol('lb1', 0, 8), self.vcol('lb0', 0, 8), ALU.subtract, r=[], w=['lbv'])
            em.act(lbv[:], lbv[:], AF.Sigmoid, r=['lbv'], w=['lbv'])
            em.ts('dve', omlb[:], lbv[:], -1.0, 1.0, ALU.mult, ALU.add, r=['lbv'], w=['omlb'])
            wv = self.load_w(wd[:, 1536:2048].rearrange("(k p) n -> p k n", p=128), (8, 512))
            wg = self.load_w(wd[:, 2048:2560].rearrange("(k p) n -> p k n", p=128), (8, 512))
            for tt in range(8):
                for bi, (wt, wkey) in enumerate((wv, wg)):
                    pm = ptk[bi]

                    def mm(wt=wt, pm=pm, tt=tt):
                        ins = None
                        for k in range(8):
                            ins = nc.tensor.matmul(pm[:], self.hT[:, k, tt * 128:(tt + 1) * 128], wt[:, k, :], start=(k == 0), stop=(k == 7))
                        return ins
                    P.op('pe', mm, r=[wkey], w=[('ptk', bi)])
                    if bi == 0:
                        em.cp('dve', vtok[:, tt, :], pm[:], r=[('ptk', bi)], w=[('vtok', tt)])
                    else:
                        em.act(gsil[:, tt, :], pm[:], AF.Silu, r=[('ptk', bi)], w=[('gsil', tt)])
            def hchain(slot, dr, hc):
                rev = dr == 1
                mask = masks[:, 1 if rev else 0, :]
                ci = dr * 4 + hc
                Kk = lambda n: ('h', slot, n)
                hb = HB[slot]
                fl, kf, lg, Gp, qsc = hb['fl'], hb['kf'], hb['lg'], hb['Gp'], hb['qs']
                qt, qA, qB, ktl = hb['qt'], hb['qA'], hb['qB'], hb['ktl']
                Am, ktok, S, Stmp, Sb = hb['Am'], hb['ktok'], hb['S'], hb['Stmp'], hb['Sb']
                hbA, hbB = hb['hbA'], hb['hbB']
                kA0, kA1, kB0, kB1 = ('hbA', slot, 0), ('hbA', slot, 1), ('hbB', slot, 0), ('hbB', slot, 1)
                Dd = lg
                enD = fl
                eD = Gp
                psc = hbA[:, 0:128]
                pktr = hbA[:, 512:1024].bitcast(BF16)[:, 0:128]
                po = hbB[:, 0:128]
                pds = hbB[:, 512:640]
                _proj_fm(self, em, wd, hc * 128, 128, hbA, None, pkeys=[kA0, kA1])
                em.act(qsc[:], hbA[:], AF.Silu, r=[kA0, kA1], w=[Kk('qs')])
                yield
                _proj_fm(self, em, wd, 512 + dr * 512 + hc * 128, 128, hbA, None, pkeys=[kA0, kA1])
                em.act(fl[:], hbA[:], AF.Sigmoid, r=[kA0, kA1], w=[Kk('fl')])
                em.ts('dve', fl[:], fl[:], omlb[:, ci:ci + 1], lbv[:, ci:ci + 1], ALU.mult, ALU.add, r=[Kk('fl'), 'omlb', 'lbv'], w=[Kk('fl')])
                yield
                em.ts('pool', kf[:], fl[:], -1.0, 1.0, ALU.mult, ALU.add, r=[Kk('fl')], w=[Kk('kf')])
                em.act(lg[:], fl[:], AF.Ln, r=[Kk('fl')], w=[Kk('lg')])
                em.memset('pool', Gp[:, 0:1], 0.0, [Kk('Gp')])
                P.op('dve', lambda: nc.vector.tensor_tensor_scan(out=Gp[:, 1:T + 1], data0=self.onesT[:], data1=lg[:], initial=0.0,
                                                                 op0=ALU.mult, op1=ALU.add), r=[Kk('lg'), 'onesT'], w=[Kk('Gp')])
                yield
                v3 = lambda ap: ap.rearrange("p (c l) -> p c l", l=64)
                if not rev:
                    em.tt('dve', v3(Dd[:]), v3(Gp[:, 1:T + 1]), Gp[:, 0:T:64].unsqueeze(2).broadcast_to([128, 16, 64]), ALU.subtract,
                          r=[Kk('Gp'), Kk('lg')], w=[Kk('lg')])
                else:
                    em.tt('dve', v3(Dd[:]), Gp[:, 64:T + 1:64].unsqueeze(2).broadcast_to([128, 16, 64]), v3(Gp[:, 0:T]), ALU.subtract,
                          r=[Kk('Gp'), Kk('lg')], w=[Kk('lg')])
                em.act(eD[:, 0:T], Dd[:], AF.Exp, r=[Kk('lg'), Kk('Gp')], w=[Kk('Gp')])
                em.act(enD[:], Dd[:], AF.Exp, r=[Kk('lg'), Kk('fl')], w=[Kk('fl')], scale=-1.0)
                yield
                em.tt('dve', qt[:], qsc[:], eD[:, 0:T], ALU.mult, r=[Kk('qs'), Kk('Gp')], w=[Kk('qt')])
                h3 = lambda ap: ap.rearrange("p (t l) -> p t l", l=128)
                em.cp('pool', h3(qA[:])[:, :, 0:64], h3(qt[:])[:, :, 0:64], r=[Kk('qt')], w=[Kk('qA')])
                em.cp('pool', h3(qB[:])[:, :, 64:128], h3(qt[:])[:, :, 64:128], r=[Kk('qt')], w=[Kk('qB')])
                em.tt('dve', ktl[:], kf[:], enD[:], ALU.mult, r=[Kk('kf'), Kk('fl')], w=[Kk('ktl')])
                em.dma('sp', S[:], self.st_h_d[dr, hc], [], [Kk('S')])
                em.cp('pool', Sb[0][:], S[:], r=[Kk('S')], w=[Kk('Sb0')])
                yield
                sbi = 0
                for ti in range(8):
                    tt = 7 - ti if rev else ti
                    tl = slice(tt * 128, (tt + 1) * 128)
                    b_ = ti % 2
                    em.mm(psc, ktl[:, tl], qt[:, tl], r=[Kk('ktl'), Kk('qt')], w=[kA0])
                    em.tt('dve', Am[b_][:], psc, mask, ALU.mult, r=[kA0, 'masks'], w=[Kk('Am%d' % b_)])
                    em.tr(pktr, ktl[:, tl], self.ident_b[:], r=[Kk('ktl')], w=[kA1])
                    em.cp('act', ktok[b_][:], pktr, r=[kA1], w=[Kk('ktok%d' % b_)])
                    vt = vtok[:, tt, hc * 128:(hc + 1) * 128]
                    order = (1, 0) if rev else (0, 1)
                    em.mm(po, Am[b_][:], vt, r=[Kk('Am%d' % b_), ('vtok', tt)], w=[kB0], start=True, stop=False)
                    yield
                    for oi, c in enumerate(order):
                        qh = qA if c == 0 else qB
                        cr = slice(c * 64, (c + 1) * 64)
                        em.mm(pds, ktok[b_][cr, :], vtok[cr, tt, hc * 128:(hc + 1) * 128], r=[Kk('ktok%d' % b_), ('vtok', tt)], w=[kB1])
                        em.mm(po, qh[:, tl], Sb[sbi][:], r=[Kk('qA'), Kk('qB'), Kk('Sb%d' % sbi)], w=[kB0], start=False, stop=(oi == 1))
                        em.tt('dve', Stmp[:], pds, S[:], ALU.add, r=[kB1, Kk('S')], w=[Kk('Stmp')])
                        cg = tt * 2 + c
                        ecol = cg * 64 + (0 if rev else 63)
                        em.ts('dve', S[:], Stmp[:], eD[:, ecol:ecol + 1], None, ALU.mult, None, r=[Kk('Stmp'), Kk('Gp')], w=[Kk('S')])
                        seg_end = (cg % 4 == 0) if rev else (cg % 4 == 3)
                        if seg_end:
                            seg = cg // 4
                            em.dma('sp', self.o_sh_d[seg, dr, hc], S[:], r=[Kk('S')], w=[('o_sh', seg, dr, hc)])
                            last = (cg == 0) if rev else (cg == 15)
                            if not last:
                                em.ts('dve', S[:], S[:], self.vcol('keep'), None, ALU.mult, None, r=[Kk('S')], w=[Kk('S')])
                        sbi = 1 - sbi
                        em.cp('pool', Sb[sbi][:], S[:], r=[Kk('S')], w=[Kk('Sb%d' % sbi)])
                        yield
                    if dr == 0:
                        em.cp('act', osum[:, tt, hc * 128:(hc + 1) * 128], po, r=[kB0], w=[('osum', tt, hc)])
                    else:
                        em.tt('dve', osum[:, tt, hc * 128:(hc + 1) * 128], po, osum[:, tt, hc * 128:(hc + 1) * 128], ALU.add,
                              r=[kB0, ('osum', tt, hc)], w=[('osum', tt, hc)])

            for dr in range(2):
                for hp in range(2):
                    gens = [hchain(0, dr, 2 * hp), hchain(1, dr, 2 * hp + 1)]
                    while gens:
                        for g_ in list(gens):
                            try:
                                next(g_)
                            except StopIteration:
                                gens.remove(g_)
            for tt in range(8):
                for hc in range(4):
                    u = tt * 4 + hc
                    em.stt(junk[:], osum[:, tt, hc * 128:(hc + 1) * 128], 1.0, osum[:, tt, hc * 128:(hc + 1) * 128], ALU.mult, ALU.mult,
                           r=[('osum', tt, hc)], w=['junkh', ('ssh', u)], accum=ssh[:, u:u + 1])
            em.act(ssh[:], ssh[:], AF.Ln, r=[('ssh', u) for u in range(32)], w=['rsh'], scale=1.0 / 128, bias=self.epsc[:, 0:1])
            em.act(ssh[:], ssh[:], AF.Exp, r=['rsh'], w=['rsh'], scale=-1.0 * 0.5)
            for tt in range(8):
                for hc in range(4):
                    u = tt * 4 + hc
                    em.stt(osum[:, tt, hc * 128:(hc + 1) * 128], osum[:, tt, hc * 128:(hc + 1) * 128], ssh[:, u:u + 1], bv1[:, 0:128],
                           ALU.mult, ALU.mult, r=[('osum', tt, hc), 'rsh', 'bv1'], w=[('osum', tt, hc)])
                em.tt('pool', ocat[:, tt, 0:512], osum[:, tt, :], gsil[:, tt, :], ALU.mult,
                      r=[('osum', tt, hc) for hc in range(4)] + [('gsil', tt)], w=[('ocat', tt, 0)])
            self.dump('ocat_h', ocat, [128, 8, D], [])
            P.flush()
        if self.stop == 'C1':
            return
        self.odd_rwkv(em, ocat, masks, bv1)
        with ExitStack() as e3:
            self.alloc_w(e3, 2, 4096)
            ptp = [self.ps(e3, "otp%d" % i, [128, 1024], BF16) for i in range(2)]
            pmx = [self.ps(e3, "pmx%d" % i, [128, 512]) for i in range(2)]
            n = 0
            for c in range(8):
                for gq in range(2):
                    pp = ptp[n % 2]
                    for j in range(4):
                        qt_ = gq * 4 + j
                        em.tr(pp[:, j * 128:(j + 1) * 128], ocat[:, qt_, c * 128:(c + 1) * 128], self.ident_b[:], r=[], w=[('otp', n % 2)])
                    em.cp('dve' if n % 2 == 0 else 'act', self.hT[:, c, gq * 512:(gq + 1) * 512], pp[:, 0:512], r=[('otp', n % 2)], w=[('hT', c)])
                    n += 1
            self.out_proj(self.od_w_out_d, pmx, l)
            self.dump('xm1', self.xT, [128, NCH, T], [])
            P.flush()


Builder.odd_mixer = _odd_mixer


def _odd_rwkv(self, em, ocat, masks, bv1):
    P, nc = self.P, self.nc
    wd = self.od_w_in_d
    idf = self.ident_f
    with ExitStack() as e2:
        sbf = lambda n, s, d=F32: self.sb(e2, n, s, d)
        e2b = ExitStack()
        sbb = lambda n, s, d=F32: self.sb(e2b, n, s, d)
        self.alloc_w(e2, 3, 1024)
        osum = sbf("osum_r", [128, 8, 512])
        bonus = sbf("bonus", [128, 8, 8])
        twd = sbf("twd", [128, T], BF16)
        adT = sbf("adT", [64, T], BF16)
        sgd = sbf("sgd", [128, T], BF16)
        wup = sbf("wup", [128, 512], BF16)
        aup = sbf("aup", [64, 512], BF16)
        gup = sbf("gup", [128, 512], BF16)
        hsel = sbf("hsel", [128, 2])
        omm = sbf("omm", [128, 15]); hmu = sbf("hmu", [128, 15]); hk = sbf("hk", [128, 15])
        zst = sbf("zst", [64, 16, 64]); sstg = [sbf("sstg%d" % i, [64, 64]) for i in range(2)]
        zst_b = sbf("zst_b", [64, 16, 64], BF16)

        m2 = sbf("m2", [128, 2, 256])
        pA2 = [self.ps(e2, "pA2_%d" % i, [128, T]) for i in range(2)]
        pBC = [self.ps(e2, "pBC_%d" % i, [128, T]) for i in range(2)]
        pj = pA2[0]
        pT = pj[:, 0:512]
        pS = pBC[0][:, 512:1024]
        P.excl.update(['pA', 'pB'])
        P.alias.update({('pj', 0): ('pA', 0, 0), ('pj', 1): ('pA', 0, 1), 'pT': ('pA', 0, 0), 'pS': ('pB', 0, 1)})
        vtok = sbb("vtok_p", [128, 8, 128], BF16)
        r_p = sbb("r_p", [128, T]); k_p = sbb("k_p", [128, T]); v_p = sbb("v_p", [128, T])
        a_p = sbb("a_p", [128, T]); kk_p = sbb("kk_p", [128, T]); kt_p = sbb("kt_p", [128, T]); b_p = sbb("b_p", [128, T])
        tmpf = sbb("tmpf", [128, T]); sqb = sbb("sqr", [128, T], BF16)
        Gp = sbb("Gpr", [128, T + 1]); Dd = self.rstd
        KR = [sbb("kr%d" % i, [128, 8, 256], BF16) for i in range(2)]; BE = [sbb("be%d" % i, [128, T], BF16) for i in range(2)]; TA = [sbb("ta%d" % i, [128, T], BF16) for i in range(2)]
        ED1 = sbb("eD1", [128, T])
        pd = tmpf; lw = v_p; Dp = a_p; enD = Gp; ED = [tmpf, ED1]
        P.alias.update({'pd': 'tmpf', 'lwraw': 'v_p', 'r_lw': 'v_p', 'Dp': 'a_p', 'enDr': 'r_Gp', ('eDr', 0): 'tmpf'})
        btk = [sbb("btk%d" % i, [128, 128], BF16) for i in range(2)]; ttk = [sbb("ttk%d" % i, [128, 128], BF16) for i in range(2)]
        nktk = [sbb("nktk%d" % i, [128, 128], BF16) for i in range(2)]
        M1b = [sbb("M1b%d" % i, [128, 2, 256], BF16) for i in range(2)]; M2b = [sbb("M2b%d" % i, [128, 2, 256], BF16) for i in range(2)]
        YA = [[sbb("YA%d_%d" % (s_, i), [128, 2, 128], BF16) for i in range(2)] for s_ in range(2)]
        YT_ = [[sbb("YT%d_%d" % (s_, i), [128, 2, 128], BF16) for i in range(2)] for s_ in range(2)]
        QQ = [[sbb("QQ%d_%d" % (s_, i), [128, 2, 128], BF16) for i in range(2)] for s_ in range(2)]
        rxb = [sbb("rxb%d" % i, [128, 2, 128], BF16) for i in range(2)]; xsb = [sbb("xsb%d" % i, [128, 2, 128], BF16) for i in range(2)]
        RAb = [sbb("RAb%d" % i, [64, 2, 128], BF16) for i in range(2)]; RBb = [sbb("RBb%d" % i, [64, 2, 128], BF16) for i in range(2)]
        GTb = [sbb("GTb%d" % i, [64, 2, 128]) for i in range(2)]; Z1b = [sbb("Z1b%d" % i, [64, 2, 128]) for i in range(2)]
        ztb = [sbb("ztb%d" % i, [64, 2, 64]) for i in range(2)]

        em.memset('pool', hsel[:], 0.0, ['hsel'])
        em.memset('pool', hsel[0:64, 0:1], 1.0, ['hsel'])
        em.memset('pool', hsel[64:128, 1:2], 1.0, ['hsel'])
        for s_ in range(2):
            em.memset('pool', RAb[s_][:], 0.0, [('t', s_, 'ra')])
            em.memset('pool', RBb[s_][:], 0.0, [('t', s_, 'rb')])
        em.memset('pool', Gp[:, 0:1], 0.0, ['r_Gp'])
        for dr in range(2):
            em.cp('pool', m2[:, dr, 0:128], masks[:, 2 + dr, :], r=['masks'], w=['m2'])
            em.cp('pool', m2[:, dr, 128:256], masks[:, dr, :], r=['masks'], w=['m2'])
        em.ts('dve', omm[:], self.vcol('mu', 0, 15), -1.0, 1.0, ALU.mult, ALU.add, r=[], w=['omm'])
        em.ts('dve', hmu[:], self.vcol('mu', 0, 15), 0.5, None, ALU.mult, None, r=[], w=['hmu'])
        em.ts('dve', hk[:], hmu[:], self.vcol('km1'), None, ALU.mult, None, r=['hmu'], w=['hk'])
        em.dma('pool', wup[:], self.w_up_d, [], ['wup'])
        em.dma('pool', aup[:], self.a_up_d, [], ['aup'])
        em.dma('pool', gup[:], self.g_up_d, [], ['gup'])
        sld = osum[0:64, 0:2, :].rearrange("p a (b k) -> p (a b) k", k=64)
        em.dma('sp', sld, self.st_r_d.rearrange("d h v k -> v (d h) k"), [], ['sld'])
        for i in range(16):
            em.tr(pS[0:64, 0:64], sld[:, i, :], idf[0:64, 0:64], r=['sld'], w=['pS'])
            em.cp('dve', zst[:, i, :], pS[0:64, 0:64], r=['pS'], w=[('zst', i)])
            em.cp('pool', zst_b[:, i, :], zst[:, i, :], r=[('zst', i)], w=[('zstb', i)])
        P.flush()

        def tshift(dst, j, nrows, rkeys, wkeys):
            pr = pj[0:nrows, :]
            em.act(dst, pr, AF.Identity, r=rkeys + ['omm'], w=wkeys, scale=omm[0:nrows, j:j + 1])
            em.stt(dst[:, 1:T], pr[:, 0:T - 1], hmu[0:nrows, j:j + 1], dst[:, 1:T], ALU.mult, ALU.add, r=rkeys + wkeys + ['hmu'], w=wkeys)
            em.stt(dst[:, 0:T - 1], pr[:, 1:T], hmu[0:nrows, j:j + 1], dst[:, 0:T - 1], ALU.mult, ALU.add, r=rkeys + wkeys + ['hmu'], w=wkeys)
            em.stt(dst[:, 256:T:256], pr[:, 255:T - 1:256], hk[0:nrows, j:j + 1], dst[:, 256:T:256], ALU.mult, ALU.add,
                   r=rkeys + wkeys + ['hk'], w=wkeys)
            em.stt(dst[:, 255:T - 1:256], pr[:, 256:T:256], hk[0:nrows, j:j + 1], dst[:, 255:T - 1:256], ALU.mult, ALU.add,
                   r=rkeys + wkeys + ['hk'], w=wkeys)

        PJ = [('pj', 0), ('pj', 1)]
        _proj_fm(self, em, wd, RW0 + 1536, 128, pj, 'pj')
        tshift(pd[:], 12, 128, PJ, ['pd'])
        em.act(twd[:], pd[:], AF.Tanh, r=['pd'], w=['twd'])
        _proj_fm(self, em, wd, RW0 + 1664, 64, pj, 'pj')
        tshift(pd[0:64, :], 13, 64, PJ, ['pd'])
        em.cp('pool', adT[:], pd[0:64, :], r=['pd'], w=['adT'])
        _proj_fm(self, em, wd, RW0 + 1728, 128, pj, 'pj')
        tshift(pd[:], 14, 128, PJ, ['pd'])
        em.act(sgd[:], pd[:], AF.Sigmoid, r=['pd'], w=['sgd'])
        if self.stop == 'C2a':
            P.flush(); e2b.close(); return
        for p in range(4):
            for nm, dst, j in (('r', r_p, p), ('k', k_p, 4 + p), ('v', v_p, 8 + p)):
                _proj_fm(self, em, wd, RW0 + j * 128, 128, pj, 'pj')
                tshift(dst[:], j, 128, PJ, [nm + '_p'])
            for tt in range(8):
                em.tr(pT[:, 0:128], v_p[:, tt * 128:(tt + 1) * 128], idf[:], r=['v_p'], w=['pT'])
                em.cp('act', vtok[:, tt, :], pT[:, 0:128], r=['pT'], w=[('vtok', tt)])
            for th in range(2):
                em.mm(pj[:, th * 512:(th + 1) * 512], aup[:, p * 128:(p + 1) * 128], adT[:, th * 512:(th + 1) * 512], r=['aup', 'adT'], w=[('pj', th)])
            em.act(a_p[:], pj[:], AF.Sigmoid, r=PJ, w=['a_p'], bias=self.vcol('a0', p))
            em.ts('dve', kk_p[:], k_p[:], self.vcol('k_k', p), None, ALU.mult, None, r=['k_p'], w=['kk_p'])
            em.act(sqb[:], kk_p[:], AF.Square, r=['kk_p'], w=['sqr'])
            for th in range(2):
                em.mm(pj[:, th * 512:(th + 1) * 512], self.bd_ones[:], sqb[:, th * 512:(th + 1) * 512], r=['sqr'], w=[('pj', th)])
            em.act(tmpf[:], pj[:], AF.Ln, r=PJ, w=['tmpf'], bias=self.epsc[:, 1:2])
            em.act(tmpf[:], tmpf[:], AF.Exp, r=['tmpf'], w=['tmpf'], scale=-0.5)
            em.tt('dve', kk_p[:], kk_p[:], tmpf[:], ALU.mult, r=['kk_p', 'tmpf'], w=['kk_p'])
            em.ts('dve', kt_p[:], a_p[:], -1.0, self.vcol('k_a', p), ALU.add, ALU.mult, r=['a_p'], w=['kt_p'])
            em.stt(kt_p[:], kt_p[:], 1.0, k_p[:], ALU.add, ALU.mult, r=['kt_p', 'k_p'], w=['kt_p'])
            em.tt('pool', b_p[:], a_p[:], kk_p[:], ALU.mult, r=['a_p', 'kk_p'], w=['b_p'])
            em.stt(tmpf[:], r_p[:], self.vcol('r_k', p), kt_p[:], ALU.mult, ALU.mult, r=['r_p', 'kt_p', 'tmpf'], w=['tmpf'])
            for tt in range(8):
                em.mm(pT[:, 0:2], tmpf[:, tt * 128:(tt + 1) * 128], hsel[:], r=['tmpf', 'hsel'], w=['pT'])
                em.cp('act', bonus[:, tt, 2 * p:2 * p + 2], pT[:, 0:2], r=['pT'], w=[('bonus', tt, p)])
                em.tt('pool', ocat[:, tt, 512 + p * 128:512 + (p + 1) * 128].rearrange('p (h e) -> p h e', e=64),
                      vtok[:, tt, :].rearrange('p (h e) -> p h e', e=64), bonus[:, tt, 2 * p:2 * p + 2].unsqueeze(2).broadcast_to([128, 2, 64]),
                      ALU.mult, r=[('vtok', tt), ('bonus', tt, p)], w=[('ocat', tt, 1)])
            def dir_prep(dr):
                rev = dr == 1
                kr, be, ta, eD = KR[dr], BE[dr], TA[dr], ED[dr]
                dslc = slice(dr * 64, (dr + 1) * 64)
                for th in range(2):
                    em.mm(pj[:, th * 512:(th + 1) * 512], wup[dslc, p * 128:(p + 1) * 128], twd[dslc, th * 512:(th + 1) * 512],
                          r=['wup', 'twd'], w=[('pj', th)])
                em.act(lw[:], pj[:], AF.Sigmoid, r=PJ, w=['lwraw'], bias=self.vcol('w0', dr * 4 + p))
                yield
                em.ts('dve', lw[:], lw[:], LWS, None, ALU.mult, None, r=['lwraw'], w=['r_lw'])
                yield
                em.memset('pool', Gp[:, 0:1], 0.0, ['r_Gp'])
                yield
                _decay(self, em, lw[:], Gp, Dd[:], rev, 'r')
                yield
                em.tt('pool', Dp[:], Dd[:], lw[:], ALU.subtract, r=['r_D', 'r_lw'], w=['Dp'])
                yield
                em.act(eD[:], Dd[:], AF.Exp, r=['r_D'], w=[('eDr', dr)])
                yield
                em.act(enD[:, 0:T], Dd[:], AF.Exp, r=['r_D'], w=['enDr'], scale=-1.0)
                yield
                em.act(Dp[:], Dp[:], AF.Exp, r=['Dp'], w=['Dp'])
                yield
                t3 = lambda ap: ap.rearrange("p (t l) -> p t l", l=128)
                em.tt('dve', kr[:, :, 0:128], t3(kk_p[:]), t3(Dp[:]), ALU.mult, r=['kk_p', 'Dp'], w=[('kr', dr)])
                yield
                em.tt('dve', kr[:, :, 128:256], t3(r_p[:]), t3(eD[:]), ALU.mult, r=['r_p', ('eDr', dr)], w=[('kr', dr)])
                yield
                em.tt('pool', be[:], b_p[:], enD[:, 0:T], ALU.mult, r=['b_p', 'enDr'], w=[('be', dr)])
                yield
                em.tt('pool', ta[:], kt_p[:], enD[:, 0:T], ALU.mult, r=['kt_p', 'enDr'], w=[('ta', dr)])
                yield

            def dir_tiles(dr, extra):
                rev = dr == 1
                kr, be, ta, eD = KR[dr], BE[dr], TA[dr], ED[dr]
                mk = m2[:, dr, :]
                mk3 = masks[:, 3 - dr, :]
                def tchain(slot, ti):
                    tt = 7 - ti if rev else ti
                    tl = slice(tt * 128, (tt + 1) * 128)
                    bb = slot
                    A2, BC = pA2[slot], pBC[slot]
                    kA = [('pA', slot, 0), ('pA', slot, 1)]
                    kB = [('pB', slot, 0), ('pB', slot, 1)]
                    m1, m2_, ya, yt, qq = M1b[slot], M2b[slot], YA[slot], YT_[slot], QQ[slot]
                    rx, xs, ra, rb, gt, z1, zt = rxb[slot], xsb[slot], RAb[slot], RBb[slot], GTb[slot], Z1b[slot], ztb[slot]
                    K_ = lambda n: ('t', slot, n)
                    zi0 = dr * 8 + 2 * p
                    HR = [slice(0, 64), slice(64, 128)]
                    v2 = lambda ap, n: ap.rearrange("p (h n) -> p h n", n=n)
                    vh = lambda ap: ap.rearrange("p (h n) -> p h n", n=512)
                    pTs = BC[:, 0:512].bitcast(BF16)
                    em.tr(pTs[:, 0:128], kr[:, tt, 0:128], self.ident_b[:], r=[('kr', dr)], w=[kB[0]])
                    em.tr(pTs[:, 128:256], be[:, tl], self.ident_b[:], r=[('be', dr)], w=[kB[0]])
                    em.tr(pTs[:, 256:384], ta[:, tl], self.ident_b[:], r=[('ta', dr)], w=[kB[0]])
                    em.act(nktk[bb][:], pTs[:, 0:128], AF.Identity, r=[kB[0]], w=[('nktk', bb)], scale=-1.0)
                    em.cp('dve', btk[bb][:], pTs[:, 128:256], r=[kB[0]], w=[('btk', bb)])
                    em.cp('act', ttk[bb][:], pTs[:, 256:384], r=[kB[0]], w=[('ttk', bb)])
                    for hh in range(2):
                        em.mm(A2[:, hh * 512:hh * 512 + 256], be[HR[hh], tl], kr[HR[hh], tt, :], r=[('be', dr), ('kr', dr)], w=[kA[hh]])
                    em.tt('dve', m1[:], vh(A2[:])[:, :, 0:256], mk.unsqueeze(1).broadcast_to([128, 2, 256]), ALU.mult, r=kA + ['m2'], w=[K_('m1')])
                    for hh in range(2):
                        em.mm(BC[:, hh * 512:hh * 512 + 128], kr[HR[hh], tt, 0:128], be[HR[hh], tl], r=[('kr', dr), ('be', dr)], w=[kB[hh]])
                    em.tt('dve', ya[0][:], vh(BC[:])[:, :, 0:128], mk3.unsqueeze(1).broadcast_to([128, 2, 128]), ALU.mult,
                          r=kB + ['masks'], w=[K_('yy0')])
                    yield
                    for hh in range(2):
                        em.mm(A2[:, hh * 512:hh * 512 + 256], ta[HR[hh], tl], kr[HR[hh], tt, :], r=[('ta', dr), ('kr', dr)], w=[kA[hh]])
                    em.tt('dve', m2_[:], vh(A2[:])[:, :, 0:256], mk.unsqueeze(1).broadcast_to([128, 2, 256]), ALU.mult, r=kA + ['m2'], w=[K_('m2')])
                    em.tt('pool', qq[0][:], idf[:].unsqueeze(1).broadcast_to([128, 2, 128]), m1[:, :, 0:128], ALU.subtract, r=[K_('m1')], w=[K_('qq0')])
                    yield
                    for hh in range(2):
                        em.mm(BC[:, 768 + hh * 64:832 + hh * 64], m2_[:, hh, 0:128], vtok[:, tt, hh * 64:(hh + 1) * 64], r=[K_('m2'), ('vtok', tt)], w=[kB[1]])
                    em.act(rx[:, :, 64:128], v2(BC[:, 768:896], 64), AF.Identity, r=[kB[1]], w=[K_('rx')], scale=-1.0)
                    em.cp('pool', rx[:, :, 0:64], v2(nktk[bb][:], 64), r=[('nktk', bb)], w=[K_('rx')])
                    yi, qi = 0, 0
                    for j in range(6):
                        yn = 1 - yi
                        for hh in range(2):
                            Yc = ya[yi][:, hh, :]
                            YTc = m1[:, hh, 0:128] if j == 0 else yt[yi][:, hh, :]
                            rk = [K_('yy%d' % yi)] + ([K_('m1')] if j == 0 else [])
                            if j < 5:
                                em.mm(BC[:, hh * 256:hh * 256 + 128], YTc, Yc, r=rk, w=[kB[0]])
                                if j < 4:
                                    em.mm(BC[:, hh * 256 + 128:hh * 256 + 256], Yc, YTc, r=rk, w=[kB[0]])
                            if j >= 1:
                                em.mm(BC[:, 512 + hh * 128:640 + hh * 128], Yc, qq[qi][:, hh, :], r=[K_('yy%d' % yi), K_('qq%d' % qi)], w=[kB[1]])
                        if j < 5:
                            em.cp('act', ya[yn][:], v2(BC[:, 0:512], 256)[:, :, 0:128], r=[kB[0]], w=[K_('yy%d' % yn)])
                            if j < 4:
                                em.cp('dve', yt[yn][:], v2(BC[:, 0:512], 256)[:, :, 128:256], r=[kB[0]], w=[K_('yy%d' % yn)])
                        if j >= 1:
                            em.tt('dve', qq[1 - qi][:], v2(BC[:, 512:768], 128), qq[qi][:], ALU.add, r=[kB[1], K_('qq%d' % qi)], w=[K_('qq%d' % (1 - qi))])
                            qi = 1 - qi
                        yi = yn
                        yield
                    for hh in range(2):
                        em.mm(BC[:, 512 + hh * 128:640 + hh * 128], qq[qi][:, hh, :], rx[:, hh, :], r=[K_('qq%d' % qi), K_('rx')], w=[kB[1]])
                    em.cp('act', xs[:], v2(BC[:, 512:768], 128), r=[kB[1]], w=[K_('xs')])
                    yield
                    for hh in range(2):
                        em.mm(BC[0:64, 768 + hh * 128:896 + hh * 128], xs[:, hh, 0:64], m1[:, hh, 128:256], r=[K_('xs'), K_('m1')], w=[kB[1]])
                    for hh in range(2):
                        em.tt('dve', ra[:, hh, 0:64], BC[0:64, 768 + hh * 128:832 + hh * 128], kr[HR[hh], tt, 128:192], ALU.add, r=[kB[1], ('kr', dr)], w=[K_('ra')])
                        em.tt('dve', rb[:, hh, 64:128], BC[0:64, 832 + hh * 128:896 + hh * 128], kr[HR[hh], tt, 192:256], ALU.add, r=[kB[1], ('kr', dr)], w=[K_('rb')])
                    for hh in range(2):
                        em.mm(A2[:, 256 + hh * 64:320 + hh * 64], m1[:, hh, 128:256], xs[:, hh, 64:128], r=[K_('m1'), K_('xs')], w=[kA[0]],
                              start=(hh == 0), stop=False, sgc=True)
                        em.mm(A2[:, 256 + hh * 64:320 + hh * 64], m2_[:, hh, 128:256], vtok[:, tt, hh * 64:(hh + 1) * 64], r=[K_('m2'), ('vtok', tt)], w=[kA[0]],
                              start=False, stop=False, sgc=True)
                    for hh in range(2):
                        for c in range(2):
                            cr = slice(c * 64, (c + 1) * 64)
                            o_ = c * 512 + hh * 64
                            em.mm(BC[0:64, o_:o_ + 64], xs[cr, hh, 0:64], btk[bb][cr, hh * 64:(hh + 1) * 64], r=[K_('xs'), ('btk', bb)], w=[kB[c]])
                    g4 = lambda ap: ap.rearrange("p c (h e) -> p c h e", e=64)
                    em.tt('dve', g4(gt[:]), g4(vh(BC[0:64, :])[:, :, 0:128]),
                          idf[0:64, 0:64].unsqueeze(1).unsqueeze(1).broadcast_to([64, 2, 2, 64]), ALU.add, r=kB, w=[K_('gt')])
                    yield
                    for hh in range(2):
                        for c in range(2):
                            cr = slice(c * 64, (c + 1) * 64)
                            o_ = c * 512 + 128 + hh * 64
                            em.mm(BC[0:64, o_:o_ + 64], btk[bb][cr, hh * 64:(hh + 1) * 64], xs[cr, hh, 64:128], r=[K_('xs'), ('btk', bb)], w=[kB[c]],
                                  start=True, stop=False, sgc=True)
                            em.mm(BC[0:64, o_:o_ + 64], ttk[bb][cr, hh * 64:(hh + 1) * 64], vtok[cr, tt, hh * 64:(hh + 1) * 64],
                                  r=[('ttk', bb), ('vtok', tt)], w=[kB[c]], start=False, stop=True, sgc=True)
                    em.cp('act', z1[:], vh(BC[0:64, :])[:, :, 128:256], r=kB, w=[K_('z1')])
                    yield
                    order = (1, 0) if rev else (0, 1)
                    for oi, c in enumerate(order):
                        for hh in range(2):
                            zi = zi0 + hh
                            em.mm(BC[0:64, 256 + hh * 64:320 + hh * 64], gt[:, c, hh * 64:(hh + 1) * 64], zst[:, zi, :], r=[K_('gt'), ('zst', zi)], w=[kB[0]])
                        for hh in range(2):
                            zi = zi0 + hh
                            Rh = ra if c == 0 else rb
                            em.mm(A2[:, 256 + hh * 64:320 + hh * 64], Rh[:, hh, :], zst_b[:, zi, :], r=[K_('ra'), K_('rb'), ('zstb', zi)], w=[kA[0]],
                                  start=False, stop=(oi == 1), sgc=True)
                        em.tt('dve', zt[:], v2(BC[0:64, 256:384], 64), v2(z1[:, c, :], 64), ALU.add, r=[kB[0], K_('z1')], w=[K_('zt')])
                        cg = tt * 2 + c
                        ecol = cg * 64 + (0 if rev else 63)
                        seg_end = (cg % 4 == 0) if rev else (cg % 4 == 3)
                        for hh in range(2):
                            zi = zi0 + hh
                            head = 2 * p + hh
                            em.ts('dve', zst[:, zi, :], zt[:, hh, :], eD[HR[hh], ecol:ecol + 1], None, ALU.mult, None, r=[K_('zt'), ('eDr', dr)], w=[('zst', zi)])
                            if seg_end:
                                seg = cg // 4
                                sg = sstg[hh]
                                sk = ('sstg', hh)
                                em.tr(BC[0:64, 640 + hh * 64:704 + hh * 64], zst[:, zi, :], idf[0:64, 0:64], r=[('zst', zi)], w=[kB[1]])
                                em.cp('act', sg[:], BC[0:64, 640 + hh * 64:704 + hh * 64], r=[kB[1]], w=[sk])
                                em.dma('sp', self.o_sr_d[seg, dr, head], sg[:], r=[sk], w=[('o_sr', seg, dr, head)])
                                last = (cg == 0) if rev else (cg == 15)
                                if not last:
                                    em.ts('dve', zst[:, zi, :], zst[:, zi, :], self.vcol('keep')[0:64, :], None, ALU.mult, None,
                                          r=[('zst', zi)], w=[('zst', zi)])
                        em.cp('pool', zst_b[:, zi0:zi0 + 2, :], zst[:, zi0:zi0 + 2, :], r=[('zst', zi0), ('zst', zi0 + 1)],
                              w=[('zstb', zi0), ('zstb', zi0 + 1)])
                        yield
                    ocols = slice(p * 128, (p + 1) * 128)
                    if dr == 0:
                        em.cp('act', osum[:, tt, ocols], A2[:, 256:384], r=[kA[0]], w=[('osum', tt, 2 * p), ('osum', tt, 2 * p + 1)])
                    else:
                        em.tt('dve', osum[:, tt, ocols], A2[:, 256:384], osum[:, tt, ocols], ALU.add,
                              r=[kA[0], ('osum', tt, 2 * p), ('osum', tt, 2 * p + 1)], w=[('osum', tt, 2 * p), ('osum', tt, 2 * p + 1)])

                NSTART = 2
                active = []
                nxt = 0
                while active or nxt < 8:
                    if nxt < 8 and len(active) < 2 and (not active or active[0][1] >= NSTART):
                        active.append([tchain(nxt % 2, nxt), 0])
                        nxt += 1
                    for ent in list(active):
                        try:
                            next(ent[0])
                            ent[1] += 1
                        except StopIteration:
                            active.remove(ent)
                    if extra is not None:
                        try:
                            next(extra)
                        except StopIteration:
                            extra = None
                if extra is not None:
                    for _ in extra:
                        pass
            for _ in dir_prep(0):
                pass
            dir_tiles(0, dir_prep(1))
            dir_tiles(1, None)
        P.flush()
        e2b.close()
        gtok = sbf("gtok", [128, 8, 512], BF16)
        for tt in range(8):
            em.mm(pT[:], sgd[:, tt * 128:(tt + 1) * 128], gup[:], r=['sgd', 'gup'], w=['pT'])
            em.cp('act', gtok[:, tt, :], pT[:], r=['pT'], w=[('gtok', tt)])
        mean = sbf("mean", [128, 8, 8]); var = sbf("var", [128, 8, 8]); sq2 = sbf("sq2", [128, 512]); cen = sbf("cen", [128, 8, 512])
        h4 = lambda ap: ap.rearrange("p (h e) -> p h e", e=64)
        for tt in range(8):
            OK = [('osum', tt, h) for h in range(8)]
            P.op('dve', lambda tt=tt: nc.vector.reduce_sum(out=mean[:, tt, :], in_=h4(osum[:, tt, :]), axis=AX.X), r=OK, w=[('mean', tt)])
            em.ts('dve', mean[:, tt, :], mean[:, tt, :], 1.0 / 64, None, ALU.mult, None, r=[('mean', tt)], w=[('mean', tt)])
            em.tt('dve', h4(cen[:, tt, :]), h4(osum[:, tt, :]), mean[:, tt, :].unsqueeze(2).broadcast_to([128, 8, 64]), ALU.subtract,
                  r=OK + [('mean', tt)], w=[('cen', tt)])
            em.tt('pool', sq2[:], cen[:, tt, :], cen[:, tt, :], ALU.mult, r=[('cen', tt)], w=['sq2'])
            P.op('dve', lambda tt=tt: nc.vector.reduce_sum(out=var[:, tt, :], in_=h4(sq2[:]), axis=AX.X), r=['sq2'], w=[('var', tt)])
        VK = [('var', tt) for tt in range(8)]
        em.act(var[:], var[:], AF.Ln, r=VK, w=['rstdr'], scale=1.0 / 64, bias=self.epsc[:, 2:3])
        em.act(var[:], var[:], AF.Exp, r=['rstdr'], w=['rstdr'], scale=-0.5)
        for tt in range(8):
            c3 = h4(cen[:, tt, :])
            em.tt('dve', c3, c3, var[:, tt, :].unsqueeze(2).broadcast_to([128, 8, 64]), ALU.mult, r=[('cen', tt), 'rstdr'], w=[('cen', tt)])
            em.tt('pool', cen[:, tt, :], cen[:, tt, :], bv1[:, 128:640], ALU.mult, r=[('cen', tt)], w=[('cen', tt)])
            em.tt('pool', cen[:, tt, :], cen[:, tt, :], bv1[:, 640:1152], ALU.add, r=[('cen', tt)], w=[('cen', tt)])
            em.tt('dve', cen[:, tt, :], cen[:, tt, :], ocat[:, tt, 512:1024], ALU.add, r=[('cen', tt), ('ocat', tt, 1)], w=[('cen', tt)])
            em.tt('pool', ocat[:, tt, 512:1024], cen[:, tt, :], gtok[:, tt, :], ALU.mult, r=[('cen', tt), ('gtok', tt)], w=[('ocat', tt, 1)])
        self.dump('ocat_r', ocat, [128, 8, D], [])
        P.flush()


Builder.odd_rwkv = _odd_rwkv


GRID_W = 64
def rope_tables(prompt):
    T = 1024
    if prompt:
        return np.stack([np.ones((128, T), np.float32), np.zeros((128, T), np.float32)])
    rows = (np.arange(T) // GRID_W).astype(np.float32)
    cols = (np.arange(T) % GRID_W).astype(np.float32)
    half = 16
    freq = np.power(np.float32(10000.0), -np.arange(half, dtype=np.float32) / half).astype(np.float32)
    cos = np.zeros((64, T), np.float32); sin = np.zeros((64, T), np.float32)
    for d in range(64):
        pos = rows if d < 32 else cols
        w = d % 32
        i = w % 16
        ang = (pos * freq[i]).astype(np.float32)
        cos[d] = np.cos(ang)
        sin[d] = -np.sin(ang) if w < 16 else np.sin(ang)
    return np.stack([np.concatenate([cos, cos]), np.concatenate([sin, sin])]).astype(np.float32)

def partner64():
    p = np.arange(64)
    w = p % 32
    return np.where(w < 16, p + 16, p - 16)

def ev_wx(ev_w_in):
    W = ev_w_in
    pr = partner64()
    qa = W[:, 0:512]; ka = W[:, 512:1024]; qb = W[:, 1536:2048]; kb = W[:, 2048:2176]
    hb_order = [0, 4, 1, 5, 2, 6, 3, 7]
    qbp = np.concatenate([qb[:, h * 64:(h + 1) * 64] for h in hb_order], axis=1)
    def sw(M):
        n = M.shape[1] // 64
        return np.concatenate([M[:, h * 64:(h + 1) * 64][:, pr] for h in range(n)], axis=1)
    return np.ascontiguousarray(np.concatenate([qbp, sw(qa), sw(ka), sw(qbp), sw(kb)], axis=1))

def amask(prompt):
    M = np.zeros((128, 48), np.float32)
    if prompt:
        for s in range(4):
            for kb in range(12):
                ok = kb >= 4 and (kb - 4) // 2 == s
                if not ok:
                    M[:, s * 12 + kb] = -30000.0
    return M

def host_vecs(vp, inp, cond, prompt):
    keep = 0.0 if prompt else 1.0
    V = np.zeros((128, vp.n), np.float32)
    def put(name, arr):
        c0, n = vp.cols[name]
        assert arr.shape == (128, n), (name, arr.shape, n)
        V[:, c0:c0+n] = arr
    put('cond', fm(cond))
    put('km1', np.full((128,1), keep - 1.0, np.float32))
    put('keep', np.full((128,1), keep, np.float32))
    for l in range(2):
        put('ada_b%d'%l, fm(inp['ada_b'][l]))
        put('nmg%d'%l, fm(inp['norm_mix_g'][l]))
        put('nfg%d'%l, fm(inp['norm_ffn_g'][l]))
        for i in range(3):
            put('cw%d_%d'%(i,l), fm(inp['ffn_conv_w'][l, i]))
        put('cb_%d'%l, fm(inp['ffn_conv_b'][l]))
    put('fng', fm(inp['final_norm_g']))
    pr = partner64()
    gq = inp['b_q_norm_g'][0]; gk = inp['b_k_norm_g'][0]
    put('gq', np.tile(gq, 2)[:, None]); put('gq_sw', np.tile(gq[pr], 2)[:, None])
    put('gk', np.tile(gk, 2)[:, None]); put('gk_sw', np.tile(gk[pr], 2)[:, None])
    put('amask', amask(prompt))
    return V

def bvec(inp):
    b = np.concatenate([inp['a_subln_g'][0], inp['b_k_norm_g'][0], inp['a_lambda'][0].reshape(-1)]).astype(np.float32)
    return np.ascontiguousarray(np.broadcast_to(b[None, :], (128, 448)))

def core_inputs(vp, inp, core):
    prompt = core < 4
    m = {}
    if prompt:
        m['x'] = np.ascontiguousarray(inp['x_prompt'][4 * core:4 * core + 4].reshape(1024, 1024))
        cond = inp['c_ctx']
        m['ctx_ak'] = np.zeros((4, 512, 128), np.float32); m['ctx_av'] = np.zeros((4, 512, 128), np.float32)
        m['ctx_bk'] = np.zeros((2, 512, 64), np.float32); m['ctx_bv'] = np.zeros((2, 512, 64), np.float32)
    else:
        b = core - 4
        m['x'] = np.ascontiguousarray(inp['x_sample'][b])
        cond = inp['c'][b]
        m['ctx_ak'] = np.ascontiguousarray(inp['cache_a_k'][b, 0]); m['ctx_av'] = np.ascontiguousarray(inp['cache_a_v'][b, 0])
        m['ctx_bk'] = np.ascontiguousarray(inp['cache_b_k'][b, 0]); m['ctx_bv'] = np.ascontiguousarray(inp['cache_b_v'][b, 0])
    m['vecs'] = host_vecs(vp, inp, cond, prompt)
    m['ident'] = np.eye(128, dtype=np.float32)
    m['rope'] = rope_tables(prompt)
    m['bvec'] = bvec(inp)
    m['ada_w'] = inp['ada_w']; m['ffn_w_up'] = inp['ffn_w_up']; m['ffn_w_down'] = inp['ffn_w_down']
    m['ev_w_in'] = np.ascontiguousarray(inp['ev_w_in'][0]); m['ev_wx'] = ev_wx(inp['ev_w_in'][0])
    m['ev_w_out'] = np.ascontiguousarray(inp['ev_w_out'][0])
    return m


def masks_const():
    idx = np.arange(128)
    blk = (idx[:, None] // 64) == (idx[None, :] // 64)
    s, t = idx[:, None], idx[None, :]
    M = np.stack([blk & (s <= t), blk & (s >= t), blk & (s < t), blk & (s > t)], axis=1)
    return np.ascontiguousarray(M.astype(np.float32))


def odd_inputs(vp, inp, core, m):
    prompt = core < 4
    V = m['vecs']
    def put(name, arr):
        c0, n = vp.cols[name]
        assert arr.shape == (128, n), (name, arr.shape, n)
        V[:, c0:c0+n] = arr
    lb = inp['hgrn_lb_logits']
    put('lb0', fm(lb[:, 0, :].reshape(-1)))
    put('lb1', fm(lb[:, 1, :].reshape(-1)))
    mu = inp['rwkv_mu'][0]
    MU = np.zeros((128, 15), np.float32)
    MU[:, 0:13] = fm(mu[0:1664])
    MU[0:64, 13] = mu[1664:1728]
    MU[:, 14] = mu[1728:1856]
    put('mu', MU)
    put('w0', fm(inp['rwkv_w0'][0].reshape(-1)))
    put('a0', fm(inp['rwkv_a0'][0]))
    put('k_k', fm(inp['rwkv_k_k'][0]))
    put('k_a', fm(inp['rwkv_k_a'][0]))
    put('r_k', fm(inp['rwkv_r_k'][0]))
    m['od_w_in'] = np.ascontiguousarray(inp['od_w_in'][0])
    m['od_w_out'] = np.ascontiguousarray(inp['od_w_out'][0])
    m['masks'] = masks_const()
    b = np.concatenate([inp['hgrn_norm_g'][0], inp['rwkv_ln_g'][0], inp['rwkv_ln_b'][0], np.zeros(128, np.float32)]).astype(np.float32)
    m['bv1'] = np.ascontiguousarray(np.broadcast_to(b[None, :], (128, 1280)))
    if prompt:
        m['st_h'] = np.zeros((2, 4, 128, 128), np.float32)
        m['st_r'] = np.zeros((2, 8, 64, 64), np.float32)
    else:
        bb = core - 4
        m['st_h'] = np.ascontiguousarray(inp['state_hgrn'][bb, 0])
        m['st_r'] = np.ascontiguousarray(inp['state_rwkv'][bb, 0])
    m['w_up'] = np.ascontiguousarray(inp['rwkv_w_up'][0].reshape(128, 512))
    m['a_up'] = np.ascontiguousarray(inp['rwkv_a_up'][0])
    m['g_up'] = np.ascontiguousarray(inp['rwkv_g_up'][0])
    return m


_BUILT = {}


def kernel(**inputs):
    inp = {k: np.asarray(v) for k, v in inputs.items()}
    B = Builder(debug=(), layers=(0, 1))
    B.build()
    maps = []
    for core in range(8):
        m = core_inputs(B.vp, inp, core)
        m = odd_inputs(B.vp, inp, core, m)
        maps.append(m)
    res = run_bass_kernel_spmd(B.nc, maps, core_ids=list(range(8)))
    R = res.results
    y_prompt = np.zeros((16, 256, 1024), np.float32)
    y_sample = np.zeros((4, 1024, 1024), np.float32)
    ak = np.zeros((16, 1, 4, 256, 128), np.float32)
    av = np.zeros((16, 1, 4, 256, 128), np.float32)
    bk = np.zeros((16, 1, 2, 256, 64), np.float32)
    bv = np.zeros((16, 1, 2, 256, 64), np.float32)
    sh = np.zeros((16, 1, 2, 4, 128, 128), np.float32)
    sr = np.zeros((16, 1, 2, 8, 64, 64), np.float32)
    for c in range(4):
        r = R[c]
        y_prompt[4 * c:4 * c + 4] = np.asarray(r['y']).reshape(4, 256, 1024)
        ak[4 * c:4 * c + 4, 0] = np.asarray(r['o_ak'])
        av[4 * c:4 * c + 4, 0] = np.asarray(r['o_av'])
        bk[4 * c:4 * c + 4, 0] = np.asarray(r['o_bk'])
        bv[4 * c:4 * c + 4, 0] = np.asarray(r['o_bv'])
        sh[4 * c:4 * c + 4, 0] = np.asarray(r['o_sh'])
        sr[4 * c:4 * c + 4, 0] = np.asarray(r['o_sr'])
    for b in range(4):
        y_sample[b] = np.asarray(R[4 + b]['y'])
    return (y_prompt, y_sample, ak, av, bk, bv, sh, sr)
```
